# Optimizing a Trainium2 kernel written in Bass

```python
import jax, jax.numpy as jnp
from jax import lax
import numpy as np

D_MODEL = 2048
BATCH = 2
SEQ = 8192
DEPTH = 1
DEC_BATCH = 8
DEC_SEQ = 4096
PAST_LEN = 128

MIX_WIDTH = D_MODEL
HEAD_DIM = 64
ATT_WIDTH = MIX_WIDTH // 2
ATT_HEADS = ATT_WIDTH // HEAD_DIM
RWKV_WIDTH = MIX_WIDTH - ATT_WIDTH
RWKV_HEAD_DIM = 64
RWKV_HEADS = RWKV_WIDTH // RWKV_HEAD_DIM
DECAY_RANK = 64
ICLR_RANK = 64
GATE_RANK = 128
D_FF = 5632
WINDOWS = (128, 512, 2048)
DILATIONS = (1, 4, 16)
NORM_EPS = 1e-6
RWKV_LN_EPS = 64e-5
NEG_INF = -1e30
N_ATT_COLS = 3 * ATT_WIDTH
RWKV_COL_SIZES = (RWKV_WIDTH, RWKV_WIDTH, RWKV_WIDTH, DECAY_RANK, DECAY_RANK, ICLR_RANK, ICLR_RANK, GATE_RANK)
N_RWKV_COLS = sum(RWKV_COL_SIZES)
N_IN_COLS = N_ATT_COLS + N_RWKV_COLS

kernel_name = 'hymba_longnet_rwkv7_macaron_encoder'


def _rmsnorm(x, g):
    xf = x.astype(jnp.float32)
    y = xf * lax.rsqrt(jnp.mean(xf * xf, axis=-1, keepdims=True) + NORM_EPS) * g.astype(jnp.float32)
    return y.astype(x.dtype)


def _swiglu(x, w_gate, w_up, w_down):
    return (jax.nn.silu(x @ w_gate) * (x @ w_up)) @ w_down


def _to_residue(x, d):
    B, T = x.shape[0], x.shape[1]
    rest = x.shape[2:]
    return x.reshape((B, T // d, d) + rest).swapaxes(1, 2).reshape((B * d, T // d) + rest)


def _from_residue(x, B, d):
    L = x.shape[1]
    rest = x.shape[2:]
    return x.reshape((B, d, L) + rest).swapaxes(1, 2).reshape((B, d * L) + rest)


def _band_attention(q, k, v, slopes, dil, half):
    n, L, H, E = q.shape
    W = half
    nb = -(-L // W)
    Lp = nb * W
    pad = Lp - L
    qb = jnp.pad(q * (E ** -0.5), ((0, 0), (0, pad), (0, 0), (0, 0))).reshape(n, nb, W, H, E)
    kp = jnp.pad(k, ((0, 0), (W, pad + W), (0, 0), (0, 0))).reshape(n, nb + 2, W, H, E)
    vp = jnp.pad(v, ((0, 0), (W, pad + W), (0, 0), (0, 0))).reshape(n, nb + 2, W, H, E)
    kw = jnp.concatenate([kp[:, :-2], kp[:, 1:-1], kp[:, 2:]], axis=2)
    vw = jnp.concatenate([vp[:, :-2], vp[:, 1:-1], vp[:, 2:]], axis=2)
    qpos = jnp.arange(nb)[:, None] * W + jnp.arange(W)[None, :]
    kpos = jnp.arange(nb)[:, None] * W - W + jnp.arange(3 * W)[None, :]
    rel = kpos[:, None, :] - qpos[:, :, None]
    valid = (jnp.abs(rel) <= W) & (kpos[:, None, :] >= 0) & (kpos[:, None, :] < L)
    dist = (jnp.abs(rel) * dil).astype(jnp.float32)
    logits = jnp.einsum('nbqhe,nbkhe->nbhqk', qb, kw).astype(jnp.float32)
    logits = logits - slopes[:, None, None] * dist[:, None]
    logits = jnp.where(valid[:, None], logits, NEG_INF)
    m = jnp.max(logits, axis=-1, keepdims=True)
    p = jnp.exp(logits - m)
    l = jnp.sum(p, axis=-1, keepdims=True)
    o = jnp.einsum('nbhqk,nbkhe->nbqhe', p, vw) / jnp.moveaxis(l, 2, 3)
    lse = jnp.moveaxis((m + jnp.log(l))[..., 0], 2, 3)
    o = o.reshape(n, Lp, H, E)[:, :L]
    lse = lse.reshape(n, Lp, H)[:, :L]
    return o, lse


def _dilated_alibi_attention(q, k, v):
    B, T, H, E = q.shape
    slopes = jnp.exp2(-8.0 * jnp.arange(1, H + 1, dtype=jnp.float32) / H)
    outs, lses = [], []
    for window, dil in zip(WINDOWS, DILATIONS):
        half = window // (2 * dil)
        o, lse = _band_attention(_to_residue(q, dil), _to_residue(k, dil), _to_residue(v, dil), slopes, dil, half)
        outs.append(_from_residue(o, B, dil))
        lses.append(_from_residue(lse, B, dil))
    wts = jax.nn.softmax(jnp.stack(lses), axis=0)
    return jnp.sum(wts[..., None] * jnp.stack(outs), axis=0)


def _centred_shift(z, mu_prev, mu_next):
    z_prev = jnp.pad(z, ((0, 0), (1, 0), (0, 0)))[:, :-1]
    z_next = jnp.pad(z, ((0, 0), (0, 1), (0, 0)))[:, 1:]
    return z + mu_prev * (z_prev - z) + mu_next * (z_next - z)


def _wkv7_scan(r, w, k, v, a, b, reverse):
    B, T, H, E = r.shape

    def step(S, inp):
        r_t, w_t, k_t, v_t, a_t, b_t = inp
        sa = jnp.einsum('bhij,bhj->bhi', S, a_t)
        S = S * w_t[:, :, None, :] + sa[..., None] * b_t[:, :, None, :] + v_t[..., None] * k_t[:, :, None, :]
        y = jnp.einsum('bhij,bhj->bhi', S, r_t)
        return S, y

    xs = tuple(jnp.moveaxis(t, 1, 0) for t in (r, w, k, v, a, b))
    S0 = jnp.zeros((B, H, E, E), jnp.float32)
    _, ys = lax.scan(step, S0, xs, reverse=reverse)
    return jnp.moveaxis(ys, 0, 1)


def _rwkv7_bidir(z, mu_prev, mu_next, w0_f, w2_f, w0_b, w2_b, a0_f, a2_f, a0_b, a2_b, g2, k_k, k_a, r_k, ln_x_w, ln_x_b):
    B, T, _ = z.shape
    z = _centred_shift(z.astype(jnp.float32), mu_prev, mu_next)
    splits = list(np.cumsum(RWKV_COL_SIZES)[:-1])
    r, k, v, wdf, wdb, adf, adb, gd = jnp.split(z, splits, axis=-1)

    def decay(wd, w0, w2):
        w = -jax.nn.softplus(-(w0 + jnp.tanh(wd) @ w2)) - 0.5
        return jnp.exp(-jnp.exp(w))

    def heads(t):
        return t.reshape(B, T, RWKV_HEADS, RWKV_HEAD_DIM)

    dec_f = decay(wdf, w0_f, w2_f)
    dec_b = decay(wdb, w0_b, w2_b)
    a_f = jax.nn.sigmoid(a0_f + adf @ a2_f)
    a_b = jax.nn.sigmoid(a0_b + adb @ a2_b)
    g = jax.nn.sigmoid(gd) @ g2
    kk = heads(k * k_k)
    kk = kk / jnp.maximum(jnp.sqrt(jnp.sum(kk * kk, axis=-1, keepdims=True)), 1e-12)
    k_f = heads(k * (1.0 + (a_f - 1.0) * k_a))
    k_b = heads(k * (1.0 + (a_b - 1.0) * k_a))
    rh, vh = heads(r), heads(v)
    y_f = _wkv7_scan(rh, heads(dec_f), k_f, vh, -kk, kk * heads(a_f), reverse=False)
    y_b = _wkv7_scan(rh, heads(dec_b), k_b, vh, -kk, kk * heads(a_b), reverse=True)
    y = y_f + y_b
    mu = jnp.mean(y, axis=-1, keepdims=True)
    var = jnp.mean(jnp.square(y - mu), axis=-1, keepdims=True)
    y = ((y - mu) * lax.rsqrt(var + RWKV_LN_EPS)).reshape(B, T, RWKV_WIDTH) * ln_x_w + ln_x_b
    bonus = (jnp.sum(rh * k_f * r_k, axis=-1, keepdims=True) + jnp.sum(rh * k_b * r_k, axis=-1, keepdims=True)) * vh
    return (y + bonus.reshape(B, T, RWKV_WIDTH)) * g


def _layer(x, ffn1_norm, ffn1_gate, ffn1_up, ffn1_down, mix_norm, w_in, w_out, mu_prev, mu_next,
           w0_f, w2_f, w0_b, w2_b, a0_f, a2_f, a0_b, a2_b, g2, k_k, k_a, r_k, ln_x_w, ln_x_b,
           ffn2_norm, ffn2_gate, ffn2_up, ffn2_down):
    B, T, _ = x.shape
    x = x + 0.5 * _swiglu(_rmsnorm(x, ffn1_norm), ffn1_gate, ffn1_up, ffn1_down)
    proj = _rmsnorm(x, mix_norm) @ w_in
    q, k, v = jnp.split(proj[..., :N_ATT_COLS], 3, axis=-1)
    q, k, v = (t.reshape(B, T, ATT_HEADS, HEAD_DIM).astype(jnp.float32) for t in (q, k, v))
    att = _dilated_alibi_attention(q, k, v).reshape(B, T, ATT_WIDTH)
    rw = _rwkv7_bidir(proj[..., N_ATT_COLS:], mu_prev, mu_next, w0_f, w2_f, w0_b, w2_b,
                      a0_f, a2_f, a0_b, a2_b, g2, k_k, k_a, r_k, ln_x_w, ln_x_b)
    mixed = jnp.concatenate([att, rw], axis=-1).astype(x.dtype)
    x = x + mixed @ w_out
    x = x + 0.5 * _swiglu(_rmsnorm(x, ffn2_norm), ffn2_gate, ffn2_up, ffn2_down)
    return x


def setup_inputs(seed: int = 0) -> dict:
    key = jax.random.key(seed)
    ks = iter(jax.random.split(key, 40))
    f32 = jnp.float32

    def nrm(shape, scale):
        return jax.random.normal(next(ks), shape, f32) * scale

    def gain(shape):
        return 1.0 + 0.02 * jax.random.normal(next(ks), shape, f32)

    def unif(shape, lo, hi):
        return jax.random.uniform(next(ks), shape, f32, lo, hi)

    L = DEPTH
    return {
        'x_prompt': nrm((BATCH, SEQ, D_MODEL), 1.0),
        'x_sample': nrm((DEC_BATCH, DEC_SEQ, D_MODEL), 1.0),
        'ffn1_norm': gain((L, D_MODEL)),
        'ffn1_gate': nrm((L, D_MODEL, D_FF), D_MODEL ** -0.5),
        'ffn1_up': nrm((L, D_MODEL, D_FF), D_MODEL ** -0.5),
        'ffn1_down': nrm((L, D_FF, D_MODEL), D_FF ** -0.5),
        'mix_norm': gain((L, D_MODEL)),
        'w_in': nrm((L, D_MODEL, N_IN_COLS), D_MODEL ** -0.5),
        'w_out': nrm((L, MIX_WIDTH, D_MODEL), MIX_WIDTH ** -0.5),
        'mu_prev': unif((L, N_RWKV_COLS), 0.0, 0.5),
        'mu_next': unif((L, N_RWKV_COLS), 0.0, 0.5),
        'w0_f': unif((L, RWKV_WIDTH), -6.0, -1.0),
        'w2_f': nrm((L, DECAY_RANK, RWKV_WIDTH), 0.5 * DECAY_RANK ** -0.5),
        'w0_b': unif((L, RWKV_WIDTH), -6.0, -1.0),
        'w2_b': nrm((L, DECAY_RANK, RWKV_WIDTH), 0.5 * DECAY_RANK ** -0.5),
        'a0_f': nrm((L, RWKV_WIDTH), 0.5),
        'a2_f': nrm((L, ICLR_RANK, RWKV_WIDTH), 0.5 * ICLR_RANK ** -0.5),
        'a0_b': nrm((L, RWKV_WIDTH), 0.5),
        'a2_b': nrm((L, ICLR_RANK, RWKV_WIDTH), 0.5 * ICLR_RANK ** -0.5),
        'g2': nrm((L, GATE_RANK, RWKV_WIDTH), GATE_RANK ** -0.5),
        'k_k': 0.85 + nrm((L, RWKV_WIDTH), 0.05),
        'k_a': 1.0 + nrm((L, RWKV_WIDTH), 0.05),
        'r_k': nrm((L, RWKV_HEADS, RWKV_HEAD_DIM), 0.1),
        'ln_x_w': gain((L, RWKV_WIDTH)),
        'ln_x_b': nrm((L, RWKV_WIDTH), 0.01),
        'ffn2_norm': gain((L, D_MODEL)),
        'ffn2_gate': nrm((L, D_MODEL, D_FF), D_MODEL ** -0.5),
        'ffn2_up': nrm((L, D_MODEL, D_FF), D_MODEL ** -0.5),
        'ffn2_down': nrm((L, D_FF, D_MODEL), D_FF ** -0.5),
        'final_norm': gain((D_MODEL,)),
    }


def reference(x_prompt, x_sample, ffn1_norm, ffn1_gate, ffn1_up, ffn1_down, mix_norm, w_in, w_out,
              mu_prev, mu_next, w0_f, w2_f, w0_b, w2_b, a0_f, a2_f, a0_b, a2_b, g2, k_k, k_a, r_k,
              ln_x_w, ln_x_b, ffn2_norm, ffn2_gate, ffn2_up, ffn2_down, final_norm):
    layer_params = (ffn1_norm, ffn1_gate, ffn1_up, ffn1_down, mix_norm, w_in, w_out, mu_prev, mu_next,
                    w0_f, w2_f, w0_b, w2_b, a0_f, a2_f, a0_b, a2_b, g2, k_k, k_a, r_k, ln_x_w, ln_x_b,
                    ffn2_norm, ffn2_gate, ffn2_up, ffn2_down)

    def trunk(x):
        for layer in range(DEPTH):
            x = _layer(x, *[p[layer] for p in layer_params])
        return _rmsnorm(x, final_norm)

    y_prompt = trunk(x_prompt)
    y_sample = trunk(x_sample)
    return (y_prompt, y_sample)
```

```python
from contextlib import ExitStack
import numpy as np
import concourse.bass as bass
import concourse.mybir as mybir
from concourse.bass_utils import run_bass_kernel_spmd

F32 = mybir.dt.float32
BF16 = mybir.dt.bfloat16
ALU = mybir.AluOpType
AF = mybir.ActivationFunctionType

D = 2048
DC = 16
FF = 5632
FC = 44
TT = 512
NSEG = 2
APAD = 1024
NFM = 43
ZC = 27
C0 = 0.6065306597126334
NORM_EPS = 1e-6
LN_EPS = 64e-5

ENGS = ("pe", "dve", "act", "pool", "sp")
NDMASEM = 64
NHWSEM = 40


class Sched:
    def __init__(self, nc, es):
        self.nc = nc
        self.csem = {e: es.enter_context(nc.semaphore("c_" + e)) for e in ENGS}
        self.dsem = [es.enter_context(nc.semaphore("d%d" % i)) for i in range(NDMASEM)]
        self.dval = [0] * NDMASEM
        self.dnext = 0
        self.dnext_sw = 0
        self.count = {e: 0 for e in ENGS}
        self.prog = {e: [] for e in ENGS}
        self.waited = {e: {} for e in ENGS}
        self.lastw = {}
        self.reads = {}

    def _need(self, eng, tok, deps):
        if tok is None:
            return
        if tok[0] == 'e':
            _, e2, idx = tok
            if e2 == eng:
                if eng == 'pe':
                    return
                if idx <= self.count[eng] - 3:
                    return
            key = e2
        else:
            _, i, idx = tok
            key = ('d', i)
        if self.waited[eng].get(key, 0) >= idx:
            return
        if deps.get(key, 0) < idx:
            deps[key] = idx

    def _emit_waits(self, eng, deps):
        for key, idx in deps.items():
            sem = self.csem[key] if isinstance(key, str) else self.dsem[key[1]]
            self.prog[eng].append(('w', sem, idx))
            self.waited[eng][key] = idx

    def _deps(self, eng, reads, writes):
        deps = {}
        for k in reads:
            self._need(eng, self.lastw.get(k), deps)
        for k in writes:
            self._need(eng, self.lastw.get(k), deps)
            for t in self.reads.get(k, ()):
                self._need(eng, t, deps)
        return deps

    def _record(self, tok, reads, writes):
        for k in reads:
            self.reads.setdefault(k, []).append(tok)
        for k in writes:
            self.lastw[k] = tok
            self.reads[k] = []

    def op(self, eng, fn, reads=(), writes=()):
        deps = self._deps(eng, reads, writes)
        self._emit_waits(eng, deps)
        self.count[eng] += 1
        self.prog[eng].append(('o', fn, self.csem[eng], 1))
        tok = ('e', eng, self.count[eng])
        self._record(tok, reads, writes)
        return tok

    def mm(self, fn, reads=(), writes=(), last=True):
        return self.op('pe', fn, reads, writes)

    def dma(self, eng, fn, reads=(), writes=()):
        deps = self._deps(eng, reads, writes)
        if eng == 'pool':
            i = NHWSEM + self.dnext_sw
            self.dnext_sw = (self.dnext_sw + 1) % (NDMASEM - NHWSEM)
        else:
            i = self.dnext
            self.dnext = (self.dnext + 1) % NHWSEM
        if self.dval[i] > 0:
            self._need(eng, ('d', i, self.dval[i]), deps)
        self._emit_waits(eng, deps)
        self.dval[i] += 16
        self.prog[eng].append(('o', fn, self.dsem[i], 16))
        tok = ('d', i, self.dval[i])
        self._record(tok, reads, writes)
        return tok

    def barrier(self):
        for eng in ENGS:
            deps = {}
            for e2 in ENGS:
                if e2 != eng and self.count[e2] > 0:
                    self._need(eng, ('e', e2, self.count[e2]), deps)
            for i in range(NDMASEM):
                if self.dval[i] > 0:
                    self._need(eng, ('d', i, self.dval[i]), deps)
            self._emit_waits(eng, deps)

    def emit(self):
        prog = self.prog

        def run(engobj, items):
            for it in items:
                if it[0] == 'w':
                    engobj.wait_ge(it[1], it[2])
                else:
                    ins = it[1](engobj)
                    if it[2] is not None:
                        ins.then_inc(it[2], it[3])

        with self.nc.Block() as block:
            @block.sync
            def _(e):
                run(e, prog['sp'])

            @block.tensor
            def _(e):
                run(e, prog['pe'])

            @block.vector
            def _(e):
                run(e, prog['dve'])

            @block.scalar
            def _(e):
                run(e, prog['act'])

            @block.gpsimd
            def _(e):
                run(e, prog['pool'])


def build(SEG, debug=False, phases=('A', 'fix', 'att', 'rwkv', 'C'), RW_PAIRS=8, ATT_PAIRS=8):
    NTOK = NSEG * SEG
    NT = NTOK // TT
    TPS = SEG // TT
    SEGA = SEG + 2 * APAD
    SEGZ = SEG + 2
    nc = bass.Bass("TRN2", target_bir_lowering=False)
    es = ExitStack()
    S = Sched(nc, es)

    def din(name, shape, dt=F32):
        return nc.dram_tensor(name, list(shape), dt, kind="ExternalInput").ap()

    def dscr(name, shape, dt):
        if debug:
            return nc.dram_tensor(name, list(shape), dt, kind="ExternalOutput").ap()
        return nc.dram_tensor(name, list(shape), dt).ap()

    x_d = din("x", [NTOK, D])
    link_d = din("link", [128, 1])
    wgu_h = [din("wgu1", [FF, 4096]), din("wgu2", [FF, 4096])]
    wd_h = [din("wd1", [D, FF]), din("wd2", [D, FF])]
    winfm_h = din("winfm", [NFM * 128, 2048])
    wintm_h = din("wintm", [4 * 128, 4096])
    wo_h = din("wo", [D, D])
    gains_d = din("gains", [128, 4 * DC])
    mu_d = din("mu", [128, 2 * ZC])
    rwp_d = din("rwp", [128, 8 * 10])
    w2_d = din("w2", [128, 1024])
    a2_d = din("a2", [128, 1024])
    g2_d = din("g2", [128, 1024])
    etab_d = din("etab", [8, 128, 2 * 3 * 256])
    cst_d = din("cst", [128, 8 * 128])
    rmask_d = din("rmask", [128, 512])
    y_d = nc.dram_tensor("y", [NTOK, D], F32, kind="ExternalOutput").ap()

    wgu_b = [dscr("wgu1b", [FF, 4096], BF16), dscr("wgu2b", [FF, 4096], BF16)]
    wd_b = [dscr("wd1b", [D, FF], BF16), dscr("wd2b", [D, FF], BF16)]
    winfm_b = dscr("winfmb", [NFM * 128, 2048], BF16)
    wintm_b = dscr("wintmb", [4 * 128, 4096], BF16)
    wo_b = dscr("wob", [D, D], BF16)
    x1T_d = dscr("x1T", [D, NTOK], F32)
    qT_d = dscr("qT", [1024, NTOK], BF16)
    kT_d = dscr("kT", [1024, NSEG * SEGA], BF16)
    vatt_d = dscr("vatt", [NSEG * SEGA + 64, 16 * 65], BF16)
    zT_d = dscr("zT", [ZC * 128, NSEG * SEGZ], F32)
    ybT_d = dscr("ybT", [1024, NTOK], F32)
    mixT_d = dscr("mixT", [D, NTOK], BF16)

    def sb(name, shape, dt, stack=es):
        return stack.enter_context(nc.sbuf_tensor("s_" + name, list(shape), dt))

    PS = [es.enter_context(nc.psum_tensor("ps%d" % i, [128, 512], F32)) for i in range(8)]

    def dma(eng, out, in_, reads, writes, **kw):
        return S.dma(eng, lambda e, o=out, i=in_, k=kw: e.dma_start(out=o, in_=i, **k), reads, writes)

    cst = sb("cst", [128, 8 * 128], F32)
    cstb = sb("cstb", [128, 8 * 128], BF16)
    gains = sb("gains", [128, 4 * DC], F32)
    link = sb("link", [128, 1], F32)
    zeros = sb("zeros", [128, 1040], BF16)
    zerosf = sb("zerosf", [128, 128], F32)
    epsn = sb("epsn", [128, 1], F32)
    epsl = sb("epsl", [128, 1], F32)
    onesb = sb("onesb", [128, 128], BF16)
    dma('sp', cst[:], cst_d[:, :], [], ['cst'])
    dma('sp', gains[:], gains_d[:, :], [], ['gains'])
    dma('sp', link[:], link_d[:, :], [], ['link'])
    S.op('dve', lambda e: e.tensor_copy(out=cstb[:], in_=cst[:]), ['cst'], ['cstb'])
    S.op('pool', lambda e: e.memset(zeros[:], 0.0), [], ['zeros'])
    S.op('pool', lambda e: e.memset(zerosf[:], 0.0), [], ['zerosf'])
    S.op('pool', lambda e: e.memset(epsn[:], NORM_EPS), [], ['epsn'])
    S.op('pool', lambda e: e.memset(epsl[:], LN_EPS), [], ['epsl'])
    S.op('pool', lambda e: e.memset(onesb[:], 1.0), [], ['onesb'])
    ident = cst[:, 0:128]
    identb = cstb[:, 0:128]
    bones = cst[:, 128:256]
    m_su = cst[:, 256:384]
    m_sl = cst[:, 384:512]
    m_iu = cst[:, 512:640]
    m_il = cst[:, 640:768]

    def cast_rows(src, dst, r0, r1, key):
        dma('pool', dst[r0:r1, :], src[r0:r1, :], [], [key], max_dma_last_dim=4096)

    for j in range(0, FC, 2):
        cast_rows(wgu_h[0], wgu_b[0], j * 128, (j + 2) * 128, ('wgu0', j // 2))
    for m in range(DC):
        cast_rows(wd_h[0], wd_b[0], m * 128, (m + 1) * 128, ('wd0', m))
    for c in range(0, NFM, 4):
        cast_rows(winfm_h, winfm_b, c * 128, min(NFM, c + 4) * 128, ('winfm', c // 4))
    for g in range(0, 4, 2):
        cast_rows(wintm_h, wintm_b, g * 128, (g + 2) * 128, ('wintm', g // 2))
    for m in range(0, DC, 4):
        cast_rows(wo_h, wo_b, m * 128, (m + 4) * 128, ('wo', m // 4))
    for j in range(0, FC, 2):
        cast_rows(wgu_h[1], wgu_b[1], j * 128, (j + 2) * 128, ('wgu1', j // 2))
    for m in range(DC):
        cast_rows(wd_h[1], wd_b[1], m * 128, (m + 1) * 128, ('wd1', m))

    for s in range(NSEG):
        for off in (0, APAD + SEG):
            for c in range(8):
                dma('act', kT_d[c * 128:(c + 1) * 128, s * SEGA + off: s * SEGA + off + APAD], zeros[:, 0:APAD],
                    ['zeros'], [('kTpad', s, off)])
            for a in range(APAD // 128):
                r0 = s * SEGA + off + a * 128
                dma('act', vatt_d[r0:r0 + 128, :], zeros[:, 0:1040], ['zeros'], [('vapad', s, off)])
        for off in (0, SEG + 1):
            dma('act', zT_d[:, s * SEGZ + off: s * SEGZ + off + 1].rearrange("(c p) o -> p c o", p=128),
                zerosf[:, 0:ZC].unsqueeze(2), ['zerosf'], [('zpad', s, off)], allow_slow_non_contiguous=True)

    _sb_outer = sb

    def ffn_env(pa, tag):
        def sb(name, shape, dt, stack=es):
            return _sb_outer(name + tag, shape, dt, stack)
        xT = sb("xT", [128, DC, TT], F32, pa)
        nT = sb("nT", [128, DC, TT], BF16, pa)
        actT = sb("actT", [128, FC, TT], BF16, pa)
        wring = [sb("wring%d" % i, [128, 4096], BF16, pa) for i in range(3)]
        dring = [sb("dring%d" % i, [128, 22 * 128], BF16, pa) for i in range(3)]
        xtok = [sb("xtok%d" % i, [128, 1024], F32, pa) for i in range(2)]
        sq = [sb("sq%d" % i, [128, TT], BF16, pa) for i in range(2)]
        rstd = sb("rstd", [128, TT], F32, pa)
        sg = [sb("sg%d" % i, [128, TT], F32, pa) for i in range(2)]
        stg = [sb("stg%d" % i, [128, TT], F32, pa) for i in range(3)]
        stgb = [sb("stgb%d" % i, [128, TT], BF16, pa) for i in range(2)]
        vstg = [sb("vstg%d" % i, [128, 4 * 65], BF16, pa) for i in range(2)]
        cnt = {'w': 0, 'd': 0, 'x': 0, 'sq': 0, 'sg': 0, 'stg': 0, 'stgb': 0, 'tp': 0, 'vs': 0, 'ev': 0}
        xT_keys = [('xT', dc) for dc in range(DC)]

        def evac_copy(out, in_, reads, writes):
            cnt['ev'] += 1
            if cnt['ev'] % 2:
                S.op('act', lambda e, o=out, i=in_: e.activation(out=o, in_=i, func=AF.Copy), reads, writes)
            else:
                S.op('dve', lambda e, o=out, i=in_: e.tensor_copy(out=o, in_=i), reads, writes)

        def load_x_tile(it):
            for s in range(4):
                for hf in range(2):
                    xb_ = cnt['x'] % 2
                    cnt['x'] += 1
                    r0 = it * TT + s * 128
                    dma('sp', xtok[xb_][:], x_d[r0:r0 + 128, hf * 1024:(hf + 1) * 1024], [], [('xtok', xb_)])
                    for g in range(2):
                        bank = 6 + cnt['tp'] % 2
                        cnt['tp'] += 1
                        for q in range(4):
                            S.mm(lambda e, b=bank, q=q, g=g, xb_=xb_: e.transpose(PS[b][:, q * 128:(q + 1) * 128],
                                                                                 xtok[xb_][:, (g * 4 + q) * 128:(g * 4 + q + 1) * 128], ident),
                                 [('xtok', xb_), 'cst'], [('ps', bank)], last=(q == 3))
                        dc0 = hf * 8 + g * 4
                        evac_copy(xT[:, dc0:dc0 + 4, s * 128:(s + 1) * 128], PS[bank][:].rearrange("p (q t) -> p q t", q=4),
                                  [('ps', bank)], [('xT', dc0 + q) for q in range(4)])

        def store_y_tile(it, gi):
            rms_stats()
            for dc in range(DC):
                if dc % 2 == 0:
                    S.op('dve', lambda e, dc=dc: e.scalar_tensor_tensor(out=xT[:, dc, :], in0=xT[:, dc, :], scalar=gains[:, gi * DC + dc: gi * DC + dc + 1],
                                                                        in1=rstd[:], op0=ALU.mult, op1=ALU.mult), [('xT', dc), 'rstd', 'gains'], [('xT', dc)])
                else:
                    S.op('pool', lambda e, dc=dc: e.tensor_scalar(out=xT[:, dc, :], in0=xT[:, dc, :], scalar1=gains[:, gi * DC + dc: gi * DC + dc + 1], scalar2=0.0, op0=ALU.mult, op1=ALU.add),
                         [('xT', dc), 'gains'], [('xT', dc)])
                    S.op('pool', lambda e, dc=dc: e.tensor_tensor(out=xT[:, dc, :], in0=xT[:, dc, :], in1=rstd[:], op=ALU.mult), [('xT', dc), 'rstd'], [('xT', dc)])
            for s in range(4):
                for hf in range(2):
                    xb_ = cnt['x'] % 2
                    cnt['x'] += 1
                    for g in range(2):
                        bank = 6 + cnt['tp'] % 2
                        cnt['tp'] += 1
                        for q in range(4):
                            dc = hf * 8 + g * 4 + q
                            S.mm(lambda e, b=bank, q=q, dc=dc, s=s: e.transpose(PS[b][:, q * 128:(q + 1) * 128], xT[:, dc, s * 128:(s + 1) * 128], ident),
                                 [('xT', dc), 'cst'], [('ps', bank)], last=(q == 3))
                        evac_copy(xtok[xb_][:, g * 512:(g + 1) * 512], PS[bank][:], [('ps', bank)], [('xtok', xb_)])
                    r0 = it * TT + s * 128
                    dma('sp', y_d[r0:r0 + 128, hf * 1024:(hf + 1) * 1024], xtok[xb_][:], [('xtok', xb_)], ['y'])

        def rms_stats():
            for dc in range(DC):
                k = cnt['sq'] % 2
                cnt['sq'] += 1
                S.op('act', lambda e, k=k, dc=dc: e.activation(out=sq[k][:], in_=xT[:, dc, :], func=AF.Square), [('xT', dc)], [('sq', k)])
                S.mm(lambda e, k=k, dc=dc: e.matmul(PS[4][:], onesb[:], sq[k][:], start=(dc == 0), stop=(dc == DC - 1)),
                     [('sq', k), 'onesb'], [('ps', 4)], last=(dc == DC - 1))
            S.op('act', lambda e: e.activation(out=rstd[:], in_=PS[4][:], func=AF.Sqrt, scale=1.0 / D, bias=epsn[:]), [('ps', 4), 'epsn'], ['rstd'])
            S.op('dve', lambda e: e.reciprocal(out=rstd[:], in_=rstd[:]), ['rstd'], ['rstd'])

        def rmsnorm_to_nT(gi):
            rms_stats()
            for dc in range(DC):
                if dc % 2 == 0:
                    S.op('dve', lambda e, dc=dc: e.scalar_tensor_tensor(out=nT[:, dc, :], in0=xT[:, dc, :], scalar=gains[:, gi * DC + dc: gi * DC + dc + 1],
                                                                        in1=rstd[:], op0=ALU.mult, op1=ALU.mult), [('xT', dc), 'rstd', 'gains'], [('nT', dc)])
                else:
                    k = cnt['stg'] % 3
                    cnt['stg'] += 1
                    S.op('pool', lambda e, dc=dc, k=k: e.tensor_scalar(out=stg[k][:], in0=xT[:, dc, :], scalar1=gains[:, gi * DC + dc: gi * DC + dc + 1], scalar2=0.0, op0=ALU.mult, op1=ALU.add),
                         [('xT', dc), 'gains'], [('stg', k)])
                    S.op('pool', lambda e, dc=dc, k=k: e.tensor_tensor(out=nT[:, dc, :], in0=stg[k][:], in1=rstd[:], op=ALU.mult), [('stg', k), 'rstd'], [('nT', dc)])

        def ffn(fi):
            for j in range(FC):
                w = cnt['w'] % 3
                cnt['w'] += 1
                dma('sp', wring[w][:], wgu_b[fi][j * 128:(j + 1) * 128, :], [('wgu%d' % fi, j // 2)], [('wring', w)])
                bg = j % 2
                bu = 2 + j % 2
                for kc in range(DC):
                    S.mm(lambda e, w=w, kc=kc, bg=bg: e.matmul(PS[bg][:], wring[w][:, kc * 128:(kc + 1) * 128], nT[:, kc, :], start=(kc == 0), stop=(kc == DC - 1)),
                         [('wring', w), ('nT', kc)], [('ps', bg)], last=(kc == DC - 1))
                for kc in range(DC):
                    S.mm(lambda e, w=w, kc=kc, bu=bu: e.matmul(PS[bu][:], wring[w][:, 2048 + kc * 128: 2048 + (kc + 1) * 128], nT[:, kc, :], start=(kc == 0), stop=(kc == DC - 1)),
                         [('wring', w), ('nT', kc)], [('ps', bu)], last=(kc == DC - 1))
                k = cnt['sg'] % 2
                cnt['sg'] += 1
                S.op('act', lambda e, k=k, bg=bg: e.activation(out=sg[k][:], in_=PS[bg][:], func=AF.Silu), [('ps', bg)], [('sg', k)])
                S.op('dve', lambda e, k=k, bu=bu, j=j: e.tensor_tensor(out=actT[:, j, :], in0=sg[k][:], in1=PS[bu][:], op=ALU.mult), [('sg', k), ('ps', bu)], [('actT', j)])
            for m in range(DC):
                bd_ = 4 + m % 2
                for hf in range(2):
                    dd = cnt['d'] % 3
                    cnt['d'] += 1
                    dma('sp', dring[dd][:], wd_b[fi][m * 128:(m + 1) * 128, hf * 2816:(hf + 1) * 2816], [('wd%d' % fi, m)], [('dring', dd)])
                    for f2 in range(22):
                        fc = hf * 22 + f2
                        S.mm(lambda e, dd=dd, f2=f2, fc=fc, bd_=bd_: e.matmul(PS[bd_][:], dring[dd][:, f2 * 128:(f2 + 1) * 128], actT[:, fc, :], start=(fc == 0), stop=(fc == FC - 1)),
                             [('dring', dd), ('actT', fc)], [('ps', bd_)], last=(fc == FC - 1))
                S.op('dve', lambda e, m=m, bd_=bd_: e.scalar_tensor_tensor(out=xT[:, m, :], in0=PS[bd_][:], scalar=0.5, in1=xT[:, m, :], op0=ALU.mult, op1=ALU.add),
                     [('ps', bd_), ('xT', m)], [('xT', m)])

        fm_dest = [('q', c) for c in range(8)] + [('k', c) for c in range(8)] + [('z', c) for c in range(ZC)]

        def proj(it):
            seg = it // TPS
            t0 = (it % TPS) * TT
            for ci, (kind, c) in enumerate(fm_dest):
                w = cnt['w'] % 3
                cnt['w'] += 1
                dma('sp', wring[w][:, 0:2048], winfm_b[ci * 128:(ci + 1) * 128, :], [('winfm', ci // 4)], [('wring', w)])
                bank = ci % 2
                for kc in range(DC):
                    S.mm(lambda e, w=w, kc=kc, bank=bank: e.matmul(PS[bank][:], wring[w][:, kc * 128:(kc + 1) * 128], nT[:, kc, :], start=(kc == 0), stop=(kc == DC - 1)),
                         [('wring', w), ('nT', kc)], [('ps', bank)], last=(kc == DC - 1))
                if kind in ('q', 'k'):
                    k = cnt['stgb'] % 2
                    cnt['stgb'] += 1
                    evac_copy(stgb[k][:], PS[bank][:], [('ps', bank)], [('stgb', k)])
                    if kind == 'q':
                        dma('act', qT_d[c * 128:(c + 1) * 128, it * TT:(it + 1) * TT], stgb[k][:], [('stgb', k)], [('qT', seg)])
                    else:
                        col = seg * SEGA + APAD + t0
                        dma('act', kT_d[c * 128:(c + 1) * 128, col:col + TT], stgb[k][:], [('stgb', k)], [('kT', seg)])
                else:
                    k = cnt['stg'] % 3
                    cnt['stg'] += 1
                    evac_copy(stg[k][:], PS[bank][:], [('ps', bank)], [('stg', k)])
                    col = seg * SEGZ + 1 + t0
                    dma('act', zT_d[c * 128:(c + 1) * 128, col:col + TT], stg[k][:], [('stg', k)], [('zT', seg)])
            for g in range(4):
                w = cnt['w'] % 3
                cnt['w'] += 1
                dma('sp', wring[w][:], wintm_b[g * 128:(g + 1) * 128, :], [('wintm', g // 2)], [('wring', w)])
                for s in range(4):
                    bank = 2 + (g * 4 + s) % 2
                    for kc in range(DC):
                        S.mm(lambda e, w=w, kc=kc, bank=bank, s=s: e.matmul(PS[bank][:, 0:256], nT[:, kc, s * 128:(s + 1) * 128], wring[w][:, kc * 256:(kc + 1) * 256],
                                                                             start=(kc == 0), stop=(kc == DC - 1)),
                             [('wring', w), ('nT', kc)], [('ps', bank)], last=(kc == DC - 1))
                    if True:
                        k = cnt['vs'] % 2
                        cnt['vs'] += 1
                        S.op('pool', lambda e, k=k: e.memset(vstg[k][:], 1.0), [], [('vstg', k)])
                        evac_copy(vstg[k][:].rearrange("p (h c) -> p h c", h=4)[:, :, 0:64], PS[bank][:, 0:256].rearrange("p (h c) -> p h c", h=4),
                                  [('ps', bank), ('vstg', k)], [('vstg', k)])
                        row = seg * SEGA + APAD + t0 + s * 128
                        dma('act', vatt_d[row:row + 128, g * 260:(g + 1) * 260], vstg[k][:], [('vstg', k)], [('vatt', seg)])

        def wout(it):
            dma('sp', nT[:], mixT_d[:, it * TT:(it + 1) * TT].rearrange("(c p) t -> p c t", p=128), [('mixT', it // TPS)], [('nT', dc) for dc in range(DC)])
            dma('sp', xT[:], x1T_d[:, it * TT:(it + 1) * TT].rearrange("(c p) t -> p c t", p=128), [('x1T', it)], xT_keys)
            for m in range(DC):
                w = cnt['w'] % 3
                cnt['w'] += 1
                dma('sp', wring[w][:, 0:2048], wo_b[m * 128:(m + 1) * 128, :], [('wo', m // 4)], [('wring', w)])
                bank = m % 2
                for kc in range(DC):
                    S.mm(lambda e, w=w, kc=kc, bank=bank: e.matmul(PS[bank][:], wring[w][:, kc * 128:(kc + 1) * 128], nT[:, kc, :], start=(kc == 0), stop=(kc == DC - 1)),
                         [('wring', w), ('nT', kc)], [('ps', bank)], last=(kc == DC - 1))
                S.op('dve', lambda e, m=m, bank=bank: e.tensor_tensor(out=xT[:, m, :], in0=xT[:, m, :], in1=PS[bank][:], op=ALU.add), [('ps', bank), ('xT', m)], [('xT', m)])

        return dict(load_x_tile=load_x_tile, rmsnorm_to_nT=rmsnorm_to_nT, ffn=ffn, proj=proj, wout=wout, store_y_tile=store_y_tile, xT=xT, xT_keys=xT_keys)

    def phase_A():
        pa = ExitStack()
        env = ffn_env(pa, 'A')
        for it in range(NT):
            env['load_x_tile'](it)
            env['rmsnorm_to_nT'](0)
            env['ffn'](0)
            dma('act', x1T_d[:, it * TT:(it + 1) * TT].rearrange("(c p) t -> p c t", p=128), env['xT'][:], env['xT_keys'], [('x1T', it)])
            env['rmsnorm_to_nT'](1)
            env['proj'](it)
        S.barrier()
        pa.close()


    if 'A' in phases:
        phase_A()
    def phase_fix():
        pf = ExitStack()
        fk = sb("fk", [128, APAD], BF16, pf)
        fv = sb("fv", [128, APAD // 128, 1040], BF16, pf)
        fz = sb("fz", [128, ZC], F32, pf)
        for (ds, doff, ss, soff) in ((0, APAD + SEG, 1, APAD), (1, 0, 0, SEG)):
            for c in range(8):
                dma('sp', fk[:], kT_d[c * 128:(c + 1) * 128, ss * SEGA + soff: ss * SEGA + soff + APAD], [('kT', ss)], ['fk'])
                S.op('dve', lambda e: e.tensor_scalar(out=fk[:], in0=fk[:], scalar1=link[:, 0:1], scalar2=None, op0=ALU.mult), ['fk', 'link'], ['fk'])
                dma('sp', kT_d[c * 128:(c + 1) * 128, ds * SEGA + doff: ds * SEGA + doff + APAD], fk[:], ['fk', ('kTpad', ds, doff)], [('kTpad', ds, doff)])
            dma('sp', fv[:], vatt_d[ss * SEGA + soff: ss * SEGA + soff + APAD, :].rearrange("(a p) c -> p a c", p=128), [('vatt', ss)], ['fv'])
            S.op('dve', lambda e: e.tensor_scalar(out=fv[:], in0=fv[:], scalar1=link[:, 0:1], scalar2=None, op0=ALU.mult), ['fv', 'link'], ['fv'])
            dma('sp', vatt_d[ds * SEGA + doff: ds * SEGA + doff + APAD, :].rearrange("(a p) c -> p a c", p=128), fv[:], ['fv', ('vapad', ds, doff)], [('vapad', ds, doff)])
        for (ds, doff, ss, soff) in ((0, SEG + 1, 1, 1), (1, 0, 0, SEG)):
            dma('sp', fz[:].unsqueeze(2), zT_d[:, ss * SEGZ + soff: ss * SEGZ + soff + 1].rearrange("(c p) o -> p c o", p=128), [('zT', ss)], ['fz'], allow_slow_non_contiguous=True)
            S.op('dve', lambda e: e.tensor_scalar(out=fz[:], in0=fz[:], scalar1=link[:, 0:1], scalar2=None, op0=ALU.mult), ['fz', 'link'], ['fz'])
            dma('sp', zT_d[:, ds * SEGZ + doff: ds * SEGZ + doff + 1].rearrange("(c p) o -> p c o", p=128), fz[:].unsqueeze(2), ['fz', ('zpad', ds, doff)], [('zpad', ds, doff)], allow_slow_non_contiguous=True)
        S.barrier()
        pf.close()

    if 'fix' in phases:
        phase_fix()
    def phase_att():
        pt = ExitStack()
        NKT = {1: SEG // 128 + 1, 4: SEG // 512 + 1, 16: SEG // 2048 + 1}
        qt = sb("qt", [128, SEG], BF16, pt)
        kt = sb("kt", [128, SEGA], BF16, pt)
        vres = {d: sb("vres%d" % d, [128, d * NKT[d], 130], BF16, pt) for d in (1, 4, 16)}
        acc = [sb("acc%d" % h, [65, SEG], F32, pt) for h in range(2)]
        etf = sb("etf", [128, 1536], F32, pt)
        etb = sb("etb", [128, 1536], BF16, pt)
        pex = [sb("pex%d" % i, [128, 256], BF16, pt) for i in range(6)]
        pm = [sb("pm%d" % i, [128, 256], BF16, pt) for i in range(6)]
        rden = sb("rden", [64, 512], F32, pt)
        ostg = [sb("ostg%d" % i, [64, 512], BF16, pt) for i in range(2)]
        sel = sb("sel", [65, 64], F32, pt)
        S.op('pool', lambda e: e.memset(sel[:], 0.0), [], ['sel'])
        S.op('pool', lambda e: e.memset(sel[64:65, :], 1.0), ['sel'], ['sel'])
        ac = {'u': 0, 'o': 0, 'os': 0}

        def emit_pv(pi, d, r, b, h):
            ob = 4 + ac['o'] % 2
            ac['o'] += 1
            for j in range(2):
                S.mm(lambda e, ob=ob, j=j, pi=pi, d=d, r=r, b=b, h=h: e.matmul(PS[ob][0:65, 0:128], vres[d][:, r * NKT[d] + b + j, h * 65:(h + 1) * 65],
                                                                               pm[pi][:, j * 128:(j + 1) * 128], start=(j == 0), stop=(j == 1)),
                     [('vres', d), ('pm', pi)], [('ps', ob)])
            asl = acc[h][:, r + d * 128 * b: r + d * 128 * b + d * 127 + 1: d]
            if d == 1:
                S.op('dve', lambda e, asl=asl, ob=ob: e.tensor_copy(out=asl, in_=PS[ob][0:65, 0:128]), [('ps', ob)], [('acc', h)])
            else:
                S.op('dve', lambda e, asl=asl, ob=ob: e.tensor_tensor(out=asl, in0=asl, in1=PS[ob][0:65, 0:128], op=ALU.add), [('ps', ob), ('acc', h)], [('acc', h)])

        pend = []
        for seg in range(NSEG):
            for hp in range(ATT_PAIRS):
                dma('sp', qt[:], qT_d[hp * 128:(hp + 1) * 128, seg * SEG:(seg + 1) * SEG], [('qT', seg)], ['qt'])
                dma('sp', kt[:], kT_d[hp * 128:(hp + 1) * 128, seg * SEGA:(seg + 1) * SEGA], [('kT', seg)] + [('kTpad', seg, o) for o in (0, APAD + SEG)], ['kt'])
                dma('sp', etf[:], etab_d[hp], [], ['etf'])
                S.op('pool', lambda e: e.tensor_copy(out=etb[:], in_=etf[:]), ['etf'], ['etb'])
                for d in (1, 4, 16):
                    for r in range(d):
                        base = seg * SEGA + APAD + r - 64 * d
                        src = vatt_d[base: base + d * 128 * NKT[d], hp * 130:(hp + 1) * 130].rearrange("(k p dd) c -> dd p k c", p=128, dd=d)[0]
                        dma('act', vres[d][:, r * NKT[d]:(r + 1) * NKT[d], :], src, [('vatt', seg)] + [('vapad', seg, o) for o in (0, APAD + SEG)], [('vres', d)])
                for h in range(2):
                    for di, d in enumerate((1, 4, 16)):
                        L = SEG // d
                        for r in range(d):
                            for b in range(L // 128):
                                u = ac['u']
                                ac['u'] += 1
                                sbank = u % 4
                                qs = qt[h * 64:(h + 1) * 64, r + d * 128 * b: r + d * 128 * b + d * 127 + 1: d]
                                for j in range(2):
                                    k0 = APAD + r + d * (128 * b - 64 + 128 * j)
                                    ks = kt[h * 64:(h + 1) * 64, k0: k0 + d * 127 + 1: d]
                                    S.mm(lambda e, sbank=sbank, j=j, ks=ks, qs=qs: e.matmul(PS[sbank][:, j * 128:(j + 1) * 128], ks, qs, start=True, stop=True),
                                         ['kt', 'qt'], [('ps', sbank)], last=(j == 1))
                                pi = u % 6
                                S.op('act', lambda e, pi=pi, sbank=sbank: e.activation(out=pex[pi][:], in_=PS[sbank][:, 0:256], func=AF.Exp, scale=0.125),
                                     [('ps', sbank)], [('pex', pi)])
                                eo = (h * 3 + di) * 256
                                S.op('pool', lambda e, pi=pi, eo=eo: e.tensor_tensor(out=pm[pi][:], in0=pex[pi][:], in1=etb[:, eo:eo + 256], op=ALU.mult),
                                     [('pex', pi), 'etb'], [('pm', pi)])
                                pend.append((pi, d, r, b, h))
                                if len(pend) > 2:
                                    emit_pv(*pend.pop(0))
                    while pend:
                        emit_pv(*pend.pop(0))
                    for t in range(SEG // 512):
                        S.mm(lambda e, h=h, t=t: e.matmul(PS[6][0:64, :], sel[:], acc[h][:, t * 512:(t + 1) * 512], start=True, stop=True), ['sel', ('acc', h)], [('ps', 6)])
                        S.op('dve', lambda e: e.reciprocal(out=rden[:], in_=PS[6][0:64, :]), [('ps', 6)], ['rden'])
                        k = ac['os'] % 2
                        ac['os'] += 1
                        S.op('dve', lambda e, k=k, h=h, t=t: e.tensor_tensor(out=ostg[k][:], in0=acc[h][0:64, t * 512:(t + 1) * 512], in1=rden[:], op=ALU.mult),
                             [('acc', h), 'rden'], [('ostg', k)])
                        row = hp * 128 + h * 64
                        dma('sp', mixT_d[row:row + 64, seg * SEG + t * 512: seg * SEG + (t + 1) * 512], ostg[k][:], [('ostg', k)], [('mixT', seg)])
        S.barrier()
        pt.close()


    if 'att' in phases:
        phase_att()
    def phase_rwkv():
        pr = ExitStack()
        twd = sb("twd", [128, SEG], BF16, pr)
        sad = sb("sad", [128, SEG], BF16, pr)
        sgd = sb("sgd", [128, SEG], BF16, pr)
        mu = sb("mu", [128, 2 * ZC], F32, pr)
        muc = sb("muc", [128, ZC], F32, pr)
        rwp = sb("rwp", [128, 80], F32, pr)
        lrf = sb("lrf", [128, 1024], F32, pr)
        w2b_ = sb("w2b", [128, 1024], BF16, pr)
        a2b_ = sb("a2b", [128, 1024], BF16, pr)
        g2b_ = sb("g2b", [128, 1024], BF16, pr)
        rmask = sb("rmask", [128, 512], F32, pr)
        cmask = [sb("cmask%d" % i, [128, 192], F32, pr) for i in range(2)]
        mSxb = [sb("mSxb%d" % i, [128, 128], BF16, pr) for i in range(2)]
        zin = [sb("zin%d" % i, [128, TT + 2], F32, pr) for i in range(3)]
        NW = 30
        wk = [sb("wk%d" % i, [128, TT], F32, pr) for i in range(NW)]
        bdt = {n: [sb("bd_%s%d" % (n, i), [128, 8, 192 if n == "At" else 128], BF16, pr) for i in range(2)] for n in ("At", "Bt", "Kt", "Vt")}
        gam = [sb("gam%d" % i, [128, TT], F32, pr) for i in range(2)]
        vTs = [sb("vTs%d" % i, [128, TT], F32, pr) for i in range(2)]
        bsum = [sb("bsum%d" % i, [128, TT], F32, pr) for i in range(2)]
        NU = 8
        ub = {n: [sb("u_%s%d" % (n, i), [128, 320 if n == "XM" else 256], BF16, pr) for i in range(NU)] for n in ("XM", "LM", "Z", "G")}
        ub1 = {n: [sb("u1_%s%d" % (n, i), [128, 128], BF16, pr) for i in range(NU)] for n in ("X", "Tt", "Kk", "V", "PpT", "RbT")}
        pw = [[sb("pw%d_%d" % (u, i), [128, 128], BF16, pr) for i in range(4)] for u in range(NU)]
        qg = [sb("qg%d" % i, [128, 128], F32, pr) for i in range(NU)]
        Hf = sb("Hf", [128, 128], F32, pr)
        Hb = [sb("Hb%d" % i, [128, 128], BF16, pr) for i in range(2)]
        ysb = [sb("ysb%d" % i, [128, TT], F32, pr) for i in range(2)]
        ybl = sb("ybl", [128, TT], F32, pr)
        ostr = [sb("ostr%d" % i, [128, TT], BF16, pr) for i in range(2)]
        for n in bdt:
            for i in range(2):
                S.op('pool', lambda e, n=n, i=i: e.memset(bdt[n][i][:], 0.0), [], [('bd', n, i)])
        dma('sp', mu[:], mu_d[:, :], [], ['mu'])
        dma('sp', rwp[:], rwp_d[:, :], [], ['rwp'])
        dma('sp', rmask[:], rmask_d[:, :], [], ['rmask'])
        for (b_, d_) in ((w2b_, w2_d), (a2b_, a2_d), (g2b_, g2_d)):
            dma('sp', lrf[:], d_[:, :], [], ['lrf'])
            S.op('dve', lambda e, b_=b_: e.tensor_copy(out=b_[:], in_=lrf[:]), ['lrf'], ['lrw'])
        S.op('dve', lambda e: e.tensor_tensor(out=muc[:], in0=mu[:, 0:ZC], in1=mu[:, ZC:2 * ZC], op=ALU.add), ['mu'], ['muc'])
        S.op('dve', lambda e: e.tensor_scalar(out=muc[:], in0=muc[:], scalar1=-1.0, scalar2=1.0, op0=ALU.mult, op1=ALU.add), ['muc'], ['muc'])
        for c in range(8):
            S.op('dve', lambda e, c=c: e.tensor_scalar(out=rwp[:, c * 10 + 6:c * 10 + 7], in0=rwp[:, c * 10 + 5:c * 10 + 6], scalar1=-1.0, scalar2=1.0, op0=ALU.mult, op1=ALU.add), ['rwp'], ['rwp'])
        for dirn in range(2):
            mI_ = m_iu if dirn == 0 else m_il
            mSx_ = m_sl if dirn == 0 else m_su
            mS_ = m_su if dirn == 0 else m_sl
            S.op('dve', lambda e, dirn=dirn, mS_=mS_: e.tensor_copy(out=cmask[dirn][:, 0:128], in_=mS_), ['cst'], ['cmask'])
            S.op('dve', lambda e, dirn=dirn, mI_=mI_: e.tensor_tensor(out=cmask[dirn][:, 128:192], in0=mI_[:, 0:64], in1=mI_[:, 64:128], op=ALU.add), ['cst', 'cmask'], ['cmask'])
            S.op('dve', lambda e, dirn=dirn, mSx_=mSx_: e.tensor_copy(out=mSxb[dirn][:], in_=mSx_), ['cst'], ['mSxb'])
        rc = {'z': 0, 'wk': 0, 'ps': 0, 'pf': 0, 'os': 0}

        def newwk():
            i = rc['wk'] % NW
            rc['wk'] += 1
            return wk[i], ('wk', i)

        def psreg():
            i = rc['ps'] % 6
            rc['ps'] += 1
            return PS[i][:, 0:256], ('psr', i)

        def psfull():
            i = 6 + rc['pf'] % 2
            rc['pf'] += 1
            return PS[i], ('psf', i)

        def load_shift(zc, seg, t0, dst=None, dkey=None):
            if dst is None:
                o, ok = newwk()
            else:
                o, ok = dst, dkey
            i = rc['z'] % 3
            rc['z'] += 1
            col = seg * SEGZ + t0
            dma('sp', zin[i][:], zT_d[zc * 128:(zc + 1) * 128, col:col + TT + 2], [('zT', seg)] + [('zpad', seg, o_) for o_ in (0, SEG + 1)], [('zin', i)])
            t1, t1k = newwk()
            t2, t2k = newwk()
            S.op('act', lambda e, i=i, o=o: e.activation(out=o[:], in_=zin[i][:, 1:TT + 1], func=AF.Copy, scale=muc[:, zc:zc + 1]), [('zin', i), 'muc'], [ok])
            S.op('act', lambda e, i=i, t1=t1: e.activation(out=t1[:], in_=zin[i][:, 0:TT], func=AF.Copy, scale=mu[:, zc:zc + 1]), [('zin', i), 'mu'], [t1k])
            S.op('act', lambda e, i=i, t2=t2: e.activation(out=t2[:], in_=zin[i][:, 2:TT + 2], func=AF.Copy, scale=mu[:, ZC + zc:ZC + zc + 1]), [('zin', i), 'mu'], [t2k])
            S.op('pool', lambda e, t1=t1, t2=t2: e.tensor_tensor(out=t1[:], in0=t1[:], in1=t2[:], op=ALU.add), [t1k, t2k], [t1k])
            S.op('dve', lambda e, o=o, t1=t1: e.tensor_tensor(out=o[:], in0=o[:], in1=t1[:], op=ALU.add), [ok, t1k], [ok])
            return o, ok

        def lowrank_prep(seg):
            for t in range(TPS):
                t0 = t * TT
                for (zc, dst, fn) in ((24, twd, AF.Tanh), (25, sad, AF.Copy), (26, sgd, AF.Sigmoid)):
                    o, ok = load_shift(zc, seg, t0)
                    S.op('act', lambda e, o=o, dst=dst, fn=fn, t0=t0: e.activation(out=dst[:, t0:t0 + TT], in_=o[:], func=fn), [ok], ['lr'])

        def sig_lowrank(wb, rows, src, c, t0, bias_ap):
            p, pk = psfull()
            r0, r1 = rows
            S.mm(lambda e, p=p: e.matmul(p[:], wb[r0:r1, c * 128:(c + 1) * 128], src[r0:r1, t0:t0 + TT], start=True, stop=True), ['lrw', 'lr'], [pk])
            o, ok = newwk()
            S.op('act', lambda e, p=p, o=o: e.activation(out=o[:], in_=p[:], func=AF.Sigmoid, bias=bias_ap, scale=1.0), [pk, 'rwp'], [ok])
            return o, ok

        def prep_tile(dirn, c, seg, t, bi):
            t0 = t * TT
            fwd = (dirn == 0)
            P = lambda j: rwp[:, c * 10 + j: c * 10 + j + 1]
            rs, rsk = load_shift(c, seg, t0)
            yield
            ks, ksk = load_shift(8 + c, seg, t0)
            yield
            load_shift(16 + c, seg, t0, vTs[bi], ('vTs', bi))
            yield
            for h in range(2):
                S.op('act', lambda e, h=h: e.activation(out=bdt["Vt"][bi][h * 64:(h + 1) * 64, :, h * 64:(h + 1) * 64],
                                                        in_=vTs[bi][h * 64:(h + 1) * 64, :].rearrange("p (n t) -> p n t", t=64), func=AF.Copy),
                     [('vTs', bi)], [('bd', "Vt", bi)])
            kkr, kkrk = newwk()
            S.op('act', lambda e: e.activation(out=kkr[:], in_=ks[:], func=AF.Copy, scale=P(4)), [ksk, 'rwp'], [kkrk])
            sq_, sqk = newwk()
            S.op('act', lambda e: e.activation(out=sq_[:], in_=kkr[:], func=AF.Square), [kkrk], [sqk])
            p, pk = psfull()
            S.mm(lambda e, p=p: e.matmul(p[:], bones, sq_[:], start=True, stop=True), ['cst', sqk], [pk])
            rn, rnk = newwk()
            S.op('act', lambda e, p=p: e.activation(out=rn[:], in_=p[:], func=AF.Sqrt), [pk], [rnk])
            yield
            S.op('dve', lambda e: e.tensor_scalar(out=rn[:], in0=rn[:], scalar1=1e-12, scalar2=None, op0=ALU.max), [rnk], [rnk])
            S.op('dve', lambda e: e.reciprocal(out=rn[:], in_=rn[:]), [rnk], [rnk])
            kkn, kknk = newwk()
            S.op('dve', lambda e: e.scalar_tensor_tensor(out=kkn[:], in0=kkr[:], scalar=-1.0, in1=rn[:], op0=ALU.mult, op1=ALU.mult), [kkrk, rnk], [kknk])
            yield
            rows = (0, 64) if fwd else (64, 128)
            sgw, sgwk = sig_lowrank(w2b_, rows, twd, c, t0, P(0 if fwd else 1))
            ag, agk = sig_lowrank(a2b_, rows, sad, c, t0, P(2 if fwd else 3))
            yield
            kp_, kpk = newwk()
            S.op('pool', lambda e: e.tensor_scalar(out=kp_[:], in0=ag[:], scalar1=P(5), scalar2=P(6), op0=ALU.mult, op1=ALU.add), [agk, 'rwp'], [kpk])
            S.op('pool', lambda e: e.tensor_tensor(out=kp_[:], in0=kp_[:], in1=ks[:], op=ALU.mult), [kpk, ksk], [kpk])
            bb, bbk = newwk()
            S.op('dve', lambda e: e.scalar_tensor_tensor(out=bb[:], in0=kkn[:], scalar=-1.0, in1=ag[:], op0=ALU.mult, op1=ALU.mult), [kknk, agk], [bbk])
            yield
            cs, csk = newwk()
            S.op('dve', lambda e: e.tensor_tensor_scan(out=cs[:], data0=rmask[:], data1=sgw[:], initial=0.0, op0=ALU.mult, op1=ALU.add), ['rmask', sgwk], [csk])
            if fwd:
                lg, lgk = cs, csk
            else:
                lg, lgk = newwk()
                S.op('pool', lambda e: e.tensor_tensor(out=lg[:], in0=sgw[:], in1=cs[:], op=ALU.subtract), [sgwk, csk], [lgk])
                S.op('pool', lambda e: e.tensor_tensor(out=lg[:].rearrange("p (n t) -> p n t", t=64), in0=lg[:].rearrange("p (n t) -> p n t", t=64),
                                                       in1=cs[:].rearrange("p (n t) -> p n t", t=64)[:, :, 63:64].to_broadcast([128, 8, 64]), op=ALU.add), [lgk, csk], [lgk])
            yield
            gkey = ('gam', bi)
            S.op('act', lambda e: e.activation(out=gam[bi][:], in_=lg[:], func=AF.Exp, scale=-C0), [lgk], [gkey])
            ig, igk = newwk()
            S.op('act', lambda e: e.activation(out=ig[:], in_=lg[:], func=AF.Exp, scale=C0), [lgk], [igk])
            gp, gpk = newwk()
            S.op('dve', lambda e: e.tensor_tensor(out=gp[:], in0=lg[:], in1=sgw[:], op=ALU.subtract), [lgk, sgwk], [gpk])
            S.op('act', lambda e: e.activation(out=gp[:], in_=gp[:], func=AF.Exp, scale=-C0), [gpk], [gpk])
            yield
            for (nm, a_, ak, b_, bk) in (("At", kkn, kknk, gp, gpk), ("Bt", bb, bbk, ig, igk), ("Kt", kp_, kpk, ig, igk)):
                for h in range(2):
                    eng = 'pool'
                    S.op(eng, lambda e, nm=nm, a_=a_, b_=b_, h=h: e.tensor_tensor(out=bdt[nm][bi][h * 64:(h + 1) * 64, :, h * 64:(h + 1) * 64],
                                                                                   in0=a_[h * 64:(h + 1) * 64, :].rearrange("p (n t) -> p n t", t=64),
                                                                                   in1=b_[h * 64:(h + 1) * 64, :].rearrange("p (n t) -> p n t", t=64), op=ALU.mult),
                         [ak, bk], [('bd', nm, bi)])
                yield
            S.op('dve', lambda e: e.tensor_tensor(out=bdt["At"][bi][:, :, 128:192], in0=rs[:].rearrange("p (n t) -> p n t", t=64),
                                                  in1=gam[bi][:].rearrange("p (n t) -> p n t", t=64), op=ALU.mult), [rsk, gkey], [('Rt', bi)])
            if fwd:
                yield
                agb, agbk = sig_lowrank(a2b_, (64, 128), sad, c, t0, P(3))
                kb_, kbk = newwk()
                S.op('pool', lambda e: e.tensor_scalar(out=kb_[:], in0=agb[:], scalar1=P(5), scalar2=P(6), op0=ALU.mult, op1=ALU.add), [agbk, 'rwp'], [kbk])
                S.op('pool', lambda e: e.tensor_tensor(out=kb_[:], in0=kb_[:], in1=ks[:], op=ALU.mult), [kbk, ksk], [kbk])
                yield
                S.op('pool', lambda e: e.tensor_tensor(out=kb_[:], in0=kb_[:], in1=kp_[:], op=ALU.add), [kbk, kpk], [kbk])
                S.op('dve', lambda e: e.scalar_tensor_tensor(out=kb_[:], in0=rs[:], scalar=P(7), in1=kb_[:], op0=ALU.mult, op1=ALU.mult), [rsk, kbk, 'rwp'], [kbk])
                p2, pk2 = psfull()
                S.mm(lambda e, p2=p2: e.matmul(p2[:], bones, kb_[:], start=True, stop=True), ['cst', kbk], [pk2])
                S.op('act', lambda e, p2=p2: e.activation(out=bsum[bi][:], in_=p2[:], func=AF.Copy), [pk2], [('bsum', bi)])

        def offchain(dirn, n, bi, ui):
            fwd = (dirn == 0)
            mS = (m_su if fwd else m_sl)
            At = bdt["At"][bi][:, n, 0:128]
            AR = bdt["At"][bi][:, n, :]
            Bt = bdt["Bt"][bi][:, n, :]
            Kt = bdt["Kt"][bi][:, n, :]
            Vt = bdt["Vt"][bi][:, n, :]
            Rs = bdt["At"][bi][:, n, 128:192]
            gcol = gam[bi][:, n * 64 + 63: n * 64 + 64] if fwd else gam[bi][:, n * 64: n * 64 + 1]
            kA, kB, kK, kVt, kR, kG = ('bd', "At", bi), ('bd', "Bt", bi), ('bd', "Kt", bi), ('bd', "Vt", bi), ('Rt', bi), ('gam', bi)
            XM, LM, Z, G = ub["XM"][ui], ub["LM"][ui], ub["Z"][ui], ub["G"][ui]
            X, Tt, Kk, V, PpT, RbT = (ub1[k][ui] for k in ("X", "Tt", "Kk", "V", "PpT", "RbT"))
            Bk = XM[:, 192:320]
            K_ = lambda nm: ('u', nm, ui)
            cpc = [ui]

            def cp(dst, src, reads, writes):
                cpc[0] += 1
                if cpc[0] % 3 != 0:
                    S.op('act', lambda e, dst=dst, src=src: e.activation(out=dst, in_=src, func=AF.Copy), reads, writes)
                else:
                    S.op('dve', lambda e, dst=dst, src=src: e.tensor_copy(out=dst, in_=src), reads, writes)
            for (lh, lk, dst, dk) in ((Bt, kB, XM, 'XM'), (Kt, kK, LM, 'LM')):
                p, pk = psreg()
                S.mm(lambda e, p=p, lh=lh: e.matmul(p[:, 0:192], lh, AR, start=True, stop=True), [lk, kA, kR], [pk])
                S.op('dve', lambda e, p=p, dst=dst: e.tensor_tensor(out=dst[:, 0:192], in0=p[:, 0:192], in1=cmask[dirn][:], op=ALU.mult), [pk, 'cmask'], [K_(dk)])
                yield
            p, pk = psreg()
            S.mm(lambda e, p=p: e.matmul(p[:, 0:128], At, Bt, start=True, stop=True), [kA, kB], [pk])
            cp(X[:], p[:, 0:128], [pk], [K_('X')])
            S.op('pool', lambda e: e.tensor_tensor(out=X[:], in0=X[:], in1=mSxb[dirn][:], op=ALU.mult), [K_('X'), 'mSxb'], [K_('X')])
            S.op('pool', lambda e: e.tensor_tensor(out=Tt[:], in0=XM[:, 0:128], in1=identb, op=ALU.add), [K_('XM'), 'cstb'], [K_('Tt')])
            yield
            for (src, sk, dk) in ((At, kA, 'A'), (Bt, kB, 'Bk'), (Kt, kK, 'Kk'), (Vt, kVt, 'V')):
                p, pk = psreg()
                S.mm(lambda e, p=p, src=src: e.matmul(p[:, 0:128], src, identb, start=True, stop=True), [sk, 'cstb'], [pk])
                if dk == 'A':
                    cp(Z[:, 0:128], p[:, 0:128], [pk], [K_('Z0')])
                elif dk == 'Bk':
                    cp(Bk, p[:, 0:128], [pk], [K_('Bk')])
                elif dk == 'Kk':
                    cp(Kk[:], p[:, 0:128], [pk], [K_('Kk')])
                else:
                    cp(V[:], p[:, 0:128], [pk], [K_('V')])
                yield
            Pc, Pck = X[:], K_('X')
            Ptc, Ptck = XM[:, 0:128], K_('XM')
            for lvl in range(5):
                pn = pw[ui][(2 * lvl) % 4]
                pnk = ('pw', ui, (2 * lvl) % 4)
                p1, pk1 = psreg()
                S.mm(lambda e, p1=p1, Pc=Pc, Ptc=Ptc: e.matmul(p1[:, 0:128], Ptc, Pc, start=True, stop=True), [Pck, Ptck], [pk1])
                if lvl < 4:
                    ptn = pw[ui][(2 * lvl + 1) % 4]
                    ptnk = ('pw', ui, (2 * lvl + 1) % 4)
                    p2, pk2 = psreg()
                    S.mm(lambda e, p2=p2, Pc=Pc, Ptc=Ptc: e.matmul(p2[:, 0:128], Pc, Ptc, start=True, stop=True), [Pck, Ptck], [pk2])
                    if lvl % 2 == 0:
                        cp(ptn[:], p2[:, 0:128], [pk2], [ptnk])
                    else:
                        cp(ptn[:], p2[:, 0:128], [pk2], [ptnk])
                cp(pn[:], p1[:, 0:128], [pk1], [pnk])
                Pc, Pck = pn[:], pnk
                if lvl < 4:
                    Ptc, Ptck = ptn[:], ptnk
                yield
                p3, pk3 = psreg()
                S.mm(lambda e, p3=p3, Pc=Pc: e.matmul(p3[:, 0:128], Pc, Tt[:], start=True, stop=True), [Pck, K_('Tt')], [pk3])
                S.op('dve', lambda e, p3=p3: e.tensor_tensor(out=Tt[:], in0=Tt[:], in1=p3[:, 0:128], op=ALU.add), [pk3, K_('Tt')], [K_('Tt')])
                if lvl == 4:
                    yield
            p, pk = psreg()
            S.mm(lambda e, p=p: e.matmul(p[:, 0:128], LM[:, 0:128], V[:], start=True, stop=True), [K_('LM'), K_('V')], [pk])
            cp(Z[:, 128:256], p[:, 0:128], [pk], [K_('Z1')])
            yield
            p, pk = psreg()
            S.mm(lambda e, p=p: e.matmul(p[:, 0:256], Tt[:], Z[:], start=True, stop=True), [K_('Tt'), K_('Z0'), K_('Z1')], [pk])
            cp(G[:], p[:, 0:256], [pk], [K_('G')])
            yield
            p, pk = psreg()
            S.mm(lambda e, p=p: e.matmul(p[:, 0:192], G[:, 0:128], XM[:, 128:320], start=True, stop=True), [K_('G'), K_('Bk'), K_('XM')], [pk])
            S.op('dve', lambda e, p=p: e.tensor_tensor(out=PpT[:], in0=p[:, 64:192], in1=ident, op=ALU.add), [pk, 'cst'], [K_('PpT')])
            S.op('dve', lambda e, p=p: e.tensor_tensor(out=RbT[:, 0:64], in0=p[:, 0:64], in1=Rs, op=ALU.add), [pk, kR], [K_('RbT')])
            p, pk = psreg()
            S.mm(lambda e, p=p: e.matmul(p[:, 0:128], Bk, G[:, 128:256], start=True, stop=False), [K_('G'), K_('Bk')], [pk])
            S.mm(lambda e, p=p: e.matmul(p[:, 0:128], Kk[:], V[:], start=False, stop=True), [K_('Kk'), K_('V')], [pk])
            S.op('dve', lambda e, p=p: e.tensor_scalar(out=qg[ui][:], in0=p[:, 0:128], scalar1=gcol, scalar2=None, op0=ALU.mult), [pk, kG], [K_('qg')])

        def chain(dirn, n, bi, ui, Hcur, Hnew, ysl, yslk):
            fwd = (dirn == 0)
            gcol = gam[bi][:, n * 64 + 63: n * 64 + 64] if fwd else gam[bi][:, n * 64: n * 64 + 1]
            kG = ('gam', bi)
            XM, LM, G = ub["XM"][ui], ub["LM"][ui], ub["G"][ui]
            V, PpT, RbT = (ub1[k][ui] for k in ("V", "PpT", "RbT"))
            K_ = lambda nm: ('u', nm, ui)
            p, pk = psreg()
            S.mm(lambda e, p=p: e.matmul(p[:, 0:64], G[:, 128:256], XM[:, 128:192], start=True, stop=False), [K_('G'), K_('XM')], [pk])
            S.mm(lambda e, p=p: e.matmul(p[:, 0:64], V[:], LM[:, 128:192], start=False, stop=False), [K_('V'), K_('LM')], [pk])
            S.mm(lambda e, p=p: e.matmul(p[:, 0:64], Hcur[0][:], RbT[:, 0:64], start=False, stop=True), [Hcur[1], K_('RbT')], [pk])
            S.op('act', lambda e, p=p: e.activation(out=ysl, in_=p[:, 0:64], func=AF.Copy), [pk], [yslk])
            p2, pk2 = psreg()
            S.mm(lambda e, p2=p2: e.matmul(p2[:, 0:128], PpT[:], Hcur[0][:], start=True, stop=True), [K_('PpT'), Hcur[1]], [pk2])
            S.op('dve', lambda e, p2=p2: e.scalar_tensor_tensor(out=Hf[:], in0=p2[:, 0:128], scalar=gcol, in1=qg[ui][:], op0=ALU.mult, op1=ALU.add),
                 [pk2, K_('qg'), kG], ['Hf'])
            S.op('act', lambda e: e.activation(out=Hnew[0][:], in_=Hf[:], func=AF.Copy), ['Hf'], [Hnew[1]])

        def epilogue(c, seg, t, bi, yt, ytk):
            t0 = t * TT
            P = lambda j: rwp[:, c * 10 + j: c * 10 + j + 1]
            bon, bonk = newwk()
            S.op('pool', lambda e: e.tensor_tensor(out=bon[:], in0=vTs[bi][:], in1=bsum[bi][:], op=ALU.mult), [('vTs', bi), ('bsum', bi)], [bonk])
            dma('sp', ybl[:], ybT_d[c * 128:(c + 1) * 128, seg * SEG + t0: seg * SEG + t0 + TT], [('ybT', c)], ['ybl'])
            ysum, ysk = newwk()
            S.op('dve', lambda e: e.tensor_tensor(out=ysum[:], in0=yt[:], in1=ybl[:], op=ALU.add), [ytk, 'ybl'], [ysk])
            yield
            p, pk = psfull()
            S.mm(lambda e, p=p: e.matmul(p[:], bones, ysum[:], start=True, stop=True), ['cst', ysk], [pk])
            yield
            yc, yck = newwk()
            S.op('dve', lambda e, p=p: e.scalar_tensor_tensor(out=yc[:], in0=p[:], scalar=-1.0 / 64, in1=ysum[:], op0=ALU.mult, op1=ALU.add), [pk, ysk], [yck])
            yield
            sq_, sqk = newwk()
            S.op('act', lambda e: e.activation(out=sq_[:], in_=yc[:], func=AF.Square), [yck], [sqk])
            yield
            p2, pk2 = psfull()
            S.mm(lambda e, p2=p2: e.matmul(p2[:], bones, sq_[:], start=True, stop=True), ['cst', sqk], [pk2])
            yield
            sd, sdk = newwk()
            S.op('act', lambda e, p2=p2: e.activation(out=sd[:], in_=p2[:], func=AF.Sqrt, scale=1.0 / 64, bias=epsl[:]), [pk2, 'epsl'], [sdk])
            yield
            S.op('dve', lambda e: e.reciprocal(out=sd[:], in_=sd[:]), [sdk], [sdk])
            yield
            S.op('dve', lambda e: e.tensor_tensor(out=yc[:], in0=yc[:], in1=sd[:], op=ALU.mult), [yck, sdk], [yck])
            yield
            S.op('pool', lambda e: e.tensor_scalar(out=yc[:], in0=yc[:], scalar1=P(8), scalar2=P(9), op0=ALU.mult, op1=ALU.add), [yck, 'rwp'], [yck])
            yield
            S.op('pool', lambda e: e.tensor_tensor(out=yc[:], in0=yc[:], in1=bon[:], op=ALU.add), [yck, bonk], [yck])
            p3, pk3 = psfull()
            S.mm(lambda e, p3=p3: e.matmul(p3[:], g2b_[:, c * 128:(c + 1) * 128], sgd[:, t0:t0 + TT], start=True, stop=True), ['lrw', 'lr'], [pk3])
            yield
            k = rc['os'] % 2
            rc['os'] += 1
            S.op('dve', lambda e, p3=p3, k=k: e.tensor_tensor(out=ostr[k][:], in0=yc[:], in1=p3[:], op=ALU.mult), [yck, pk3], [('ostr', k)])
            dma('sp', mixT_d[1024 + c * 128: 1024 + (c + 1) * 128, seg * SEG + t0: seg * SEG + t0 + TT], ostr[k][:], [('ostr', k)], [('mixT', seg)])

        tiles = []
        for c in range(RW_PAIRS):
            for dirn in (1, 0):
                segs = (1, 0) if dirn == 1 else (0, 1)
                for si, seg in enumerate(segs):
                    trange = range(TPS - 1, -1, -1) if dirn == 1 else range(TPS)
                    for ti, t in enumerate(trange):
                        tiles.append(dict(c=c, dirn=dirn, seg=seg, t=t, first=(si == 0 and ti == 0), link=(si == 1 and ti == 0)))
        cur_seg = None
        hi = 0
        early = None
        pend_epi = None
        for k, tl in enumerate(tiles):
            bi = k % 2
            c, dirn, seg, t = tl['c'], tl['dirn'], tl['seg'], tl['t']
            if cur_seg != seg:
                lowrank_prep(seg)
                cur_seg = seg
            if early is None:
                early = prep_tile(dirn, c, seg, t, bi)
            for _ in early:
                pass
            early = None
            if tl['first']:
                hi = 0
                S.op('pool', lambda e: e.memset(Hb[0][:], 0.0), [], [('Hb', 0)])
            if tl['link']:
                S.op('dve', lambda e, hi=hi: e.tensor_scalar(out=Hb[hi][:], in0=Hb[hi][:], scalar1=link[:, 0:1], scalar2=None, op0=ALU.mult), [('Hb', hi), 'link'], [('Hb', hi)])
            chunks = list(range(7, -1, -1)) if dirn == 1 else list(range(8))
            nprep = None
            if k + 1 < len(tiles) and tiles[k + 1]['seg'] == seg:
                n2 = tiles[k + 1]
                nprep = prep_tile(n2['dirn'], n2['c'], n2['seg'], n2['t'], (k + 1) % 2)
                early = nprep

            def adv():
                nonlocal nprep, pend_epi
                if pend_epi is not None:
                    try:
                        next(pend_epi)
                    except StopIteration:
                        pend_epi = None
                if nprep is not None:
                    try:
                        next(nprep)
                    except StopIteration:
                        nprep = None
            live = [offchain(dirn, n, bi, ui) for ui, n in enumerate(chunks)]
            while live:
                nxt = []
                for g in live:
                    try:
                        next(g)
                        nxt.append(g)
                    except StopIteration:
                        pass
                live = nxt
                adv()
            yi = k % 2
            for ui, n in enumerate(chunks):
                chain(dirn, n, bi, ui, (Hb[hi], ('Hb', hi)), (Hb[1 - hi], ('Hb', 1 - hi)), ysb[yi][:, n * 64:(n + 1) * 64], ('ysb', yi))
                hi = 1 - hi
                adv()
            if pend_epi is not None:
                for _ in pend_epi:
                    pass
                pend_epi = None
            if dirn == 1:
                dma('sp', ybT_d[c * 128:(c + 1) * 128, seg * SEG + t * TT: seg * SEG + (t + 1) * TT], ysb[yi][:], [('ysb', yi)], [('ybT', c)])
            else:
                pend_epi = epilogue(c, seg, t, bi, ysb[yi], ('ysb', yi))
                if not (k + 1 < len(tiles) and tiles[k + 1]['seg'] == seg):
                    for _ in pend_epi:
                        pass
                    pend_epi = None
        S.barrier()
        pr.close()

    if 'rwkv' in phases:
        phase_rwkv()
    def phase_C():
        pc = ExitStack()
        env = ffn_env(pc, 'C')
        for it in range(NT):
            env['wout'](it)
            env['rmsnorm_to_nT'](2)
            env['ffn'](1)
            env['store_y_tile'](it, 3)
        S.barrier()
        pc.close()

    if 'C' in phases:
        phase_C()
    S.barrier()
    S.emit()
    global LAST_S
    LAST_S = S
    return nc

LAST_S = None

def _const_tables():
    tri = np.ones((64, 64), np.float32)
    su1, sl1, iu1, il1 = np.triu(tri, 1), np.tril(tri, -1), np.triu(tri, 0), np.tril(tri, 0)

    def bdm(m):
        o = np.zeros((128, 128), np.float32)
        o[:64, :64] = m
        o[64:, 64:] = m
        return o
    cst = np.zeros((128, 1024), np.float32)
    cst[:, 0:128] = np.eye(128, dtype=np.float32)
    cst[:, 128:256] = bdm(tri)
    cst[:, 256:384] = bdm(su1)
    cst[:, 384:512] = bdm(sl1)
    cst[:, 512:640] = bdm(iu1)
    cst[:, 640:768] = bdm(il1)
    rmask = np.ones((128, 512), np.float32)
    rmask[:, ::64] = 0.0
    slopes = np.exp2(-8.0 * np.arange(1, 17, dtype=np.float64) / 16)
    kp = np.arange(128)[:, None]
    qp = np.arange(128)[None, :]
    etab = np.zeros((8, 128, 1536), np.float32)
    for h in range(16):
        for di, d in enumerate((1, 4, 16)):
            for j in range(2):
                rel = kp + 128 * j - 64 - qp
                e = np.where(np.abs(rel) <= 64, np.exp(-slopes[h] * d * np.abs(rel)), 0.0)
                o = ((h % 2) * 3 + di) * 256 + j * 128
                etab[h // 2, :, o:o + 128] = e
    return cst, rmask, etab


def _prep_weights(inp):
    f = np.float32
    A = lambda a: np.ascontiguousarray(np.asarray(a, dtype=f))
    out = {}
    for fi, pre in enumerate(("ffn1", "ffn2")):
        g = np.asarray(inp[pre + "_gate"][0]).reshape(DC, 128, FC, 128).transpose(2, 1, 0, 3).reshape(FC, 128, 2048)
        u = np.asarray(inp[pre + "_up"][0]).reshape(DC, 128, FC, 128).transpose(2, 1, 0, 3).reshape(FC, 128, 2048)
        out["wgu%d" % (fi + 1)] = A(np.concatenate([g, u], axis=2).reshape(FF, 4096))
        dn = np.asarray(inp[pre + "_down"][0]).reshape(FC, 128, DC, 128).transpose(2, 1, 0, 3).reshape(D, FF)
        out["wd%d" % (fi + 1)] = A(dn)
    w_in = np.asarray(inp["w_in"][0])
    fm_cols = [c * 128 for c in range(16)] + [3072 + c * 128 for c in range(ZC)]
    fm = [w_in[:, c0:c0 + 128].reshape(DC, 128, 128).transpose(1, 0, 2).reshape(128, 2048) for c0 in fm_cols]
    out["winfm"] = A(np.concatenate(fm, axis=0))
    tm_cols = [2048 + g * 256 for g in range(4)]
    tm = [w_in[:, c0:c0 + 256].reshape(DC, 128, 256).transpose(1, 0, 2).reshape(128, 4096) for c0 in tm_cols]
    out["wintm"] = A(np.concatenate(tm, axis=0))
    out["wo"] = A(np.asarray(inp["w_out"][0]).reshape(DC, 128, DC, 128).transpose(2, 1, 0, 3).reshape(D, D))
    gains = np.zeros((128, 4 * DC), f)
    for gi, g in enumerate((inp["ffn1_norm"][0], inp["mix_norm"][0], inp["ffn2_norm"][0], inp["final_norm"])):
        gains[:, gi * DC:(gi + 1) * DC] = np.asarray(g).reshape(DC, 128).T
    out["gains"] = gains
    mu = np.zeros((128, 2 * ZC), f)
    mu[:, 0:ZC] = np.asarray(inp["mu_prev"][0]).reshape(ZC, 128).T
    mu[:, ZC:] = np.asarray(inp["mu_next"][0]).reshape(ZC, 128).T
    out["mu"] = mu
    rwp = np.zeros((128, 80), f)
    vecs = (inp["w0_f"][0], inp["w0_b"][0], inp["a0_f"][0], inp["a0_b"][0], inp["k_k"][0], inp["k_a"][0], None,
            np.asarray(inp["r_k"][0]).reshape(1024), inp["ln_x_w"][0], inp["ln_x_b"][0])
    for j, v in enumerate(vecs):
        if v is None:
            continue
        rwp[:, j::10] = np.asarray(v).reshape(8, 128).T
    out["rwp"] = rwp
    out["w2"] = A(np.concatenate([inp["w2_f"][0], inp["w2_b"][0]], axis=0))
    out["a2"] = A(np.concatenate([inp["a2_f"][0], inp["a2_b"][0]], axis=0))
    out["g2"] = A(inp["g2"][0])
    cst, rmask, etab = _const_tables()
    out["cst"], out["rmask"], out["etab"] = cst, rmask, etab
    return out


def _core_plan(SEG, x_prompt, x_sample):
    plan = []
    xp, xs = np.asarray(x_prompt), np.asarray(x_sample)
    nprompt_cores = xp.shape[0] * (xp.shape[1] // (NSEG * SEG))
    for b in range(xp.shape[0]):
        for part in range(xp.shape[1] // (NSEG * SEG)):
            assert xp.shape[1] == NSEG * SEG
            plan.append(([('p', b, 0), ('p', b, SEG)], 1.0))
    nb = xs.shape[0]
    assert xs.shape[1] == SEG
    rest = 8 - len(plan)
    two = nb - rest
    i = 0
    for c in range(rest):
        if c < two:
            plan.append(([('s', i, 0), ('s', i + 1, 0)], 0.0))
            i += 2
        elif i < nb:
            plan.append(([('s', i, 0), None], 0.0))
            i += 1
        else:
            plan.append(([None, None], 0.0))
    assert i == nb
    return plan


_NC_CACHE = {}


def run(inputs, SEG, debug=False, phases=('A', 'fix', 'att', 'rwkv', 'C')):
    wts = _prep_weights(inputs)
    xp, xs = np.asarray(inputs["x_prompt"], np.float32), np.asarray(inputs["x_sample"], np.float32)
    plan = _core_plan(SEG, xp, xs)
    in_maps = []
    for segs, lk in plan:
        x = np.zeros((NSEG * SEG, D), np.float32)
        for si, sdesc in enumerate(segs):
            if sdesc is None:
                continue
            kind, b, st = sdesc
            src = xp if kind == 'p' else xs
            x[si * SEG:(si + 1) * SEG] = src[b, st:st + SEG]
        m = dict(wts)
        m["x"] = x
        m["link"] = np.full((128, 1), lk, np.float32)
        in_maps.append(m)
    key = (SEG, debug, tuple(phases))
    if key not in _NC_CACHE:
        _NC_CACHE[key] = build(SEG, debug=debug, phases=phases)
    nc = _NC_CACHE[key]
    res = run_bass_kernel_spmd(nc, in_maps, core_ids=list(range(8)))
    yp = np.zeros(xp.shape, np.float32)
    ys = np.zeros(xs.shape, np.float32)
    for ci, (segs, lk) in enumerate(plan):
        y = res.results[ci]["y"]
        for si, sdesc in enumerate(segs):
            if sdesc is None:
                continue
            kind, b, st = sdesc
            (yp if kind == 'p' else ys)[b, st:st + SEG] = y[si * SEG:(si + 1) * SEG]
    return (yp, ys), res, plan


def kernel(**inputs):
    (yp, ys), _, _ = run(inputs, 4096)
    return (yp, ys)
```

```python
from contextlib import ExitStack
import numpy as np
import concourse.bass as bass
import concourse.mybir as mybir
from concourse.bass_utils import run_bass_kernel_spmd

F32 = mybir.dt.float32
BF16 = mybir.dt.bfloat16
ALU = mybir.AluOpType
AF = mybir.ActivationFunctionType

D = 2048
DC = 16
FF = 5632
FC = 44
TT = 512
NSEG = 2
APAD = 1024
NFM = 43
ZC = 27
C0 = 0.6065306597126334
NORM_EPS = 1e-6
LN_EPS = 64e-5

ENGS = ("pe", "dve", "act", "pool", "sp")
NDMASEM = 64
NHWSEM = 40


class Sched:
    def __init__(self, nc, es):
        self.nc = nc
        self.csem = {e: es.enter_context(nc.semaphore("c_" + e)) for e in ENGS}
        self.dsem = [es.enter_context(nc.semaphore("d%d" % i)) for i in range(NDMASEM)]
        self.dval = [0] * NDMASEM
        self.dnext = 0
        self.dnext_sw = 0
        self.count = {e: 0 for e in ENGS}
        self.prog = {e: [] for e in ENGS}
        self.waited = {e: {} for e in ENGS}
        self.lastw = {}
        self.reads = {}

    def _need(self, eng, tok, deps):
        if tok is None:
            return
        if tok[0] == 'e':
            _, e2, idx = tok
            if e2 == eng:
                if eng == 'pe':
                    return
                if idx <= self.count[eng] - 3:
                    return
            key = e2
        else:
            _, i, idx = tok
            key = ('d', i)
        if self.waited[eng].get(key, 0) >= idx:
            return
        if deps.get(key, 0) < idx:
            deps[key] = idx

    def _emit_waits(self, eng, deps):
        for key, idx in deps.items():
            sem = self.csem[key] if isinstance(key, str) else self.dsem[key[1]]
            self.prog[eng].append(('w', sem, idx))
            self.waited[eng][key] = idx

    def _deps(self, eng, reads, writes):
        deps = {}
        for k in reads:
            self._need(eng, self.lastw.get(k), deps)
        for k in writes:
            self._need(eng, self.lastw.get(k), deps)
            for t in self.reads.get(k, ()):
                self._need(eng, t, deps)
        return deps

    def _record(self, tok, reads, writes):
        for k in reads:
            self.reads.setdefault(k, []).append(tok)
        for k in writes:
            self.lastw[k] = tok
            self.reads[k] = []

    def op(self, eng, fn, reads=(), writes=()):
        deps = self._deps(eng, reads, writes)
        self._emit_waits(eng, deps)
        self.count[eng] += 1
        self.prog[eng].append(('o', fn, self.csem[eng], 1))
        tok = ('e', eng, self.count[eng])
        self._record(tok, reads, writes)
        return tok

    def mm(self, fn, reads=(), writes=(), last=True):
        return self.op('pe', fn, reads, writes)

    def dma(self, eng, fn, reads=(), writes=()):
        deps = self._deps(eng, reads, writes)
        if eng == 'pool':
            i = NHWSEM + self.dnext_sw
            self.dnext_sw = (self.dnext_sw + 1) % (NDMASEM - NHWSEM)
        else:
            i = self.dnext
            self.dnext = (self.dnext + 1) % NHWSEM
        if self.dval[i] > 0:
            self._need(eng, ('d', i, self.dval[i]), deps)
        self._emit_waits(eng, deps)
        self.dval[i] += 16
        self.prog[eng].append(('o', fn, self.dsem[i], 16))
        tok = ('d', i, self.dval[i])
        self._record(tok, reads, writes)
        return tok

    def barrier(self):
        for eng in ENGS:
            deps = {}
            for e2 in ENGS:
                if e2 != eng and self.count[e2] > 0:
                    self._need(eng, ('e', e2, self.count[e2]), deps)
            for i in range(NDMASEM):
                if self.dval[i] > 0:
                    self._need(eng, ('d', i, self.dval[i]), deps)
            self._emit_waits(eng, deps)

    def emit(self):
        prog = self.prog

        def run(engobj, items):
            for it in items:
                if it[0] == 'w':
                    engobj.wait_ge(it[1], it[2])
                else:
                    ins = it[1](engobj)
                    if it[2] is not None:
                        ins.then_inc(it[2], it[3])

        with self.nc.Block() as block:
            @block.sync
            def _(e):
                run(e, prog['sp'])

            @block.tensor
            def _(e):
                run(e, prog['pe'])

            @block.vector
            def _(e):
                run(e, prog['dve'])

            @block.scalar
            def _(e):
                run(e, prog['act'])

            @block.gpsimd
            def _(e):
                run(e, prog['pool'])


def build(SEG, debug=False, phases=('A', 'fix', 'att', 'rwkv', 'C'), RW_PAIRS=8, ATT_PAIRS=8):
    NTOK = NSEG * SEG
    NT = NTOK // TT
    TPS = SEG // TT
    SEGA = SEG + 2 * APAD
    SEGZ = SEG + 2
    nc = bass.Bass("TRN2", target_bir_lowering=False)
    es = ExitStack()
    S = Sched(nc, es)

    def din(name, shape, dt=F32):
        return nc.dram_tensor(name, list(shape), dt, kind="ExternalInput").ap()

    def dscr(name, shape, dt):
        if debug:
            return nc.dram_tensor(name, list(shape), dt, kind="ExternalOutput").ap()
        return nc.dram_tensor(name, list(shape), dt).ap()

    x_d = din("x", [NTOK, D])
    link_d = din("link", [128, 1])
    wgu_h = [din("wgu1", [FF, 4096]), din("wgu2", [FF, 4096])]
    wd_h = [din("wd1", [D, FF]), din("wd2", [D, FF])]
    winfm_h = din("winfm", [NFM * 128, 2048])
    wintm_h = din("wintm", [4 * 128, 4096])
    wo_h = din("wo", [D, D])
    gains_d = din("gains", [128, 4 * DC])
    mu_d = din("mu", [128, 2 * ZC])
    rwp_d = din("rwp", [128, 8 * 10])
    w2_d = din("w2", [128, 1024])
    a2_d = din("a2", [128, 1024])
    g2_d = din("g2", [128, 1024])
    etab_d = din("etab", [8, 128, 2 * 3 * 256])
    cst_d = din("cst", [128, 8 * 128])
    rmask_d = din("rmask", [128, 512])
    y_d = nc.dram_tensor("y", [NTOK, D], F32, kind="ExternalOutput").ap()

    wgu_b = [dscr("wgu1b", [FF, 4096], BF16), dscr("wgu2b", [FF, 4096], BF16)]
    wd_b = [dscr("wd1b", [D, FF], BF16), dscr("wd2b", [D, FF], BF16)]
    winfm_b = dscr("winfmb", [NFM * 128, 2048], BF16)
    wintm_b = dscr("wintmb", [4 * 128, 4096], BF16)
    wo_b = dscr("wob", [D, D], BF16)
    x1T_d = dscr("x1T", [D, NTOK], F32)
    qT_d = dscr("qT", [1024, NTOK], BF16)
    kT_d = dscr("kT", [1024, NSEG * SEGA], BF16)
    vatt_d = dscr("vatt", [NSEG * SEGA + 64, 16 * 65], BF16)
    zT_d = dscr("zT", [ZC * 128, NSEG * SEGZ], F32)
    ybT_d = dscr("ybT", [1024, NTOK], F32)
    mixT_d = dscr("mixT", [D, NTOK], BF16)

    def sb(name, shape, dt, stack=es):
        return stack.enter_context(nc.sbuf_tensor("s_" + name, list(shape), dt))

    PS = [es.enter_context(nc.psum_tensor("ps%d" % i, [128, 512], F32)) for i in range(8)]

    def dma(eng, out, in_, reads, writes, **kw):
        return S.dma(eng, lambda e, o=out, i=in_, k=kw: e.dma_start(out=o, in_=i, **k), reads, writes)

    cst = sb("cst", [128, 8 * 128], F32)
    cstb = sb("cstb", [128, 8 * 128], BF16)
    gains = sb("gains", [128, 4 * DC], F32)
    link = sb("link", [128, 1], F32)
    zeros = sb("zeros", [128, 1040], BF16)
    zerosf = sb("zerosf", [128, 128], F32)
    epsn = sb("epsn", [128, 1], F32)
    epsl = sb("epsl", [128, 1], F32)
    onesb = sb("onesb", [128, 128], BF16)
    dma('sp', cst[:], cst_d[:, :], [], ['cst'])
    dma('sp', gains[:], gains_d[:, :], [], ['gains'])
    dma('sp', link[:], link_d[:, :], [], ['link'])
    S.op('dve', lambda e: e.tensor_copy(out=cstb[:], in_=cst[:]), ['cst'], ['cstb'])
    S.op('pool', lambda e: e.memset(zeros[:], 0.0), [], ['zeros'])
    S.op('pool', lambda e: e.memset(zerosf[:], 0.0), [], ['zerosf'])
    S.op('pool', lambda e: e.memset(epsn[:], NORM_EPS), [], ['epsn'])
    S.op('pool', lambda e: e.memset(epsl[:], LN_EPS), [], ['epsl'])
    S.op('pool', lambda e: e.memset(onesb[:], 1.0), [], ['onesb'])
    ident = cst[:, 0:128]
    identb = cstb[:, 0:128]
    bones = cst[:, 128:256]
    m_su = cst[:, 256:384]
    m_sl = cst[:, 384:512]
    m_iu = cst[:, 512:640]
    m_il = cst[:, 640:768]

    def cast_rows(src, dst, r0, r1, key):
        dma('pool', dst[r0:r1, :], src[r0:r1, :], [], [key], max_dma_last_dim=4096)

    for j in range(0, FC, 2):
        cast_rows(wgu_h[0], wgu_b[0], j * 128, (j + 2) * 128, ('wgu0', j // 2))
    for m in range(DC):
        cast_rows(wd_h[0], wd_b[0], m * 128, (m + 1) * 128, ('wd0', m))
    for c in range(0, NFM, 4):
        cast_rows(winfm_h, winfm_b, c * 128, min(NFM, c + 4) * 128, ('winfm', c // 4))
    for g in range(0, 4, 2):
        cast_rows(wintm_h, wintm_b, g * 128, (g + 2) * 128, ('wintm', g // 2))
    for m in range(0, DC, 4):
        cast_rows(wo_h, wo_b, m * 128, (m + 4) * 128, ('wo', m // 4))
    for j in range(0, FC, 2):
        cast_rows(wgu_h[1], wgu_b[1], j * 128, (j + 2) * 128, ('wgu1', j // 2))
    for m in range(DC):
        cast_rows(wd_h[1], wd_b[1], m * 128, (m + 1) * 128, ('wd1', m))

    for s in range(NSEG):
        for off in (0, APAD + SEG):
            for c in range(8):
                dma('act', kT_d[c * 128:(c + 1) * 128, s * SEGA + off: s * SEGA + off + APAD], zeros[:, 0:APAD],
                    ['zeros'], [('kTpad', s, off)])
            for a in range(APAD // 128):
                r0 = s * SEGA + off + a * 128
                dma('act', vatt_d[r0:r0 + 128, :], zeros[:, 0:1040], ['zeros'], [('vapad', s, off)])
        for off in (0, SEG + 1):
            dma('act', zT_d[:, s * SEGZ + off: s * SEGZ + off + 1].rearrange("(c p) o -> p c o", p=128),
                zerosf[:, 0:ZC].unsqueeze(2), ['zerosf'], [('zpad', s, off)], allow_slow_non_contiguous=True)

    _sb_outer = sb

    def ffn_env(pa, tag):
        def sb(name, shape, dt, stack=es):
            return _sb_outer(name + tag, shape, dt, stack)
        xT = sb("xT", [128, DC, TT], F32, pa)
        nT = sb("nT", [128, DC, TT], BF16, pa)
        actT = sb("actT", [128, FC, TT], BF16, pa)
        wring = [sb("wring%d" % i, [128, 4096], BF16, pa) for i in range(3)]
        dring = [sb("dring%d" % i, [128, 22 * 128], BF16, pa) for i in range(3)]
        xtok = [sb("xtok%d" % i, [128, 1024], F32, pa) for i in range(2)]
        sq = [sb("sq%d" % i, [128, TT], BF16, pa) for i in range(2)]
        rstd = sb("rstd", [128, TT], F32, pa)
        sg = [sb("sg%d" % i, [128, TT], F32, pa) for i in range(2)]
        stg = [sb("stg%d" % i, [128, TT], F32, pa) for i in range(3)]
        stgb = [sb("stgb%d" % i, [128, TT], BF16, pa) for i in range(2)]
        vstg = [sb("vstg%d" % i, [128, 4 * 65], BF16, pa) for i in range(2)]
        cnt = {'w': 0, 'd': 0, 'x': 0, 'sq': 0, 'sg': 0, 'stg': 0, 'stgb': 0, 'tp': 0, 'vs': 0, 'ev': 0}
        xT_keys = [('xT', dc) for dc in range(DC)]

        def evac_copy(out, in_, reads, writes):
            cnt['ev'] += 1
            if cnt['ev'] % 2:
                S.op('act', lambda e, o=out, i=in_: e.activation(out=o, in_=i, func=AF.Copy), reads, writes)
            else:
                S.op('dve', lambda e, o=out, i=in_: e.tensor_copy(out=o, in_=i), reads, writes)

        def load_x_tile(it):
            for s in range(4):
                for hf in range(2):
                    xb_ = cnt['x'] % 2
                    cnt['x'] += 1
                    r0 = it * TT + s * 128
                    dma('sp', xtok[xb_][:], x_d[r0:r0 + 128, hf * 1024:(hf + 1) * 1024], [], [('xtok', xb_)])
                    for g in range(2):
                        bank = 6 + cnt['tp'] % 2
                        cnt['tp'] += 1
                        for q in range(4):
                            S.mm(lambda e, b=bank, q=q, g=g, xb_=xb_: e.transpose(PS[b][:, q * 128:(q + 1) * 128],
                                                                                 xtok[xb_][:, (g * 4 + q) * 128:(g * 4 + q + 1) * 128], ident),
                                 [('xtok', xb_), 'cst'], [('ps', bank)], last=(q == 3))
                        dc0 = hf * 8 + g * 4
                        evac_copy(xT[:, dc0:dc0 + 4, s * 128:(s + 1) * 128], PS[bank][:].rearrange("p (q t) -> p q t", q=4),
                                  [('ps', bank)], [('xT', dc0 + q) for q in range(4)])

        def store_y_tile(it, gi):
            rms_stats()
            for dc in range(DC):
                if dc % 2 == 0:
                    S.op('dve', lambda e, dc=dc: e.scalar_tensor_tensor(out=xT[:, dc, :], in0=xT[:, dc, :], scalar=gains[:, gi * DC + dc: gi * DC + dc + 1],
                                                                        in1=rstd[:], op0=ALU.mult, op1=ALU.mult), [('xT', dc), 'rstd', 'gains'], [('xT', dc)])
                else:
                    S.op('pool', lambda e, dc=dc: e.tensor_scalar(out=xT[:, dc, :], in0=xT[:, dc, :], scalar1=gains[:, gi * DC + dc: gi * DC + dc + 1], scalar2=0.0, op0=ALU.mult, op1=ALU.add),
                         [('xT', dc), 'gains'], [('xT', dc)])
                    S.op('pool', lambda e, dc=dc: e.tensor_tensor(out=xT[:, dc, :], in0=xT[:, dc, :], in1=rstd[:], op=ALU.mult), [('xT', dc), 'rstd'], [('xT', dc)])
            for s in range(4):
                for hf in range(2):
                    xb_ = cnt['x'] % 2
                    cnt['x'] += 1
                    for g in range(2):
                        bank = 6 + cnt['tp'] % 2
                        cnt['tp'] += 1
                        for q in range(4):
                            dc = hf * 8 + g * 4 + q
                            S.mm(lambda e, b=bank, q=q, dc=dc, s=s: e.transpose(PS[b][:, q * 128:(q + 1) * 128], xT[:, dc, s * 128:(s + 1) * 128], ident),
                                 [('xT', dc), 'cst'], [('ps', bank)], last=(q == 3))
                        evac_copy(xtok[xb_][:, g * 512:(g + 1) * 512], PS[bank][:], [('ps', bank)], [('xtok', xb_)])
                    r0 = it * TT + s * 128
                    dma('sp', y_d[r0:r0 + 128, hf * 1024:(hf + 1) * 1024], xtok[xb_][:], [('xtok', xb_)], ['y'])

        def rms_stats():
            for dc in range(DC):
                k = cnt['sq'] % 2
                cnt['sq'] += 1
                S.op('act', lambda e, k=k, dc=dc: e.activation(out=sq[k][:], in_=xT[:, dc, :], func=AF.Square), [('xT', dc)], [('sq', k)])
                S.mm(lambda e, k=k, dc=dc: e.matmul(PS[4][:], onesb[:], sq[k][:], start=(dc == 0), stop=(dc == DC - 1)),
                     [('sq', k), 'onesb'], [('ps', 4)], last=(dc == DC - 1))
            S.op('act', lambda e: e.activation(out=rstd[:], in_=PS[4][:], func=AF.Sqrt, scale=1.0 / D, bias=epsn[:]), [('ps', 4), 'epsn'], ['rstd'])
            S.op('dve', lambda e: e.reciprocal(out=rstd[:], in_=rstd[:]), ['rstd'], ['rstd'])

        def rmsnorm_to_nT(gi):
            rms_stats()
            for dc in range(DC):
                if dc % 2 == 0:
                    S.op('dve', lambda e, dc=dc: e.scalar_tensor_tensor(out=nT[:, dc, :], in0=xT[:, dc, :], scalar=gains[:, gi * DC + dc: gi * DC + dc + 1],
                                                                        in1=rstd[:], op0=ALU.mult, op1=ALU.mult), [('xT', dc), 'rstd', 'gains'], [('nT', dc)])
                else:
                    k = cnt['stg'] % 3
                    cnt['stg'] += 1
                    S.op('pool', lambda e, dc=dc, k=k: e.tensor_scalar(out=stg[k][:], in0=xT[:, dc, :], scalar1=gains[:, gi * DC + dc: gi * DC + dc + 1], scalar2=0.0, op0=ALU.mult, op1=ALU.add),
                         [('xT', dc), 'gains'], [('stg', k)])
                    S.op('pool', lambda e, dc=dc, k=k: e.tensor_tensor(out=nT[:, dc, :], in0=stg[k][:], in1=rstd[:], op=ALU.mult), [('stg', k), 'rstd'], [('nT', dc)])

        def ffn(fi):
            for j in range(FC):
                w = cnt['w'] % 3
                cnt['w'] += 1
                dma('sp', wring[w][:], wgu_b[fi][j * 128:(j + 1) * 128, :], [('wgu%d' % fi, j // 2)], [('wring', w)])
                bg = j % 2
                bu = 2 + j % 2
                for kc in range(DC):
                    S.mm(lambda e, w=w, kc=kc, bg=bg: e.matmul(PS[bg][:], wring[w][:, kc * 128:(kc + 1) * 128], nT[:, kc, :], start=(kc == 0), stop=(kc == DC - 1)),
                         [('wring', w), ('nT', kc)], [('ps', bg)], last=(kc == DC - 1))
                for kc in range(DC):
                    S.mm(lambda e, w=w, kc=kc, bu=bu: e.matmul(PS[bu][:], wring[w][:, 2048 + kc * 128: 2048 + (kc + 1) * 128], nT[:, kc, :], start=(kc == 0), stop=(kc == DC - 1)),
                         [('wring', w), ('nT', kc)], [('ps', bu)], last=(kc == DC - 1))
                k = cnt['sg'] % 2
                cnt['sg'] += 1
                S.op('act', lambda e, k=k, bg=bg: e.activation(out=sg[k][:], in_=PS[bg][:], func=AF.Silu), [('ps', bg)], [('sg', k)])
                S.op('dve', lambda e, k=k, bu=bu, j=j: e.tensor_tensor(out=actT[:, j, :], in0=sg[k][:], in1=PS[bu][:], op=ALU.mult), [('sg', k), ('ps', bu)], [('actT', j)])
            for m in range(DC):
                bd_ = 4 + m % 2
                for hf in range(2):
                    dd = cnt['d'] % 3
                    cnt['d'] += 1
                    dma('sp', dring[dd][:], wd_b[fi][m * 128:(m + 1) * 128, hf * 2816:(hf + 1) * 2816], [('wd%d' % fi, m)], [('dring', dd)])
                    for f2 in range(22):
                        fc = hf * 22 + f2
                        S.mm(lambda e, dd=dd, f2=f2, fc=fc, bd_=bd_: e.matmul(PS[bd_][:], dring[dd][:, f2 * 128:(f2 + 1) * 128], actT[:, fc, :], start=(fc == 0), stop=(fc == FC - 1)),
                             [('dring', dd), ('actT', fc)], [('ps', bd_)], last=(fc == FC - 1))
                S.op('dve', lambda e, m=m, bd_=bd_: e.scalar_tensor_tensor(out=xT[:, m, :], in0=PS[bd_][:], scalar=0.5, in1=xT[:, m, :], op0=ALU.mult, op1=ALU.add),
                     [('ps', bd_), ('xT', m)], [('xT', m)])

        fm_dest = [('q', c) for c in range(8)] + [('k', c) for c in range(8)] + [('z', c) for c in range(ZC)]

        def proj(it):
            seg = it // TPS
            t0 = (it % TPS) * TT
            for ci, (kind, c) in enumerate(fm_dest):
                w = cnt['w'] % 3
                cnt['w'] += 1
                dma('sp', wring[w][:, 0:2048], winfm_b[ci * 128:(ci + 1) * 128, :], [('winfm', ci // 4)], [('wring', w)])
                bank = ci % 2
                for kc in range(DC):
                    S.mm(lambda e, w=w, kc=kc, bank=bank: e.matmul(PS[bank][:], wring[w][:, kc * 128:(kc + 1) * 128], nT[:, kc, :], start=(kc == 0), stop=(kc == DC - 1)),
                         [('wring', w), ('nT', kc)], [('ps', bank)], last=(kc == DC - 1))
                if kind in ('q', 'k'):
                    k = cnt['stgb'] % 2
                    cnt['stgb'] += 1
                    evac_copy(stgb[k][:], PS[bank][:], [('ps', bank)], [('stgb', k)])
                    if kind == 'q':
                        dma('act', qT_d[c * 128:(c + 1) * 128, it * TT:(it + 1) * TT], stgb[k][:], [('stgb', k)], [('qT', seg)])
                    else:
                        col = seg * SEGA + APAD + t0
                        dma('act', kT_d[c * 128:(c + 1) * 128, col:col + TT], stgb[k][:], [('stgb', k)], [('kT', seg)])
                else:
                    k = cnt['stg'] % 3
                    cnt['stg'] += 1
                    evac_copy(stg[k][:], PS[bank][:], [('ps', bank)], [('stg', k)])
                    col = seg * SEGZ + 1 + t0
                    dma('act', zT_d[c * 128:(c + 1) * 128, col:col + TT], stg[k][:], [('stg', k)], [('zT', seg)])
            for g in range(4):
                w = cnt['w'] % 3
                cnt['w'] += 1
                dma('sp', wring[w][:], wintm_b[g * 128:(g + 1) * 128, :], [('wintm', g // 2)], [('wring', w)])
                for s in range(4):
                    bank = 2 + (g * 4 + s) % 2
                    for kc in range(DC):
                        S.mm(lambda e, w=w, kc=kc, bank=bank, s=s: e.matmul(PS[bank][:, 0:256], nT[:, kc, s * 128:(s + 1) * 128], wring[w][:, kc * 256:(kc + 1) * 256],
                                                                             start=(kc == 0), stop=(kc == DC - 1)),
                             [('wring', w), ('nT', kc)], [('ps', bank)], last=(kc == DC - 1))
                    if True:
                        k = cnt['vs'] % 2
                        cnt['vs'] += 1
                        S.op('pool', lambda e, k=k: e.memset(vstg[k][:], 1.0), [], [('vstg', k)])
                        evac_copy(vstg[k][:].rearrange("p (h c) -> p h c", h=4)[:, :, 0:64], PS[bank][:, 0:256].rearrange("p (h c) -> p h c", h=4),
                                  [('ps', bank), ('vstg', k)], [('vstg', k)])
                        row = seg * SEGA + APAD + t0 + s * 128
                        dma('act', vatt_d[row:row + 128, g * 260:(g + 1) * 260], vstg[k][:], [('vstg', k)], [('vatt', seg)])

        def wout(it):
            dma('sp', nT[:], mixT_d[:, it * TT:(it + 1) * TT].rearrange("(c p) t -> p c t", p=128), [('mixT', it // TPS)], [('nT', dc) for dc in range(DC)])
            dma('sp', xT[:], x1T_d[:, it * TT:(it + 1) * TT].rearrange("(c p) t -> p c t", p=128), [('x1T', it)], xT_keys)
            for m in range(DC):
                w = cnt['w'] % 3
                cnt['w'] += 1
                dma('sp', wring[w][:, 0:2048], wo_b[m * 128:(m + 1) * 128, :], [('wo', m // 4)], [('wring', w)])
                bank = m % 2
                for kc in range(DC):
                    S.mm(lambda e, w=w, kc=kc, bank=bank: e.matmul(PS[bank][:], wring[w][:, kc * 128:(kc + 1) * 128], nT[:, kc, :], start=(kc == 0), stop=(kc == DC - 1)),
                         [('wring', w), ('nT', kc)], [('ps', bank)], last=(kc == DC - 1))
                S.op('dve', lambda e, m=m, bank=bank: e.tensor_tensor(out=xT[:, m, :], in0=xT[:, m, :], in1=PS[bank][:], op=ALU.add), [('ps', bank), ('xT', m)], [('xT', m)])

        return dict(load_x_tile=load_x_tile, rmsnorm_to_nT=rmsnorm_to_nT, ffn=ffn, proj=proj, wout=wout, store_y_tile=store_y_tile, xT=xT, xT_keys=xT_keys)

    def phase_A():
        pa = ExitStack()
        env = ffn_env(pa, 'A')
        for it in range(NT):
            env['load_x_tile'](it)
            env['rmsnorm_to_nT'](0)
            env['ffn'](0)
            dma('act', x1T_d[:, it * TT:(it + 1) * TT].rearrange("(c p) t -> p c t", p=128), env['xT'][:], env['xT_keys'], [('x1T', it)])
            env['rmsnorm_to_nT'](1)
            env['proj'](it)
        S.barrier()
        pa.close()


    if 'A' in phases:
        phase_A()
    def phase_fix():
        pf = ExitStack()
        fk = sb("fk", [128, APAD], BF16, pf)
        fv = sb("fv", [128, APAD // 128, 1040], BF16, pf)
        fz = sb("fz", [128, ZC], F32, pf)
        for (ds, doff, ss, soff) in ((0, APAD + SEG, 1, APAD), (1, 0, 0, SEG)):
            for c in range(8):
                dma('sp', fk[:], kT_d[c * 128:(c + 1) * 128, ss * SEGA + soff: ss * SEGA + soff + APAD], [('kT', ss)], ['fk'])
                S.op('dve', lambda e: e.tensor_scalar(out=fk[:], in0=fk[:], scalar1=link[:, 0:1], scalar2=None, op0=ALU.mult), ['fk', 'link'], ['fk'])
                dma('sp', kT_d[c * 128:(c + 1) * 128, ds * SEGA + doff: ds * SEGA + doff + APAD], fk[:], ['fk', ('kTpad', ds, doff)], [('kTpad', ds, doff)])
            dma('sp', fv[:], vatt_d[ss * SEGA + soff: ss * SEGA + soff + APAD, :].rearrange("(a p) c -> p a c", p=128), [('vatt', ss)], ['fv'])
            S.op('dve', lambda e: e.tensor_scalar(out=fv[:], in0=fv[:], scalar1=link[:, 0:1], scalar2=None, op0=ALU.mult), ['fv', 'link'], ['fv'])
            dma('sp', vatt_d[ds * SEGA + doff: ds * SEGA + doff + APAD, :].rearrange("(a p) c -> p a c", p=128), fv[:], ['fv', ('vapad', ds, doff)], [('vapad', ds, doff)])
        for (ds, doff, ss, soff) in ((0, SEG + 1, 1, 1), (1, 0, 0, SEG)):
            dma('sp', fz[:].unsqueeze(2), zT_d[:, ss * SEGZ + soff: ss * SEGZ + soff + 1].rearrange("(c p) o -> p c o", p=128), [('zT', ss)], ['fz'], allow_slow_non_contiguous=True)
            S.op('dve', lambda e: e.tensor_scalar(out=fz[:], in0=fz[:], scalar1=link[:, 0:1], scalar2=None, op0=ALU.mult), ['fz', 'link'], ['fz'])
            dma('sp', zT_d[:, ds * SEGZ + doff: ds * SEGZ + doff + 1].rearrange("(c p) o -> p c o", p=128), fz[:].unsqueeze(2), ['fz', ('zpad', ds, doff)], [('zpad', ds, doff)], allow_slow_non_contiguous=True)
        S.barrier()
        pf.close()

    if 'fix' in phases:
        phase_fix()
    def phase_att():
        pt = ExitStack()
        NKT = {1: SEG // 128 + 1, 4: SEG // 512 + 1, 16: SEG // 2048 + 1}
        qt = sb("qt", [128, SEG], BF16, pt)
        kt = sb("kt", [128, SEGA], BF16, pt)
        vres = {d: sb("vres%d" % d, [128, d * NKT[d], 130], BF16, pt) for d in (1, 4, 16)}
        acc = [sb("acc%d" % h, [65, SEG], F32, pt) for h in range(2)]
        etf = sb("etf", [128, 1536], F32, pt)
        etb = sb("etb", [128, 1536], BF16, pt)
        pex = [sb("pex%d" % i, [128, 256], BF16, pt) for i in range(6)]
        pm = [sb("pm%d" % i, [128, 256], BF16, pt) for i in range(6)]
        rden = sb("rden", [64, 512], F32, pt)
        ostg = [sb("ostg%d" % i, [64, 512], BF16, pt) for i in range(2)]
        sel = sb("sel", [65, 64], F32, pt)
        S.op('pool', lambda e: e.memset(sel[:], 0.0), [], ['sel'])
        S.op('pool', lambda e: e.memset(sel[64:65, :], 1.0), ['sel'], ['sel'])
        ac = {'u': 0, 'o': 0, 'os': 0}

        def emit_pv(pi, d, r, b, h):
            ob = 4 + ac['o'] % 2
            ac['o'] += 1
            for j in range(2):
                S.mm(lambda e, ob=ob, j=j, pi=pi, d=d, r=r, b=b, h=h: e.matmul(PS[ob][0:65, 0:128], vres[d][:, r * NKT[d] + b + j, h * 65:(h + 1) * 65],
                                                                               pm[pi][:, j * 128:(j + 1) * 128], start=(j == 0), stop=(j == 1)),
                     [('vres', d), ('pm', pi)], [('ps', ob)])
            asl = acc[h][:, r + d * 128 * b: r + d * 128 * b + d * 127 + 1: d]
            if d == 1:
                S.op('dve', lambda e, asl=asl, ob=ob: e.tensor_copy(out=asl, in_=PS[ob][0:65, 0:128]), [('ps', ob)], [('acc', h)])
            else:
                S.op('dve', lambda e, asl=asl, ob=ob: e.tensor_tensor(out=asl, in0=asl, in1=PS[ob][0:65, 0:128], op=ALU.add), [('ps', ob), ('acc', h)], [('acc', h)])

        pend = []
        for seg in range(NSEG):
            for hp in range(ATT_PAIRS):
                dma('sp', qt[:], qT_d[hp * 128:(hp + 1) * 128, seg * SEG:(seg + 1) * SEG], [('qT', seg)], ['qt'])
                dma('sp', kt[:], kT_d[hp * 128:(hp + 1) * 128, seg * SEGA:(seg + 1) * SEGA], [('kT', seg)] + [('kTpad', seg, o) for o in (0, APAD + SEG)], ['kt'])
                dma('sp', etf[:], etab_d[hp], [], ['etf'])
                S.op('pool', lambda e: e.tensor_copy(out=etb[:], in_=etf[:]), ['etf'], ['etb'])
                for d in (1, 4, 16):
                    for r in range(d):
                        base = seg * SEGA + APAD + r - 64 * d
                        src = vatt_d[base: base + d * 128 * NKT[d], hp * 130:(hp + 1) * 130].rearrange("(k p dd) c -> dd p k c", p=128, dd=d)[0]
                        dma('act', vres[d][:, r * NKT[d]:(r + 1) * NKT[d], :], src, [('vatt', seg)] + [('vapad', seg, o) for o in (0, APAD + SEG)], [('vres', d)])
                for h in range(2):
                    for di, d in enumerate((1, 4, 16)):
                        L = SEG // d
                        for r in range(d):
                            for b in range(L // 128):
                                u = ac['u']
                                ac['u'] += 1
                                sbank = u % 4
                                qs = qt[h * 64:(h + 1) * 64, r + d * 128 * b: r + d * 128 * b + d * 127 + 1: d]
                                for j in range(2):
                                    k0 = APAD + r + d * (128 * b - 64 + 128 * j)
                                    ks = kt[h * 64:(h + 1) * 64, k0: k0 + d * 127 + 1: d]
                                    S.mm(lambda e, sbank=sbank, j=j, ks=ks, qs=qs: e.matmul(PS[sbank][:, j * 128:(j + 1) * 128], ks, qs, start=True, stop=True),
                                         ['kt', 'qt'], [('ps', sbank)], last=(j == 1))
                                pi = u % 6
                                S.op('act', lambda e, pi=pi, sbank=sbank: e.activation(out=pex[pi][:], in_=PS[sbank][:, 0:256], func=AF.Exp, scale=0.125),
                                     [('ps', sbank)], [('pex', pi)])
                                eo = (h * 3 + di) * 256
                                S.op('pool', lambda e, pi=pi, eo=eo: e.tensor_tensor(out=pm[pi][:], in0=pex[pi][:], in1=etb[:, eo:eo + 256], op=ALU.mult),
                                     [('pex', pi), 'etb'], [('pm', pi)])
                                pend.append((pi, d, r, b, h))
                                if len(pend) > 2:
                                    emit_pv(*pend.pop(0))
                    while pend:
                        emit_pv(*pend.pop(0))
                    for t in range(SEG // 512):
                        S.mm(lambda e, h=h, t=t: e.matmul(PS[6][0:64, :], sel[:], acc[h][:, t * 512:(t + 1) * 512], start=True, stop=True), ['sel', ('acc', h)], [('ps', 6)])
                        S.op('dve', lambda e: e.reciprocal(out=rden[:], in_=PS[6][0:64, :]), [('ps', 6)], ['rden'])
                        k = ac['os'] % 2
                        ac['os'] += 1
                        S.op('dve', lambda e, k=k, h=h, t=t: e.tensor_tensor(out=ostg[k][:], in0=acc[h][0:64, t * 512:(t + 1) * 512], in1=rden[:], op=ALU.mult),
                             [('acc', h), 'rden'], [('ostg', k)])
                        row = hp * 128 + h * 64
                        dma('sp', mixT_d[row:row + 64, seg * SEG + t * 512: seg * SEG + (t + 1) * 512], ostg[k][:], [('ostg', k)], [('mixT', seg)])
        S.barrier()
        pt.close()


    if 'att' in phases:
        phase_att()
    def phase_rwkv():
        pr = ExitStack()
        twd = sb("twd", [128, SEG], BF16, pr)
        sad = sb("sad", [128, SEG], BF16, pr)
        sgd = sb("sgd", [128, SEG], BF16, pr)
        mu = sb("mu", [128, 2 * ZC], F32, pr)
        muc = sb("muc", [128, ZC], F32, pr)
        rwp = sb("rwp", [128, 80], F32, pr)
        lrf = sb("lrf", [128, 1024], F32, pr)
        w2b_ = sb("w2b", [128, 1024], BF16, pr)
        a2b_ = sb("a2b", [128, 1024], BF16, pr)
        g2b_ = sb("g2b", [128, 1024], BF16, pr)
        rmask = sb("rmask", [128, 512], F32, pr)
        cmask = [sb("cmask%d" % i, [128, 192], F32, pr) for i in range(2)]
        mSxb = [sb("mSxb%d" % i, [128, 128], BF16, pr) for i in range(2)]
        zin = [sb("zin%d" % i, [128, TT + 2], F32, pr) for i in range(3)]
        NW = 30
        wk = [sb("wk%d" % i, [128, TT], F32, pr) for i in range(NW)]
        bdt = {n: [sb("bd_%s%d" % (n, i), [128, 8, 192 if n == "At" else 128], BF16, pr) for i in range(2)] for n in ("At", "Bt", "Kt", "Vt")}
        gam = [sb("gam%d" % i, [128, TT], F32, pr) for i in range(2)]
        vTs = [sb("vTs%d" % i, [128, TT], F32, pr) for i in range(2)]
        bsum = [sb("bsum%d" % i, [128, TT], F32, pr) for i in range(2)]
        NU = 8
        ub = {n: [sb("u_%s%d" % (n, i), [128, 320 if n == "XM" else 256], BF16, pr) for i in range(NU)] for n in ("XM", "LM", "Z", "G")}
        ub1 = {n: [sb("u1_%s%d" % (n, i), [128, 128], BF16, pr) for i in range(NU)] for n in ("X", "Tt", "Kk", "V", "PpT", "RbT")}
        pw = [[sb("pw%d_%d" % (u, i), [128, 128], BF16, pr) for i in range(4)] for u in range(NU)]
        qg = [sb("qg%d" % i, [128, 128], F32, pr) for i in range(NU)]
        Hf = sb("Hf", [128, 128], F32, pr)
        Hb = [[sb("Hb%d_%d" % (c_, i), [128, 128], BF16, pr) for i in range(2)] for c_ in range(8)]
        ysb = [sb("ysb%d" % i, [128, TT], F32, pr) for i in range(2)]
        ybl = sb("ybl", [128, TT], F32, pr)
        ostr = [sb("ostr%d" % i, [128, TT], BF16, pr) for i in range(2)]
        for n in bdt:
            for i in range(2):
                S.op('pool', lambda e, n=n, i=i: e.memset(bdt[n][i][:], 0.0), [], [('bd', n, i)])
        dma('sp', mu[:], mu_d[:, :], [], ['mu'])
        dma('sp', rwp[:], rwp_d[:, :], [], ['rwp'])
        dma('sp', rmask[:], rmask_d[:, :], [], ['rmask'])
        for (b_, d_) in ((w2b_, w2_d), (a2b_, a2_d), (g2b_, g2_d)):
            dma('sp', lrf[:], d_[:, :], [], ['lrf'])
            S.op('dve', lambda e, b_=b_: e.tensor_copy(out=b_[:], in_=lrf[:]), ['lrf'], ['lrw'])
        S.op('dve', lambda e: e.tensor_tensor(out=muc[:], in0=mu[:, 0:ZC], in1=mu[:, ZC:2 * ZC], op=ALU.add), ['mu'], ['muc'])
        S.op('dve', lambda e: e.tensor_scalar(out=muc[:], in0=muc[:], scalar1=-1.0, scalar2=1.0, op0=ALU.mult, op1=ALU.add), ['muc'], ['muc'])
        for c in range(8):
            S.op('dve', lambda e, c=c: e.tensor_scalar(out=rwp[:, c * 10 + 6:c * 10 + 7], in0=rwp[:, c * 10 + 5:c * 10 + 6], scalar1=-1.0, scalar2=1.0, op0=ALU.mult, op1=ALU.add), ['rwp'], ['rwp'])
        for dirn in range(2):
            mI_ = m_iu if dirn == 0 else m_il
            mSx_ = m_sl if dirn == 0 else m_su
            mS_ = m_su if dirn == 0 else m_sl
            S.op('dve', lambda e, dirn=dirn, mS_=mS_: e.tensor_copy(out=cmask[dirn][:, 0:128], in_=mS_), ['cst'], ['cmask'])
            S.op('dve', lambda e, dirn=dirn, mI_=mI_: e.tensor_tensor(out=cmask[dirn][:, 128:192], in0=mI_[:, 0:64], in1=mI_[:, 64:128], op=ALU.add), ['cst', 'cmask'], ['cmask'])
            S.op('dve', lambda e, dirn=dirn, mSx_=mSx_: e.tensor_copy(out=mSxb[dirn][:], in_=mSx_), ['cst'], ['mSxb'])
        rc = {'z': 0, 'wk': 0, 'ps': 0, 'pf': 0, 'os': 0}

        def newwk():
            i = rc['wk'] % NW
            rc['wk'] += 1
            return wk[i], ('wk', i)

        def psreg():
            i = rc['ps'] % 6
            rc['ps'] += 1
            return PS[i][:, 0:256], ('psr', i)

        def psfull():
            i = 6 + rc['pf'] % 2
            rc['pf'] += 1
            return PS[i], ('psf', i)

        def load_shift(zc, seg, t0, dst=None, dkey=None):
            if dst is None:
                o, ok = newwk()
            else:
                o, ok = dst, dkey
            i = rc['z'] % 3
            rc['z'] += 1
            col = seg * SEGZ + t0
            dma('sp', zin[i][:], zT_d[zc * 128:(zc + 1) * 128, col:col + TT + 2], [('zT', seg)] + [('zpad', seg, o_) for o_ in (0, SEG + 1)], [('zin', i)])
            t1, t1k = newwk()
            S.op('pool', lambda e, i=i, o=o: e.tensor_scalar(out=o[:], in0=zin[i][:, 1:TT + 1], scalar1=muc[:, zc:zc + 1], scalar2=0.0, op0=ALU.mult, op1=ALU.add), [('zin', i), 'muc'], [ok])
            S.op('pool', lambda e, i=i, t1=t1: e.tensor_scalar(out=t1[:], in0=zin[i][:, 0:TT], scalar1=mu[:, zc:zc + 1], scalar2=0.0, op0=ALU.mult, op1=ALU.add), [('zin', i), 'mu'], [t1k])
            S.op('pool', lambda e, o=o, t1=t1: e.tensor_tensor(out=o[:], in0=o[:], in1=t1[:], op=ALU.add), [ok, t1k], [ok])
            S.op('pool', lambda e, i=i, t1=t1: e.tensor_scalar(out=t1[:], in0=zin[i][:, 2:TT + 2], scalar1=mu[:, ZC + zc:ZC + zc + 1], scalar2=0.0, op0=ALU.mult, op1=ALU.add), [('zin', i), 'mu', t1k], [t1k])
            S.op('pool', lambda e, o=o, t1=t1: e.tensor_tensor(out=o[:], in0=o[:], in1=t1[:], op=ALU.add), [ok, t1k], [ok])
            return o, ok

        def lowrank_prep(seg):
            for t in range(TPS):
                t0 = t * TT
                for (zc, dst, fn) in ((24, twd, AF.Tanh), (25, sad, AF.Copy), (26, sgd, AF.Sigmoid)):
                    o, ok = load_shift(zc, seg, t0)
                    S.op('act', lambda e, o=o, dst=dst, fn=fn, t0=t0: e.activation(out=dst[:, t0:t0 + TT], in_=o[:], func=fn), [ok], ['lr'])

        def sig_lowrank(wb, rows, src, c, t0, bias_ap):
            p, pk = psfull()
            r0, r1 = rows
            S.mm(lambda e, p=p: e.matmul(p[:], wb[r0:r1, c * 128:(c + 1) * 128], src[r0:r1, t0:t0 + TT], start=True, stop=True), ['lrw', 'lr'], [pk])
            o, ok = newwk()
            S.op('act', lambda e, p=p, o=o: e.activation(out=o[:], in_=p[:], func=AF.Sigmoid, bias=bias_ap, scale=1.0), [pk, 'rwp'], [ok])
            return o, ok

        def prep_tile(dirn, c, seg, t, bi):
            t0 = t * TT
            fwd = (dirn == 0)
            P = lambda j: rwp[:, c * 10 + j: c * 10 + j + 1]
            rs, rsk = load_shift(c, seg, t0)
            yield
            ks, ksk = load_shift(8 + c, seg, t0)
            yield
            load_shift(16 + c, seg, t0, vTs[bi], ('vTs', bi))
            yield
            for h in range(2):
                S.op('pool', lambda e, h=h: e.tensor_copy(out=bdt["Vt"][bi][h * 64:(h + 1) * 64, :, h * 64:(h + 1) * 64],
                                                          in_=vTs[bi][h * 64:(h + 1) * 64, :].rearrange("p (n t) -> p n t", t=64)),
                     [('vTs', bi)], [('bd', "Vt", bi)])
            kkr, kkrk = newwk()
            S.op('pool', lambda e: e.tensor_scalar(out=kkr[:], in0=ks[:], scalar1=P(4), scalar2=0.0, op0=ALU.mult, op1=ALU.add), [ksk, 'rwp'], [kkrk])
            sq_, sqk = newwk()
            S.op('act', lambda e: e.activation(out=sq_[:], in_=kkr[:], func=AF.Square), [kkrk], [sqk])
            p, pk = psfull()
            S.mm(lambda e, p=p: e.matmul(p[:], bones, sq_[:], start=True, stop=True), ['cst', sqk], [pk])
            rn, rnk = newwk()
            S.op('act', lambda e, p=p: e.activation(out=rn[:], in_=p[:], func=AF.Sqrt), [pk], [rnk])
            yield
            S.op('dve', lambda e: e.tensor_scalar(out=rn[:], in0=rn[:], scalar1=1e-12, scalar2=None, op0=ALU.max), [rnk], [rnk])
            S.op('dve', lambda e: e.reciprocal(out=rn[:], in_=rn[:]), [rnk], [rnk])
            kkn, kknk = newwk()
            S.op('dve', lambda e: e.scalar_tensor_tensor(out=kkn[:], in0=kkr[:], scalar=-1.0, in1=rn[:], op0=ALU.mult, op1=ALU.mult), [kkrk, rnk], [kknk])
            yield
            rows = (0, 64) if fwd else (64, 128)
            sgw, sgwk = sig_lowrank(w2b_, rows, twd, c, t0, P(0 if fwd else 1))
            ag, agk = sig_lowrank(a2b_, rows, sad, c, t0, P(2 if fwd else 3))
            yield
            kp_, kpk = newwk()
            S.op('pool', lambda e: e.tensor_scalar(out=kp_[:], in0=ag[:], scalar1=P(5), scalar2=P(6), op0=ALU.mult, op1=ALU.add), [agk, 'rwp'], [kpk])
            S.op('pool', lambda e: e.tensor_tensor(out=kp_[:], in0=kp_[:], in1=ks[:], op=ALU.mult), [kpk, ksk], [kpk])
            bb, bbk = newwk()
            S.op('dve', lambda e: e.scalar_tensor_tensor(out=bb[:], in0=kkn[:], scalar=-1.0, in1=ag[:], op0=ALU.mult, op1=ALU.mult), [kknk, agk], [bbk])
            yield
            cs, csk = newwk()
            S.op('dve', lambda e: e.tensor_tensor_scan(out=cs[:], data0=rmask[:], data1=sgw[:], initial=0.0, op0=ALU.mult, op1=ALU.add), ['rmask', sgwk], [csk])
            if fwd:
                lg, lgk = cs, csk
            else:
                lg, lgk = newwk()
                S.op('pool', lambda e: e.tensor_tensor(out=lg[:], in0=sgw[:], in1=cs[:], op=ALU.subtract), [sgwk, csk], [lgk])
                S.op('pool', lambda e: e.tensor_tensor(out=lg[:].rearrange("p (n t) -> p n t", t=64), in0=lg[:].rearrange("p (n t) -> p n t", t=64),
                                                       in1=cs[:].rearrange("p (n t) -> p n t", t=64)[:, :, 63:64].to_broadcast([128, 8, 64]), op=ALU.add), [lgk, csk], [lgk])
            yield
            gkey = ('gam', bi)
            S.op('act', lambda e: e.activation(out=gam[bi][:], in_=lg[:], func=AF.Exp, scale=-C0), [lgk], [gkey])
            ig, igk = newwk()
            S.op('act', lambda e: e.activation(out=ig[:], in_=lg[:], func=AF.Exp, scale=C0), [lgk], [igk])
            gp, gpk = newwk()
            S.op('dve', lambda e: e.tensor_tensor(out=gp[:], in0=lg[:], in1=sgw[:], op=ALU.subtract), [lgk, sgwk], [gpk])
            S.op('act', lambda e: e.activation(out=gp[:], in_=gp[:], func=AF.Exp, scale=-C0), [gpk], [gpk])
            yield
            for (nm, a_, ak, b_, bk) in (("At", kkn, kknk, gp, gpk), ("Bt", bb, bbk, ig, igk), ("Kt", kp_, kpk, ig, igk)):
                for h in range(2):
                    eng = 'pool'
                    S.op(eng, lambda e, nm=nm, a_=a_, b_=b_, h=h: e.tensor_tensor(out=bdt[nm][bi][h * 64:(h + 1) * 64, :, h * 64:(h + 1) * 64],
                                                                                   in0=a_[h * 64:(h + 1) * 64, :].rearrange("p (n t) -> p n t", t=64),
                                                                                   in1=b_[h * 64:(h + 1) * 64, :].rearrange("p (n t) -> p n t", t=64), op=ALU.mult),
                         [ak, bk], [('bd', nm, bi)])
                yield
            S.op('dve', lambda e: e.tensor_tensor(out=bdt["At"][bi][:, :, 128:192], in0=rs[:].rearrange("p (n t) -> p n t", t=64),
                                                  in1=gam[bi][:].rearrange("p (n t) -> p n t", t=64), op=ALU.mult), [rsk, gkey], [('Rt', bi)])
            if fwd:
                yield
                agb, agbk = sig_lowrank(a2b_, (64, 128), sad, c, t0, P(3))
                kb_, kbk = newwk()
                S.op('pool', lambda e: e.tensor_scalar(out=kb_[:], in0=agb[:], scalar1=P(5), scalar2=P(6), op0=ALU.mult, op1=ALU.add), [agbk, 'rwp'], [kbk])
                S.op('pool', lambda e: e.tensor_tensor(out=kb_[:], in0=kb_[:], in1=ks[:], op=ALU.mult), [kbk, ksk], [kbk])
                yield
                S.op('pool', lambda e: e.tensor_tensor(out=kb_[:], in0=kb_[:], in1=kp_[:], op=ALU.add), [kbk, kpk], [kbk])
                S.op('dve', lambda e: e.scalar_tensor_tensor(out=kb_[:], in0=rs[:], scalar=P(7), in1=kb_[:], op0=ALU.mult, op1=ALU.mult), [rsk, kbk, 'rwp'], [kbk])
                p2, pk2 = psfull()
                S.mm(lambda e, p2=p2: e.matmul(p2[:], bones, kb_[:], start=True, stop=True), ['cst', kbk], [pk2])
                S.op('act', lambda e, p2=p2: e.activation(out=bsum[bi][:], in_=p2[:], func=AF.Copy), [pk2], [('bsum', bi)])

        def offchain(dirn, n, bi, ui):
            fwd = (dirn == 0)
            mS = (m_su if fwd else m_sl)
            At = bdt["At"][bi][:, n, 0:128]
            AR = bdt["At"][bi][:, n, :]
            Bt = bdt["Bt"][bi][:, n, :]
            Kt = bdt["Kt"][bi][:, n, :]
            Vt = bdt["Vt"][bi][:, n, :]
            Rs = bdt["At"][bi][:, n, 128:192]
            gcol = gam[bi][:, n * 64 + 63: n * 64 + 64] if fwd else gam[bi][:, n * 64: n * 64 + 1]
            kA, kB, kK, kVt, kR, kG = ('bd', "At", bi), ('bd', "Bt", bi), ('bd', "Kt", bi), ('bd', "Vt", bi), ('Rt', bi), ('gam', bi)
            XM, LM, Z, G = ub["XM"][ui], ub["LM"][ui], ub["Z"][ui], ub["G"][ui]
            X, Tt, Kk, V, PpT, RbT = (ub1[k][ui] for k in ("X", "Tt", "Kk", "V", "PpT", "RbT"))
            Bk = XM[:, 192:320]
            K_ = lambda nm: ('u', nm, ui)
            cpc = [ui]

            def cp(dst, src, reads, writes):
                cpc[0] += 1
                if cpc[0] % 3 != 0:
                    S.op('act', lambda e, dst=dst, src=src: e.activation(out=dst, in_=src, func=AF.Copy), reads, writes)
                else:
                    S.op('dve', lambda e, dst=dst, src=src: e.tensor_copy(out=dst, in_=src), reads, writes)
            for (lh, lk, dst, dk) in ((Bt, kB, XM, 'XM'), (Kt, kK, LM, 'LM')):
                p, pk = psreg()
                S.mm(lambda e, p=p, lh=lh: e.matmul(p[:, 0:192], lh, AR, start=True, stop=True), [lk, kA, kR], [pk])
                S.op('dve', lambda e, p=p, dst=dst: e.tensor_tensor(out=dst[:, 0:192], in0=p[:, 0:192], in1=cmask[dirn][:], op=ALU.mult), [pk, 'cmask'], [K_(dk)])
                yield
            p, pk = psreg()
            S.mm(lambda e, p=p: e.matmul(p[:, 0:128], At, Bt, start=True, stop=True), [kA, kB], [pk])
            cp(X[:], p[:, 0:128], [pk], [K_('X')])
            S.op('pool', lambda e: e.tensor_tensor(out=X[:], in0=X[:], in1=mSxb[dirn][:], op=ALU.mult), [K_('X'), 'mSxb'], [K_('X')])
            S.op('pool', lambda e: e.tensor_tensor(out=Tt[:], in0=XM[:, 0:128], in1=identb, op=ALU.add), [K_('XM'), 'cstb'], [K_('Tt')])
            yield
            for (src, sk, dk) in ((At, kA, 'A'), (Bt, kB, 'Bk'), (Kt, kK, 'Kk'), (Vt, kVt, 'V')):
                p, pk = psreg()
                S.mm(lambda e, p=p, src=src: e.matmul(p[:, 0:128], src, identb, start=True, stop=True), [sk, 'cstb'], [pk])
                if dk == 'A':
                    cp(Z[:, 0:128], p[:, 0:128], [pk], [K_('Z0')])
                elif dk == 'Bk':
                    cp(Bk, p[:, 0:128], [pk], [K_('Bk')])
                elif dk == 'Kk':
                    cp(Kk[:], p[:, 0:128], [pk], [K_('Kk')])
                else:
                    cp(V[:], p[:, 0:128], [pk], [K_('V')])
                yield
            Pc, Pck = X[:], K_('X')
            Ptc, Ptck = XM[:, 0:128], K_('XM')
            for lvl in range(5):
                pn = pw[ui][(2 * lvl) % 4]
                pnk = ('pw', ui, (2 * lvl) % 4)
                p1, pk1 = psreg()
                S.mm(lambda e, p1=p1, Pc=Pc, Ptc=Ptc: e.matmul(p1[:, 0:128], Ptc, Pc, start=True, stop=True), [Pck, Ptck], [pk1])
                if lvl < 4:
                    ptn = pw[ui][(2 * lvl + 1) % 4]
                    ptnk = ('pw', ui, (2 * lvl + 1) % 4)
                    p2, pk2 = psreg()
                    S.mm(lambda e, p2=p2, Pc=Pc, Ptc=Ptc: e.matmul(p2[:, 0:128], Pc, Ptc, start=True, stop=True), [Pck, Ptck], [pk2])
                    if lvl % 2 == 0:
                        cp(ptn[:], p2[:, 0:128], [pk2], [ptnk])
                    else:
                        cp(ptn[:], p2[:, 0:128], [pk2], [ptnk])
                cp(pn[:], p1[:, 0:128], [pk1], [pnk])
                Pc, Pck = pn[:], pnk
                if lvl < 4:
                    Ptc, Ptck = ptn[:], ptnk
                yield
                p3, pk3 = psreg()
                S.mm(lambda e, p3=p3, Pc=Pc: e.matmul(p3[:, 0:128], Pc, Tt[:], start=True, stop=True), [Pck, K_('Tt')], [pk3])
                S.op('dve', lambda e, p3=p3: e.tensor_tensor(out=Tt[:], in0=Tt[:], in1=p3[:, 0:128], op=ALU.add), [pk3, K_('Tt')], [K_('Tt')])
                if lvl == 4:
                    yield
            p, pk = psreg()
            S.mm(lambda e, p=p: e.matmul(p[:, 0:128], LM[:, 0:128], V[:], start=True, stop=True), [K_('LM'), K_('V')], [pk])
            cp(Z[:, 128:256], p[:, 0:128], [pk], [K_('Z1')])
            yield
            p, pk = psreg()
            S.mm(lambda e, p=p: e.matmul(p[:, 0:256], Tt[:], Z[:], start=True, stop=True), [K_('Tt'), K_('Z0'), K_('Z1')], [pk])
            cp(G[:], p[:, 0:256], [pk], [K_('G')])
            yield
            p, pk = psreg()
            S.mm(lambda e, p=p: e.matmul(p[:, 0:192], G[:, 0:128], XM[:, 128:320], start=True, stop=True), [K_('G'), K_('Bk'), K_('XM')], [pk])
            S.op('dve', lambda e, p=p: e.tensor_tensor(out=PpT[:], in0=p[:, 64:192], in1=ident, op=ALU.add), [pk, 'cst'], [K_('PpT')])
            S.op('dve', lambda e, p=p: e.tensor_tensor(out=RbT[:, 0:64], in0=p[:, 0:64], in1=Rs, op=ALU.add), [pk, kR], [K_('RbT')])
            p, pk = psreg()
            S.mm(lambda e, p=p: e.matmul(p[:, 0:128], Bk, G[:, 128:256], start=True, stop=False), [K_('G'), K_('Bk')], [pk])
            S.mm(lambda e, p=p: e.matmul(p[:, 0:128], Kk[:], V[:], start=False, stop=True), [K_('Kk'), K_('V')], [pk])
            S.op('dve', lambda e, p=p: e.tensor_scalar(out=qg[ui][:], in0=p[:, 0:128], scalar1=gcol, scalar2=None, op0=ALU.mult), [pk, kG], [K_('qg')])

        def chain(dirn, n, bi, ui, Hcur, Hnew, ysl, yslk):
            fwd = (dirn == 0)
            gcol = gam[bi][:, n * 64 + 63: n * 64 + 64] if fwd else gam[bi][:, n * 64: n * 64 + 1]
            kG = ('gam', bi)
            XM, LM, G = ub["XM"][ui], ub["LM"][ui], ub["G"][ui]
            V, PpT, RbT = (ub1[k][ui] for k in ("V", "PpT", "RbT"))
            K_ = lambda nm: ('u', nm, ui)
            p, pk = psreg()
            S.mm(lambda e, p=p: e.matmul(p[:, 0:64], G[:, 128:256], XM[:, 128:192], start=True, stop=False), [K_('G'), K_('XM')], [pk])
            S.mm(lambda e, p=p: e.matmul(p[:, 0:64], V[:], LM[:, 128:192], start=False, stop=False), [K_('V'), K_('LM')], [pk])
            S.mm(lambda e, p=p: e.matmul(p[:, 0:64], Hcur[0][:], RbT[:, 0:64], start=False, stop=True), [Hcur[1], K_('RbT')], [pk])
            S.op('act', lambda e, p=p: e.activation(out=ysl, in_=p[:, 0:64], func=AF.Copy), [pk], [yslk])
            p2, pk2 = psreg()
            S.mm(lambda e, p2=p2: e.matmul(p2[:, 0:128], PpT[:], Hcur[0][:], start=True, stop=True), [K_('PpT'), Hcur[1]], [pk2])
            S.op('dve', lambda e, p2=p2: e.scalar_tensor_tensor(out=Hf[:], in0=p2[:, 0:128], scalar=gcol, in1=qg[ui][:], op0=ALU.mult, op1=ALU.add),
                 [pk2, K_('qg'), kG], ['Hf'])
            S.op('act', lambda e: e.activation(out=Hnew[0][:], in_=Hf[:], func=AF.Copy), ['Hf'], [Hnew[1]])

        def epilogue(c, seg, t, bi, yt, ytk):
            t0 = t * TT
            P = lambda j: rwp[:, c * 10 + j: c * 10 + j + 1]
            bon, bonk = newwk()
            S.op('pool', lambda e: e.tensor_tensor(out=bon[:], in0=vTs[bi][:], in1=bsum[bi][:], op=ALU.mult), [('vTs', bi), ('bsum', bi)], [bonk])
            dma('sp', ybl[:], ybT_d[c * 128:(c + 1) * 128, seg * SEG + t0: seg * SEG + t0 + TT], [('ybT', c)], ['ybl'])
            ysum, ysk = newwk()
            S.op('dve', lambda e: e.tensor_tensor(out=ysum[:], in0=yt[:], in1=ybl[:], op=ALU.add), [ytk, 'ybl'], [ysk])
            yield
            p, pk = psfull()
            S.mm(lambda e, p=p: e.matmul(p[:], bones, ysum[:], start=True, stop=True), ['cst', ysk], [pk])
            yield
            yc, yck = newwk()
            S.op('dve', lambda e, p=p: e.scalar_tensor_tensor(out=yc[:], in0=p[:], scalar=-1.0 / 64, in1=ysum[:], op0=ALU.mult, op1=ALU.add), [pk, ysk], [yck])
            yield
            sq_, sqk = newwk()
            S.op('act', lambda e: e.activation(out=sq_[:], in_=yc[:], func=AF.Square), [yck], [sqk])
            yield
            p2, pk2 = psfull()
            S.mm(lambda e, p2=p2: e.matmul(p2[:], bones, sq_[:], start=True, stop=True), ['cst', sqk], [pk2])
            yield
            sd, sdk = newwk()
            S.op('act', lambda e, p2=p2: e.activation(out=sd[:], in_=p2[:], func=AF.Sqrt, scale=1.0 / 64, bias=epsl[:]), [pk2, 'epsl'], [sdk])
            yield
            S.op('dve', lambda e: e.reciprocal(out=sd[:], in_=sd[:]), [sdk], [sdk])
            yield
            S.op('dve', lambda e: e.tensor_tensor(out=yc[:], in0=yc[:], in1=sd[:], op=ALU.mult), [yck, sdk], [yck])
            yield
            S.op('pool', lambda e: e.tensor_scalar(out=yc[:], in0=yc[:], scalar1=P(8), scalar2=P(9), op0=ALU.mult, op1=ALU.add), [yck, 'rwp'], [yck])
            yield
            S.op('pool', lambda e: e.tensor_tensor(out=yc[:], in0=yc[:], in1=bon[:], op=ALU.add), [yck, bonk], [yck])
            p3, pk3 = psfull()
            S.mm(lambda e, p3=p3: e.matmul(p3[:], g2b_[:, c * 128:(c + 1) * 128], sgd[:, t0:t0 + TT], start=True, stop=True), ['lrw', 'lr'], [pk3])
            yield
            k = rc['os'] % 2
            rc['os'] += 1
            S.op('dve', lambda e, p3=p3, k=k: e.tensor_tensor(out=ostr[k][:], in0=yc[:], in1=p3[:], op=ALU.mult), [yck, pk3], [('ostr', k)])
            dma('sp', mixT_d[1024 + c * 128: 1024 + (c + 1) * 128, seg * SEG + t0: seg * SEG + t0 + TT], ostr[k][:], [('ostr', k)], [('mixT', seg)])

        tiles = []
        for (dirn, seg, si) in ((1, 1, 0), (1, 0, 1), (0, 0, 0), (0, 1, 1)):
            for c in range(RW_PAIRS):
                trange = range(TPS - 1, -1, -1) if dirn == 1 else range(TPS)
                for ti, t in enumerate(trange):
                    tiles.append(dict(c=c, dirn=dirn, seg=seg, t=t, first=(si == 0 and ti == 0), link=(si == 1 and ti == 0)))
        cur_seg = None
        hic = [0] * 8
        early = None
        pend_epi = None
        for k, tl in enumerate(tiles):
            bi = k % 2
            c, dirn, seg, t = tl['c'], tl['dirn'], tl['seg'], tl['t']
            if cur_seg != seg:
                lowrank_prep(seg)
                cur_seg = seg
            if early is None:
                early = prep_tile(dirn, c, seg, t, bi)
            for _ in early:
                pass
            early = None
            hi = hic[c]
            if tl['first']:
                hi = 0
                S.op('pool', lambda e, c=c: e.memset(Hb[c][0][:], 0.0), [], [('Hb', c, 0)])
            if tl['link']:
                S.op('dve', lambda e, hi=hi, c=c: e.tensor_scalar(out=Hb[c][hi][:], in0=Hb[c][hi][:], scalar1=link[:, 0:1], scalar2=None, op0=ALU.mult), [('Hb', c, hi), 'link'], [('Hb', c, hi)])
            chunks = list(range(7, -1, -1)) if dirn == 1 else list(range(8))
            nprep = None
            if k + 1 < len(tiles) and tiles[k + 1]['seg'] == seg:
                n2 = tiles[k + 1]
                nprep = prep_tile(n2['dirn'], n2['c'], n2['seg'], n2['t'], (k + 1) % 2)
                early = nprep

            def adv():
                nonlocal nprep, pend_epi
                if pend_epi is not None:
                    try:
                        next(pend_epi)
                    except StopIteration:
                        pend_epi = None
                if nprep is not None:
                    try:
                        next(nprep)
                    except StopIteration:
                        nprep = None
            live = [offchain(dirn, n, bi, ui) for ui, n in enumerate(chunks)]
            while live:
                nxt = []
                for g in live:
                    try:
                        next(g)
                        nxt.append(g)
                    except StopIteration:
                        pass
                live = nxt
                adv()
            yi = k % 2
            for ui, n in enumerate(chunks):
                chain(dirn, n, bi, ui, (Hb[c][hi], ('Hb', c, hi)), (Hb[c][1 - hi], ('Hb', c, 1 - hi)), ysb[yi][:, n * 64:(n + 1) * 64], ('ysb', yi))
                hi = 1 - hi
                adv()
            hic[c] = hi
            if pend_epi is not None:
                for _ in pend_epi:
                    pass
                pend_epi = None
            if dirn == 1:
                dma('sp', ybT_d[c * 128:(c + 1) * 128, seg * SEG + t * TT: seg * SEG + (t + 1) * TT], ysb[yi][:], [('ysb', yi)], [('ybT', c)])
            else:
                pend_epi = epilogue(c, seg, t, bi, ysb[yi], ('ysb', yi))
                if not (k + 1 < len(tiles) and tiles[k + 1]['seg'] == seg):
                    for _ in pend_epi:
                        pass
                    pend_epi = None
        S.barrier()
        pr.close()

    if 'rwkv' in phases:
        phase_rwkv()
    def phase_C():
        pc = ExitStack()
        env = ffn_env(pc, 'C')
        for it in range(NT):
            env['wout'](it)
            env['rmsnorm_to_nT'](2)
            env['ffn'](1)
            env['store_y_tile'](it, 3)
        S.barrier()
        pc.close()

    if 'C' in phases:
        phase_C()
    S.barrier()
    S.emit()
    global LAST_S
    LAST_S = S
    return nc

LAST_S = None

def _const_tables():
    tri = np.ones((64, 64), np.float32)
    su1, sl1, iu1, il1 = np.triu(tri, 1), np.tril(tri, -1), np.triu(tri, 0), np.tril(tri, 0)

    def bdm(m):
        o = np.zeros((128, 128), np.float32)
        o[:64, :64] = m
        o[64:, 64:] = m
        return o
    cst = np.zeros((128, 1024), np.float32)
    cst[:, 0:128] = np.eye(128, dtype=np.float32)
    cst[:, 128:256] = bdm(tri)
    cst[:, 256:384] = bdm(su1)
    cst[:, 384:512] = bdm(sl1)
    cst[:, 512:640] = bdm(iu1)
    cst[:, 640:768] = bdm(il1)
    rmask = np.ones((128, 512), np.float32)
    rmask[:, ::64] = 0.0
    slopes = np.exp2(-8.0 * np.arange(1, 17, dtype=np.float64) / 16)
    kp = np.arange(128)[:, None]
    qp = np.arange(128)[None, :]
    etab = np.zeros((8, 128, 1536), np.float32)
    for h in range(16):
        for di, d in enumerate((1, 4, 16)):
            for j in range(2):
                rel = kp + 128 * j - 64 - qp
                e = np.where(np.abs(rel) <= 64, np.exp(-slopes[h] * d * np.abs(rel)), 0.0)
                o = ((h % 2) * 3 + di) * 256 + j * 128
                etab[h // 2, :, o:o + 128] = e
    return cst, rmask, etab


def _prep_weights(inp):
    f = np.float32
    A = lambda a: np.ascontiguousarray(np.asarray(a, dtype=f))
    out = {}
    for fi, pre in enumerate(("ffn1", "ffn2")):
        g = np.asarray(inp[pre + "_gate"][0]).reshape(DC, 128, FC, 128).transpose(2, 1, 0, 3).reshape(FC, 128, 2048)
        u = np.asarray(inp[pre + "_up"][0]).reshape(DC, 128, FC, 128).transpose(2, 1, 0, 3).reshape(FC, 128, 2048)
        out["wgu%d" % (fi + 1)] = A(np.concatenate([g, u], axis=2).reshape(FF, 4096))
        dn = np.asarray(inp[pre + "_down"][0]).reshape(FC, 128, DC, 128).transpose(2, 1, 0, 3).reshape(D, FF)
        out["wd%d" % (fi + 1)] = A(dn)
    w_in = np.asarray(inp["w_in"][0])
    fm_cols = [c * 128 for c in range(16)] + [3072 + c * 128 for c in range(ZC)]
    fm = [w_in[:, c0:c0 + 128].reshape(DC, 128, 128).transpose(1, 0, 2).reshape(128, 2048) for c0 in fm_cols]
    out["winfm"] = A(np.concatenate(fm, axis=0))
    tm_cols = [2048 + g * 256 for g in range(4)]
    tm = [w_in[:, c0:c0 + 256].reshape(DC, 128, 256).transpose(1, 0, 2).reshape(128, 4096) for c0 in tm_cols]
    out["wintm"] = A(np.concatenate(tm, axis=0))
    out["wo"] = A(np.asarray(inp["w_out"][0]).reshape(DC, 128, DC, 128).transpose(2, 1, 0, 3).reshape(D, D))
    gains = np.zeros((128, 4 * DC), f)
    for gi, g in enumerate((inp["ffn1_norm"][0], inp["mix_norm"][0], inp["ffn2_norm"][0], inp["final_norm"])):
        gains[:, gi * DC:(gi + 1) * DC] = np.asarray(g).reshape(DC, 128).T
    out["gains"] = gains
    mu = np.zeros((128, 2 * ZC), f)
    mu[:, 0:ZC] = np.asarray(inp["mu_prev"][0]).reshape(ZC, 128).T
    mu[:, ZC:] = np.asarray(inp["mu_next"][0]).reshape(ZC, 128).T
    out["mu"] = mu
    rwp = np.zeros((128, 80), f)
    vecs = (inp["w0_f"][0], inp["w0_b"][0], inp["a0_f"][0], inp["a0_b"][0], inp["k_k"][0], inp["k_a"][0], None,
            np.asarray(inp["r_k"][0]).reshape(1024), inp["ln_x_w"][0], inp["ln_x_b"][0])
    for j, v in enumerate(vecs):
        if v is None:
            continue
        rwp[:, j::10] = np.asarray(v).reshape(8, 128).T
    out["rwp"] = rwp
    out["w2"] = A(np.concatenate([inp["w2_f"][0], inp["w2_b"][0]], axis=0))
    out["a2"] = A(np.concatenate([inp["a2_f"][0], inp["a2_b"][0]], axis=0))
    out["g2"] = A(inp["g2"][0])
    cst, rmask, etab = _const_tables()
    out["cst"], out["rmask"], out["etab"] = cst, rmask, etab
    return out


def _core_plan(SEG, x_prompt, x_sample):
    plan = []
    xp, xs = np.asarray(x_prompt), np.asarray(x_sample)
    nprompt_cores = xp.shape[0] * (xp.shape[1] // (NSEG * SEG))
    for b in range(xp.shape[0]):
        for part in range(xp.shape[1] // (NSEG * SEG)):
            assert xp.shape[1] == NSEG * SEG
            plan.append(([('p', b, 0), ('p', b, SEG)], 1.0))
    nb = xs.shape[0]
    assert xs.shape[1] == SEG
    rest = 8 - len(plan)
    two = nb - rest
    i = 0
    for c in range(rest):
        if c < two:
            plan.append(([('s', i, 0), ('s', i + 1, 0)], 0.0))
            i += 2
        elif i < nb:
            plan.append(([('s', i, 0), None], 0.0))
            i += 1
        else:
            plan.append(([None, None], 0.0))
    assert i == nb
    return plan


_NC_CACHE = {}


def run(inputs, SEG, debug=False, phases=('A', 'fix', 'att', 'rwkv', 'C')):
    wts = _prep_weights(inputs)
    xp, xs = np.asarray(inputs["x_prompt"], np.float32), np.asarray(inputs["x_sample"], np.float32)
    plan = _core_plan(SEG, xp, xs)
    in_maps = []
    for segs, lk in plan:
        x = np.zeros((NSEG * SEG, D), np.float32)
        for si, sdesc in enumerate(segs):
            if sdesc is None:
                continue
            kind, b, st = sdesc
            src = xp if kind == 'p' else xs
            x[si * SEG:(si + 1) * SEG] = src[b, st:st + SEG]
        m = dict(wts)
        m["x"] = x
        m["link"] = np.full((128, 1), lk, np.float32)
        in_maps.append(m)
    key = (SEG, debug, tuple(phases))
    if key not in _NC_CACHE:
        _NC_CACHE[key] = build(SEG, debug=debug, phases=phases)
    nc = _NC_CACHE[key]
    res = run_bass_kernel_spmd(nc, in_maps, core_ids=list(range(8)))
    yp = np.zeros(xp.shape, np.float32)
    ys = np.zeros(xs.shape, np.float32)
    for ci, (segs, lk) in enumerate(plan):
        y = res.results[ci]["y"]
        for si, sdesc in enumerate(segs):
            if sdesc is None:
                continue
            kind, b, st = sdesc
            (yp if kind == 'p' else ys)[b, st:st + SEG] = y[si * SEG:(si + 1) * SEG]
    return (yp, ys), res, plan


def kernel(**inputs):
    (yp, ys), _, _ = run(inputs, 4096)
    return (yp, ys)
```

```python
from contextlib import ExitStack
import numpy as np
import concourse.bass as bass
import concourse.mybir as mybir
from concourse.bass_utils import run_bass_kernel_spmd

F32 = mybir.dt.float32
BF16 = mybir.dt.bfloat16
ALU = mybir.AluOpType
AF = mybir.ActivationFunctionType

D = 2048
DC = 16
FF = 5632
FC = 44
TT = 512
NSEG = 2
APAD = 1024
NFM = 43
ZC = 27
C0 = 0.6065306597126334
NORM_EPS = 1e-6
LN_EPS = 64e-5

ENGS = ("pe", "dve", "act", "pool", "sp")
NDMASEM = 64
NHWSEM = 40


class Sched:
    def __init__(self, nc, es):
        self.nc = nc
        self.csem = {e: es.enter_context(nc.semaphore("c_" + e)) for e in ENGS}
        self.dsem = [es.enter_context(nc.semaphore("d%d" % i)) for i in range(NDMASEM)]
        self.dval = [0] * NDMASEM
        self.dnext = 0
        self.dnext_sw = 0
        self.count = {e: 0 for e in ENGS}
        self.prog = {e: [] for e in ENGS}
        self.waited = {e: {} for e in ENGS}
        self.lastw = {}
        self.reads = {}

    def _need(self, eng, tok, deps):
        if tok is None:
            return
        if tok[0] == 'e':
            _, e2, idx = tok
            if e2 == eng:
                if eng == 'pe':
                    return
            key = e2
        else:
            _, i, idx = tok
            key = ('d', i)
        if self.waited[eng].get(key, 0) >= idx:
            return
        if deps.get(key, 0) < idx:
            deps[key] = idx

    def _emit_waits(self, eng, deps):
        for key, idx in deps.items():
            sem = self.csem[key] if isinstance(key, str) else self.dsem[key[1]]
            self.prog[eng].append(('w', sem, idx))
            self.waited[eng][key] = idx

    def _deps(self, eng, reads, writes):
        deps = {}
        for k in reads:
            self._need(eng, self.lastw.get(k), deps)
        for k in writes:
            self._need(eng, self.lastw.get(k), deps)
            for t in self.reads.get(k, ()):
                self._need(eng, t, deps)
        return deps

    def _record(self, tok, reads, writes):
        for k in reads:
            self.reads.setdefault(k, []).append(tok)
        for k in writes:
            self.lastw[k] = tok
            self.reads[k] = []

    def op(self, eng, fn, reads=(), writes=()):
        deps = self._deps(eng, reads, writes)
        self._emit_waits(eng, deps)
        self.count[eng] += 1
        self.prog[eng].append(('o', fn, self.csem[eng], 1))
        tok = ('e', eng, self.count[eng])
        self._record(tok, reads, writes)
        return tok

    def mm(self, fn, reads=(), writes=(), last=True):
        return self.op('pe', fn, reads, writes)

    def dma(self, eng, fn, reads=(), writes=()):
        deps = self._deps(eng, reads, writes)
        if eng == 'pool':
            i = NHWSEM + self.dnext_sw
            self.dnext_sw = (self.dnext_sw + 1) % (NDMASEM - NHWSEM)
        else:
            i = self.dnext
            self.dnext = (self.dnext + 1) % NHWSEM
        if self.dval[i] > 0:
            self._need(eng, ('d', i, self.dval[i]), deps)
        self._emit_waits(eng, deps)
        self.dval[i] += 16
        self.prog[eng].append(('o', fn, self.dsem[i], 16))
        tok = ('d', i, self.dval[i])
        self._record(tok, reads, writes)
        return tok

    def barrier(self):
        for eng in ENGS:
            deps = {}
            for e2 in ENGS:
                if e2 != eng and self.count[e2] > 0:
                    self._need(eng, ('e', e2, self.count[e2]), deps)
            for i in range(NDMASEM):
                if self.dval[i] > 0:
                    self._need(eng, ('d', i, self.dval[i]), deps)
            self._emit_waits(eng, deps)

    def emit(self):
        prog = self.prog

        def run(engobj, items):
            for it in items:
                if it[0] == 'w':
                    engobj.wait_ge(it[1], it[2])
                else:
                    ins = it[1](engobj)
                    if it[2] is not None:
                        ins.then_inc(it[2], it[3])

        with self.nc.Block() as block:
            @block.sync
            def _(e):
                run(e, prog['sp'])

            @block.tensor
            def _(e):
                run(e, prog['pe'])

            @block.vector
            def _(e):
                run(e, prog['dve'])

            @block.scalar
            def _(e):
                run(e, prog['act'])

            @block.gpsimd
            def _(e):
                run(e, prog['pool'])


def build(SEG, debug=False, phases=('A', 'fix', 'att', 'rwkv', 'C'), RW_PAIRS=8, ATT_PAIRS=8):
    NTOK = NSEG * SEG
    NT = NTOK // TT
    TPS = SEG // TT
    SEGA = SEG + 2 * APAD
    SEGZ = SEG + 2
    nc = bass.Bass("TRN2", target_bir_lowering=False)
    es = ExitStack()
    S = Sched(nc, es)

    def din(name, shape, dt=F32):
        return nc.dram_tensor(name, list(shape), dt, kind="ExternalInput").ap()

    def dscr(name, shape, dt):
        if debug:
            return nc.dram_tensor(name, list(shape), dt, kind="ExternalOutput").ap()
        return nc.dram_tensor(name, list(shape), dt).ap()

    x_d = din("x", [NTOK, D])
    link_d = din("link", [128, 1])
    wgu_h = [din("wgu1", [FF, 4096]), din("wgu2", [FF, 4096])]
    wd_h = [din("wd1", [D, FF]), din("wd2", [D, FF])]
    winfm_h = din("winfm", [NFM * 128, 2048])
    wintm_h = din("wintm", [4 * 128, 4096])
    wo_h = din("wo", [D, D])
    gains_d = din("gains", [128, 4 * DC])
    mu_d = din("mu", [128, 2 * ZC])
    rwp_d = din("rwp", [128, 8 * 10])
    w2_d = din("w2", [128, 1024])
    a2_d = din("a2", [128, 1024])
    g2_d = din("g2", [128, 1024])
    etab_d = din("etab", [8, 128, 2 * 3 * 256])
    cst_d = din("cst", [128, 8 * 128])
    rmask_d = din("rmask", [128, 512])
    y_d = nc.dram_tensor("y", [NTOK, D], F32, kind="ExternalOutput").ap()

    wgu_b = [dscr("wgu1b", [FF, 4096], BF16), dscr("wgu2b", [FF, 4096], BF16)]
    wd_b = [dscr("wd1b", [D, FF], BF16), dscr("wd2b", [D, FF], BF16)]
    winfm_b = dscr("winfmb", [NFM * 128, 2048], BF16)
    wintm_b = dscr("wintmb", [4 * 128, 4096], BF16)
    wo_b = dscr("wob", [D, D], BF16)
    x1T_d = dscr("x1T", [D, NTOK], F32)
    qT_d = dscr("qT", [1024, NTOK], BF16)
    kT_d = dscr("kT", [1024, NSEG * SEGA], BF16)
    vatt_d = dscr("vatt", [NSEG * SEGA + 64, 16 * 65], BF16)
    zT_d = dscr("zT", [ZC * 128, NSEG * SEGZ], F32)
    ybT_d = dscr("ybT", [1024, NTOK], F32)
    mixT_d = dscr("mixT", [D, NTOK], BF16)

    def sb(name, shape, dt, stack=es):
        return stack.enter_context(nc.sbuf_tensor("s_" + name, list(shape), dt))

    PS = [es.enter_context(nc.psum_tensor("ps%d" % i, [128, 512], F32)) for i in range(8)]

    def dma(eng, out, in_, reads, writes, **kw):
        return S.dma(eng, lambda e, o=out, i=in_, k=kw: e.dma_start(out=o, in_=i, **k), reads, writes)

    cst = sb("cst", [128, 8 * 128], F32)
    cstb = sb("cstb", [128, 8 * 128], BF16)
    gains = sb("gains", [128, 4 * DC], F32)
    link = sb("link", [128, 1], F32)
    zeros = sb("zeros", [128, 1040], BF16)
    zerosf = sb("zerosf", [128, 128], F32)
    epsn = sb("epsn", [128, 1], F32)
    epsl = sb("epsl", [128, 1], F32)
    onesb = sb("onesb", [128, 128], BF16)
    dma('sp', cst[:], cst_d[:, :], [], ['cst'])
    dma('sp', gains[:], gains_d[:, :], [], ['gains'])
    dma('sp', link[:], link_d[:, :], [], ['link'])
    S.op('dve', lambda e: e.tensor_copy(out=cstb[:], in_=cst[:]), ['cst'], ['cstb'])
    S.op('pool', lambda e: e.memset(zeros[:], 0.0), [], ['zeros'])
    S.op('pool', lambda e: e.memset(zerosf[:], 0.0), [], ['zerosf'])
    S.op('pool', lambda e: e.memset(epsn[:], NORM_EPS), [], ['epsn'])
    S.op('pool', lambda e: e.memset(epsl[:], LN_EPS), [], ['epsl'])
    S.op('pool', lambda e: e.memset(onesb[:], 1.0), [], ['onesb'])
    ident = cst[:, 0:128]
    identb = cstb[:, 0:128]
    bones = cst[:, 128:256]
    m_su = cst[:, 256:384]
    m_sl = cst[:, 384:512]
    m_iu = cst[:, 512:640]
    m_il = cst[:, 640:768]

    def cast_rows(src, dst, r0, r1, key):
        dma('pool', dst[r0:r1, :], src[r0:r1, :], [], [key], max_dma_last_dim=4096)

    for j in range(0, FC, 2):
        cast_rows(wgu_h[0], wgu_b[0], j * 128, (j + 2) * 128, ('wgu0', j // 2))
    for m in range(DC):
        cast_rows(wd_h[0], wd_b[0], m * 128, (m + 1) * 128, ('wd0', m))
    for c in range(0, NFM, 4):
        cast_rows(winfm_h, winfm_b, c * 128, min(NFM, c + 4) * 128, ('winfm', c // 4))
    for g in range(0, 4, 2):
        cast_rows(wintm_h, wintm_b, g * 128, (g + 2) * 128, ('wintm', g // 2))
    for m in range(0, DC, 4):
        cast_rows(wo_h, wo_b, m * 128, (m + 4) * 128, ('wo', m // 4))
    for j in range(0, FC, 2):
        cast_rows(wgu_h[1], wgu_b[1], j * 128, (j + 2) * 128, ('wgu1', j // 2))
    for m in range(DC):
        cast_rows(wd_h[1], wd_b[1], m * 128, (m + 1) * 128, ('wd1', m))

    for s in range(NSEG):
        for off in (0, APAD + SEG):
            for c in range(8):
                dma('act', kT_d[c * 128:(c + 1) * 128, s * SEGA + off: s * SEGA + off + APAD], zeros[:, 0:APAD],
                    ['zeros'], [('kTpad', s, off)])
            for a in range(APAD // 128):
                r0 = s * SEGA + off + a * 128
                dma('act', vatt_d[r0:r0 + 128, :], zeros[:, 0:1040], ['zeros'], [('vapad', s, off)])
        for off in (0, SEG + 1):
            dma('act', zT_d[:, s * SEGZ + off: s * SEGZ + off + 1].rearrange("(c p) o -> p c o", p=128),
                zerosf[:, 0:ZC].unsqueeze(2), ['zerosf'], [('zpad', s, off)], allow_slow_non_contiguous=True)

    _sb_outer = sb

    def ffn_env(pa, tag):
        def sb(name, shape, dt, stack=es):
            return _sb_outer(name + tag, shape, dt, stack)
        xT = sb("xT", [128, DC, TT], F32, pa)
        nT = sb("nT", [128, DC, TT], BF16, pa)
        actT = sb("actT", [128, FC, TT], BF16, pa)
        wring = [sb("wring%d" % i, [128, 4096], BF16, pa) for i in range(3)]
        dring = [sb("dring%d" % i, [128, 22 * 128], BF16, pa) for i in range(3)]
        xtok = [sb("xtok%d" % i, [128, 1024], F32, pa) for i in range(2)]
        sq = [sb("sq%d" % i, [128, TT], BF16, pa) for i in range(2)]
        rstd = sb("rstd", [128, TT], F32, pa)
        sg = [sb("sg%d" % i, [128, TT], F32, pa) for i in range(2)]
        stg = [sb("stg%d" % i, [128, TT], F32, pa) for i in range(3)]
        stgb = [sb("stgb%d" % i, [128, TT], BF16, pa) for i in range(2)]
        vstg = [sb("vstg%d" % i, [128, 4 * 65], BF16, pa) for i in range(2)]
        cnt = {'w': 0, 'd': 0, 'x': 0, 'sq': 0, 'sg': 0, 'stg': 0, 'stgb': 0, 'tp': 0, 'vs': 0, 'ev': 0}
        xT_keys = [('xT', dc) for dc in range(DC)]

        def evac_copy(out, in_, reads, writes):
            cnt['ev'] += 1
            if cnt['ev'] % 2:
                S.op('act', lambda e, o=out, i=in_: e.activation(out=o, in_=i, func=AF.Copy), reads, writes)
            else:
                S.op('dve', lambda e, o=out, i=in_: e.tensor_copy(out=o, in_=i), reads, writes)

        def load_x_tile(it):
            for s in range(4):
                for hf in range(2):
                    xb_ = cnt['x'] % 2
                    cnt['x'] += 1
                    r0 = it * TT + s * 128
                    dma('sp', xtok[xb_][:], x_d[r0:r0 + 128, hf * 1024:(hf + 1) * 1024], [], [('xtok', xb_)])
                    for g in range(2):
                        bank = 6 + cnt['tp'] % 2
                        cnt['tp'] += 1
                        for q in range(4):
                            S.mm(lambda e, b=bank, q=q, g=g, xb_=xb_: e.transpose(PS[b][:, q * 128:(q + 1) * 128],
                                                                                 xtok[xb_][:, (g * 4 + q) * 128:(g * 4 + q + 1) * 128], ident),
                                 [('xtok', xb_), 'cst'], [('ps', bank)], last=(q == 3))
                        dc0 = hf * 8 + g * 4
                        evac_copy(xT[:, dc0:dc0 + 4, s * 128:(s + 1) * 128], PS[bank][:].rearrange("p (q t) -> p q t", q=4),
                                  [('ps', bank)], [('xT', dc0 + q) for q in range(4)])

        def store_y_tile(it, gi):
            rms_stats()
            for dc in range(DC):
                if dc % 2 == 0:
                    S.op('dve', lambda e, dc=dc: e.scalar_tensor_tensor(out=xT[:, dc, :], in0=xT[:, dc, :], scalar=gains[:, gi * DC + dc: gi * DC + dc + 1],
                                                                        in1=rstd[:], op0=ALU.mult, op1=ALU.mult), [('xT', dc), 'rstd', 'gains'], [('xT', dc)])
                else:
                    S.op('pool', lambda e, dc=dc: e.tensor_scalar(out=xT[:, dc, :], in0=xT[:, dc, :], scalar1=gains[:, gi * DC + dc: gi * DC + dc + 1], scalar2=0.0, op0=ALU.mult, op1=ALU.add),
                         [('xT', dc), 'gains'], [('xT', dc)])
                    S.op('pool', lambda e, dc=dc: e.tensor_tensor(out=xT[:, dc, :], in0=xT[:, dc, :], in1=rstd[:], op=ALU.mult), [('xT', dc), 'rstd'], [('xT', dc)])
            for s in range(4):
                for hf in range(2):
                    xb_ = cnt['x'] % 2
                    cnt['x'] += 1
                    for g in range(2):
                        bank = 6 + cnt['tp'] % 2
                        cnt['tp'] += 1
                        for q in range(4):
                            dc = hf * 8 + g * 4 + q
                            S.mm(lambda e, b=bank, q=q, dc=dc, s=s: e.transpose(PS[b][:, q * 128:(q + 1) * 128], xT[:, dc, s * 128:(s + 1) * 128], ident),
                                 [('xT', dc), 'cst'], [('ps', bank)], last=(q == 3))
                        evac_copy(xtok[xb_][:, g * 512:(g + 1) * 512], PS[bank][:], [('ps', bank)], [('xtok', xb_)])
                    r0 = it * TT + s * 128
                    dma('sp', y_d[r0:r0 + 128, hf * 1024:(hf + 1) * 1024], xtok[xb_][:], [('xtok', xb_)], ['y'])

        def rms_stats():
            for dc in range(DC):
                k = cnt['sq'] % 2
                cnt['sq'] += 1
                S.op('act', lambda e, k=k, dc=dc: e.activation(out=sq[k][:], in_=xT[:, dc, :], func=AF.Square), [('xT', dc)], [('sq', k)])
                S.mm(lambda e, k=k, dc=dc: e.matmul(PS[4][:], onesb[:], sq[k][:], start=(dc == 0), stop=(dc == DC - 1)),
                     [('sq', k), 'onesb'], [('ps', 4)], last=(dc == DC - 1))
            S.op('act', lambda e: e.activation(out=rstd[:], in_=PS[4][:], func=AF.Sqrt, scale=1.0 / D, bias=epsn[:]), [('ps', 4), 'epsn'], ['rstd'])
            S.op('dve', lambda e: e.reciprocal(out=rstd[:], in_=rstd[:]), ['rstd'], ['rstd'])

        def rmsnorm_to_nT(gi):
            rms_stats()
            for dc in range(DC):
                if dc % 2 == 0:
                    S.op('dve', lambda e, dc=dc: e.scalar_tensor_tensor(out=nT[:, dc, :], in0=xT[:, dc, :], scalar=gains[:, gi * DC + dc: gi * DC + dc + 1],
                                                                        in1=rstd[:], op0=ALU.mult, op1=ALU.mult), [('xT', dc), 'rstd', 'gains'], [('nT', dc)])
                else:
                    k = cnt['stg'] % 3
                    cnt['stg'] += 1
                    S.op('pool', lambda e, dc=dc, k=k: e.tensor_scalar(out=stg[k][:], in0=xT[:, dc, :], scalar1=gains[:, gi * DC + dc: gi * DC + dc + 1], scalar2=0.0, op0=ALU.mult, op1=ALU.add),
                         [('xT', dc), 'gains'], [('stg', k)])
                    S.op('pool', lambda e, dc=dc, k=k: e.tensor_tensor(out=nT[:, dc, :], in0=stg[k][:], in1=rstd[:], op=ALU.mult), [('stg', k), 'rstd'], [('nT', dc)])

        def ffn(fi):
            for j in range(FC):
                w = cnt['w'] % 3
                cnt['w'] += 1
                dma('sp', wring[w][:], wgu_b[fi][j * 128:(j + 1) * 128, :], [('wgu%d' % fi, j // 2)], [('wring', w)])
                bg = j % 2
                bu = 2 + j % 2
                for kc in range(DC):
                    S.mm(lambda e, w=w, kc=kc, bg=bg: e.matmul(PS[bg][:], wring[w][:, kc * 128:(kc + 1) * 128], nT[:, kc, :], start=(kc == 0), stop=(kc == DC - 1)),
                         [('wring', w), ('nT', kc)], [('ps', bg)], last=(kc == DC - 1))
                for kc in range(DC):
                    S.mm(lambda e, w=w, kc=kc, bu=bu: e.matmul(PS[bu][:], wring[w][:, 2048 + kc * 128: 2048 + (kc + 1) * 128], nT[:, kc, :], start=(kc == 0), stop=(kc == DC - 1)),
                         [('wring', w), ('nT', kc)], [('ps', bu)], last=(kc == DC - 1))
                k = cnt['sg'] % 2
                cnt['sg'] += 1
                S.op('act', lambda e, k=k, bg=bg: e.activation(out=sg[k][:], in_=PS[bg][:], func=AF.Silu), [('ps', bg)], [('sg', k)])
                S.op('dve', lambda e, k=k, bu=bu, j=j: e.tensor_tensor(out=actT[:, j, :], in0=sg[k][:], in1=PS[bu][:], op=ALU.mult), [('sg', k), ('ps', bu)], [('actT', j)])
            for m in range(DC):
                bd_ = 4 + m % 2
                for hf in range(2):
                    dd = cnt['d'] % 3
                    cnt['d'] += 1
                    dma('sp', dring[dd][:], wd_b[fi][m * 128:(m + 1) * 128, hf * 2816:(hf + 1) * 2816], [('wd%d' % fi, m)], [('dring', dd)])
                    for f2 in range(22):
                        fc = hf * 22 + f2
                        S.mm(lambda e, dd=dd, f2=f2, fc=fc, bd_=bd_: e.matmul(PS[bd_][:], dring[dd][:, f2 * 128:(f2 + 1) * 128], actT[:, fc, :], start=(fc == 0), stop=(fc == FC - 1)),
                             [('dring', dd), ('actT', fc)], [('ps', bd_)], last=(fc == FC - 1))
                S.op('dve', lambda e, m=m, bd_=bd_: e.scalar_tensor_tensor(out=xT[:, m, :], in0=PS[bd_][:], scalar=0.5, in1=xT[:, m, :], op0=ALU.mult, op1=ALU.add),
                     [('ps', bd_), ('xT', m)], [('xT', m)])

        fm_dest = [('q', c) for c in range(8)] + [('k', c) for c in range(8)] + [('z', c) for c in range(ZC)]

        def proj(it):
            seg = it // TPS
            t0 = (it % TPS) * TT
            for ci, (kind, c) in enumerate(fm_dest):
                w = cnt['w'] % 3
                cnt['w'] += 1
                dma('sp', wring[w][:, 0:2048], winfm_b[ci * 128:(ci + 1) * 128, :], [('winfm', ci // 4)], [('wring', w)])
                bank = ci % 2
                for kc in range(DC):
                    S.mm(lambda e, w=w, kc=kc, bank=bank: e.matmul(PS[bank][:], wring[w][:, kc * 128:(kc + 1) * 128], nT[:, kc, :], start=(kc == 0), stop=(kc == DC - 1)),
                         [('wring', w), ('nT', kc)], [('ps', bank)], last=(kc == DC - 1))
                if kind in ('q', 'k'):
                    k = cnt['stgb'] % 2
                    cnt['stgb'] += 1
                    evac_copy(stgb[k][:], PS[bank][:], [('ps', bank)], [('stgb', k)])
                    if kind == 'q':
                        dma('act', qT_d[c * 128:(c + 1) * 128, it * TT:(it + 1) * TT], stgb[k][:], [('stgb', k)], [('qT', seg)])
                    else:
                        col = seg * SEGA + APAD + t0
                        dma('act', kT_d[c * 128:(c + 1) * 128, col:col + TT], stgb[k][:], [('stgb', k)], [('kT', seg)])
                else:
                    k = cnt['stg'] % 3
                    cnt['stg'] += 1
                    evac_copy(stg[k][:], PS[bank][:], [('ps', bank)], [('stg', k)])
                    col = seg * SEGZ + 1 + t0
                    dma('act', zT_d[c * 128:(c + 1) * 128, col:col + TT], stg[k][:], [('stg', k)], [('zT', seg)])
            for g in range(4):
                w = cnt['w'] % 3
                cnt['w'] += 1
                dma('sp', wring[w][:], wintm_b[g * 128:(g + 1) * 128, :], [('wintm', g // 2)], [('wring', w)])
                for s in range(4):
                    bank = 2 + (g * 4 + s) % 2
                    for kc in range(DC):
                        S.mm(lambda e, w=w, kc=kc, bank=bank, s=s: e.matmul(PS[bank][:, 0:256], nT[:, kc, s * 128:(s + 1) * 128], wring[w][:, kc * 256:(kc + 1) * 256],
                                                                             start=(kc == 0), stop=(kc == DC - 1)),
                             [('wring', w), ('nT', kc)], [('ps', bank)], last=(kc == DC - 1))
                    if True:
                        k = cnt['vs'] % 2
                        cnt['vs'] += 1
                        S.op('pool', lambda e, k=k: e.memset(vstg[k][:], 1.0), [], [('vstg', k)])
                        evac_copy(vstg[k][:].rearrange("p (h c) -> p h c", h=4)[:, :, 0:64], PS[bank][:, 0:256].rearrange("p (h c) -> p h c", h=4),
                                  [('ps', bank), ('vstg', k)], [('vstg', k)])
                        row = seg * SEGA + APAD + t0 + s * 128
                        dma('act', vatt_d[row:row + 128, g * 260:(g + 1) * 260], vstg[k][:], [('vstg', k)], [('vatt', seg)])

        def wout(it):
            dma('sp', nT[:], mixT_d[:, it * TT:(it + 1) * TT].rearrange("(c p) t -> p c t", p=128), [('mixT', it // TPS)], [('nT', dc) for dc in range(DC)])
            dma('sp', xT[:], x1T_d[:, it * TT:(it + 1) * TT].rearrange("(c p) t -> p c t", p=128), [('x1T', it)], xT_keys)
            for m in range(DC):
                w = cnt['w'] % 3
                cnt['w'] += 1
                dma('sp', wring[w][:, 0:2048], wo_b[m * 128:(m + 1) * 128, :], [('wo', m // 4)], [('wring', w)])
                bank = m % 2
                for kc in range(DC):
                    S.mm(lambda e, w=w, kc=kc, bank=bank: e.matmul(PS[bank][:], wring[w][:, kc * 128:(kc + 1) * 128], nT[:, kc, :], start=(kc == 0), stop=(kc == DC - 1)),
                         [('wring', w), ('nT', kc)], [('ps', bank)], last=(kc == DC - 1))
                S.op('dve', lambda e, m=m, bank=bank: e.tensor_tensor(out=xT[:, m, :], in0=xT[:, m, :], in1=PS[bank][:], op=ALU.add), [('ps', bank), ('xT', m)], [('xT', m)])

        return dict(load_x_tile=load_x_tile, rmsnorm_to_nT=rmsnorm_to_nT, ffn=ffn, proj=proj, wout=wout, store_y_tile=store_y_tile, xT=xT, xT_keys=xT_keys)

    def phase_A():
        pa = ExitStack()
        env = ffn_env(pa, 'A')
        for it in range(NT):
            env['load_x_tile'](it)
            env['rmsnorm_to_nT'](0)
            env['ffn'](0)
            dma('act', x1T_d[:, it * TT:(it + 1) * TT].rearrange("(c p) t -> p c t", p=128), env['xT'][:], env['xT_keys'], [('x1T', it)])
            env['rmsnorm_to_nT'](1)
            env['proj'](it)
        S.barrier()
        pa.close()


    if 'A' in phases:
        phase_A()
    def phase_fix():
        pf = ExitStack()
        fk = sb("fk", [128, APAD], BF16, pf)
        fv = sb("fv", [128, APAD // 128, 1040], BF16, pf)
        fz = sb("fz", [128, ZC], F32, pf)
        for (ds, doff, ss, soff) in ((0, APAD + SEG, 1, APAD), (1, 0, 0, SEG)):
            for c in range(8):
                dma('sp', fk[:], kT_d[c * 128:(c + 1) * 128, ss * SEGA + soff: ss * SEGA + soff + APAD], [('kT', ss)], ['fk'])
                S.op('dve', lambda e: e.tensor_scalar(out=fk[:], in0=fk[:], scalar1=link[:, 0:1], scalar2=None, op0=ALU.mult), ['fk', 'link'], ['fk'])
                dma('sp', kT_d[c * 128:(c + 1) * 128, ds * SEGA + doff: ds * SEGA + doff + APAD], fk[:], ['fk', ('kTpad', ds, doff)], [('kTpad', ds, doff)])
            dma('sp', fv[:], vatt_d[ss * SEGA + soff: ss * SEGA + soff + APAD, :].rearrange("(a p) c -> p a c", p=128), [('vatt', ss)], ['fv'])
            S.op('dve', lambda e: e.tensor_scalar(out=fv[:], in0=fv[:], scalar1=link[:, 0:1], scalar2=None, op0=ALU.mult), ['fv', 'link'], ['fv'])
            dma('sp', vatt_d[ds * SEGA + doff: ds * SEGA + doff + APAD, :].rearrange("(a p) c -> p a c", p=128), fv[:], ['fv', ('vapad', ds, doff)], [('vapad', ds, doff)])
        for (ds, doff, ss, soff) in ((0, SEG + 1, 1, 1), (1, 0, 0, SEG)):
            dma('sp', fz[:].unsqueeze(2), zT_d[:, ss * SEGZ + soff: ss * SEGZ + soff + 1].rearrange("(c p) o -> p c o", p=128), [('zT', ss)], ['fz'], allow_slow_non_contiguous=True)
            S.op('dve', lambda e: e.tensor_scalar(out=fz[:], in0=fz[:], scalar1=link[:, 0:1], scalar2=None, op0=ALU.mult), ['fz', 'link'], ['fz'])
            dma('sp', zT_d[:, ds * SEGZ + doff: ds * SEGZ + doff + 1].rearrange("(c p) o -> p c o", p=128), fz[:].unsqueeze(2), ['fz', ('zpad', ds, doff)], [('zpad', ds, doff)], allow_slow_non_contiguous=True)
        S.barrier()
        pf.close()

    if 'fix' in phases:
        phase_fix()
    def phase_att():
        pt = ExitStack()
        NKT = {1: SEG // 128 + 1, 4: SEG // 512 + 1, 16: SEG // 2048 + 1}
        qt = sb("qt", [128, SEG], BF16, pt)
        kt = sb("kt", [128, SEGA], BF16, pt)
        vres = {d: sb("vres%d" % d, [128, d * NKT[d], 130], BF16, pt) for d in (1, 4, 16)}
        acc = [sb("acc%d" % h, [65, SEG], F32, pt) for h in range(2)]
        etf = sb("etf", [128, 1536], F32, pt)
        etb = sb("etb", [128, 1536], BF16, pt)
        pex = [sb("pex%d" % i, [128, 256], BF16, pt) for i in range(6)]
        pm = [sb("pm%d" % i, [128, 256], BF16, pt) for i in range(6)]
        rden = sb("rden", [64, 512], F32, pt)
        ostg = [sb("ostg%d" % i, [64, 512], BF16, pt) for i in range(2)]
        sel = sb("sel", [65, 64], F32, pt)
        S.op('pool', lambda e: e.memset(sel[:], 0.0), [], ['sel'])
        S.op('pool', lambda e: e.memset(sel[64:65, :], 1.0), ['sel'], ['sel'])
        ac = {'u': 0, 'o': 0, 'os': 0}

        def emit_pv(pi, d, r, b, h):
            ob = 4 + ac['o'] % 2
            ac['o'] += 1
            for j in range(2):
                S.mm(lambda e, ob=ob, j=j, pi=pi, d=d, r=r, b=b, h=h: e.matmul(PS[ob][0:65, 0:128], vres[d][:, r * NKT[d] + b + j, h * 65:(h + 1) * 65],
                                                                               pm[pi][:, j * 128:(j + 1) * 128], start=(j == 0), stop=(j == 1)),
                     [('vres', d), ('pm', pi)], [('ps', ob)])
            asl = acc[h][:, r + d * 128 * b: r + d * 128 * b + d * 127 + 1: d]
            if d == 1:
                S.op('dve', lambda e, asl=asl, ob=ob: e.tensor_copy(out=asl, in_=PS[ob][0:65, 0:128]), [('ps', ob)], [('acc', h)])
            else:
                S.op('dve', lambda e, asl=asl, ob=ob: e.tensor_tensor(out=asl, in0=asl, in1=PS[ob][0:65, 0:128], op=ALU.add), [('ps', ob), ('acc', h)], [('acc', h)])

        pend = []
        for seg in range(NSEG):
            for hp in range(ATT_PAIRS):
                dma('sp', qt[:], qT_d[hp * 128:(hp + 1) * 128, seg * SEG:(seg + 1) * SEG], [('qT', seg)], ['qt'])
                dma('sp', kt[:], kT_d[hp * 128:(hp + 1) * 128, seg * SEGA:(seg + 1) * SEGA], [('kT', seg)] + [('kTpad', seg, o) for o in (0, APAD + SEG)], ['kt'])
                dma('sp', etf[:], etab_d[hp], [], ['etf'])
                S.op('pool', lambda e: e.tensor_copy(out=etb[:], in_=etf[:]), ['etf'], ['etb'])
                for d in (1, 4, 16):
                    for r in range(d):
                        base = seg * SEGA + APAD + r - 64 * d
                        src = vatt_d[base: base + d * 128 * NKT[d], hp * 130:(hp + 1) * 130].rearrange("(k p dd) c -> dd p k c", p=128, dd=d)[0]
                        dma('act', vres[d][:, r * NKT[d]:(r + 1) * NKT[d], :], src, [('vatt', seg)] + [('vapad', seg, o) for o in (0, APAD + SEG)], [('vres', d)])
                for h in range(2):
                    for di, d in enumerate((1, 4, 16)):
                        L = SEG // d
                        for r in range(d):
                            for b in range(L // 128):
                                u = ac['u']
                                ac['u'] += 1
                                sbank = u % 4
                                qs = qt[h * 64:(h + 1) * 64, r + d * 128 * b: r + d * 128 * b + d * 127 + 1: d]
                                for j in range(2):
                                    k0 = APAD + r + d * (128 * b - 64 + 128 * j)
                                    ks = kt[h * 64:(h + 1) * 64, k0: k0 + d * 127 + 1: d]
                                    S.mm(lambda e, sbank=sbank, j=j, ks=ks, qs=qs: e.matmul(PS[sbank][:, j * 128:(j + 1) * 128], ks, qs, start=True, stop=True),
                                         ['kt', 'qt'], [('ps', sbank)], last=(j == 1))
                                pi = u % 6
                                S.op('act', lambda e, pi=pi, sbank=sbank: e.activation(out=pex[pi][:], in_=PS[sbank][:, 0:256], func=AF.Exp, scale=0.125),
                                     [('ps', sbank)], [('pex', pi)])
                                eo = (h * 3 + di) * 256
                                S.op('pool', lambda e, pi=pi, eo=eo: e.tensor_tensor(out=pm[pi][:], in0=pex[pi][:], in1=etb[:, eo:eo + 256], op=ALU.mult),
                                     [('pex', pi), 'etb'], [('pm', pi)])
                                pend.append((pi, d, r, b, h))
                                if len(pend) > 2:
                                    emit_pv(*pend.pop(0))
                    while pend:
                        emit_pv(*pend.pop(0))
                    for t in range(SEG // 512):
                        S.mm(lambda e, h=h, t=t: e.matmul(PS[6][0:64, :], sel[:], acc[h][:, t * 512:(t + 1) * 512], start=True, stop=True), ['sel', ('acc', h)], [('ps', 6)])
                        S.op('dve', lambda e: e.reciprocal(out=rden[:], in_=PS[6][0:64, :]), [('ps', 6)], ['rden'])
                        k = ac['os'] % 2
                        ac['os'] += 1
                        S.op('dve', lambda e, k=k, h=h, t=t: e.tensor_tensor(out=ostg[k][:], in0=acc[h][0:64, t * 512:(t + 1) * 512], in1=rden[:], op=ALU.mult),
                             [('acc', h), 'rden'], [('ostg', k)])
                        row = hp * 128 + h * 64
                        dma('sp', mixT_d[row:row + 64, seg * SEG + t * 512: seg * SEG + (t + 1) * 512], ostg[k][:], [('ostg', k)], [('mixT', seg)])
        S.barrier()
        pt.close()


    if 'att' in phases:
        phase_att()
    def phase_rwkv():
        pr = ExitStack()
        twd = sb("twd", [128, SEG], BF16, pr)
        sad = sb("sad", [128, SEG], BF16, pr)
        sgd = sb("sgd", [128, SEG], BF16, pr)
        mu = sb("mu", [128, 2 * ZC], F32, pr)
        muc = sb("muc", [128, ZC], F32, pr)
        rwp = sb("rwp", [128, 80], F32, pr)
        lrf = sb("lrf", [128, 1024], F32, pr)
        w2b_ = sb("w2b", [128, 1024], BF16, pr)
        a2b_ = sb("a2b", [128, 1024], BF16, pr)
        g2b_ = sb("g2b", [128, 1024], BF16, pr)
        rmask = sb("rmask", [128, 512], F32, pr)
        cmask = [sb("cmask%d" % i, [128, 192], F32, pr) for i in range(2)]
        mSxb = [sb("mSxb%d" % i, [128, 128], BF16, pr) for i in range(2)]
        zin = [sb("zin%d" % i, [128, TT + 2], F32, pr) for i in range(3)]
        NW = 30
        wk = [sb("wk%d" % i, [128, TT], F32, pr) for i in range(NW)]
        bdt = {n: [sb("bd_%s%d" % (n, i), [128, 8, 192 if n == "At" else 128], BF16, pr) for i in range(2)] for n in ("At", "Bt", "Kt", "Vt")}
        gam = [sb("gam%d" % i, [128, TT], F32, pr) for i in range(2)]
        vTs = [sb("vTs%d" % i, [128, TT], F32, pr) for i in range(2)]
        bsum = [sb("bsum%d" % i, [128, TT], F32, pr) for i in range(2)]
        NU = 8
        ub = {n: [sb("u_%s%d" % (n, i), [128, 320 if n == "XM" else 256], BF16, pr) for i in range(NU)] for n in ("XM", "LM", "Z", "G")}
        ub1 = {n: [sb("u1_%s%d" % (n, i), [128, 128], BF16, pr) for i in range(NU)] for n in ("X", "Tt", "Kk", "V", "PpT", "RbT")}
        pw = [[sb("pw%d_%d" % (u, i), [128, 128], BF16, pr) for i in range(4)] for u in range(NU)]
        qg = [sb("qg%d" % i, [128, 128], F32, pr) for i in range(NU)]
        Hf = sb("Hf", [128, 128], F32, pr)
        Hb = [[sb("Hb%d_%d" % (c_, i), [128, 128], BF16, pr) for i in range(2)] for c_ in range(8)]
        ysb = [sb("ysb%d" % i, [128, TT], F32, pr) for i in range(2)]
        ybl = sb("ybl", [128, TT], F32, pr)
        ostr = [sb("ostr%d" % i, [128, TT], BF16, pr) for i in range(2)]
        for n in bdt:
            for i in range(2):
                S.op('pool', lambda e, n=n, i=i: e.memset(bdt[n][i][:], 0.0), [], [('bd', n, i)])
        dma('sp', mu[:], mu_d[:, :], [], ['mu'])
        dma('sp', rwp[:], rwp_d[:, :], [], ['rwp'])
        dma('sp', rmask[:], rmask_d[:, :], [], ['rmask'])
        for (b_, d_) in ((w2b_, w2_d), (a2b_, a2_d), (g2b_, g2_d)):
            dma('sp', lrf[:], d_[:, :], [], ['lrf'])
            S.op('dve', lambda e, b_=b_: e.tensor_copy(out=b_[:], in_=lrf[:]), ['lrf'], ['lrw'])
        S.op('dve', lambda e: e.tensor_tensor(out=muc[:], in0=mu[:, 0:ZC], in1=mu[:, ZC:2 * ZC], op=ALU.add), ['mu'], ['muc'])
        S.op('dve', lambda e: e.tensor_scalar(out=muc[:], in0=muc[:], scalar1=-1.0, scalar2=1.0, op0=ALU.mult, op1=ALU.add), ['muc'], ['muc'])
        for c in range(8):
            S.op('dve', lambda e, c=c: e.tensor_scalar(out=rwp[:, c * 10 + 6:c * 10 + 7], in0=rwp[:, c * 10 + 5:c * 10 + 6], scalar1=-1.0, scalar2=1.0, op0=ALU.mult, op1=ALU.add), ['rwp'], ['rwp'])
        for dirn in range(2):
            mI_ = m_iu if dirn == 0 else m_il
            mSx_ = m_sl if dirn == 0 else m_su
            mS_ = m_su if dirn == 0 else m_sl
            S.op('dve', lambda e, dirn=dirn, mS_=mS_: e.tensor_copy(out=cmask[dirn][:, 0:128], in_=mS_), ['cst'], ['cmask'])
            S.op('dve', lambda e, dirn=dirn, mI_=mI_: e.tensor_tensor(out=cmask[dirn][:, 128:192], in0=mI_[:, 0:64], in1=mI_[:, 64:128], op=ALU.add), ['cst', 'cmask'], ['cmask'])
            S.op('dve', lambda e, dirn=dirn, mSx_=mSx_: e.tensor_copy(out=mSxb[dirn][:], in_=mSx_), ['cst'], ['mSxb'])
        rc = {'z': 0, 'wk': 0, 'ps': 0, 'pf': 0, 'os': 0}

        def newwk():
            i = rc['wk'] % NW
            rc['wk'] += 1
            return wk[i], ('wk', i)

        def psreg():
            i = rc['ps'] % 6
            rc['ps'] += 1
            return PS[i][:, 0:256], ('psr', i)

        def psfull():
            i = 6 + rc['pf'] % 2
            rc['pf'] += 1
            return PS[i], ('psf', i)

        def load_shift(zc, seg, t0, dst=None, dkey=None):
            if dst is None:
                o, ok = newwk()
            else:
                o, ok = dst, dkey
            i = rc['z'] % 3
            rc['z'] += 1
            col = seg * SEGZ + t0
            dma('sp', zin[i][:], zT_d[zc * 128:(zc + 1) * 128, col:col + TT + 2], [('zT', seg)] + [('zpad', seg, o_) for o_ in (0, SEG + 1)], [('zin', i)])
            t1, t1k = newwk()
            S.op('pool', lambda e, i=i, o=o: e.tensor_scalar(out=o[:], in0=zin[i][:, 1:TT + 1], scalar1=muc[:, zc:zc + 1], scalar2=0.0, op0=ALU.mult, op1=ALU.add), [('zin', i), 'muc'], [ok])
            S.op('pool', lambda e, i=i, t1=t1: e.tensor_scalar(out=t1[:], in0=zin[i][:, 0:TT], scalar1=mu[:, zc:zc + 1], scalar2=0.0, op0=ALU.mult, op1=ALU.add), [('zin', i), 'mu'], [t1k])
            S.op('pool', lambda e, o=o, t1=t1: e.tensor_tensor(out=o[:], in0=o[:], in1=t1[:], op=ALU.add), [ok, t1k], [ok])
            S.op('pool', lambda e, i=i, t1=t1: e.tensor_scalar(out=t1[:], in0=zin[i][:, 2:TT + 2], scalar1=mu[:, ZC + zc:ZC + zc + 1], scalar2=0.0, op0=ALU.mult, op1=ALU.add), [('zin', i), 'mu', t1k], [t1k])
            S.op('pool', lambda e, o=o, t1=t1: e.tensor_tensor(out=o[:], in0=o[:], in1=t1[:], op=ALU.add), [ok, t1k], [ok])
            return o, ok

        def lowrank_prep(seg):
            for t in range(TPS):
                t0 = t * TT
                for (zc, dst, fn) in ((24, twd, AF.Tanh), (25, sad, AF.Copy), (26, sgd, AF.Sigmoid)):
                    o, ok = load_shift(zc, seg, t0)
                    S.op('act', lambda e, o=o, dst=dst, fn=fn, t0=t0: e.activation(out=dst[:, t0:t0 + TT], in_=o[:], func=fn), [ok], ['lr'])

        def sig_lowrank(wb, rows, src, c, t0, bias_ap):
            p, pk = psfull()
            r0, r1 = rows
            S.mm(lambda e, p=p: e.matmul(p[:], wb[r0:r1, c * 128:(c + 1) * 128], src[r0:r1, t0:t0 + TT], start=True, stop=True), ['lrw', 'lr'], [pk])
            o, ok = newwk()
            S.op('act', lambda e, p=p, o=o: e.activation(out=o[:], in_=p[:], func=AF.Sigmoid, bias=bias_ap, scale=1.0), [pk, 'rwp'], [ok])
            return o, ok

        def prep_tile(dirn, c, seg, t, bi):
            t0 = t * TT
            fwd = (dirn == 0)
            P = lambda j: rwp[:, c * 10 + j: c * 10 + j + 1]
            rs, rsk = load_shift(c, seg, t0)
            yield
            ks, ksk = load_shift(8 + c, seg, t0)
            yield
            load_shift(16 + c, seg, t0, vTs[bi], ('vTs', bi))
            yield
            for h in range(2):
                S.op('pool', lambda e, h=h: e.tensor_copy(out=bdt["Vt"][bi][h * 64:(h + 1) * 64, :, h * 64:(h + 1) * 64],
                                                          in_=vTs[bi][h * 64:(h + 1) * 64, :].rearrange("p (n t) -> p n t", t=64)),
                     [('vTs', bi)], [('bd', "Vt", bi)])
            kkr, kkrk = newwk()
            S.op('pool', lambda e: e.tensor_scalar(out=kkr[:], in0=ks[:], scalar1=P(4), scalar2=0.0, op0=ALU.mult, op1=ALU.add), [ksk, 'rwp'], [kkrk])
            sq_, sqk = newwk()
            S.op('act', lambda e: e.activation(out=sq_[:], in_=kkr[:], func=AF.Square), [kkrk], [sqk])
            p, pk = psfull()
            S.mm(lambda e, p=p: e.matmul(p[:], bones, sq_[:], start=True, stop=True), ['cst', sqk], [pk])
            rn, rnk = newwk()
            S.op('act', lambda e, p=p: e.activation(out=rn[:], in_=p[:], func=AF.Sqrt), [pk], [rnk])
            yield
            S.op('dve', lambda e: e.tensor_scalar(out=rn[:], in0=rn[:], scalar1=1e-12, scalar2=None, op0=ALU.max), [rnk], [rnk])
            S.op('dve', lambda e: e.reciprocal(out=rn[:], in_=rn[:]), [rnk], [rnk])
            kkn, kknk = newwk()
            S.op('dve', lambda e: e.scalar_tensor_tensor(out=kkn[:], in0=kkr[:], scalar=-1.0, in1=rn[:], op0=ALU.mult, op1=ALU.mult), [kkrk, rnk], [kknk])
            yield
            rows = (0, 64) if fwd else (64, 128)
            sgw, sgwk = sig_lowrank(w2b_, rows, twd, c, t0, P(0 if fwd else 1))
            ag, agk = sig_lowrank(a2b_, rows, sad, c, t0, P(2 if fwd else 3))
            yield
            kp_, kpk = newwk()
            S.op('pool', lambda e: e.tensor_scalar(out=kp_[:], in0=ag[:], scalar1=P(5), scalar2=P(6), op0=ALU.mult, op1=ALU.add), [agk, 'rwp'], [kpk])
            S.op('pool', lambda e: e.tensor_tensor(out=kp_[:], in0=kp_[:], in1=ks[:], op=ALU.mult), [kpk, ksk], [kpk])
            bb, bbk = newwk()
            S.op('dve', lambda e: e.scalar_tensor_tensor(out=bb[:], in0=kkn[:], scalar=-1.0, in1=ag[:], op0=ALU.mult, op1=ALU.mult), [kknk, agk], [bbk])
            yield
            cs, csk = newwk()
            S.op('dve', lambda e: e.tensor_tensor_scan(out=cs[:], data0=rmask[:], data1=sgw[:], initial=0.0, op0=ALU.mult, op1=ALU.add), ['rmask', sgwk], [csk])
            if fwd:
                lg, lgk = cs, csk
            else:
                lg, lgk = newwk()
                S.op('pool', lambda e: e.tensor_tensor(out=lg[:], in0=sgw[:], in1=cs[:], op=ALU.subtract), [sgwk, csk], [lgk])
                S.op('pool', lambda e: e.tensor_tensor(out=lg[:].rearrange("p (n t) -> p n t", t=64), in0=lg[:].rearrange("p (n t) -> p n t", t=64),
                                                       in1=cs[:].rearrange("p (n t) -> p n t", t=64)[:, :, 63:64].to_broadcast([128, 8, 64]), op=ALU.add), [lgk, csk], [lgk])
            yield
            gkey = ('gam', bi)
            S.op('act', lambda e: e.activation(out=gam[bi][:], in_=lg[:], func=AF.Exp, scale=-C0), [lgk], [gkey])
            ig, igk = newwk()
            S.op('act', lambda e: e.activation(out=ig[:], in_=lg[:], func=AF.Exp, scale=C0), [lgk], [igk])
            gp, gpk = newwk()
            S.op('dve', lambda e: e.tensor_tensor(out=gp[:], in0=lg[:], in1=sgw[:], op=ALU.subtract), [lgk, sgwk], [gpk])
            S.op('act', lambda e: e.activation(out=gp[:], in_=gp[:], func=AF.Exp, scale=-C0), [gpk], [gpk])
            yield
            for (nm, a_, ak, b_, bk) in (("At", kkn, kknk, gp, gpk), ("Bt", bb, bbk, ig, igk), ("Kt", kp_, kpk, ig, igk)):
                for h in range(2):
                    eng = 'pool'
                    S.op(eng, lambda e, nm=nm, a_=a_, b_=b_, h=h: e.tensor_tensor(out=bdt[nm][bi][h * 64:(h + 1) * 64, :, h * 64:(h + 1) * 64],
                                                                                   in0=a_[h * 64:(h + 1) * 64, :].rearrange("p (n t) -> p n t", t=64),
                                                                                   in1=b_[h * 64:(h + 1) * 64, :].rearrange("p (n t) -> p n t", t=64), op=ALU.mult),
                         [ak, bk], [('bd', nm, bi)])
                yield
            S.op('dve', lambda e: e.tensor_tensor(out=bdt["At"][bi][:, :, 128:192], in0=rs[:].rearrange("p (n t) -> p n t", t=64),
                                                  in1=gam[bi][:].rearrange("p (n t) -> p n t", t=64), op=ALU.mult), [rsk, gkey], [('Rt', bi)])
            if fwd:
                yield
                agb, agbk = sig_lowrank(a2b_, (64, 128), sad, c, t0, P(3))
                kb_, kbk = newwk()
                S.op('pool', lambda e: e.tensor_scalar(out=kb_[:], in0=agb[:], scalar1=P(5), scalar2=P(6), op0=ALU.mult, op1=ALU.add), [agbk, 'rwp'], [kbk])
                S.op('pool', lambda e: e.tensor_tensor(out=kb_[:], in0=kb_[:], in1=ks[:], op=ALU.mult), [kbk, ksk], [kbk])
                yield
                S.op('pool', lambda e: e.tensor_tensor(out=kb_[:], in0=kb_[:], in1=kp_[:], op=ALU.add), [kbk, kpk], [kbk])
                S.op('dve', lambda e: e.scalar_tensor_tensor(out=kb_[:], in0=rs[:], scalar=P(7), in1=kb_[:], op0=ALU.mult, op1=ALU.mult), [rsk, kbk, 'rwp'], [kbk])
                p2, pk2 = psfull()
                S.mm(lambda e, p2=p2: e.matmul(p2[:], bones, kb_[:], start=True, stop=True), ['cst', kbk], [pk2])
                S.op('act', lambda e, p2=p2: e.activation(out=bsum[bi][:], in_=p2[:], func=AF.Copy), [pk2], [('bsum', bi)])

        def offchain(dirn, n, bi, ui):
            fwd = (dirn == 0)
            mS = (m_su if fwd else m_sl)
            At = bdt["At"][bi][:, n, 0:128]
            AR = bdt["At"][bi][:, n, :]
            Bt = bdt["Bt"][bi][:, n, :]
            Kt = bdt["Kt"][bi][:, n, :]
            Vt = bdt["Vt"][bi][:, n, :]
            Rs = bdt["At"][bi][:, n, 128:192]
            gcol = gam[bi][:, n * 64 + 63: n * 64 + 64] if fwd else gam[bi][:, n * 64: n * 64 + 1]
            kA, kB, kK, kVt, kR, kG = ('bd', "At", bi), ('bd', "Bt", bi), ('bd', "Kt", bi), ('bd', "Vt", bi), ('Rt', bi), ('gam', bi)
            XM, LM, Z, G = ub["XM"][ui], ub["LM"][ui], ub["Z"][ui], ub["G"][ui]
            X, Tt, Kk, V, PpT, RbT = (ub1[k][ui] for k in ("X", "Tt", "Kk", "V", "PpT", "RbT"))
            Bk = XM[:, 192:320]
            K_ = lambda nm: ('u', nm, ui)
            cpc = [ui]

            def cp(dst, src, reads, writes):
                cpc[0] += 1
                if cpc[0] % 3 != 0:
                    S.op('act', lambda e, dst=dst, src=src: e.activation(out=dst, in_=src, func=AF.Copy), reads, writes)
                else:
                    S.op('dve', lambda e, dst=dst, src=src: e.tensor_copy(out=dst, in_=src), reads, writes)
            for (lh, lk, dst, dk) in ((Bt, kB, XM, 'XM'), (Kt, kK, LM, 'LM')):
                p, pk = psreg()
                S.mm(lambda e, p=p, lh=lh: e.matmul(p[:, 0:192], lh, AR, start=True, stop=True), [lk, kA, kR], [pk])
                S.op('dve', lambda e, p=p, dst=dst: e.tensor_tensor(out=dst[:, 0:192], in0=p[:, 0:192], in1=cmask[dirn][:], op=ALU.mult), [pk, 'cmask'], [K_(dk)])
                yield
            p, pk = psreg()
            S.mm(lambda e, p=p: e.matmul(p[:, 0:128], At, Bt, start=True, stop=True), [kA, kB], [pk])
            cp(X[:], p[:, 0:128], [pk], [K_('X')])
            S.op('pool', lambda e: e.tensor_tensor(out=X[:], in0=X[:], in1=mSxb[dirn][:], op=ALU.mult), [K_('X'), 'mSxb'], [K_('X')])
            S.op('pool', lambda e: e.tensor_tensor(out=Tt[:], in0=XM[:, 0:128], in1=identb, op=ALU.add), [K_('XM'), 'cstb'], [K_('Tt')])
            yield
            for (src, sk, dk) in ((At, kA, 'A'), (Bt, kB, 'Bk'), (Kt, kK, 'Kk'), (Vt, kVt, 'V')):
                p, pk = psreg()
                S.mm(lambda e, p=p, src=src: e.matmul(p[:, 0:128], src, identb, start=True, stop=True), [sk, 'cstb'], [pk])
                if dk == 'A':
                    cp(Z[:, 0:128], p[:, 0:128], [pk], [K_('Z0')])
                elif dk == 'Bk':
                    cp(Bk, p[:, 0:128], [pk], [K_('Bk')])
                elif dk == 'Kk':
                    cp(Kk[:], p[:, 0:128], [pk], [K_('Kk')])
                else:
                    cp(V[:], p[:, 0:128], [pk], [K_('V')])
                yield
            Pc, Pck = X[:], K_('X')
            Ptc, Ptck = XM[:, 0:128], K_('XM')
            for lvl in range(5):
                pn = pw[ui][(2 * lvl) % 4]
                pnk = ('pw', ui, (2 * lvl) % 4)
                p1, pk1 = psreg()
                S.mm(lambda e, p1=p1, Pc=Pc, Ptc=Ptc: e.matmul(p1[:, 0:128], Ptc, Pc, start=True, stop=True), [Pck, Ptck], [pk1])
                if lvl < 4:
                    ptn = pw[ui][(2 * lvl + 1) % 4]
                    ptnk = ('pw', ui, (2 * lvl + 1) % 4)
                    p2, pk2 = psreg()
                    S.mm(lambda e, p2=p2, Pc=Pc, Ptc=Ptc: e.matmul(p2[:, 0:128], Pc, Ptc, start=True, stop=True), [Pck, Ptck], [pk2])
                    if lvl % 2 == 0:
                        cp(ptn[:], p2[:, 0:128], [pk2], [ptnk])
                    else:
                        cp(ptn[:], p2[:, 0:128], [pk2], [ptnk])
                cp(pn[:], p1[:, 0:128], [pk1], [pnk])
                Pc, Pck = pn[:], pnk
                if lvl < 4:
                    Ptc, Ptck = ptn[:], ptnk
                yield
                p3, pk3 = psreg()
                S.mm(lambda e, p3=p3, Pc=Pc: e.matmul(p3[:, 0:128], Pc, Tt[:], start=True, stop=True), [Pck, K_('Tt')], [pk3])
                S.op('dve', lambda e, p3=p3: e.tensor_tensor(out=Tt[:], in0=Tt[:], in1=p3[:, 0:128], op=ALU.add), [pk3, K_('Tt')], [K_('Tt')])
                if lvl == 4:
                    yield
            p, pk = psreg()
            S.mm(lambda e, p=p: e.matmul(p[:, 0:128], LM[:, 0:128], V[:], start=True, stop=True), [K_('LM'), K_('V')], [pk])
            cp(Z[:, 128:256], p[:, 0:128], [pk], [K_('Z1')])
            yield
            p, pk = psreg()
            S.mm(lambda e, p=p: e.matmul(p[:, 0:256], Tt[:], Z[:], start=True, stop=True), [K_('Tt'), K_('Z0'), K_('Z1')], [pk])
            cp(G[:], p[:, 0:256], [pk], [K_('G')])
            yield
            p, pk = psreg()
            S.mm(lambda e, p=p: e.matmul(p[:, 0:192], G[:, 0:128], XM[:, 128:320], start=True, stop=True), [K_('G'), K_('Bk'), K_('XM')], [pk])
            S.op('dve', lambda e, p=p: e.tensor_tensor(out=PpT[:], in0=p[:, 64:192], in1=ident, op=ALU.add), [pk, 'cst'], [K_('PpT')])
            S.op('dve', lambda e, p=p: e.tensor_tensor(out=RbT[:, 0:64], in0=p[:, 0:64], in1=Rs, op=ALU.add), [pk, kR], [K_('RbT')])
            p, pk = psreg()
            S.mm(lambda e, p=p: e.matmul(p[:, 0:128], Bk, G[:, 128:256], start=True, stop=False), [K_('G'), K_('Bk')], [pk])
            S.mm(lambda e, p=p: e.matmul(p[:, 0:128], Kk[:], V[:], start=False, stop=True), [K_('Kk'), K_('V')], [pk])
            S.op('dve', lambda e, p=p: e.tensor_scalar(out=qg[ui][:], in0=p[:, 0:128], scalar1=gcol, scalar2=None, op0=ALU.mult), [pk, kG], [K_('qg')])

        def chain(dirn, n, bi, ui, Hcur, Hnew, ysl, yslk):
            fwd = (dirn == 0)
            gcol = gam[bi][:, n * 64 + 63: n * 64 + 64] if fwd else gam[bi][:, n * 64: n * 64 + 1]
            kG = ('gam', bi)
            XM, LM, G = ub["XM"][ui], ub["LM"][ui], ub["G"][ui]
            V, PpT, RbT = (ub1[k][ui] for k in ("V", "PpT", "RbT"))
            K_ = lambda nm: ('u', nm, ui)
            p, pk = psreg()
            S.mm(lambda e, p=p: e.matmul(p[:, 0:64], G[:, 128:256], XM[:, 128:192], start=True, stop=False), [K_('G'), K_('XM')], [pk])
            S.mm(lambda e, p=p: e.matmul(p[:, 0:64], V[:], LM[:, 128:192], start=False, stop=False), [K_('V'), K_('LM')], [pk])
            S.mm(lambda e, p=p: e.matmul(p[:, 0:64], Hcur[0][:], RbT[:, 0:64], start=False, stop=True), [Hcur[1], K_('RbT')], [pk])
            S.op('act', lambda e, p=p: e.activation(out=ysl, in_=p[:, 0:64], func=AF.Copy), [pk], [yslk])
            p2, pk2 = psreg()
            S.mm(lambda e, p2=p2: e.matmul(p2[:, 0:128], PpT[:], Hcur[0][:], start=True, stop=True), [K_('PpT'), Hcur[1]], [pk2])
            S.op('dve', lambda e, p2=p2: e.scalar_tensor_tensor(out=Hnew[0][:], in0=p2[:, 0:128], scalar=gcol, in1=qg[ui][:], op0=ALU.mult, op1=ALU.add),
                 [pk2, K_('qg'), kG], [Hnew[1]])

        def epilogue(c, seg, t, bi, yt, ytk):
            t0 = t * TT
            P = lambda j: rwp[:, c * 10 + j: c * 10 + j + 1]
            bon, bonk = newwk()
            S.op('pool', lambda e: e.tensor_tensor(out=bon[:], in0=vTs[bi][:], in1=bsum[bi][:], op=ALU.mult), [('vTs', bi), ('bsum', bi)], [bonk])
            dma('sp', ybl[:], ybT_d[c * 128:(c + 1) * 128, seg * SEG + t0: seg * SEG + t0 + TT], [('ybT', c)], ['ybl'])
            ysum, ysk = newwk()
            S.op('dve', lambda e: e.tensor_tensor(out=ysum[:], in0=yt[:], in1=ybl[:], op=ALU.add), [ytk, 'ybl'], [ysk])
            yield
            p, pk = psfull()
            S.mm(lambda e, p=p: e.matmul(p[:], bones, ysum[:], start=True, stop=True), ['cst', ysk], [pk])
            yield
            yc, yck = newwk()
            S.op('dve', lambda e, p=p: e.scalar_tensor_tensor(out=yc[:], in0=p[:], scalar=-1.0 / 64, in1=ysum[:], op0=ALU.mult, op1=ALU.add), [pk, ysk], [yck])
            yield
            sq_, sqk = newwk()
            S.op('act', lambda e: e.activation(out=sq_[:], in_=yc[:], func=AF.Square), [yck], [sqk])
            yield
            p2, pk2 = psfull()
            S.mm(lambda e, p2=p2: e.matmul(p2[:], bones, sq_[:], start=True, stop=True), ['cst', sqk], [pk2])
            yield
            sd, sdk = newwk()
            S.op('act', lambda e, p2=p2: e.activation(out=sd[:], in_=p2[:], func=AF.Sqrt, scale=1.0 / 64, bias=epsl[:]), [pk2, 'epsl'], [sdk])
            yield
            S.op('dve', lambda e: e.reciprocal(out=sd[:], in_=sd[:]), [sdk], [sdk])
            yield
            S.op('dve', lambda e: e.tensor_tensor(out=yc[:], in0=yc[:], in1=sd[:], op=ALU.mult), [yck, sdk], [yck])
            yield
            S.op('pool', lambda e: e.tensor_scalar(out=yc[:], in0=yc[:], scalar1=P(8), scalar2=P(9), op0=ALU.mult, op1=ALU.add), [yck, 'rwp'], [yck])
            yield
            S.op('pool', lambda e: e.tensor_tensor(out=yc[:], in0=yc[:], in1=bon[:], op=ALU.add), [yck, bonk], [yck])
            p3, pk3 = psfull()
            S.mm(lambda e, p3=p3: e.matmul(p3[:], g2b_[:, c * 128:(c + 1) * 128], sgd[:, t0:t0 + TT], start=True, stop=True), ['lrw', 'lr'], [pk3])
            yield
            k = rc['os'] % 2
            rc['os'] += 1
            S.op('dve', lambda e, p3=p3, k=k: e.tensor_tensor(out=ostr[k][:], in0=yc[:], in1=p3[:], op=ALU.mult), [yck, pk3], [('ostr', k)])
            dma('sp', mixT_d[1024 + c * 128: 1024 + (c + 1) * 128, seg * SEG + t0: seg * SEG + t0 + TT], ostr[k][:], [('ostr', k)], [('mixT', seg)])

        tiles = []
        for (dirn, seg, si) in ((1, 1, 0), (1, 0, 1), (0, 0, 0), (0, 1, 1)):
            for c in range(RW_PAIRS):
                trange = range(TPS - 1, -1, -1) if dirn == 1 else range(TPS)
                for ti, t in enumerate(trange):
                    tiles.append(dict(c=c, dirn=dirn, seg=seg, t=t, first=(si == 0 and ti == 0), link=(si == 1 and ti == 0)))
        cur_seg = None
        hic = [0] * 8
        early = None
        pend_epi = None
        for k, tl in enumerate(tiles):
            bi = k % 2
            c, dirn, seg, t = tl['c'], tl['dirn'], tl['seg'], tl['t']
            if cur_seg != seg:
                lowrank_prep(seg)
                cur_seg = seg
            if early is None:
                early = prep_tile(dirn, c, seg, t, bi)
            for _ in early:
                pass
            early = None
            hi = hic[c]
            if tl['first']:
                hi = 0
                S.op('pool', lambda e, c=c: e.memset(Hb[c][0][:], 0.0), [], [('Hb', c, 0)])
            if tl['link']:
                S.op('dve', lambda e, hi=hi, c=c: e.tensor_scalar(out=Hb[c][hi][:], in0=Hb[c][hi][:], scalar1=link[:, 0:1], scalar2=None, op0=ALU.mult), [('Hb', c, hi), 'link'], [('Hb', c, hi)])
            chunks = list(range(7, -1, -1)) if dirn == 1 else list(range(8))
            nprep = None
            if k + 1 < len(tiles) and tiles[k + 1]['seg'] == seg:
                n2 = tiles[k + 1]
                nprep = prep_tile(n2['dirn'], n2['c'], n2['seg'], n2['t'], (k + 1) % 2)
                early = nprep

            def adv():
                nonlocal nprep, pend_epi
                if pend_epi is not None:
                    try:
                        next(pend_epi)
                    except StopIteration:
                        pend_epi = None
                if nprep is not None:
                    try:
                        next(nprep)
                    except StopIteration:
                        nprep = None
            yi = k % 2
            gens = [offchain(dirn, n, bi, ui) for ui, n in enumerate(chunks)]
            live = []
            started = 0
            finished = [False] * 8
            nchain = 0
            while nchain < 8:
                if started < 8:
                    live.append((started, gens[started]))
                    started += 1
                nxt = []
                for (ui, g) in live:
                    try:
                        next(g)
                        nxt.append((ui, g))
                    except StopIteration:
                        finished[ui] = True
                live = nxt
                if finished[nchain]:
                    n = chunks[nchain]
                    chain(dirn, n, bi, nchain, (Hb[c][hi], ('Hb', c, hi)), (Hb[c][1 - hi], ('Hb', c, 1 - hi)), ysb[yi][:, n * 64:(n + 1) * 64], ('ysb', yi))
                    hi = 1 - hi
                    nchain += 1
                adv()
            hic[c] = hi
            if pend_epi is not None:
                for _ in pend_epi:
                    pass
                pend_epi = None
            if dirn == 1:
                dma('sp', ybT_d[c * 128:(c + 1) * 128, seg * SEG + t * TT: seg * SEG + (t + 1) * TT], ysb[yi][:], [('ysb', yi)], [('ybT', c)])
            else:
                pend_epi = epilogue(c, seg, t, bi, ysb[yi], ('ysb', yi))
                if not (k + 1 < len(tiles) and tiles[k + 1]['seg'] == seg):
                    for _ in pend_epi:
                        pass
                    pend_epi = None
        S.barrier()
        pr.close()

    if 'rwkv' in phases:
        phase_rwkv()
    def phase_C():
        pc = ExitStack()
        env = ffn_env(pc, 'C')
        for it in range(NT):
            env['wout'](it)
            env['rmsnorm_to_nT'](2)
            env['ffn'](1)
            env['store_y_tile'](it, 3)
        S.barrier()
        pc.close()

    if 'C' in phases:
        phase_C()
    S.barrier()
    S.emit()
    global LAST_S
    LAST_S = S
    return nc

LAST_S = None

def _const_tables():
    tri = np.ones((64, 64), np.float32)
    su1, sl1, iu1, il1 = np.triu(tri, 1), np.tril(tri, -1), np.triu(tri, 0), np.tril(tri, 0)

    def bdm(m):
        o = np.zeros((128, 128), np.float32)
        o[:64, :64] = m
        o[64:, 64:] = m
        return o
    cst = np.zeros((128, 1024), np.float32)
    cst[:, 0:128] = np.eye(128, dtype=np.float32)
    cst[:, 128:256] = bdm(tri)
    cst[:, 256:384] = bdm(su1)
    cst[:, 384:512] = bdm(sl1)
    cst[:, 512:640] = bdm(iu1)
    cst[:, 640:768] = bdm(il1)
    rmask = np.ones((128, 512), np.float32)
    rmask[:, ::64] = 0.0
    slopes = np.exp2(-8.0 * np.arange(1, 17, dtype=np.float64) / 16)
    kp = np.arange(128)[:, None]
    qp = np.arange(128)[None, :]
    etab = np.zeros((8, 128, 1536), np.float32)
    for h in range(16):
        for di, d in enumerate((1, 4, 16)):
            for j in range(2):
                rel = kp + 128 * j - 64 - qp
                e = np.where(np.abs(rel) <= 64, np.exp(-slopes[h] * d * np.abs(rel)), 0.0)
                o = ((h % 2) * 3 + di) * 256 + j * 128
                etab[h // 2, :, o:o + 128] = e
    return cst, rmask, etab


def _prep_weights(inp):
    f = np.float32
    A = lambda a: np.ascontiguousarray(np.asarray(a, dtype=f))
    out = {}
    for fi, pre in enumerate(("ffn1", "ffn2")):
        g = np.asarray(inp[pre + "_gate"][0]).reshape(DC, 128, FC, 128).transpose(2, 1, 0, 3).reshape(FC, 128, 2048)
        u = np.asarray(inp[pre + "_up"][0]).reshape(DC, 128, FC, 128).transpose(2, 1, 0, 3).reshape(FC, 128, 2048)
        out["wgu%d" % (fi + 1)] = A(np.concatenate([g, u], axis=2).reshape(FF, 4096))
        dn = np.asarray(inp[pre + "_down"][0]).reshape(FC, 128, DC, 128).transpose(2, 1, 0, 3).reshape(D, FF)
        out["wd%d" % (fi + 1)] = A(dn)
    w_in = np.asarray(inp["w_in"][0])
    fm_cols = [c * 128 for c in range(16)] + [3072 + c * 128 for c in range(ZC)]
    fm = [w_in[:, c0:c0 + 128].reshape(DC, 128, 128).transpose(1, 0, 2).reshape(128, 2048) for c0 in fm_cols]
    out["winfm"] = A(np.concatenate(fm, axis=0))
    tm_cols = [2048 + g * 256 for g in range(4)]
    tm = [w_in[:, c0:c0 + 256].reshape(DC, 128, 256).transpose(1, 0, 2).reshape(128, 4096) for c0 in tm_cols]
    out["wintm"] = A(np.concatenate(tm, axis=0))
    out["wo"] = A(np.asarray(inp["w_out"][0]).reshape(DC, 128, DC, 128).transpose(2, 1, 0, 3).reshape(D, D))
    gains = np.zeros((128, 4 * DC), f)
    for gi, g in enumerate((inp["ffn1_norm"][0], inp["mix_norm"][0], inp["ffn2_norm"][0], inp["final_norm"])):
        gains[:, gi * DC:(gi + 1) * DC] = np.asarray(g).reshape(DC, 128).T
    out["gains"] = gains
    mu = np.zeros((128, 2 * ZC), f)
    mu[:, 0:ZC] = np.asarray(inp["mu_prev"][0]).reshape(ZC, 128).T
    mu[:, ZC:] = np.asarray(inp["mu_next"][0]).reshape(ZC, 128).T
    out["mu"] = mu
    rwp = np.zeros((128, 80), f)
    vecs = (inp["w0_f"][0], inp["w0_b"][0], inp["a0_f"][0], inp["a0_b"][0], inp["k_k"][0], inp["k_a"][0], None,
            np.asarray(inp["r_k"][0]).reshape(1024), inp["ln_x_w"][0], inp["ln_x_b"][0])
    for j, v in enumerate(vecs):
        if v is None:
            continue
        rwp[:, j::10] = np.asarray(v).reshape(8, 128).T
    out["rwp"] = rwp
    out["w2"] = A(np.concatenate([inp["w2_f"][0], inp["w2_b"][0]], axis=0))
    out["a2"] = A(np.concatenate([inp["a2_f"][0], inp["a2_b"][0]], axis=0))
    out["g2"] = A(inp["g2"][0])
    cst, rmask, etab = _const_tables()
    out["cst"], out["rmask"], out["etab"] = cst, rmask, etab
    return out


def _core_plan(SEG, x_prompt, x_sample):
    plan = []
    xp, xs = np.asarray(x_prompt), np.asarray(x_sample)
    nprompt_cores = xp.shape[0] * (xp.shape[1] // (NSEG * SEG))
    for b in range(xp.shape[0]):
        for part in range(xp.shape[1] // (NSEG * SEG)):
            assert xp.shape[1] == NSEG * SEG
            plan.append(([('p', b, 0), ('p', b, SEG)], 1.0))
    nb = xs.shape[0]
    assert xs.shape[1] == SEG
    rest = 8 - len(plan)
    two = nb - rest
    i = 0
    for c in range(rest):
        if c < two:
            plan.append(([('s', i, 0), ('s', i + 1, 0)], 0.0))
            i += 2
        elif i < nb:
            plan.append(([('s', i, 0), None], 0.0))
            i += 1
        else:
            plan.append(([None, None], 0.0))
    assert i == nb
    return plan


_NC_CACHE = {}


def run(inputs, SEG, debug=False, phases=('A', 'fix', 'att', 'rwkv', 'C')):
    wts = _prep_weights(inputs)
    xp, xs = np.asarray(inputs["x_prompt"], np.float32), np.asarray(inputs["x_sample"], np.float32)
    plan = _core_plan(SEG, xp, xs)
    in_maps = []
    for segs, lk in plan:
        x = np.zeros((NSEG * SEG, D), np.float32)
        for si, sdesc in enumerate(segs):
            if sdesc is None:
                continue
            kind, b, st = sdesc
            src = xp if kind == 'p' else xs
            x[si * SEG:(si + 1) * SEG] = src[b, st:st + SEG]
        m = dict(wts)
        m["x"] = x
        m["link"] = np.full((128, 1), lk, np.float32)
        in_maps.append(m)
    key = (SEG, debug, tuple(phases))
    if key not in _NC_CACHE:
        _NC_CACHE[key] = build(SEG, debug=debug, phases=phases)
    nc = _NC_CACHE[key]
    res = run_bass_kernel_spmd(nc, in_maps, core_ids=list(range(8)))
    yp = np.zeros(xp.shape, np.float32)
    ys = np.zeros(xs.shape, np.float32)
    for ci, (segs, lk) in enumerate(plan):
        y = res.results[ci]["y"]
        for si, sdesc in enumerate(segs):
            if sdesc is None:
                continue
            kind, b, st = sdesc
            (yp if kind == 'p' else ys)[b, st:st + SEG] = y[si * SEG:(si + 1) * SEG]
    return (yp, ys), res, plan


def kernel(**inputs):
    (yp, ys), _, _ = run(inputs, 4096)
    return (yp, ys)
```

```python
from contextlib import ExitStack
import numpy as np
import concourse.bass as bass
import concourse.mybir as mybir
from concourse.bass_utils import run_bass_kernel_spmd

F32 = mybir.dt.float32
BF16 = mybir.dt.bfloat16
ALU = mybir.AluOpType
AF = mybir.ActivationFunctionType

D = 2048
DC = 16
FF = 5632
FC = 44
TT = 512
NSEG = 2
APAD = 1024
NFM = 43
ZC = 27
C0 = 0.6065306597126334
NORM_EPS = 1e-6
LN_EPS = 64e-5

ENGS = ("pe", "dve", "act", "pool", "sp")
NDMASEM = 64
NHWSEM = 40


class Sched:
    def __init__(self, nc, es):
        self.nc = nc
        self.csem = {e: es.enter_context(nc.semaphore("c_" + e)) for e in ENGS}
        self.dsem = [es.enter_context(nc.semaphore("d%d" % i)) for i in range(NDMASEM)]
        self.dval = [0] * NDMASEM
        self.dnext = 0
        self.dnext_sw = 0
        self.count = {e: 0 for e in ENGS}
        self.prog = {e: [] for e in ENGS}
        self.waited = {e: {} for e in ENGS}
        self.lastw = {}
        self.reads = {}

    def _need(self, eng, tok, deps):
        if tok is None:
            return
        if tok[0] == 'e':
            _, e2, idx = tok
            if e2 == eng:
                if eng == 'pe':
                    return
            key = e2
        else:
            _, i, idx = tok
            key = ('d', i)
        if self.waited[eng].get(key, 0) >= idx:
            return
        if deps.get(key, 0) < idx:
            deps[key] = idx

    def _emit_waits(self, eng, deps):
        for key, idx in deps.items():
            sem = self.csem[key] if isinstance(key, str) else self.dsem[key[1]]
            self.prog[eng].append(('w', sem, idx))
            self.waited[eng][key] = idx

    def _deps(self, eng, reads, writes):
        deps = {}
        for k in reads:
            self._need(eng, self.lastw.get(k), deps)
        for k in writes:
            self._need(eng, self.lastw.get(k), deps)
            for t in self.reads.get(k, ()):
                self._need(eng, t, deps)
        return deps

    def _record(self, tok, reads, writes):
        for k in reads:
            self.reads.setdefault(k, []).append(tok)
        for k in writes:
            self.lastw[k] = tok
            self.reads[k] = []

    def op(self, eng, fn, reads=(), writes=()):
        deps = self._deps(eng, reads, writes)
        self._emit_waits(eng, deps)
        self.count[eng] += 1
        self.prog[eng].append(('o', fn, self.csem[eng], 1))
        tok = ('e', eng, self.count[eng])
        self._record(tok, reads, writes)
        return tok

    def mm(self, fn, reads=(), writes=(), last=True):
        return self.op('pe', fn, reads, writes)

    def dma(self, eng, fn, reads=(), writes=()):
        deps = self._deps(eng, reads, writes)
        if eng == 'pool':
            i = NHWSEM + self.dnext_sw
            self.dnext_sw = (self.dnext_sw + 1) % (NDMASEM - NHWSEM)
        else:
            i = self.dnext
            self.dnext = (self.dnext + 1) % NHWSEM
        if self.dval[i] > 0:
            self._need(eng, ('d', i, self.dval[i]), deps)
        self._emit_waits(eng, deps)
        self.dval[i] += 16
        self.prog[eng].append(('o', fn, self.dsem[i], 16))
        tok = ('d', i, self.dval[i])
        self._record(tok, reads, writes)
        return tok

    def barrier(self):
        for eng in ENGS:
            deps = {}
            for e2 in ENGS:
                if e2 != eng and self.count[e2] > 0:
                    self._need(eng, ('e', e2, self.count[e2]), deps)
            for i in range(NDMASEM):
                if self.dval[i] > 0:
                    self._need(eng, ('d', i, self.dval[i]), deps)
            self._emit_waits(eng, deps)

    def emit(self):
        prog = self.prog

        def run(engobj, items):
            for it in items:
                if it[0] == 'w':
                    engobj.wait_ge(it[1], it[2])
                else:
                    ins = it[1](engobj)
                    if it[2] is not None:
                        ins.then_inc(it[2], it[3])

        with self.nc.Block() as block:
            @block.sync
            def _(e):
                run(e, prog['sp'])

            @block.tensor
            def _(e):
                run(e, prog['pe'])

            @block.vector
            def _(e):
                run(e, prog['dve'])

            @block.scalar
            def _(e):
                run(e, prog['act'])

            @block.gpsimd
            def _(e):
                run(e, prog['pool'])


def build(SEG, debug=False, phases=('A', 'fix', 'att', 'rwkv', 'C'), RW_PAIRS=8, ATT_PAIRS=8):
    NTOK = NSEG * SEG
    NT = NTOK // TT
    TPS = SEG // TT
    SEGA = SEG + 2 * APAD
    SEGZ = SEG + 2
    nc = bass.Bass("TRN2", target_bir_lowering=False)
    es = ExitStack()
    S = Sched(nc, es)

    def din(name, shape, dt=F32):
        return nc.dram_tensor(name, list(shape), dt, kind="ExternalInput").ap()

    def dscr(name, shape, dt):
        if debug:
            return nc.dram_tensor(name, list(shape), dt, kind="ExternalOutput").ap()
        return nc.dram_tensor(name, list(shape), dt).ap()

    x_d = din("x", [NTOK, D])
    link_d = din("link", [128, 1])
    wgu_h = [din("wgu1", [FF, 4096]), din("wgu2", [FF, 4096])]
    wd_h = [din("wd1", [D, FF]), din("wd2", [D, FF])]
    winfm_h = din("winfm", [NFM * 128, 2048])
    wintm_h = din("wintm", [4 * 128, 4096])
    wo_h = din("wo", [D, D])
    gains_d = din("gains", [128, 4 * DC])
    mu_d = din("mu", [128, 2 * ZC])
    rwp_d = din("rwp", [128, 8 * 10])
    w2_d = din("w2", [128, 1024])
    a2_d = din("a2", [128, 1024])
    g2_d = din("g2", [128, 1024])
    etab_d = din("etab", [8, 128, 2 * 3 * 256])
    cst_d = din("cst", [128, 8 * 128])
    rmask_d = din("rmask", [128, 512])
    y_d = nc.dram_tensor("y", [NTOK, D], F32, kind="ExternalOutput").ap()

    wgu_b = [dscr("wgu1b", [FF, 4096], BF16), dscr("wgu2b", [FF, 4096], BF16)]
    wd_b = [dscr("wd1b", [D, FF], BF16), dscr("wd2b", [D, FF], BF16)]
    winfm_b = dscr("winfmb", [NFM * 128, 2048], BF16)
    wintm_b = dscr("wintmb", [4 * 128, 4096], BF16)
    wo_b = dscr("wob", [D, D], BF16)
    x1T_d = dscr("x1T", [D, NTOK], F32)
    qT_d = dscr("qT", [1024, NTOK], BF16)
    kT_d = dscr("kT", [1024, NSEG * SEGA], BF16)
    vatt_d = dscr("vatt", [NSEG * SEGA + 64, 16 * 65], BF16)
    zT_d = dscr("zT", [ZC * 128, NSEG * SEGZ], F32)
    ybT_d = dscr("ybT", [1024, NTOK], F32)
    mixT_d = dscr("mixT", [D, NTOK], BF16)

    def sb(name, shape, dt, stack=es):
        return stack.enter_context(nc.sbuf_tensor("s_" + name, list(shape), dt))

    PS = [es.enter_context(nc.psum_tensor("ps%d" % i, [128, 512], F32)) for i in range(8)]

    def dma(eng, out, in_, reads, writes, **kw):
        return S.dma(eng, lambda e, o=out, i=in_, k=kw: e.dma_start(out=o, in_=i, **k), reads, writes)

    cst = sb("cst", [128, 8 * 128], F32)
    cstb = sb("cstb", [128, 8 * 128], BF16)
    gains = sb("gains", [128, 4 * DC], F32)
    link = sb("link", [128, 1], F32)
    zeros = sb("zeros", [128, 1040], BF16)
    zerosf = sb("zerosf", [128, 128], F32)
    epsn = sb("epsn", [128, 1], F32)
    epsl = sb("epsl", [128, 1], F32)
    onesb = sb("onesb", [128, 128], BF16)
    dma('sp', cst[:], cst_d[:, :], [], ['cst'])
    dma('sp', gains[:], gains_d[:, :], [], ['gains'])
    dma('sp', link[:], link_d[:, :], [], ['link'])
    S.op('dve', lambda e: e.tensor_copy(out=cstb[:], in_=cst[:]), ['cst'], ['cstb'])
    S.op('pool', lambda e: e.memset(zeros[:], 0.0), [], ['zeros'])
    S.op('pool', lambda e: e.memset(zerosf[:], 0.0), [], ['zerosf'])
    S.op('pool', lambda e: e.memset(epsn[:], NORM_EPS), [], ['epsn'])
    S.op('pool', lambda e: e.memset(epsl[:], LN_EPS), [], ['epsl'])
    S.op('pool', lambda e: e.memset(onesb[:], 1.0), [], ['onesb'])
    ident = cst[:, 0:128]
    identb = cstb[:, 0:128]
    bones = cst[:, 128:256]
    m_su = cst[:, 256:384]
    m_sl = cst[:, 384:512]
    m_iu = cst[:, 512:640]
    m_il = cst[:, 640:768]

    def cast_rows(src, dst, r0, r1, key):
        dma('pool', dst[r0:r1, :], src[r0:r1, :], [], [key], max_dma_last_dim=4096)

    for j in range(0, FC, 2):
        cast_rows(wgu_h[0], wgu_b[0], j * 128, (j + 2) * 128, ('wgu0', j // 2))
    for m in range(DC):
        cast_rows(wd_h[0], wd_b[0], m * 128, (m + 1) * 128, ('wd0', m))
    for c in range(0, NFM, 4):
        cast_rows(winfm_h, winfm_b, c * 128, min(NFM, c + 4) * 128, ('winfm', c // 4))
    for g in range(0, 4, 2):
        cast_rows(wintm_h, wintm_b, g * 128, (g + 2) * 128, ('wintm', g // 2))
    for m in range(0, DC, 4):
        cast_rows(wo_h, wo_b, m * 128, (m + 4) * 128, ('wo', m // 4))
    for j in range(0, FC, 2):
        cast_rows(wgu_h[1], wgu_b[1], j * 128, (j + 2) * 128, ('wgu1', j // 2))
    for m in range(DC):
        cast_rows(wd_h[1], wd_b[1], m * 128, (m + 1) * 128, ('wd1', m))

    for s in range(NSEG):
        for off in (0, APAD + SEG):
            for c in range(8):
                dma('act', kT_d[c * 128:(c + 1) * 128, s * SEGA + off: s * SEGA + off + APAD], zeros[:, 0:APAD],
                    ['zeros'], [('kTpad', s, off)])
            for a in range(APAD // 128):
                r0 = s * SEGA + off + a * 128
                dma('act', vatt_d[r0:r0 + 128, :], zeros[:, 0:1040], ['zeros'], [('vapad', s, off)])
        for off in (0, SEG + 1):
            dma('act', zT_d[:, s * SEGZ + off: s * SEGZ + off + 1].rearrange("(c p) o -> p c o", p=128),
                zerosf[:, 0:ZC].unsqueeze(2), ['zerosf'], [('zpad', s, off)], allow_slow_non_contiguous=True)

    _sb_outer = sb

    def ffn_env(pa, tag):
        def sb(name, shape, dt, stack=es):
            return _sb_outer(name + tag, shape, dt, stack)
        xT = sb("xT", [128, DC, TT], F32, pa)
        nT = sb("nT", [128, DC, TT], BF16, pa)
        actT = sb("actT", [128, FC, TT], BF16, pa)
        wring = [sb("wring%d" % i, [128, 4096], BF16, pa) for i in range(3)]
        dring = [sb("dring%d" % i, [128, 22 * 128], BF16, pa) for i in range(3)]
        xtok = [sb("xtok%d" % i, [128, 1024], F32, pa) for i in range(2)]
        sq = [sb("sq%d" % i, [128, TT], BF16, pa) for i in range(2)]
        rstd = sb("rstd", [128, TT], F32, pa)
        sg = [sb("sg%d" % i, [128, TT], F32, pa) for i in range(2)]
        stg = [sb("stg%d" % i, [128, TT], F32, pa) for i in range(3)]
        stgb = [sb("stgb%d" % i, [128, TT], BF16, pa) for i in range(2)]
        vstg = [sb("vstg%d" % i, [128, 4 * 65], BF16, pa) for i in range(2)]
        cnt = {'w': 0, 'd': 0, 'x': 0, 'sq': 0, 'sg': 0, 'stg': 0, 'stgb': 0, 'tp': 0, 'vs': 0, 'ev': 0}
        xT_keys = [('xT', dc) for dc in range(DC)]

        def evac_copy(out, in_, reads, writes):
            cnt['ev'] += 1
            if cnt['ev'] % 2:
                S.op('act', lambda e, o=out, i=in_: e.activation(out=o, in_=i, func=AF.Copy), reads, writes)
            else:
                S.op('dve', lambda e, o=out, i=in_: e.tensor_copy(out=o, in_=i), reads, writes)

        def load_x_tile(it):
            for s in range(4):
                for hf in range(2):
                    xb_ = cnt['x'] % 2
                    cnt['x'] += 1
                    r0 = it * TT + s * 128
                    dma('sp', xtok[xb_][:], x_d[r0:r0 + 128, hf * 1024:(hf + 1) * 1024], [], [('xtok', xb_)])
                    for g in range(2):
                        bank = 6 + cnt['tp'] % 2
                        cnt['tp'] += 1
                        for q in range(4):
                            S.mm(lambda e, b=bank, q=q, g=g, xb_=xb_: e.transpose(PS[b][:, q * 128:(q + 1) * 128],
                                                                                 xtok[xb_][:, (g * 4 + q) * 128:(g * 4 + q + 1) * 128], ident),
                                 [('xtok', xb_), 'cst'], [('ps', bank)], last=(q == 3))
                        dc0 = hf * 8 + g * 4
                        evac_copy(xT[:, dc0:dc0 + 4, s * 128:(s + 1) * 128], PS[bank][:].rearrange("p (q t) -> p q t", q=4),
                                  [('ps', bank)], [('xT', dc0 + q) for q in range(4)])

        def store_y_tile(it, gi):
            rms_stats()
            for dc in range(DC):
                if dc % 2 == 0:
                    S.op('dve', lambda e, dc=dc: e.scalar_tensor_tensor(out=xT[:, dc, :], in0=xT[:, dc, :], scalar=gains[:, gi * DC + dc: gi * DC + dc + 1],
                                                                        in1=rstd[:], op0=ALU.mult, op1=ALU.mult), [('xT', dc), 'rstd', 'gains'], [('xT', dc)])
                else:
                    S.op('pool', lambda e, dc=dc: e.tensor_scalar(out=xT[:, dc, :], in0=xT[:, dc, :], scalar1=gains[:, gi * DC + dc: gi * DC + dc + 1], scalar2=0.0, op0=ALU.mult, op1=ALU.add),
                         [('xT', dc), 'gains'], [('xT', dc)])
                    S.op('pool', lambda e, dc=dc: e.tensor_tensor(out=xT[:, dc, :], in0=xT[:, dc, :], in1=rstd[:], op=ALU.mult), [('xT', dc), 'rstd'], [('xT', dc)])
            for s in range(4):
                for hf in range(2):
                    xb_ = cnt['x'] % 2
                    cnt['x'] += 1
                    for g in range(2):
                        bank = 6 + cnt['tp'] % 2
                        cnt['tp'] += 1
                        for q in range(4):
                            dc = hf * 8 + g * 4 + q
                            S.mm(lambda e, b=bank, q=q, dc=dc, s=s: e.transpose(PS[b][:, q * 128:(q + 1) * 128], xT[:, dc, s * 128:(s + 1) * 128], ident),
                                 [('xT', dc), 'cst'], [('ps', bank)], last=(q == 3))
                        evac_copy(xtok[xb_][:, g * 512:(g + 1) * 512], PS[bank][:], [('ps', bank)], [('xtok', xb_)])
                    r0 = it * TT + s * 128
                    dma('sp', y_d[r0:r0 + 128, hf * 1024:(hf + 1) * 1024], xtok[xb_][:], [('xtok', xb_)], ['y'])

        def rms_stats():
            for dc in range(DC):
                k = cnt['sq'] % 2
                cnt['sq'] += 1
                S.op('act', lambda e, k=k, dc=dc: e.activation(out=sq[k][:], in_=xT[:, dc, :], func=AF.Square), [('xT', dc)], [('sq', k)])
                S.mm(lambda e, k=k, dc=dc: e.matmul(PS[4][:], onesb[:], sq[k][:], start=(dc == 0), stop=(dc == DC - 1)),
                     [('sq', k), 'onesb'], [('ps', 4)], last=(dc == DC - 1))
            S.op('act', lambda e: e.activation(out=rstd[:], in_=PS[4][:], func=AF.Sqrt, scale=1.0 / D, bias=epsn[:]), [('ps', 4), 'epsn'], ['rstd'])
            S.op('dve', lambda e: e.reciprocal(out=rstd[:], in_=rstd[:]), ['rstd'], ['rstd'])

        def rmsnorm_to_nT(gi):
            rms_stats()
            for dc in range(DC):
                if dc % 2 == 0:
                    S.op('dve', lambda e, dc=dc: e.scalar_tensor_tensor(out=nT[:, dc, :], in0=xT[:, dc, :], scalar=gains[:, gi * DC + dc: gi * DC + dc + 1],
                                                                        in1=rstd[:], op0=ALU.mult, op1=ALU.mult), [('xT', dc), 'rstd', 'gains'], [('nT', dc)])
                else:
                    k = cnt['stg'] % 3
                    cnt['stg'] += 1
                    S.op('pool', lambda e, dc=dc, k=k: e.tensor_scalar(out=stg[k][:], in0=xT[:, dc, :], scalar1=gains[:, gi * DC + dc: gi * DC + dc + 1], scalar2=0.0, op0=ALU.mult, op1=ALU.add),
                         [('xT', dc), 'gains'], [('stg', k)])
                    S.op('pool', lambda e, dc=dc, k=k: e.tensor_tensor(out=nT[:, dc, :], in0=stg[k][:], in1=rstd[:], op=ALU.mult), [('stg', k), 'rstd'], [('nT', dc)])

        def ffn(fi):
            for j in range(FC):
                w = cnt['w'] % 3
                cnt['w'] += 1
                dma('sp', wring[w][:], wgu_b[fi][j * 128:(j + 1) * 128, :], [('wgu%d' % fi, j // 2)], [('wring', w)])
                bg = j % 2
                bu = 2 + j % 2
                for kc in range(DC):
                    S.mm(lambda e, w=w, kc=kc, bg=bg: e.matmul(PS[bg][:], wring[w][:, kc * 128:(kc + 1) * 128], nT[:, kc, :], start=(kc == 0), stop=(kc == DC - 1)),
                         [('wring', w), ('nT', kc)], [('ps', bg)], last=(kc == DC - 1))
                for kc in range(DC):
                    S.mm(lambda e, w=w, kc=kc, bu=bu: e.matmul(PS[bu][:], wring[w][:, 2048 + kc * 128: 2048 + (kc + 1) * 128], nT[:, kc, :], start=(kc == 0), stop=(kc == DC - 1)),
                         [('wring', w), ('nT', kc)], [('ps', bu)], last=(kc == DC - 1))
                k = cnt['sg'] % 2
                cnt['sg'] += 1
                S.op('act', lambda e, k=k, bg=bg: e.activation(out=sg[k][:], in_=PS[bg][:], func=AF.Silu), [('ps', bg)], [('sg', k)])
                S.op('dve', lambda e, k=k, bu=bu, j=j: e.tensor_tensor(out=actT[:, j, :], in0=sg[k][:], in1=PS[bu][:], op=ALU.mult), [('sg', k), ('ps', bu)], [('actT', j)])
            for m in range(DC):
                bd_ = 4 + m % 2
                for hf in range(2):
                    dd = cnt['d'] % 3
                    cnt['d'] += 1
                    dma('sp', dring[dd][:], wd_b[fi][m * 128:(m + 1) * 128, hf * 2816:(hf + 1) * 2816], [('wd%d' % fi, m)], [('dring', dd)])
                    for f2 in range(22):
                        fc = hf * 22 + f2
                        S.mm(lambda e, dd=dd, f2=f2, fc=fc, bd_=bd_: e.matmul(PS[bd_][:], dring[dd][:, f2 * 128:(f2 + 1) * 128], actT[:, fc, :], start=(fc == 0), stop=(fc == FC - 1)),
                             [('dring', dd), ('actT', fc)], [('ps', bd_)], last=(fc == FC - 1))
                S.op('dve', lambda e, m=m, bd_=bd_: e.scalar_tensor_tensor(out=xT[:, m, :], in0=PS[bd_][:], scalar=0.5, in1=xT[:, m, :], op0=ALU.mult, op1=ALU.add),
                     [('ps', bd_), ('xT', m)], [('xT', m)])

        fm_dest = [('q', c) for c in range(8)] + [('k', c) for c in range(8)] + [('z', c) for c in range(ZC)]

        def proj(it):
            seg = it // TPS
            t0 = (it % TPS) * TT
            for ci, (kind, c) in enumerate(fm_dest):
                w = cnt['w'] % 3
                cnt['w'] += 1
                dma('sp', wring[w][:, 0:2048], winfm_b[ci * 128:(ci + 1) * 128, :], [('winfm', ci // 4)], [('wring', w)])
                bank = ci % 2
                for kc in range(DC):
                    S.mm(lambda e, w=w, kc=kc, bank=bank: e.matmul(PS[bank][:], wring[w][:, kc * 128:(kc + 1) * 128], nT[:, kc, :], start=(kc == 0), stop=(kc == DC - 1)),
                         [('wring', w), ('nT', kc)], [('ps', bank)], last=(kc == DC - 1))
                if kind in ('q', 'k'):
                    k = cnt['stgb'] % 2
                    cnt['stgb'] += 1
                    evac_copy(stgb[k][:], PS[bank][:], [('ps', bank)], [('stgb', k)])
                    if kind == 'q':
                        dma('act', qT_d[c * 128:(c + 1) * 128, it * TT:(it + 1) * TT], stgb[k][:], [('stgb', k)], [('qT', seg)])
                    else:
                        col = seg * SEGA + APAD + t0
                        dma('act', kT_d[c * 128:(c + 1) * 128, col:col + TT], stgb[k][:], [('stgb', k)], [('kT', seg)])
                else:
                    k = cnt['stg'] % 3
                    cnt['stg'] += 1
                    evac_copy(stg[k][:], PS[bank][:], [('ps', bank)], [('stg', k)])
                    col = seg * SEGZ + 1 + t0
                    dma('act', zT_d[c * 128:(c + 1) * 128, col:col + TT], stg[k][:], [('stg', k)], [('zT', seg)])
            for g in range(4):
                w = cnt['w'] % 3
                cnt['w'] += 1
                dma('sp', wring[w][:], wintm_b[g * 128:(g + 1) * 128, :], [('wintm', g // 2)], [('wring', w)])
                for s in range(4):
                    bank = 2 + (g * 4 + s) % 2
                    for kc in range(DC):
                        S.mm(lambda e, w=w, kc=kc, bank=bank, s=s: e.matmul(PS[bank][:, 0:256], nT[:, kc, s * 128:(s + 1) * 128], wring[w][:, kc * 256:(kc + 1) * 256],
                                                                             start=(kc == 0), stop=(kc == DC - 1)),
                             [('wring', w), ('nT', kc)], [('ps', bank)], last=(kc == DC - 1))
                    if True:
                        k = cnt['vs'] % 2
                        cnt['vs'] += 1
                        S.op('pool', lambda e, k=k: e.memset(vstg[k][:], 1.0), [], [('vstg', k)])
                        evac_copy(vstg[k][:].rearrange("p (h c) -> p h c", h=4)[:, :, 0:64], PS[bank][:, 0:256].rearrange("p (h c) -> p h c", h=4),
                                  [('ps', bank), ('vstg', k)], [('vstg', k)])
                        row = seg * SEGA + APAD + t0 + s * 128
                        dma('act', vatt_d[row:row + 128, g * 260:(g + 1) * 260], vstg[k][:], [('vstg', k)], [('vatt', seg)])

        def wout(it):
            dma('sp', nT[:], mixT_d[:, it * TT:(it + 1) * TT].rearrange("(c p) t -> p c t", p=128), [('mixT', it // TPS)], [('nT', dc) for dc in range(DC)])
            dma('sp', xT[:], x1T_d[:, it * TT:(it + 1) * TT].rearrange("(c p) t -> p c t", p=128), [('x1T', it)], xT_keys)
            for m in range(DC):
                w = cnt['w'] % 3
                cnt['w'] += 1
                dma('sp', wring[w][:, 0:2048], wo_b[m * 128:(m + 1) * 128, :], [('wo', m // 4)], [('wring', w)])
                bank = m % 2
                for kc in range(DC):
                    S.mm(lambda e, w=w, kc=kc, bank=bank: e.matmul(PS[bank][:], wring[w][:, kc * 128:(kc + 1) * 128], nT[:, kc, :], start=(kc == 0), stop=(kc == DC - 1)),
                         [('wring', w), ('nT', kc)], [('ps', bank)], last=(kc == DC - 1))
                S.op('dve', lambda e, m=m, bank=bank: e.tensor_tensor(out=xT[:, m, :], in0=xT[:, m, :], in1=PS[bank][:], op=ALU.add), [('ps', bank), ('xT', m)], [('xT', m)])

        return dict(load_x_tile=load_x_tile, rmsnorm_to_nT=rmsnorm_to_nT, ffn=ffn, proj=proj, wout=wout, store_y_tile=store_y_tile, xT=xT, xT_keys=xT_keys)

    def phase_A():
        pa = ExitStack()
        env = ffn_env(pa, 'A')
        for it in range(NT):
            env['load_x_tile'](it)
            env['rmsnorm_to_nT'](0)
            env['ffn'](0)
            dma('act', x1T_d[:, it * TT:(it + 1) * TT].rearrange("(c p) t -> p c t", p=128), env['xT'][:], env['xT_keys'], [('x1T', it)])
            env['rmsnorm_to_nT'](1)
            env['proj'](it)
        S.barrier()
        pa.close()


    if 'A' in phases:
        phase_A()
    def phase_fix():
        pf = ExitStack()
        fk = sb("fk", [128, APAD], BF16, pf)
        fv = sb("fv", [128, APAD // 128, 1040], BF16, pf)
        fz = sb("fz", [128, ZC], F32, pf)
        for (ds, doff, ss, soff) in ((0, APAD + SEG, 1, APAD), (1, 0, 0, SEG)):
            for c in range(8):
                dma('sp', fk[:], kT_d[c * 128:(c + 1) * 128, ss * SEGA + soff: ss * SEGA + soff + APAD], [('kT', ss)], ['fk'])
                S.op('dve', lambda e: e.tensor_scalar(out=fk[:], in0=fk[:], scalar1=link[:, 0:1], scalar2=None, op0=ALU.mult), ['fk', 'link'], ['fk'])
                dma('sp', kT_d[c * 128:(c + 1) * 128, ds * SEGA + doff: ds * SEGA + doff + APAD], fk[:], ['fk', ('kTpad', ds, doff)], [('kTpad', ds, doff)])
            dma('sp', fv[:], vatt_d[ss * SEGA + soff: ss * SEGA + soff + APAD, :].rearrange("(a p) c -> p a c", p=128), [('vatt', ss)], ['fv'])
            S.op('dve', lambda e: e.tensor_scalar(out=fv[:], in0=fv[:], scalar1=link[:, 0:1], scalar2=None, op0=ALU.mult), ['fv', 'link'], ['fv'])
            dma('sp', vatt_d[ds * SEGA + doff: ds * SEGA + doff + APAD, :].rearrange("(a p) c -> p a c", p=128), fv[:], ['fv', ('vapad', ds, doff)], [('vapad', ds, doff)])
        for (ds, doff, ss, soff) in ((0, SEG + 1, 1, 1), (1, 0, 0, SEG)):
            dma('sp', fz[:].unsqueeze(2), zT_d[:, ss * SEGZ + soff: ss * SEGZ + soff + 1].rearrange("(c p) o -> p c o", p=128), [('zT', ss)], ['fz'], allow_slow_non_contiguous=True)
            S.op('dve', lambda e: e.tensor_scalar(out=fz[:], in0=fz[:], scalar1=link[:, 0:1], scalar2=None, op0=ALU.mult), ['fz', 'link'], ['fz'])
            dma('sp', zT_d[:, ds * SEGZ + doff: ds * SEGZ + doff + 1].rearrange("(c p) o -> p c o", p=128), fz[:].unsqueeze(2), ['fz', ('zpad', ds, doff)], [('zpad', ds, doff)], allow_slow_non_contiguous=True)
        S.barrier()
        pf.close()

    if 'fix' in phases:
        phase_fix()
    def phase_att():
        pt = ExitStack()
        NKT = {1: SEG // 128 + 1, 4: SEG // 512 + 1, 16: SEG // 2048 + 1}
        qt = sb("qt", [128, SEG], BF16, pt)
        kt = sb("kt", [128, SEGA], BF16, pt)
        vres = {d: sb("vres%d" % d, [128, d * NKT[d], 130], BF16, pt) for d in (1, 4, 16)}
        acc = [sb("acc%d" % h, [65, SEG], F32, pt) for h in range(2)]
        etf = sb("etf", [128, 1536], F32, pt)
        etb = sb("etb", [128, 1536], BF16, pt)
        pex = [sb("pex%d" % i, [128, 256], BF16, pt) for i in range(8)]
        pm = [sb("pm%d" % i, [128, 256], BF16, pt) for i in range(8)]
        rden = sb("rden", [64, 512], F32, pt)
        ostg = [sb("ostg%d" % i, [64, 512], BF16, pt) for i in range(2)]
        sel = sb("sel", [65, 64], F32, pt)
        S.op('pool', lambda e: e.memset(sel[:], 0.0), [], ['sel'])
        S.op('pool', lambda e: e.memset(sel[64:65, :], 1.0), ['sel'], ['sel'])
        ac = {'u': 0, 'o': 0, 'os': 0}

        def emit_pv(pi, d, r, b, h):
            ob = 4 + ac['o'] % 2
            ac['o'] += 1
            for j in range(2):
                S.mm(lambda e, ob=ob, j=j, pi=pi, d=d, r=r, b=b, h=h: e.matmul(PS[ob][0:65, 0:128], vres[d][:, r * NKT[d] + b + j, h * 65:(h + 1) * 65],
                                                                               pm[pi][:, j * 128:(j + 1) * 128], start=(j == 0), stop=(j == 1)),
                     [('vres', d), ('pm', pi)], [('ps', ob)])
            asl = acc[h][:, r + d * 128 * b: r + d * 128 * b + d * 127 + 1: d]
            if d == 1:
                S.op('dve', lambda e, asl=asl, ob=ob: e.tensor_copy(out=asl, in_=PS[ob][0:65, 0:128]), [('ps', ob)], [('acc', h)])
            else:
                S.op('dve', lambda e, asl=asl, ob=ob: e.tensor_tensor(out=asl, in0=asl, in1=PS[ob][0:65, 0:128], op=ALU.add), [('ps', ob), ('acc', h)], [('acc', h)])

        pend = []
        for seg in range(NSEG):
            for hp in range(ATT_PAIRS):
                dma('sp', qt[:], qT_d[hp * 128:(hp + 1) * 128, seg * SEG:(seg + 1) * SEG], [('qT', seg)], ['qt'])
                dma('sp', kt[:], kT_d[hp * 128:(hp + 1) * 128, seg * SEGA:(seg + 1) * SEGA], [('kT', seg)] + [('kTpad', seg, o) for o in (0, APAD + SEG)], ['kt'])
                dma('sp', etf[:], etab_d[hp], [], ['etf'])
                S.op('pool', lambda e: e.tensor_copy(out=etb[:], in_=etf[:]), ['etf'], ['etb'])
                for d in (1, 4, 16):
                    for r in range(d):
                        base = seg * SEGA + APAD + r - 64 * d
                        src = vatt_d[base: base + d * 128 * NKT[d], hp * 130:(hp + 1) * 130].rearrange("(k p dd) c -> dd p k c", p=128, dd=d)[0]
                        dma('act', vres[d][:, r * NKT[d]:(r + 1) * NKT[d], :], src, [('vatt', seg)] + [('vapad', seg, o) for o in (0, APAD + SEG)], [('vres', d)])
                for di, d in enumerate((1, 4, 16)):
                    L = SEG // d
                    for r in range(d):
                        for b in range(L // 128):
                            for h in range(2):
                                u = ac['u']
                                ac['u'] += 1
                                sbank = u % 4
                                qs = qt[h * 64:(h + 1) * 64, r + d * 128 * b: r + d * 128 * b + d * 127 + 1: d]
                                for j in range(2):
                                    k0 = APAD + r + d * (128 * b - 64 + 128 * j)
                                    ks = kt[h * 64:(h + 1) * 64, k0: k0 + d * 127 + 1: d]
                                    S.mm(lambda e, sbank=sbank, j=j, ks=ks, qs=qs: e.matmul(PS[sbank][:, j * 128:(j + 1) * 128], ks, qs, start=True, stop=True),
                                         ['kt', 'qt'], [('ps', sbank)])
                                pi = u % 8
                                S.op('act', lambda e, pi=pi, sbank=sbank: e.activation(out=pex[pi][:], in_=PS[sbank][:, 0:256], func=AF.Exp, scale=0.125),
                                     [('ps', sbank)], [('pex', pi)])
                                eo = (h * 3 + di) * 256
                                S.op('pool' if u % 3 != 2 else 'dve', lambda e, pi=pi, eo=eo: e.tensor_tensor(out=pm[pi][:], in0=pex[pi][:], in1=etb[:, eo:eo + 256], op=ALU.mult),
                                     [('pex', pi), 'etb'], [('pm', pi)])
                                pend.append((pi, d, r, b, h))
                                if len(pend) > 4:
                                    emit_pv(*pend.pop(0))
                while pend:
                    emit_pv(*pend.pop(0))
                for h in range(2):
                    for t in range(SEG // 512):
                        S.mm(lambda e, h=h, t=t: e.matmul(PS[6][0:64, :], sel[:], acc[h][:, t * 512:(t + 1) * 512], start=True, stop=True), ['sel', ('acc', h)], [('ps', 6)])
                        S.op('dve', lambda e: e.reciprocal(out=rden[:], in_=PS[6][0:64, :]), [('ps', 6)], ['rden'])
                        k = ac['os'] % 2
                        ac['os'] += 1
                        S.op('dve', lambda e, k=k, h=h, t=t: e.tensor_tensor(out=ostg[k][:], in0=acc[h][0:64, t * 512:(t + 1) * 512], in1=rden[:], op=ALU.mult),
                             [('acc', h), 'rden'], [('ostg', k)])
                        row = hp * 128 + h * 64
                        dma('sp', mixT_d[row:row + 64, seg * SEG + t * 512: seg * SEG + (t + 1) * 512], ostg[k][:], [('ostg', k)], [('mixT', seg)])
        S.barrier()
        pt.close()


    if 'att' in phases:
        phase_att()
    def phase_rwkv():
        pr = ExitStack()
        twd = sb("twd", [128, SEG], BF16, pr)
        sad = sb("sad", [128, SEG], BF16, pr)
        sgd = sb("sgd", [128, SEG], BF16, pr)
        mu = sb("mu", [128, 2 * ZC], F32, pr)
        muc = sb("muc", [128, ZC], F32, pr)
        rwp = sb("rwp", [128, 80], F32, pr)
        lrf = sb("lrf", [128, 1024], F32, pr)
        w2b_ = sb("w2b", [128, 1024], BF16, pr)
        a2b_ = sb("a2b", [128, 1024], BF16, pr)
        g2b_ = sb("g2b", [128, 1024], BF16, pr)
        rmask = sb("rmask", [128, 512], F32, pr)
        cmask = [sb("cmask%d" % i, [128, 192], F32, pr) for i in range(2)]
        mSxb = [sb("mSxb%d" % i, [128, 128], BF16, pr) for i in range(2)]
        zin = [sb("zin%d" % i, [128, TT + 2], F32, pr) for i in range(3)]
        NW = 30
        wk = [sb("wk%d" % i, [128, TT], F32, pr) for i in range(NW)]
        bdt = {n: [sb("bd_%s%d" % (n, i), [128, 8, 192 if n == "At" else 128], BF16, pr) for i in range(2)] for n in ("At", "Bt", "Kt", "Vt")}
        gam = [sb("gam%d" % i, [128, TT], F32, pr) for i in range(2)]
        vTs = [sb("vTs%d" % i, [128, TT], F32, pr) for i in range(2)]
        bsum = [sb("bsum%d" % i, [128, TT], F32, pr) for i in range(2)]
        NU = 8
        ub = {n: [sb("u_%s%d" % (n, i), [128, 320 if n == "XM" else 256], BF16, pr) for i in range(NU)] for n in ("XM", "LM", "Z", "G")}
        ub1 = {n: [sb("u1_%s%d" % (n, i), [128, 128], BF16, pr) for i in range(NU)] for n in ("X", "Tt", "Kk", "V", "PpT", "RbT")}
        pw = [[sb("pw%d_%d" % (u, i), [128, 128], BF16, pr) for i in range(4)] for u in range(NU)]
        qg = [sb("qg%d" % i, [128, 128], F32, pr) for i in range(NU)]
        Hf = sb("Hf", [128, 128], F32, pr)
        Hb = [[sb("Hb%d_%d" % (c_, i), [128, 128], BF16, pr) for i in range(2)] for c_ in range(8)]
        ysb = [sb("ysb%d" % i, [128, TT], F32, pr) for i in range(2)]
        ybl = sb("ybl", [128, TT], F32, pr)
        ostr = [sb("ostr%d" % i, [128, TT], BF16, pr) for i in range(2)]
        for n in bdt:
            for i in range(2):
                S.op('pool', lambda e, n=n, i=i: e.memset(bdt[n][i][:], 0.0), [], [('bd', n, i)])
        dma('sp', mu[:], mu_d[:, :], [], ['mu'])
        dma('sp', rwp[:], rwp_d[:, :], [], ['rwp'])
        dma('sp', rmask[:], rmask_d[:, :], [], ['rmask'])
        for (b_, d_) in ((w2b_, w2_d), (a2b_, a2_d), (g2b_, g2_d)):
            dma('sp', lrf[:], d_[:, :], [], ['lrf'])
            S.op('dve', lambda e, b_=b_: e.tensor_copy(out=b_[:], in_=lrf[:]), ['lrf'], ['lrw'])
        S.op('dve', lambda e: e.tensor_tensor(out=muc[:], in0=mu[:, 0:ZC], in1=mu[:, ZC:2 * ZC], op=ALU.add), ['mu'], ['muc'])
        S.op('dve', lambda e: e.tensor_scalar(out=muc[:], in0=muc[:], scalar1=-1.0, scalar2=1.0, op0=ALU.mult, op1=ALU.add), ['muc'], ['muc'])
        for c in range(8):
            S.op('dve', lambda e, c=c: e.tensor_scalar(out=rwp[:, c * 10 + 6:c * 10 + 7], in0=rwp[:, c * 10 + 5:c * 10 + 6], scalar1=-1.0, scalar2=1.0, op0=ALU.mult, op1=ALU.add), ['rwp'], ['rwp'])
        for dirn in range(2):
            mI_ = m_iu if dirn == 0 else m_il
            mSx_ = m_sl if dirn == 0 else m_su
            mS_ = m_su if dirn == 0 else m_sl
            S.op('dve', lambda e, dirn=dirn, mS_=mS_: e.tensor_copy(out=cmask[dirn][:, 0:128], in_=mS_), ['cst'], ['cmask'])
            S.op('dve', lambda e, dirn=dirn, mI_=mI_: e.tensor_tensor(out=cmask[dirn][:, 128:192], in0=mI_[:, 0:64], in1=mI_[:, 64:128], op=ALU.add), ['cst', 'cmask'], ['cmask'])
            S.op('dve', lambda e, dirn=dirn, mSx_=mSx_: e.tensor_copy(out=mSxb[dirn][:], in_=mSx_), ['cst'], ['mSxb'])
        rc = {'z': 0, 'wk': 0, 'ps': 0, 'pf': 0, 'os': 0}

        def newwk():
            i = rc['wk'] % NW
            rc['wk'] += 1
            return wk[i], ('wk', i)

        def psreg():
            i = rc['ps'] % 6
            rc['ps'] += 1
            return PS[i][:, 0:256], ('psr', i)

        def psfull():
            i = 6 + rc['pf'] % 2
            rc['pf'] += 1
            return PS[i], ('psf', i)

        def load_shift(zc, seg, t0, dst=None, dkey=None):
            if dst is None:
                o, ok = newwk()
            else:
                o, ok = dst, dkey
            i = rc['z'] % 3
            rc['z'] += 1
            col = seg * SEGZ + t0
            dma('sp', zin[i][:], zT_d[zc * 128:(zc + 1) * 128, col:col + TT + 2], [('zT', seg)] + [('zpad', seg, o_) for o_ in (0, SEG + 1)], [('zin', i)])
            t1, t1k = newwk()
            S.op('pool', lambda e, i=i, o=o: e.tensor_scalar(out=o[:], in0=zin[i][:, 1:TT + 1], scalar1=muc[:, zc:zc + 1], scalar2=0.0, op0=ALU.mult, op1=ALU.add), [('zin', i), 'muc'], [ok])
            S.op('pool', lambda e, i=i, t1=t1: e.tensor_scalar(out=t1[:], in0=zin[i][:, 0:TT], scalar1=mu[:, zc:zc + 1], scalar2=0.0, op0=ALU.mult, op1=ALU.add), [('zin', i), 'mu'], [t1k])
            S.op('pool', lambda e, o=o, t1=t1: e.tensor_tensor(out=o[:], in0=o[:], in1=t1[:], op=ALU.add), [ok, t1k], [ok])
            S.op('pool', lambda e, i=i, t1=t1: e.tensor_scalar(out=t1[:], in0=zin[i][:, 2:TT + 2], scalar1=mu[:, ZC + zc:ZC + zc + 1], scalar2=0.0, op0=ALU.mult, op1=ALU.add), [('zin', i), 'mu', t1k], [t1k])
            S.op('pool', lambda e, o=o, t1=t1: e.tensor_tensor(out=o[:], in0=o[:], in1=t1[:], op=ALU.add), [ok, t1k], [ok])
            return o, ok

        def lowrank_prep(seg):
            for t in range(TPS):
                t0 = t * TT
                for (zc, dst, fn) in ((24, twd, AF.Tanh), (25, sad, AF.Copy), (26, sgd, AF.Sigmoid)):
                    o, ok = load_shift(zc, seg, t0)
                    S.op('act', lambda e, o=o, dst=dst, fn=fn, t0=t0: e.activation(out=dst[:, t0:t0 + TT], in_=o[:], func=fn), [ok], ['lr'])

        def sig_lowrank(wb, rows, src, c, t0, bias_ap):
            p, pk = psfull()
            r0, r1 = rows
            S.mm(lambda e, p=p: e.matmul(p[:], wb[r0:r1, c * 128:(c + 1) * 128], src[r0:r1, t0:t0 + TT], start=True, stop=True), ['lrw', 'lr'], [pk])
            o, ok = newwk()
            S.op('act', lambda e, p=p, o=o: e.activation(out=o[:], in_=p[:], func=AF.Sigmoid, bias=bias_ap, scale=1.0), [pk, 'rwp'], [ok])
            return o, ok

        def prep_tile(dirn, c, seg, t, bi):
            t0 = t * TT
            fwd = (dirn == 0)
            P = lambda j: rwp[:, c * 10 + j: c * 10 + j + 1]
            rs, rsk = load_shift(c, seg, t0)
            yield
            ks, ksk = load_shift(8 + c, seg, t0)
            yield
            load_shift(16 + c, seg, t0, vTs[bi], ('vTs', bi))
            yield
            for h in range(2):
                S.op('pool', lambda e, h=h: e.tensor_copy(out=bdt["Vt"][bi][h * 64:(h + 1) * 64, :, h * 64:(h + 1) * 64],
                                                          in_=vTs[bi][h * 64:(h + 1) * 64, :].rearrange("p (n t) -> p n t", t=64)),
                     [('vTs', bi)], [('bd', "Vt", bi)])
            kkr, kkrk = newwk()
            S.op('pool', lambda e: e.tensor_scalar(out=kkr[:], in0=ks[:], scalar1=P(4), scalar2=0.0, op0=ALU.mult, op1=ALU.add), [ksk, 'rwp'], [kkrk])
            sq_, sqk = newwk()
            S.op('act', lambda e: e.activation(out=sq_[:], in_=kkr[:], func=AF.Square), [kkrk], [sqk])
            p, pk = psfull()
            S.mm(lambda e, p=p: e.matmul(p[:], bones, sq_[:], start=True, stop=True), ['cst', sqk], [pk])
            rn, rnk = newwk()
            S.op('act', lambda e, p=p: e.activation(out=rn[:], in_=p[:], func=AF.Sqrt), [pk], [rnk])
            yield
            S.op('dve', lambda e: e.tensor_scalar(out=rn[:], in0=rn[:], scalar1=1e-12, scalar2=None, op0=ALU.max), [rnk], [rnk])
            S.op('dve', lambda e: e.reciprocal(out=rn[:], in_=rn[:]), [rnk], [rnk])
            kkn, kknk = newwk()
            S.op('dve', lambda e: e.scalar_tensor_tensor(out=kkn[:], in0=kkr[:], scalar=-1.0, in1=rn[:], op0=ALU.mult, op1=ALU.mult), [kkrk, rnk], [kknk])
            yield
            rows = (0, 64) if fwd else (64, 128)
            sgw, sgwk = sig_lowrank(w2b_, rows, twd, c, t0, P(0 if fwd else 1))
            ag, agk = sig_lowrank(a2b_, rows, sad, c, t0, P(2 if fwd else 3))
            yield
            kp_, kpk = newwk()
            S.op('pool', lambda e: e.tensor_scalar(out=kp_[:], in0=ag[:], scalar1=P(5), scalar2=P(6), op0=ALU.mult, op1=ALU.add), [agk, 'rwp'], [kpk])
            S.op('pool', lambda e: e.tensor_tensor(out=kp_[:], in0=kp_[:], in1=ks[:], op=ALU.mult), [kpk, ksk], [kpk])
            bb, bbk = newwk()
            S.op('dve', lambda e: e.scalar_tensor_tensor(out=bb[:], in0=kkn[:], scalar=-1.0, in1=ag[:], op0=ALU.mult, op1=ALU.mult), [kknk, agk], [bbk])
            yield
            cs, csk = newwk()
            S.op('dve', lambda e: e.tensor_tensor_scan(out=cs[:], data0=rmask[:], data1=sgw[:], initial=0.0, op0=ALU.mult, op1=ALU.add), ['rmask', sgwk], [csk])
            if fwd:
                lg, lgk = cs, csk
            else:
                lg, lgk = newwk()
                S.op('pool', lambda e: e.tensor_tensor(out=lg[:], in0=sgw[:], in1=cs[:], op=ALU.subtract), [sgwk, csk], [lgk])
                S.op('pool', lambda e: e.tensor_tensor(out=lg[:].rearrange("p (n t) -> p n t", t=64), in0=lg[:].rearrange("p (n t) -> p n t", t=64),
                                                       in1=cs[:].rearrange("p (n t) -> p n t", t=64)[:, :, 63:64].to_broadcast([128, 8, 64]), op=ALU.add), [lgk, csk], [lgk])
            yield
            gkey = ('gam', bi)
            S.op('act', lambda e: e.activation(out=gam[bi][:], in_=lg[:], func=AF.Exp, scale=-C0), [lgk], [gkey])
            ig, igk = newwk()
            S.op('act', lambda e: e.activation(out=ig[:], in_=lg[:], func=AF.Exp, scale=C0), [lgk], [igk])
            gp, gpk = newwk()
            S.op('dve', lambda e: e.tensor_tensor(out=gp[:], in0=lg[:], in1=sgw[:], op=ALU.subtract), [lgk, sgwk], [gpk])
            S.op('act', lambda e: e.activation(out=gp[:], in_=gp[:], func=AF.Exp, scale=-C0), [gpk], [gpk])
            yield
            for (nm, a_, ak, b_, bk) in (("At", kkn, kknk, gp, gpk), ("Bt", bb, bbk, ig, igk), ("Kt", kp_, kpk, ig, igk)):
                for h in range(2):
                    eng = 'pool'
                    S.op(eng, lambda e, nm=nm, a_=a_, b_=b_, h=h: e.tensor_tensor(out=bdt[nm][bi][h * 64:(h + 1) * 64, :, h * 64:(h + 1) * 64],
                                                                                   in0=a_[h * 64:(h + 1) * 64, :].rearrange("p (n t) -> p n t", t=64),
                                                                                   in1=b_[h * 64:(h + 1) * 64, :].rearrange("p (n t) -> p n t", t=64), op=ALU.mult),
                         [ak, bk], [('bd', nm, bi)])
                yield
            S.op('dve', lambda e: e.tensor_tensor(out=bdt["At"][bi][:, :, 128:192], in0=rs[:].rearrange("p (n t) -> p n t", t=64),
                                                  in1=gam[bi][:].rearrange("p (n t) -> p n t", t=64), op=ALU.mult), [rsk, gkey], [('Rt', bi)])
            if fwd:
                yield
                agb, agbk = sig_lowrank(a2b_, (64, 128), sad, c, t0, P(3))
                kb_, kbk = newwk()
                S.op('pool', lambda e: e.tensor_scalar(out=kb_[:], in0=agb[:], scalar1=P(5), scalar2=P(6), op0=ALU.mult, op1=ALU.add), [agbk, 'rwp'], [kbk])
                S.op('pool', lambda e: e.tensor_tensor(out=kb_[:], in0=kb_[:], in1=ks[:], op=ALU.mult), [kbk, ksk], [kbk])
                yield
                S.op('pool', lambda e: e.tensor_tensor(out=kb_[:], in0=kb_[:], in1=kp_[:], op=ALU.add), [kbk, kpk], [kbk])
                S.op('dve', lambda e: e.scalar_tensor_tensor(out=kb_[:], in0=rs[:], scalar=P(7), in1=kb_[:], op0=ALU.mult, op1=ALU.mult), [rsk, kbk, 'rwp'], [kbk])
                p2, pk2 = psfull()
                S.mm(lambda e, p2=p2: e.matmul(p2[:], bones, kb_[:], start=True, stop=True), ['cst', kbk], [pk2])
                S.op('act', lambda e, p2=p2: e.activation(out=bsum[bi][:], in_=p2[:], func=AF.Copy), [pk2], [('bsum', bi)])

        def offchain(dirn, n, bi, ui):
            fwd = (dirn == 0)
            mS = (m_su if fwd else m_sl)
            At = bdt["At"][bi][:, n, 0:128]
            AR = bdt["At"][bi][:, n, :]
            Bt = bdt["Bt"][bi][:, n, :]
            Kt = bdt["Kt"][bi][:, n, :]
            Vt = bdt["Vt"][bi][:, n, :]
            Rs = bdt["At"][bi][:, n, 128:192]
            gcol = gam[bi][:, n * 64 + 63: n * 64 + 64] if fwd else gam[bi][:, n * 64: n * 64 + 1]
            kA, kB, kK, kVt, kR, kG = ('bd', "At", bi), ('bd', "Bt", bi), ('bd', "Kt", bi), ('bd', "Vt", bi), ('Rt', bi), ('gam', bi)
            XM, LM, Z, G = ub["XM"][ui], ub["LM"][ui], ub["Z"][ui], ub["G"][ui]
            X, Tt, Kk, V, PpT, RbT = (ub1[k][ui] for k in ("X", "Tt", "Kk", "V", "PpT", "RbT"))
            Bk = XM[:, 192:320]
            K_ = lambda nm: ('u', nm, ui)
            cpc = [ui]

            def cp(dst, src, reads, writes):
                cpc[0] += 1
                if cpc[0] % 3 != 0:
                    S.op('act', lambda e, dst=dst, src=src: e.activation(out=dst, in_=src, func=AF.Copy), reads, writes)
                else:
                    S.op('dve', lambda e, dst=dst, src=src: e.tensor_copy(out=dst, in_=src), reads, writes)
            for (lh, lk, dst, dk) in ((Bt, kB, XM, 'XM'), (Kt, kK, LM, 'LM')):
                p, pk = psreg()
                S.mm(lambda e, p=p, lh=lh: e.matmul(p[:, 0:192], lh, AR, start=True, stop=True), [lk, kA, kR], [pk])
                S.op('dve', lambda e, p=p, dst=dst: e.tensor_tensor(out=dst[:, 0:192], in0=p[:, 0:192], in1=cmask[dirn][:], op=ALU.mult), [pk, 'cmask'], [K_(dk)])
                yield
            p, pk = psreg()
            S.mm(lambda e, p=p: e.matmul(p[:, 0:128], At, Bt, start=True, stop=True), [kA, kB], [pk])
            cp(X[:], p[:, 0:128], [pk], [K_('X')])
            S.op('pool', lambda e: e.tensor_tensor(out=X[:], in0=X[:], in1=mSxb[dirn][:], op=ALU.mult), [K_('X'), 'mSxb'], [K_('X')])
            S.op('pool', lambda e: e.tensor_tensor(out=Tt[:], in0=XM[:, 0:128], in1=identb, op=ALU.add), [K_('XM'), 'cstb'], [K_('Tt')])
            yield
            for (src, sk, dk) in ((At, kA, 'A'), (Bt, kB, 'Bk'), (Kt, kK, 'Kk'), (Vt, kVt, 'V')):
                p, pk = psreg()
                S.mm(lambda e, p=p, src=src: e.matmul(p[:, 0:128], src, identb, start=True, stop=True), [sk, 'cstb'], [pk])
                if dk == 'A':
                    cp(Z[:, 0:128], p[:, 0:128], [pk], [K_('Z0')])
                elif dk == 'Bk':
                    cp(Bk, p[:, 0:128], [pk], [K_('Bk')])
                elif dk == 'Kk':
                    cp(Kk[:], p[:, 0:128], [pk], [K_('Kk')])
                else:
                    cp(V[:], p[:, 0:128], [pk], [K_('V')])
                yield
            Pc, Pck = X[:], K_('X')
            Ptc, Ptck = XM[:, 0:128], K_('XM')
            for lvl in range(5):
                pn = pw[ui][(2 * lvl) % 4]
                pnk = ('pw', ui, (2 * lvl) % 4)
                p1, pk1 = psreg()
                S.mm(lambda e, p1=p1, Pc=Pc, Ptc=Ptc: e.matmul(p1[:, 0:128], Ptc, Pc, start=True, stop=True), [Pck, Ptck], [pk1])
                if lvl < 4:
                    ptn = pw[ui][(2 * lvl + 1) % 4]
                    ptnk = ('pw', ui, (2 * lvl + 1) % 4)
                    p2, pk2 = psreg()
                    S.mm(lambda e, p2=p2, Pc=Pc, Ptc=Ptc: e.matmul(p2[:, 0:128], Pc, Ptc, start=True, stop=True), [Pck, Ptck], [pk2])
                    if lvl % 2 == 0:
                        cp(ptn[:], p2[:, 0:128], [pk2], [ptnk])
                    else:
                        cp(ptn[:], p2[:, 0:128], [pk2], [ptnk])
                cp(pn[:], p1[:, 0:128], [pk1], [pnk])
                Pc, Pck = pn[:], pnk
                if lvl < 4:
                    Ptc, Ptck = ptn[:], ptnk
                yield
                p3, pk3 = psreg()
                S.mm(lambda e, p3=p3, Pc=Pc: e.matmul(p3[:, 0:128], Pc, Tt[:], start=True, stop=True), [Pck, K_('Tt')], [pk3])
                S.op('dve', lambda e, p3=p3: e.tensor_tensor(out=Tt[:], in0=Tt[:], in1=p3[:, 0:128], op=ALU.add), [pk3, K_('Tt')], [K_('Tt')])
                if lvl == 4:
                    yield
            p, pk = psreg()
            S.mm(lambda e, p=p: e.matmul(p[:, 0:128], LM[:, 0:128], V[:], start=True, stop=True), [K_('LM'), K_('V')], [pk])
            cp(Z[:, 128:256], p[:, 0:128], [pk], [K_('Z1')])
            yield
            p, pk = psreg()
            S.mm(lambda e, p=p: e.matmul(p[:, 0:256], Tt[:], Z[:], start=True, stop=True), [K_('Tt'), K_('Z0'), K_('Z1')], [pk])
            cp(G[:], p[:, 0:256], [pk], [K_('G')])
            yield
            p, pk = psreg()
            S.mm(lambda e, p=p: e.matmul(p[:, 0:192], G[:, 0:128], XM[:, 128:320], start=True, stop=True), [K_('G'), K_('Bk'), K_('XM')], [pk])
            S.op('dve', lambda e, p=p: e.tensor_tensor(out=PpT[:], in0=p[:, 64:192], in1=ident, op=ALU.add), [pk, 'cst'], [K_('PpT')])
            S.op('dve', lambda e, p=p: e.tensor_tensor(out=RbT[:, 0:64], in0=p[:, 0:64], in1=Rs, op=ALU.add), [pk, kR], [K_('RbT')])
            p, pk = psreg()
            S.mm(lambda e, p=p: e.matmul(p[:, 0:128], Bk, G[:, 128:256], start=True, stop=False), [K_('G'), K_('Bk')], [pk])
            S.mm(lambda e, p=p: e.matmul(p[:, 0:128], Kk[:], V[:], start=False, stop=True), [K_('Kk'), K_('V')], [pk])
            S.op('dve', lambda e, p=p: e.tensor_scalar(out=qg[ui][:], in0=p[:, 0:128], scalar1=gcol, scalar2=None, op0=ALU.mult), [pk, kG], [K_('qg')])

        def chain(dirn, n, bi, ui, Hcur, Hnew, ysl, yslk):
            fwd = (dirn == 0)
            gcol = gam[bi][:, n * 64 + 63: n * 64 + 64] if fwd else gam[bi][:, n * 64: n * 64 + 1]
            kG = ('gam', bi)
            XM, LM, G = ub["XM"][ui], ub["LM"][ui], ub["G"][ui]
            V, PpT, RbT = (ub1[k][ui] for k in ("V", "PpT", "RbT"))
            K_ = lambda nm: ('u', nm, ui)
            p, pk = psreg()
            S.mm(lambda e, p=p: e.matmul(p[:, 0:64], G[:, 128:256], XM[:, 128:192], start=True, stop=False), [K_('G'), K_('XM')], [pk])
            S.mm(lambda e, p=p: e.matmul(p[:, 0:64], V[:], LM[:, 128:192], start=False, stop=False), [K_('V'), K_('LM')], [pk])
            S.mm(lambda e, p=p: e.matmul(p[:, 0:64], Hcur[0][:], RbT[:, 0:64], start=False, stop=True), [Hcur[1], K_('RbT')], [pk])
            S.op('act', lambda e, p=p: e.activation(out=ysl, in_=p[:, 0:64], func=AF.Copy), [pk], [yslk])
            p2, pk2 = psreg()
            S.mm(lambda e, p2=p2: e.matmul(p2[:, 0:128], PpT[:], Hcur[0][:], start=True, stop=True), [K_('PpT'), Hcur[1]], [pk2])
            S.op('dve', lambda e, p2=p2: e.scalar_tensor_tensor(out=Hnew[0][:], in0=p2[:, 0:128], scalar=gcol, in1=qg[ui][:], op0=ALU.mult, op1=ALU.add),
                 [pk2, K_('qg'), kG], [Hnew[1]])

        def epilogue(c, seg, t, bi, yt, ytk):
            t0 = t * TT
            P = lambda j: rwp[:, c * 10 + j: c * 10 + j + 1]
            bon, bonk = newwk()
            S.op('pool', lambda e: e.tensor_tensor(out=bon[:], in0=vTs[bi][:], in1=bsum[bi][:], op=ALU.mult), [('vTs', bi), ('bsum', bi)], [bonk])
            dma('sp', ybl[:], ybT_d[c * 128:(c + 1) * 128, seg * SEG + t0: seg * SEG + t0 + TT], [('ybT', c)], ['ybl'])
            ysum, ysk = newwk()
            S.op('dve', lambda e: e.tensor_tensor(out=ysum[:], in0=yt[:], in1=ybl[:], op=ALU.add), [ytk, 'ybl'], [ysk])
            yield
            p, pk = psfull()
            S.mm(lambda e, p=p: e.matmul(p[:], bones, ysum[:], start=True, stop=True), ['cst', ysk], [pk])
            yield
            yc, yck = newwk()
            S.op('dve', lambda e, p=p: e.scalar_tensor_tensor(out=yc[:], in0=p[:], scalar=-1.0 / 64, in1=ysum[:], op0=ALU.mult, op1=ALU.add), [pk, ysk], [yck])
            yield
            sq_, sqk = newwk()
            S.op('act', lambda e: e.activation(out=sq_[:], in_=yc[:], func=AF.Square), [yck], [sqk])
            yield
            p2, pk2 = psfull()
            S.mm(lambda e, p2=p2: e.matmul(p2[:], bones, sq_[:], start=True, stop=True), ['cst', sqk], [pk2])
            yield
            sd, sdk = newwk()
            S.op('act', lambda e, p2=p2: e.activation(out=sd[:], in_=p2[:], func=AF.Sqrt, scale=1.0 / 64, bias=epsl[:]), [pk2, 'epsl'], [sdk])
            yield
            S.op('dve', lambda e: e.reciprocal(out=sd[:], in_=sd[:]), [sdk], [sdk])
            yield
            S.op('dve', lambda e: e.tensor_tensor(out=yc[:], in0=yc[:], in1=sd[:], op=ALU.mult), [yck, sdk], [yck])
            yield
            S.op('pool', lambda e: e.tensor_scalar(out=yc[:], in0=yc[:], scalar1=P(8), scalar2=P(9), op0=ALU.mult, op1=ALU.add), [yck, 'rwp'], [yck])
            yield
            S.op('pool', lambda e: e.tensor_tensor(out=yc[:], in0=yc[:], in1=bon[:], op=ALU.add), [yck, bonk], [yck])
            p3, pk3 = psfull()
            S.mm(lambda e, p3=p3: e.matmul(p3[:], g2b_[:, c * 128:(c + 1) * 128], sgd[:, t0:t0 + TT], start=True, stop=True), ['lrw', 'lr'], [pk3])
            yield
            k = rc['os'] % 2
            rc['os'] += 1
            S.op('dve', lambda e, p3=p3, k=k: e.tensor_tensor(out=ostr[k][:], in0=yc[:], in1=p3[:], op=ALU.mult), [yck, pk3], [('ostr', k)])
            dma('sp', mixT_d[1024 + c * 128: 1024 + (c + 1) * 128, seg * SEG + t0: seg * SEG + t0 + TT], ostr[k][:], [('ostr', k)], [('mixT', seg)])

        tiles = []
        for (dirn, seg, si) in ((1, 1, 0), (1, 0, 1), (0, 0, 0), (0, 1, 1)):
            for c in range(RW_PAIRS):
                trange = range(TPS - 1, -1, -1) if dirn == 1 else range(TPS)
                for ti, t in enumerate(trange):
                    tiles.append(dict(c=c, dirn=dirn, seg=seg, t=t, first=(si == 0 and ti == 0), link=(si == 1 and ti == 0)))
        cur_seg = None
        hic = [0] * 8
        early = None
        pend_epi = None
        for k, tl in enumerate(tiles):
            bi = k % 2
            c, dirn, seg, t = tl['c'], tl['dirn'], tl['seg'], tl['t']
            if cur_seg != seg:
                lowrank_prep(seg)
                cur_seg = seg
            if early is None:
                early = prep_tile(dirn, c, seg, t, bi)
            for _ in early:
                pass
            early = None
            hi = hic[c]
            if tl['first']:
                hi = 0
                S.op('pool', lambda e, c=c: e.memset(Hb[c][0][:], 0.0), [], [('Hb', c, 0)])
            if tl['link']:
                S.op('dve', lambda e, hi=hi, c=c: e.tensor_scalar(out=Hb[c][hi][:], in0=Hb[c][hi][:], scalar1=link[:, 0:1], scalar2=None, op0=ALU.mult), [('Hb', c, hi), 'link'], [('Hb', c, hi)])
            chunks = list(range(7, -1, -1)) if dirn == 1 else list(range(8))
            nprep = None
            if k + 1 < len(tiles) and tiles[k + 1]['seg'] == seg:
                n2 = tiles[k + 1]
                nprep = prep_tile(n2['dirn'], n2['c'], n2['seg'], n2['t'], (k + 1) % 2)
                early = nprep

            def adv():
                nonlocal nprep, pend_epi
                if pend_epi is not None:
                    try:
                        next(pend_epi)
                    except StopIteration:
                        pend_epi = None
                if nprep is not None:
                    try:
                        next(nprep)
                    except StopIteration:
                        nprep = None
            yi = k % 2
            gens = [offchain(dirn, n, bi, ui) for ui, n in enumerate(chunks)]
            live = []
            started = 0
            finished = [False] * 8
            nchain = 0
            while nchain < 8:
                if started < 8:
                    live.append((started, gens[started]))
                    started += 1
                nxt = []
                for (ui, g) in live:
                    try:
                        next(g)
                        nxt.append((ui, g))
                    except StopIteration:
                        finished[ui] = True
                live = nxt
                if finished[nchain]:
                    n = chunks[nchain]
                    chain(dirn, n, bi, nchain, (Hb[c][hi], ('Hb', c, hi)), (Hb[c][1 - hi], ('Hb', c, 1 - hi)), ysb[yi][:, n * 64:(n + 1) * 64], ('ysb', yi))
                    hi = 1 - hi
                    nchain += 1
                adv()
            hic[c] = hi
            if pend_epi is not None:
                for _ in pend_epi:
                    pass
                pend_epi = None
            if dirn == 1:
                dma('sp', ybT_d[c * 128:(c + 1) * 128, seg * SEG + t * TT: seg * SEG + (t + 1) * TT], ysb[yi][:], [('ysb', yi)], [('ybT', c)])
            else:
                pend_epi = epilogue(c, seg, t, bi, ysb[yi], ('ysb', yi))
                if not (k + 1 < len(tiles) and tiles[k + 1]['seg'] == seg):
                    for _ in pend_epi:
                        pass
                    pend_epi = None
        S.barrier()
        pr.close()

    if 'rwkv' in phases:
        phase_rwkv()
    def phase_C():
        pc = ExitStack()
        env = ffn_env(pc, 'C')
        for it in range(NT):
            env['wout'](it)
            env['rmsnorm_to_nT'](2)
            env['ffn'](1)
            env['store_y_tile'](it, 3)
        S.barrier()
        pc.close()

    if 'C' in phases:
        phase_C()
    S.barrier()
    S.emit()
    global LAST_S
    LAST_S = S
    return nc

LAST_S = None

def _const_tables():
    tri = np.ones((64, 64), np.float32)
    su1, sl1, iu1, il1 = np.triu(tri, 1), np.tril(tri, -1), np.triu(tri, 0), np.tril(tri, 0)

    def bdm(m):
        o = np.zeros((128, 128), np.float32)
        o[:64, :64] = m
        o[64:, 64:] = m
        return o
    cst = np.zeros((128, 1024), np.float32)
    cst[:, 0:128] = np.eye(128, dtype=np.float32)
    cst[:, 128:256] = bdm(tri)
    cst[:, 256:384] = bdm(su1)
    cst[:, 384:512] = bdm(sl1)
    cst[:, 512:640] = bdm(iu1)
    cst[:, 640:768] = bdm(il1)
    rmask = np.ones((128, 512), np.float32)
    rmask[:, ::64] = 0.0
    slopes = np.exp2(-8.0 * np.arange(1, 17, dtype=np.float64) / 16)
    kp = np.arange(128)[:, None]
    qp = np.arange(128)[None, :]
    etab = np.zeros((8, 128, 1536), np.float32)
    for h in range(16):
        for di, d in enumerate((1, 4, 16)):
            for j in range(2):
                rel = kp + 128 * j - 64 - qp
                e = np.where(np.abs(rel) <= 64, np.exp(-slopes[h] * d * np.abs(rel)), 0.0)
                o = ((h % 2) * 3 + di) * 256 + j * 128
                etab[h // 2, :, o:o + 128] = e
    return cst, rmask, etab


def _prep_weights(inp):
    f = np.float32
    A = lambda a: np.ascontiguousarray(np.asarray(a, dtype=f))
    out = {}
    for fi, pre in enumerate(("ffn1", "ffn2")):
        g = np.asarray(inp[pre + "_gate"][0]).reshape(DC, 128, FC, 128).transpose(2, 1, 0, 3).reshape(FC, 128, 2048)
        u = np.asarray(inp[pre + "_up"][0]).reshape(DC, 128, FC, 128).transpose(2, 1, 0, 3).reshape(FC, 128, 2048)
        out["wgu%d" % (fi + 1)] = A(np.concatenate([g, u], axis=2).reshape(FF, 4096))
        dn = np.asarray(inp[pre + "_down"][0]).reshape(FC, 128, DC, 128).transpose(2, 1, 0, 3).reshape(D, FF)
        out["wd%d" % (fi + 1)] = A(dn)
    w_in = np.asarray(inp["w_in"][0])
    fm_cols = [c * 128 for c in range(16)] + [3072 + c * 128 for c in range(ZC)]
    fm = [w_in[:, c0:c0 + 128].reshape(DC, 128, 128).transpose(1, 0, 2).reshape(128, 2048) for c0 in fm_cols]
    out["winfm"] = A(np.concatenate(fm, axis=0))
    tm_cols = [2048 + g * 256 for g in range(4)]
    tm = [w_in[:, c0:c0 + 256].reshape(DC, 128, 256).transpose(1, 0, 2).reshape(128, 4096) for c0 in tm_cols]
    out["wintm"] = A(np.concatenate(tm, axis=0))
    out["wo"] = A(np.asarray(inp["w_out"][0]).reshape(DC, 128, DC, 128).transpose(2, 1, 0, 3).reshape(D, D))
    gains = np.zeros((128, 4 * DC), f)
    for gi, g in enumerate((inp["ffn1_norm"][0], inp["mix_norm"][0], inp["ffn2_norm"][0], inp["final_norm"])):
        gains[:, gi * DC:(gi + 1) * DC] = np.asarray(g).reshape(DC, 128).T
    out["gains"] = gains
    mu = np.zeros((128, 2 * ZC), f)
    mu[:, 0:ZC] = np.asarray(inp["mu_prev"][0]).reshape(ZC, 128).T
    mu[:, ZC:] = np.asarray(inp["mu_next"][0]).reshape(ZC, 128).T
    out["mu"] = mu
    rwp = np.zeros((128, 80), f)
    vecs = (inp["w0_f"][0], inp["w0_b"][0], inp["a0_f"][0], inp["a0_b"][0], inp["k_k"][0], inp["k_a"][0], None,
            np.asarray(inp["r_k"][0]).reshape(1024), inp["ln_x_w"][0], inp["ln_x_b"][0])
    for j, v in enumerate(vecs):
        if v is None:
            continue
        rwp[:, j::10] = np.asarray(v).reshape(8, 128).T
    out["rwp"] = rwp
    out["w2"] = A(np.concatenate([inp["w2_f"][0], inp["w2_b"][0]], axis=0))
    out["a2"] = A(np.concatenate([inp["a2_f"][0], inp["a2_b"][0]], axis=0))
    out["g2"] = A(inp["g2"][0])
    cst, rmask, etab = _const_tables()
    out["cst"], out["rmask"], out["etab"] = cst, rmask, etab
    return out


def _core_plan(SEG, x_prompt, x_sample):
    plan = []
    xp, xs = np.asarray(x_prompt), np.asarray(x_sample)
    nprompt_cores = xp.shape[0] * (xp.shape[1] // (NSEG * SEG))
    for b in range(xp.shape[0]):
        for part in range(xp.shape[1] // (NSEG * SEG)):
            assert xp.shape[1] == NSEG * SEG
            plan.append(([('p', b, 0), ('p', b, SEG)], 1.0))
    nb = xs.shape[0]
    assert xs.shape[1] == SEG
    rest = 8 - len(plan)
    two = nb - rest
    i = 0
    for c in range(rest):
        if c < two:
            plan.append(([('s', i, 0), ('s', i + 1, 0)], 0.0))
            i += 2
        elif i < nb:
            plan.append(([('s', i, 0), None], 0.0))
            i += 1
        else:
            plan.append(([None, None], 0.0))
    assert i == nb
    return plan


_NC_CACHE = {}


def run(inputs, SEG, debug=False, phases=('A', 'fix', 'att', 'rwkv', 'C')):
    wts = _prep_weights(inputs)
    xp, xs = np.asarray(inputs["x_prompt"], np.float32), np.asarray(inputs["x_sample"], np.float32)
    plan = _core_plan(SEG, xp, xs)
    in_maps = []
    for segs, lk in plan:
        x = np.zeros((NSEG * SEG, D), np.float32)
        for si, sdesc in enumerate(segs):
            if sdesc is None:
                continue
            kind, b, st = sdesc
            src = xp if kind == 'p' else xs
            x[si * SEG:(si + 1) * SEG] = src[b, st:st + SEG]
        m = dict(wts)
        m["x"] = x
        m["link"] = np.full((128, 1), lk, np.float32)
        in_maps.append(m)
    key = (SEG, debug, tuple(phases))
    if key not in _NC_CACHE:
        _NC_CACHE[key] = build(SEG, debug=debug, phases=phases)
    nc = _NC_CACHE[key]
    res = run_bass_kernel_spmd(nc, in_maps, core_ids=list(range(8)))
    yp = np.zeros(xp.shape, np.float32)
    ys = np.zeros(xs.shape, np.float32)
    for ci, (segs, lk) in enumerate(plan):
        y = res.results[ci]["y"]
        for si, sdesc in enumerate(segs):
            if sdesc is None:
                continue
            kind, b, st = sdesc
            (yp if kind == 'p' else ys)[b, st:st + SEG] = y[si * SEG:(si + 1) * SEG]
    return (yp, ys), res, plan


def kernel(**inputs):
    (yp, ys), _, _ = run(inputs, 4096)
    return (yp, ys)
```

```python
from contextlib import ExitStack
import numpy as np
import concourse.bass as bass
import concourse.mybir as mybir
from concourse.bass_utils import run_bass_kernel_spmd

F32 = mybir.dt.float32
BF16 = mybir.dt.bfloat16
ALU = mybir.AluOpType
AF = mybir.ActivationFunctionType

D = 2048
DC = 16
FF = 5632
FC = 44
TT = 512
NSEG = 2
APAD = 1024
NFM = 43
ZC = 27
C0 = 0.6065306597126334
NORM_EPS = 1e-6
LN_EPS = 64e-5

ENGS = ("pe", "dve", "act", "pool", "sp")
NDMASEM = 64
NHWSEM = 40


class Sched:
    def __init__(self, nc, es):
        self.nc = nc
        self.csem = {e: es.enter_context(nc.semaphore("c_" + e)) for e in ENGS}
        self.dsem = [es.enter_context(nc.semaphore("d%d" % i)) for i in range(NDMASEM)]
        self.dval = [0] * NDMASEM
        self.dnext = 0
        self.dnext_sw = 0
        self.count = {e: 0 for e in ENGS}
        self.prog = {e: [] for e in ENGS}
        self.waited = {e: {} for e in ENGS}
        self.lastw = {}
        self.reads = {}

    def _need(self, eng, tok, deps):
        if tok is None:
            return
        if tok[0] == 'e':
            _, e2, idx = tok
            if e2 == eng:
                if eng == 'pe':
                    return
            key = e2
        else:
            _, i, idx = tok
            key = ('d', i)
        if self.waited[eng].get(key, 0) >= idx:
            return
        if deps.get(key, 0) < idx:
            deps[key] = idx

    def _emit_waits(self, eng, deps):
        for key, idx in deps.items():
            sem = self.csem[key] if isinstance(key, str) else self.dsem[key[1]]
            self.prog[eng].append(('w', sem, idx))
            self.waited[eng][key] = idx

    def _deps(self, eng, reads, writes):
        deps = {}
        for k in reads:
            self._need(eng, self.lastw.get(k), deps)
        for k in writes:
            self._need(eng, self.lastw.get(k), deps)
            for t in self.reads.get(k, ()):
                self._need(eng, t, deps)
        return deps

    def _record(self, tok, reads, writes):
        for k in reads:
            self.reads.setdefault(k, []).append(tok)
        for k in writes:
            self.lastw[k] = tok
            self.reads[k] = []

    def op(self, eng, fn, reads=(), writes=()):
        deps = self._deps(eng, reads, writes)
        self._emit_waits(eng, deps)
        self.count[eng] += 1
        self.prog[eng].append(('o', fn, self.csem[eng], 1))
        tok = ('e', eng, self.count[eng])
        self._record(tok, reads, writes)
        return tok

    def mm(self, fn, reads=(), writes=(), last=True):
        return self.op('pe', fn, reads, writes)

    def dma(self, eng, fn, reads=(), writes=()):
        deps = self._deps(eng, reads, writes)
        if eng == 'pool':
            i = NHWSEM + self.dnext_sw
            self.dnext_sw = (self.dnext_sw + 1) % (NDMASEM - NHWSEM)
        else:
            i = self.dnext
            self.dnext = (self.dnext + 1) % NHWSEM
        if self.dval[i] > 0:
            self._need(eng, ('d', i, self.dval[i]), deps)
        self._emit_waits(eng, deps)
        self.dval[i] += 16
        self.prog[eng].append(('o', fn, self.dsem[i], 16))
        tok = ('d', i, self.dval[i])
        self._record(tok, reads, writes)
        return tok

    def barrier(self):
        for eng in ENGS:
            deps = {}
            for e2 in ENGS:
                if e2 != eng and self.count[e2] > 0:
                    self._need(eng, ('e', e2, self.count[e2]), deps)
            for i in range(NDMASEM):
                if self.dval[i] > 0:
                    self._need(eng, ('d', i, self.dval[i]), deps)
            self._emit_waits(eng, deps)

    def emit(self):
        prog = self.prog

        def run(engobj, items):
            for it in items:
                if it[0] == 'w':
                    engobj.wait_ge(it[1], it[2])
                else:
                    ins = it[1](engobj)
                    if it[2] is not None:
                        ins.then_inc(it[2], it[3])

        with self.nc.Block() as block:
            @block.sync
            def _(e):
                run(e, prog['sp'])

            @block.tensor
            def _(e):
                run(e, prog['pe'])

            @block.vector
            def _(e):
                run(e, prog['dve'])

            @block.scalar
            def _(e):
                run(e, prog['act'])

            @block.gpsimd
            def _(e):
                run(e, prog['pool'])


def build(SEG, debug=False, phases=('A', 'fix', 'att', 'rwkv', 'C'), RW_PAIRS=8, ATT_PAIRS=8):
    NTOK = NSEG * SEG
    NT = NTOK // TT
    TPS = SEG // TT
    SEGA = SEG + 2 * APAD
    SEGZ = SEG + 2
    nc = bass.Bass("TRN2", target_bir_lowering=False)
    es = ExitStack()
    S = Sched(nc, es)

    def din(name, shape, dt=F32):
        return nc.dram_tensor(name, list(shape), dt, kind="ExternalInput").ap()

    def dscr(name, shape, dt):
        if debug:
            return nc.dram_tensor(name, list(shape), dt, kind="ExternalOutput").ap()
        return nc.dram_tensor(name, list(shape), dt).ap()

    x_d = din("x", [NTOK, D])
    link_d = din("link", [128, 1])
    wgu_h = [din("wgu1", [FF, 4096]), din("wgu2", [FF, 4096])]
    wd_h = [din("wd1", [D, FF]), din("wd2", [D, FF])]
    winfm_h = din("winfm", [NFM * 128, 2048])
    wintm_h = din("wintm", [4 * 128, 4096])
    wo_h = din("wo", [D, D])
    gains_d = din("gains", [128, 4 * DC])
    mu_d = din("mu", [128, 2 * ZC])
    rwp_d = din("rwp", [128, 8 * 10])
    w2_d = din("w2", [128, 1024])
    a2_d = din("a2", [128, 1024])
    g2_d = din("g2", [128, 1024])
    etab_d = din("etab", [8, 128, 2 * 3 * 256])
    cst_d = din("cst", [128, 8 * 128])
    rmask_d = din("rmask", [128, 512])
    y_d = nc.dram_tensor("y", [NTOK, D], F32, kind="ExternalOutput").ap()

    wgu_b = [dscr("wgu1b", [FF, 4096], BF16), dscr("wgu2b", [FF, 4096], BF16)]
    wd_b = [dscr("wd1b", [D, FF], BF16), dscr("wd2b", [D, FF], BF16)]
    winfm_b = dscr("winfmb", [NFM * 128, 2048], BF16)
    wintm_b = dscr("wintmb", [4 * 128, 4096], BF16)
    wo_b = dscr("wob", [D, D], BF16)
    x1T_d = dscr("x1T", [D, NTOK], F32)
    qT_d = dscr("qT", [1024, NTOK], BF16)
    kT_d = dscr("kT", [1024, NSEG * SEGA], BF16)
    vatt_d = dscr("vatt", [NSEG * SEGA + 64, 16 * 65], BF16)
    zT_d = dscr("zT", [ZC * 128, NSEG * SEGZ], F32)
    ybT_d = dscr("ybT", [1024, NTOK], F32)
    mixT_d = dscr("mixT", [D, NTOK], BF16)

    def sb(name, shape, dt, stack=es):
        return stack.enter_context(nc.sbuf_tensor("s_" + name, list(shape), dt))

    PS = [es.enter_context(nc.psum_tensor("ps%d" % i, [128, 512], F32)) for i in range(8)]

    def dma(eng, out, in_, reads, writes, **kw):
        return S.dma(eng, lambda e, o=out, i=in_, k=kw: e.dma_start(out=o, in_=i, **k), reads, writes)

    cst = sb("cst", [128, 8 * 128], F32)
    cstb = sb("cstb", [128, 8 * 128], BF16)
    gains = sb("gains", [128, 4 * DC], F32)
    link = sb("link", [128, 1], F32)
    zeros = sb("zeros", [128, 1040], BF16)
    zerosf = sb("zerosf", [128, 128], F32)
    epsn = sb("epsn", [128, 1], F32)
    epsl = sb("epsl", [128, 1], F32)
    onesb = sb("onesb", [128, 128], BF16)
    dma('sp', cst[:], cst_d[:, :], [], ['cst'])
    dma('sp', gains[:], gains_d[:, :], [], ['gains'])
    dma('sp', link[:], link_d[:, :], [], ['link'])
    S.op('dve', lambda e: e.tensor_copy(out=cstb[:], in_=cst[:]), ['cst'], ['cstb'])
    S.op('pool', lambda e: e.memset(zeros[:], 0.0), [], ['zeros'])
    S.op('pool', lambda e: e.memset(zerosf[:], 0.0), [], ['zerosf'])
    S.op('pool', lambda e: e.memset(epsn[:], NORM_EPS), [], ['epsn'])
    S.op('pool', lambda e: e.memset(epsl[:], LN_EPS), [], ['epsl'])
    S.op('pool', lambda e: e.memset(onesb[:], 1.0), [], ['onesb'])
    ident = cst[:, 0:128]
    identb = cstb[:, 0:128]
    bones = cst[:, 128:256]
    m_su = cst[:, 256:384]
    m_sl = cst[:, 384:512]
    m_iu = cst[:, 512:640]
    m_il = cst[:, 640:768]

    def cast_rows(src, dst, r0, r1, key):
        dma('pool', dst[r0:r1, :], src[r0:r1, :], [], [key], max_dma_last_dim=4096)

    for j in range(0, FC, 2):
        cast_rows(wgu_h[0], wgu_b[0], j * 128, (j + 2) * 128, ('wgu0', j // 2))
    for m in range(DC):
        cast_rows(wd_h[0], wd_b[0], m * 128, (m + 1) * 128, ('wd0', m))
    for c in range(0, NFM, 4):
        cast_rows(winfm_h, winfm_b, c * 128, min(NFM, c + 4) * 128, ('winfm', c // 4))
    for g in range(0, 4, 2):
        cast_rows(wintm_h, wintm_b, g * 128, (g + 2) * 128, ('wintm', g // 2))
    for m in range(0, DC, 4):
        cast_rows(wo_h, wo_b, m * 128, (m + 4) * 128, ('wo', m // 4))
    for j in range(0, FC, 2):
        cast_rows(wgu_h[1], wgu_b[1], j * 128, (j + 2) * 128, ('wgu1', j // 2))
    for m in range(DC):
        cast_rows(wd_h[1], wd_b[1], m * 128, (m + 1) * 128, ('wd1', m))

    for s in range(NSEG):
        for off in (0, APAD + SEG):
            for c in range(8):
                dma('act', kT_d[c * 128:(c + 1) * 128, s * SEGA + off: s * SEGA + off + APAD], zeros[:, 0:APAD],
                    ['zeros'], [('kTpad', s, off)])
            for a in range(APAD // 128):
                r0 = s * SEGA + off + a * 128
                dma('act', vatt_d[r0:r0 + 128, :], zeros[:, 0:1040], ['zeros'], [('vapad', s, off)])
        for off in (0, SEG + 1):
            dma('act', zT_d[:, s * SEGZ + off: s * SEGZ + off + 1].rearrange("(c p) o -> p c o", p=128),
                zerosf[:, 0:ZC].unsqueeze(2), ['zerosf'], [('zpad', s, off)], allow_slow_non_contiguous=True)

    _sb_outer = sb

    def ffn_env(pa, tag):
        def sb(name, shape, dt, stack=es):
            return _sb_outer(name + tag, shape, dt, stack)
        xT = sb("xT", [128, DC, TT], F32, pa)
        nT = sb("nT", [128, DC, TT], BF16, pa)
        actT = sb("actT", [128, FC, TT], BF16, pa)
        wring = [sb("wring%d" % i, [128, 4096], BF16, pa) for i in range(3)]
        dring = [sb("dring%d" % i, [128, 22 * 128], BF16, pa) for i in range(3)]
        xtok = [sb("xtok%d" % i, [128, 1024], F32, pa) for i in range(2)]
        sq = [sb("sq%d" % i, [128, TT], BF16, pa) for i in range(2)]
        rstd = sb("rstd", [128, TT], F32, pa)
        sg = [sb("sg%d" % i, [128, TT], F32, pa) for i in range(2)]
        stg = [sb("stg%d" % i, [128, TT], F32, pa) for i in range(3)]
        stgb = [sb("stgb%d" % i, [128, TT], BF16, pa) for i in range(2)]
        vstg = [sb("vstg%d" % i, [128, 4 * 65], BF16, pa) for i in range(2)]
        cnt = {'w': 0, 'd': 0, 'x': 0, 'sq': 0, 'sg': 0, 'stg': 0, 'stgb': 0, 'tp': 0, 'vs': 0, 'ev': 0}
        xT_keys = [('xT', dc) for dc in range(DC)]

        def evac_copy(out, in_, reads, writes):
            cnt['ev'] += 1
            if cnt['ev'] % 2:
                S.op('act', lambda e, o=out, i=in_: e.activation(out=o, in_=i, func=AF.Copy), reads, writes)
            else:
                S.op('dve', lambda e, o=out, i=in_: e.tensor_copy(out=o, in_=i), reads, writes)

        def load_x_tile(it):
            for s in range(4):
                for hf in range(2):
                    xb_ = cnt['x'] % 2
                    cnt['x'] += 1
                    r0 = it * TT + s * 128
                    dma('sp', xtok[xb_][:], x_d[r0:r0 + 128, hf * 1024:(hf + 1) * 1024], [], [('xtok', xb_)])
                    for g in range(2):
                        bank = 6 + cnt['tp'] % 2
                        cnt['tp'] += 1
                        for q in range(4):
                            S.mm(lambda e, b=bank, q=q, g=g, xb_=xb_: e.transpose(PS[b][:, q * 128:(q + 1) * 128],
                                                                                 xtok[xb_][:, (g * 4 + q) * 128:(g * 4 + q + 1) * 128], ident),
                                 [('xtok', xb_), 'cst'], [('ps', bank)], last=(q == 3))
                        dc0 = hf * 8 + g * 4
                        evac_copy(xT[:, dc0:dc0 + 4, s * 128:(s + 1) * 128], PS[bank][:].rearrange("p (q t) -> p q t", q=4),
                                  [('ps', bank)], [('xT', dc0 + q) for q in range(4)])

        def store_y_tile(it, gi):
            rms_stats()
            for dc in range(DC):
                if dc % 2 == 0:
                    S.op('dve', lambda e, dc=dc: e.scalar_tensor_tensor(out=xT[:, dc, :], in0=xT[:, dc, :], scalar=gains[:, gi * DC + dc: gi * DC + dc + 1],
                                                                        in1=rstd[:], op0=ALU.mult, op1=ALU.mult), [('xT', dc), 'rstd', 'gains'], [('xT', dc)])
                else:
                    S.op('pool', lambda e, dc=dc: e.tensor_scalar(out=xT[:, dc, :], in0=xT[:, dc, :], scalar1=gains[:, gi * DC + dc: gi * DC + dc + 1], scalar2=0.0, op0=ALU.mult, op1=ALU.add),
                         [('xT', dc), 'gains'], [('xT', dc)])
                    S.op('pool', lambda e, dc=dc: e.tensor_tensor(out=xT[:, dc, :], in0=xT[:, dc, :], in1=rstd[:], op=ALU.mult), [('xT', dc), 'rstd'], [('xT', dc)])
            for s in range(4):
                for hf in range(2):
                    xb_ = cnt['x'] % 2
                    cnt['x'] += 1
                    for g in range(2):
                        bank = 6 + cnt['tp'] % 2
                        cnt['tp'] += 1
                        for q in range(4):
                            dc = hf * 8 + g * 4 + q
                            S.mm(lambda e, b=bank, q=q, dc=dc, s=s: e.transpose(PS[b][:, q * 128:(q + 1) * 128], xT[:, dc, s * 128:(s + 1) * 128], ident),
                                 [('xT', dc), 'cst'], [('ps', bank)], last=(q == 3))
                        evac_copy(xtok[xb_][:, g * 512:(g + 1) * 512], PS[bank][:], [('ps', bank)], [('xtok', xb_)])
                    r0 = it * TT + s * 128
                    dma('sp', y_d[r0:r0 + 128, hf * 1024:(hf + 1) * 1024], xtok[xb_][:], [('xtok', xb_)], ['y'])

        def rms_stats():
            for dc in range(DC):
                k = cnt['sq'] % 2
                cnt['sq'] += 1
                S.op('act', lambda e, k=k, dc=dc: e.activation(out=sq[k][:], in_=xT[:, dc, :], func=AF.Square), [('xT', dc)], [('sq', k)])
                S.mm(lambda e, k=k, dc=dc: e.matmul(PS[4][:], onesb[:], sq[k][:], start=(dc == 0), stop=(dc == DC - 1)),
                     [('sq', k), 'onesb'], [('ps', 4)], last=(dc == DC - 1))
            S.op('act', lambda e: e.activation(out=rstd[:], in_=PS[4][:], func=AF.Sqrt, scale=1.0 / D, bias=epsn[:]), [('ps', 4), 'epsn'], ['rstd'])
            S.op('dve', lambda e: e.reciprocal(out=rstd[:], in_=rstd[:]), ['rstd'], ['rstd'])

        def rmsnorm_to_nT(gi):
            rms_stats()
            for dc in range(DC):
                if dc % 2 == 0:
                    S.op('dve', lambda e, dc=dc: e.scalar_tensor_tensor(out=nT[:, dc, :], in0=xT[:, dc, :], scalar=gains[:, gi * DC + dc: gi * DC + dc + 1],
                                                                        in1=rstd[:], op0=ALU.mult, op1=ALU.mult), [('xT', dc), 'rstd', 'gains'], [('nT', dc)])
                else:
                    k = cnt['stg'] % 3
                    cnt['stg'] += 1
                    S.op('pool', lambda e, dc=dc, k=k: e.tensor_scalar(out=stg[k][:], in0=xT[:, dc, :], scalar1=gains[:, gi * DC + dc: gi * DC + dc + 1], scalar2=0.0, op0=ALU.mult, op1=ALU.add),
                         [('xT', dc), 'gains'], [('stg', k)])
                    S.op('pool', lambda e, dc=dc, k=k: e.tensor_tensor(out=nT[:, dc, :], in0=stg[k][:], in1=rstd[:], op=ALU.mult), [('stg', k), 'rstd'], [('nT', dc)])

        def ffn(fi):
            for j in range(FC):
                w = cnt['w'] % 3
                cnt['w'] += 1
                dma('sp', wring[w][:], wgu_b[fi][j * 128:(j + 1) * 128, :], [('wgu%d' % fi, j // 2)], [('wring', w)])
                bg = j % 2
                bu = 2 + j % 2
                for kc in range(DC):
                    S.mm(lambda e, w=w, kc=kc, bg=bg: e.matmul(PS[bg][:], wring[w][:, kc * 128:(kc + 1) * 128], nT[:, kc, :], start=(kc == 0), stop=(kc == DC - 1)),
                         [('wring', w), ('nT', kc)], [('ps', bg)], last=(kc == DC - 1))
                for kc in range(DC):
                    S.mm(lambda e, w=w, kc=kc, bu=bu: e.matmul(PS[bu][:], wring[w][:, 2048 + kc * 128: 2048 + (kc + 1) * 128], nT[:, kc, :], start=(kc == 0), stop=(kc == DC - 1)),
                         [('wring', w), ('nT', kc)], [('ps', bu)], last=(kc == DC - 1))
                k = cnt['sg'] % 2
                cnt['sg'] += 1
                S.op('act', lambda e, k=k, bg=bg: e.activation(out=sg[k][:], in_=PS[bg][:], func=AF.Silu), [('ps', bg)], [('sg', k)])
                S.op('dve', lambda e, k=k, bu=bu, j=j: e.tensor_tensor(out=actT[:, j, :], in0=sg[k][:], in1=PS[bu][:], op=ALU.mult), [('sg', k), ('ps', bu)], [('actT', j)])
            for m in range(DC):
                bd_ = 4 + m % 2
                for hf in range(2):
                    dd = cnt['d'] % 3
                    cnt['d'] += 1
                    dma('sp', dring[dd][:], wd_b[fi][m * 128:(m + 1) * 128, hf * 2816:(hf + 1) * 2816], [('wd%d' % fi, m)], [('dring', dd)])
                    for f2 in range(22):
                        fc = hf * 22 + f2
                        S.mm(lambda e, dd=dd, f2=f2, fc=fc, bd_=bd_: e.matmul(PS[bd_][:], dring[dd][:, f2 * 128:(f2 + 1) * 128], actT[:, fc, :], start=(fc == 0), stop=(fc == FC - 1)),
                             [('dring', dd), ('actT', fc)], [('ps', bd_)], last=(fc == FC - 1))
                S.op('dve', lambda e, m=m, bd_=bd_: e.scalar_tensor_tensor(out=xT[:, m, :], in0=PS[bd_][:], scalar=0.5, in1=xT[:, m, :], op0=ALU.mult, op1=ALU.add),
                     [('ps', bd_), ('xT', m)], [('xT', m)])

        fm_dest = [('q', c) for c in range(8)] + [('k', c) for c in range(8)] + [('z', c) for c in range(ZC)]

        def proj(it):
            seg = it // TPS
            t0 = (it % TPS) * TT
            for ci, (kind, c) in enumerate(fm_dest):
                w = cnt['w'] % 3
                cnt['w'] += 1
                dma('sp', wring[w][:, 0:2048], winfm_b[ci * 128:(ci + 1) * 128, :], [('winfm', ci // 4)], [('wring', w)])
                bank = ci % 2
                for kc in range(DC):
                    S.mm(lambda e, w=w, kc=kc, bank=bank: e.matmul(PS[bank][:], wring[w][:, kc * 128:(kc + 1) * 128], nT[:, kc, :], start=(kc == 0), stop=(kc == DC - 1)),
                         [('wring', w), ('nT', kc)], [('ps', bank)], last=(kc == DC - 1))
                if kind in ('q', 'k'):
                    k = cnt['stgb'] % 2
                    cnt['stgb'] += 1
                    evac_copy(stgb[k][:], PS[bank][:], [('ps', bank)], [('stgb', k)])
                    if kind == 'q':
                        dma('act', qT_d[c * 128:(c + 1) * 128, it * TT:(it + 1) * TT], stgb[k][:], [('stgb', k)], [('qT', seg)])
                    else:
                        col = seg * SEGA + APAD + t0
                        dma('act', kT_d[c * 128:(c + 1) * 128, col:col + TT], stgb[k][:], [('stgb', k)], [('kT', seg)])
                else:
                    k = cnt['stg'] % 3
                    cnt['stg'] += 1
                    evac_copy(stg[k][:], PS[bank][:], [('ps', bank)], [('stg', k)])
                    col = seg * SEGZ + 1 + t0
                    dma('act', zT_d[c * 128:(c + 1) * 128, col:col + TT], stg[k][:], [('stg', k)], [('zT', seg)])
            for g in range(4):
                w = cnt['w'] % 3
                cnt['w'] += 1
                dma('sp', wring[w][:], wintm_b[g * 128:(g + 1) * 128, :], [('wintm', g // 2)], [('wring', w)])
                for s in range(4):
                    bank = 2 + (g * 4 + s) % 2
                    for kc in range(DC):
                        S.mm(lambda e, w=w, kc=kc, bank=bank, s=s: e.matmul(PS[bank][:, 0:256], nT[:, kc, s * 128:(s + 1) * 128], wring[w][:, kc * 256:(kc + 1) * 256],
                                                                             start=(kc == 0), stop=(kc == DC - 1)),
                             [('wring', w), ('nT', kc)], [('ps', bank)], last=(kc == DC - 1))
                    if True:
                        k = cnt['vs'] % 2
                        cnt['vs'] += 1
                        S.op('pool', lambda e, k=k: e.memset(vstg[k][:], 1.0), [], [('vstg', k)])
                        evac_copy(vstg[k][:].rearrange("p (h c) -> p h c", h=4)[:, :, 0:64], PS[bank][:, 0:256].rearrange("p (h c) -> p h c", h=4),
                                  [('ps', bank), ('vstg', k)], [('vstg', k)])
                        row = seg * SEGA + APAD + t0 + s * 128
                        dma('act', vatt_d[row:row + 128, g * 260:(g + 1) * 260], vstg[k][:], [('vstg', k)], [('vatt', seg)])

        def wout(it):
            dma('sp', nT[:], mixT_d[:, it * TT:(it + 1) * TT].rearrange("(c p) t -> p c t", p=128), [('mixT', it // TPS)], [('nT', dc) for dc in range(DC)])
            dma('sp', xT[:], x1T_d[:, it * TT:(it + 1) * TT].rearrange("(c p) t -> p c t", p=128), [('x1T', it)], xT_keys)
            for m in range(DC):
                w = cnt['w'] % 3
                cnt['w'] += 1
                dma('sp', wring[w][:, 0:2048], wo_b[m * 128:(m + 1) * 128, :], [('wo', m // 4)], [('wring', w)])
                bank = m % 2
                for kc in range(DC):
                    S.mm(lambda e, w=w, kc=kc, bank=bank: e.matmul(PS[bank][:], wring[w][:, kc * 128:(kc + 1) * 128], nT[:, kc, :], start=(kc == 0), stop=(kc == DC - 1)),
                         [('wring', w), ('nT', kc)], [('ps', bank)], last=(kc == DC - 1))
                S.op('dve', lambda e, m=m, bank=bank: e.tensor_tensor(out=xT[:, m, :], in0=xT[:, m, :], in1=PS[bank][:], op=ALU.add), [('ps', bank), ('xT', m)], [('xT', m)])

        return dict(load_x_tile=load_x_tile, rmsnorm_to_nT=rmsnorm_to_nT, ffn=ffn, proj=proj, wout=wout, store_y_tile=store_y_tile, xT=xT, xT_keys=xT_keys)

    def phase_A():
        pa = ExitStack()
        env = ffn_env(pa, 'A')
        for it in range(NT):
            env['load_x_tile'](it)
            env['rmsnorm_to_nT'](0)
            env['ffn'](0)
            dma('act', x1T_d[:, it * TT:(it + 1) * TT].rearrange("(c p) t -> p c t", p=128), env['xT'][:], env['xT_keys'], [('x1T', it)])
            env['rmsnorm_to_nT'](1)
            env['proj'](it)
        S.barrier()
        pa.close()


    if 'A' in phases:
        phase_A()
    def phase_fix():
        pf = ExitStack()
        fk = sb("fk", [128, APAD], BF16, pf)
        fv = sb("fv", [128, APAD // 128, 1040], BF16, pf)
        fz = sb("fz", [128, ZC], F32, pf)
        for (ds, doff, ss, soff) in ((0, APAD + SEG, 1, APAD), (1, 0, 0, SEG)):
            for c in range(8):
                dma('sp', fk[:], kT_d[c * 128:(c + 1) * 128, ss * SEGA + soff: ss * SEGA + soff + APAD], [('kT', ss)], ['fk'])
                S.op('dve', lambda e: e.tensor_scalar(out=fk[:], in0=fk[:], scalar1=link[:, 0:1], scalar2=None, op0=ALU.mult), ['fk', 'link'], ['fk'])
                dma('sp', kT_d[c * 128:(c + 1) * 128, ds * SEGA + doff: ds * SEGA + doff + APAD], fk[:], ['fk', ('kTpad', ds, doff)], [('kTpad', ds, doff)])
            dma('sp', fv[:], vatt_d[ss * SEGA + soff: ss * SEGA + soff + APAD, :].rearrange("(a p) c -> p a c", p=128), [('vatt', ss)], ['fv'])
            S.op('dve', lambda e: e.tensor_scalar(out=fv[:], in0=fv[:], scalar1=link[:, 0:1], scalar2=None, op0=ALU.mult), ['fv', 'link'], ['fv'])
            dma('sp', vatt_d[ds * SEGA + doff: ds * SEGA + doff + APAD, :].rearrange("(a p) c -> p a c", p=128), fv[:], ['fv', ('vapad', ds, doff)], [('vapad', ds, doff)])
        for (ds, doff, ss, soff) in ((0, SEG + 1, 1, 1), (1, 0, 0, SEG)):
            dma('sp', fz[:].unsqueeze(2), zT_d[:, ss * SEGZ + soff: ss * SEGZ + soff + 1].rearrange("(c p) o -> p c o", p=128), [('zT', ss)], ['fz'], allow_slow_non_contiguous=True)
            S.op('dve', lambda e: e.tensor_scalar(out=fz[:], in0=fz[:], scalar1=link[:, 0:1], scalar2=None, op0=ALU.mult), ['fz', 'link'], ['fz'])
            dma('sp', zT_d[:, ds * SEGZ + doff: ds * SEGZ + doff + 1].rearrange("(c p) o -> p c o", p=128), fz[:].unsqueeze(2), ['fz', ('zpad', ds, doff)], [('zpad', ds, doff)], allow_slow_non_contiguous=True)
        S.barrier()
        pf.close()

    if 'fix' in phases:
        phase_fix()
    def phase_att():
        pt = ExitStack()
        NKT = {1: SEG // 128 + 1, 4: SEG // 512 + 1, 16: SEG // 2048 + 1}
        qt = sb("qt", [128, SEG], BF16, pt)
        kt = sb("kt", [128, SEGA], BF16, pt)
        vres = {d: sb("vres%d" % d, [128, d * NKT[d], 130], BF16, pt) for d in (1, 4, 16)}
        acc = [sb("acc%d" % h, [65, SEG], F32, pt) for h in range(2)]
        etf = sb("etf", [128, 1536], F32, pt)
        etb = sb("etb", [128, 1536], BF16, pt)
        pex = [sb("pex%d" % i, [128, 256], BF16, pt) for i in range(8)]
        pm = [sb("pm%d" % i, [128, 256], BF16, pt) for i in range(8)]
        rden = sb("rden", [64, 512], F32, pt)
        ostg = [sb("ostg%d" % i, [64, 512], BF16, pt) for i in range(2)]
        sel = sb("sel", [65, 64], F32, pt)
        S.op('pool', lambda e: e.memset(sel[:], 0.0), [], ['sel'])
        S.op('pool', lambda e: e.memset(sel[64:65, :], 1.0), ['sel'], ['sel'])
        ac = {'u': 0, 'o': 0, 'os': 0}

        def emit_pv(pi, d, r, b, h):
            ob = 4 + ac['o'] % 2
            ac['o'] += 1
            for j in range(2):
                S.mm(lambda e, ob=ob, j=j, pi=pi, d=d, r=r, b=b, h=h: e.matmul(PS[ob][0:65, 0:128], vres[d][:, r * NKT[d] + b + j, h * 65:(h + 1) * 65],
                                                                               pm[pi][:, j * 128:(j + 1) * 128], start=(j == 0), stop=(j == 1)),
                     [('vres', d), ('pm', pi)], [('ps', ob)])
            asl = acc[h][:, r + d * 128 * b: r + d * 128 * b + d * 127 + 1: d]
            if d == 1:
                S.op('dve', lambda e, asl=asl, ob=ob: e.tensor_copy(out=asl, in_=PS[ob][0:65, 0:128]), [('ps', ob)], [('acc', h)])
            else:
                S.op('dve', lambda e, asl=asl, ob=ob: e.tensor_tensor(out=asl, in0=asl, in1=PS[ob][0:65, 0:128], op=ALU.add), [('ps', ob), ('acc', h)], [('acc', h)])

        pend = []
        for seg in range(NSEG):
            for hp in range(ATT_PAIRS):
                dma('sp', qt[:], qT_d[hp * 128:(hp + 1) * 128, seg * SEG:(seg + 1) * SEG], [('qT', seg)], ['qt'])
                dma('sp', kt[:], kT_d[hp * 128:(hp + 1) * 128, seg * SEGA:(seg + 1) * SEGA], [('kT', seg)] + [('kTpad', seg, o) for o in (0, APAD + SEG)], ['kt'])
                dma('sp', etf[:], etab_d[hp], [], ['etf'])
                S.op('pool', lambda e: e.tensor_copy(out=etb[:], in_=etf[:]), ['etf'], ['etb'])
                for d in (1, 4, 16):
                    for r in range(d):
                        base = seg * SEGA + APAD + r - 64 * d
                        src = vatt_d[base: base + d * 128 * NKT[d], hp * 130:(hp + 1) * 130].rearrange("(k p dd) c -> dd p k c", p=128, dd=d)[0]
                        dma('act', vres[d][:, r * NKT[d]:(r + 1) * NKT[d], :], src, [('vatt', seg)] + [('vapad', seg, o) for o in (0, APAD + SEG)], [('vres', d)])
                for di, d in enumerate((1, 4, 16)):
                    L = SEG // d
                    for r in range(d):
                        for b in range(L // 128):
                            for h in range(2):
                                u = ac['u']
                                ac['u'] += 1
                                sbank = u % 4
                                qs = qt[h * 64:(h + 1) * 64, r + d * 128 * b: r + d * 128 * b + d * 127 + 1: d]
                                for j in range(2):
                                    k0 = APAD + r + d * (128 * b - 64 + 128 * j)
                                    ks = kt[h * 64:(h + 1) * 64, k0: k0 + d * 127 + 1: d]
                                    S.mm(lambda e, sbank=sbank, j=j, ks=ks, qs=qs: e.matmul(PS[sbank][:, j * 128:(j + 1) * 128], ks, qs, start=True, stop=True),
                                         ['kt', 'qt'], [('ps', sbank)])
                                pi = u % 8
                                S.op('act', lambda e, pi=pi, sbank=sbank: e.activation(out=pex[pi][:], in_=PS[sbank][:, 0:256], func=AF.Exp, scale=0.125),
                                     [('ps', sbank)], [('pex', pi)])
                                eo = (h * 3 + di) * 256
                                S.op('pool' if u % 3 != 2 else 'dve', lambda e, pi=pi, eo=eo: e.tensor_tensor(out=pm[pi][:], in0=pex[pi][:], in1=etb[:, eo:eo + 256], op=ALU.mult),
                                     [('pex', pi), 'etb'], [('pm', pi)])
                                pend.append((pi, d, r, b, h))
                                if len(pend) > 4:
                                    emit_pv(*pend.pop(0))
                while pend:
                    emit_pv(*pend.pop(0))
                for h in range(2):
                    for t in range(SEG // 512):
                        S.mm(lambda e, h=h, t=t: e.matmul(PS[6][0:64, :], sel[:], acc[h][:, t * 512:(t + 1) * 512], start=True, stop=True), ['sel', ('acc', h)], [('ps', 6)])
                        S.op('dve', lambda e: e.reciprocal(out=rden[:], in_=PS[6][0:64, :]), [('ps', 6)], ['rden'])
                        k = ac['os'] % 2
                        ac['os'] += 1
                        S.op('dve', lambda e, k=k, h=h, t=t: e.tensor_tensor(out=ostg[k][:], in0=acc[h][0:64, t * 512:(t + 1) * 512], in1=rden[:], op=ALU.mult),
                             [('acc', h), 'rden'], [('ostg', k)])
                        row = hp * 128 + h * 64
                        dma('sp', mixT_d[row:row + 64, seg * SEG + t * 512: seg * SEG + (t + 1) * 512], ostg[k][:], [('ostg', k)], [('mixT', seg)])
        S.barrier()
        pt.close()


    if 'att' in phases:
        phase_att()
    def phase_rwkv():
        pr = ExitStack()
        twd = sb("twd", [128, SEG], BF16, pr)
        sad = sb("sad", [128, SEG], BF16, pr)
        sgd = sb("sgd", [128, SEG], BF16, pr)
        mu = sb("mu", [128, 2 * ZC], F32, pr)
        muc = sb("muc", [128, ZC], F32, pr)
        rwp = sb("rwp", [128, 80], F32, pr)
        lrf = sb("lrf", [128, 1024], F32, pr)
        w2b_ = sb("w2b", [128, 1024], BF16, pr)
        a2b_ = sb("a2b", [128, 1024], BF16, pr)
        g2b_ = sb("g2b", [128, 1024], BF16, pr)
        rmask = sb("rmask", [128, 512], F32, pr)
        cmask = [sb("cmask%d" % i, [128, 192], F32, pr) for i in range(2)]
        mSxb = [sb("mSxb%d" % i, [128, 128], BF16, pr) for i in range(2)]
        zin = [sb("zin%d" % i, [128, TT + 2], F32, pr) for i in range(3)]
        NW = 30
        wk = [sb("wk%d" % i, [128, TT], F32, pr) for i in range(NW)]
        bdt = {n: [sb("bd_%s%d" % (n, i), [128, 8, 192 if n == "At" else 128], BF16, pr) for i in range(2)] for n in ("At", "Bt", "Kt", "Vt")}
        gam = [sb("gam%d" % i, [128, TT], F32, pr) for i in range(2)]
        vTs = [sb("vTs%d" % i, [128, TT], F32, pr) for i in range(2)]
        bsum = [sb("bsum%d" % i, [128, TT], F32, pr) for i in range(2)]
        NU = 8
        ub = {n: [sb("u_%s%d" % (n, i), [128, 320 if n == "XM" else 256], BF16, pr) for i in range(NU)] for n in ("XM", "LM", "Z", "G")}
        ub1 = {n: [sb("u1_%s%d" % (n, i), [128, 128], BF16, pr) for i in range(NU)] for n in ("X", "Tt", "Kk", "V", "PpT", "RbT")}
        pw = [[sb("pw%d_%d" % (u, i), [128, 128], BF16, pr) for i in range(4)] for u in range(NU)]
        qg = [sb("qg%d" % i, [128, 128], F32, pr) for i in range(NU)]
        Hf = sb("Hf", [128, 128], F32, pr)
        Hb = [[sb("Hb%d_%d" % (c_, i), [128, 128], BF16, pr) for i in range(2)] for c_ in range(8)]
        ysb = [sb("ysb%d" % i, [128, TT], F32, pr) for i in range(2)]
        ybl = sb("ybl", [128, TT], F32, pr)
        ostr = [sb("ostr%d" % i, [128, TT], BF16, pr) for i in range(2)]
        for n in bdt:
            for i in range(2):
                S.op('pool', lambda e, n=n, i=i: e.memset(bdt[n][i][:], 0.0), [], [('bd', n, i)])
        dma('sp', mu[:], mu_d[:, :], [], ['mu'])
        dma('sp', rwp[:], rwp_d[:, :], [], ['rwp'])
        dma('sp', rmask[:], rmask_d[:, :], [], ['rmask'])
        for (b_, d_) in ((w2b_, w2_d), (a2b_, a2_d), (g2b_, g2_d)):
            dma('sp', lrf[:], d_[:, :], [], ['lrf'])
            S.op('dve', lambda e, b_=b_: e.tensor_copy(out=b_[:], in_=lrf[:]), ['lrf'], ['lrw'])
        S.op('dve', lambda e: e.tensor_tensor(out=muc[:], in0=mu[:, 0:ZC], in1=mu[:, ZC:2 * ZC], op=ALU.add), ['mu'], ['muc'])
        S.op('dve', lambda e: e.tensor_scalar(out=muc[:], in0=muc[:], scalar1=-1.0, scalar2=1.0, op0=ALU.mult, op1=ALU.add), ['muc'], ['muc'])
        for c in range(8):
            S.op('dve', lambda e, c=c: e.tensor_scalar(out=rwp[:, c * 10 + 6:c * 10 + 7], in0=rwp[:, c * 10 + 5:c * 10 + 6], scalar1=-1.0, scalar2=1.0, op0=ALU.mult, op1=ALU.add), ['rwp'], ['rwp'])
        for dirn in range(2):
            mI_ = m_iu if dirn == 0 else m_il
            mSx_ = m_sl if dirn == 0 else m_su
            mS_ = m_su if dirn == 0 else m_sl
            S.op('dve', lambda e, dirn=dirn, mS_=mS_: e.tensor_copy(out=cmask[dirn][:, 0:128], in_=mS_), ['cst'], ['cmask'])
            S.op('dve', lambda e, dirn=dirn, mI_=mI_: e.tensor_tensor(out=cmask[dirn][:, 128:192], in0=mI_[:, 0:64], in1=mI_[:, 64:128], op=ALU.add), ['cst', 'cmask'], ['cmask'])
            S.op('dve', lambda e, dirn=dirn, mSx_=mSx_: e.tensor_copy(out=mSxb[dirn][:], in_=mSx_), ['cst'], ['mSxb'])
        rc = {'z': 0, 'wk': 0, 'ps': 0, 'pf': 0, 'os': 0}

        def newwk():
            i = rc['wk'] % NW
            rc['wk'] += 1
            return wk[i], ('wk', i)

        def psreg():
            i = rc['ps'] % 6
            rc['ps'] += 1
            return PS[i][:, 0:256], ('psr', i)

        def psfull():
            i = 6 + rc['pf'] % 2
            rc['pf'] += 1
            return PS[i], ('psf', i)

        def load_shift(zc, seg, t0, dst=None, dkey=None):
            if dst is None:
                o, ok = newwk()
            else:
                o, ok = dst, dkey
            i = rc['z'] % 3
            rc['z'] += 1
            col = seg * SEGZ + t0
            dma('sp', zin[i][:], zT_d[zc * 128:(zc + 1) * 128, col:col + TT + 2], [('zT', seg)] + [('zpad', seg, o_) for o_ in (0, SEG + 1)], [('zin', i)])
            t1, t1k = newwk()
            S.op('pool', lambda e, i=i, o=o: e.tensor_scalar(out=o[:], in0=zin[i][:, 1:TT + 1], scalar1=muc[:, zc:zc + 1], scalar2=0.0, op0=ALU.mult, op1=ALU.add), [('zin', i), 'muc'], [ok])
            S.op('pool', lambda e, i=i, t1=t1: e.tensor_scalar(out=t1[:], in0=zin[i][:, 0:TT], scalar1=mu[:, zc:zc + 1], scalar2=0.0, op0=ALU.mult, op1=ALU.add), [('zin', i), 'mu'], [t1k])
            S.op('pool', lambda e, o=o, t1=t1: e.tensor_tensor(out=o[:], in0=o[:], in1=t1[:], op=ALU.add), [ok, t1k], [ok])
            S.op('pool', lambda e, i=i, t1=t1: e.tensor_scalar(out=t1[:], in0=zin[i][:, 2:TT + 2], scalar1=mu[:, ZC + zc:ZC + zc + 1], scalar2=0.0, op0=ALU.mult, op1=ALU.add), [('zin', i), 'mu', t1k], [t1k])
            S.op('pool', lambda e, o=o, t1=t1: e.tensor_tensor(out=o[:], in0=o[:], in1=t1[:], op=ALU.add), [ok, t1k], [ok])
            return o, ok

        def lowrank_prep(seg):
            for t in range(TPS):
                t0 = t * TT
                for (zc, dst, fn) in ((24, twd, AF.Tanh), (25, sad, AF.Copy), (26, sgd, AF.Sigmoid)):
                    o, ok = load_shift(zc, seg, t0)
                    S.op('act', lambda e, o=o, dst=dst, fn=fn, t0=t0: e.activation(out=dst[:, t0:t0 + TT], in_=o[:], func=fn), [ok], ['lr'])

        def sig_lowrank(wb, rows, src, c, t0, bias_ap):
            p, pk = psfull()
            r0, r1 = rows
            S.mm(lambda e, p=p: e.matmul(p[:], wb[r0:r1, c * 128:(c + 1) * 128], src[r0:r1, t0:t0 + TT], start=True, stop=True), ['lrw', 'lr'], [pk])
            o, ok = newwk()
            S.op('act', lambda e, p=p, o=o: e.activation(out=o[:], in_=p[:], func=AF.Sigmoid, bias=bias_ap, scale=1.0), [pk, 'rwp'], [ok])
            return o, ok

        def prep_tile(dirn, c, seg, t, bi):
            t0 = t * TT
            fwd = (dirn == 0)
            P = lambda j: rwp[:, c * 10 + j: c * 10 + j + 1]
            rs, rsk = load_shift(c, seg, t0)
            yield
            ks, ksk = load_shift(8 + c, seg, t0)
            yield
            load_shift(16 + c, seg, t0, vTs[bi], ('vTs', bi))
            yield
            for h in range(2):
                S.op('pool', lambda e, h=h: e.tensor_copy(out=bdt["Vt"][bi][h * 64:(h + 1) * 64, :, h * 64:(h + 1) * 64],
                                                          in_=vTs[bi][h * 64:(h + 1) * 64, :].rearrange("p (n t) -> p n t", t=64)),
                     [('vTs', bi)], [('bd', "Vt", bi)])
            kkr, kkrk = newwk()
            S.op('pool', lambda e: e.tensor_scalar(out=kkr[:], in0=ks[:], scalar1=P(4), scalar2=0.0, op0=ALU.mult, op1=ALU.add), [ksk, 'rwp'], [kkrk])
            sq_, sqk = newwk()
            S.op('act', lambda e: e.activation(out=sq_[:], in_=kkr[:], func=AF.Square), [kkrk], [sqk])
            p, pk = psfull()
            S.mm(lambda e, p=p: e.matmul(p[:], bones, sq_[:], start=True, stop=True), ['cst', sqk], [pk])
            rn, rnk = newwk()
            S.op('act', lambda e, p=p: e.activation(out=rn[:], in_=p[:], func=AF.Sqrt), [pk], [rnk])
            yield
            S.op('dve', lambda e: e.tensor_scalar(out=rn[:], in0=rn[:], scalar1=1e-12, scalar2=None, op0=ALU.max), [rnk], [rnk])
            S.op('dve', lambda e: e.reciprocal(out=rn[:], in_=rn[:]), [rnk], [rnk])
            kkn, kknk = newwk()
            S.op('dve', lambda e: e.scalar_tensor_tensor(out=kkn[:], in0=kkr[:], scalar=-1.0, in1=rn[:], op0=ALU.mult, op1=ALU.mult), [kkrk, rnk], [kknk])
            yield
            rows = (0, 64) if fwd else (64, 128)
            sgw, sgwk = sig_lowrank(w2b_, rows, twd, c, t0, P(0 if fwd else 1))
            ag, agk = sig_lowrank(a2b_, rows, sad, c, t0, P(2 if fwd else 3))
            yield
            kp_, kpk = newwk()
            S.op('pool', lambda e: e.tensor_scalar(out=kp_[:], in0=ag[:], scalar1=P(5), scalar2=P(6), op0=ALU.mult, op1=ALU.add), [agk, 'rwp'], [kpk])
            S.op('pool', lambda e: e.tensor_tensor(out=kp_[:], in0=kp_[:], in1=ks[:], op=ALU.mult), [kpk, ksk], [kpk])
            bb, bbk = newwk()
            S.op('dve', lambda e: e.scalar_tensor_tensor(out=bb[:], in0=kkn[:], scalar=-1.0, in1=ag[:], op0=ALU.mult, op1=ALU.mult), [kknk, agk], [bbk])
            yield
            cs, csk = newwk()
            S.op('dve', lambda e: e.tensor_tensor_scan(out=cs[:], data0=rmask[:], data1=sgw[:], initial=0.0, op0=ALU.mult, op1=ALU.add), ['rmask', sgwk], [csk])
            if fwd:
                lg, lgk = cs, csk
            else:
                lg, lgk = newwk()
                S.op('pool', lambda e: e.tensor_tensor(out=lg[:], in0=sgw[:], in1=cs[:], op=ALU.subtract), [sgwk, csk], [lgk])
                S.op('pool', lambda e: e.tensor_tensor(out=lg[:].rearrange("p (n t) -> p n t", t=64), in0=lg[:].rearrange("p (n t) -> p n t", t=64),
                                                       in1=cs[:].rearrange("p (n t) -> p n t", t=64)[:, :, 63:64].to_broadcast([128, 8, 64]), op=ALU.add), [lgk, csk], [lgk])
            yield
            gkey = ('gam', bi)
            S.op('act', lambda e: e.activation(out=gam[bi][:], in_=lg[:], func=AF.Exp, scale=-C0), [lgk], [gkey])
            ig, igk = newwk()
            S.op('act', lambda e: e.activation(out=ig[:], in_=lg[:], func=AF.Exp, scale=C0), [lgk], [igk])
            gp, gpk = newwk()
            S.op('dve', lambda e: e.tensor_tensor(out=gp[:], in0=lg[:], in1=sgw[:], op=ALU.subtract), [lgk, sgwk], [gpk])
            S.op('act', lambda e: e.activation(out=gp[:], in_=gp[:], func=AF.Exp, scale=-C0), [gpk], [gpk])
            yield
            for (nm, a_, ak, b_, bk) in (("At", kkn, kknk, gp, gpk), ("Bt", bb, bbk, ig, igk), ("Kt", kp_, kpk, ig, igk)):
                for h in range(2):
                    eng = 'pool'
                    S.op(eng, lambda e, nm=nm, a_=a_, b_=b_, h=h: e.tensor_tensor(out=bdt[nm][bi][h * 64:(h + 1) * 64, :, h * 64:(h + 1) * 64],
                                                                                   in0=a_[h * 64:(h + 1) * 64, :].rearrange("p (n t) -> p n t", t=64),
                                                                                   in1=b_[h * 64:(h + 1) * 64, :].rearrange("p (n t) -> p n t", t=64), op=ALU.mult),
                         [ak, bk], [('bd', nm, bi)])
                yield
            S.op('dve', lambda e: e.tensor_tensor(out=bdt["At"][bi][:, :, 128:192], in0=rs[:].rearrange("p (n t) -> p n t", t=64),
                                                  in1=gam[bi][:].rearrange("p (n t) -> p n t", t=64), op=ALU.mult), [rsk, gkey], [('Rt', bi)])
            if fwd:
                yield
                agb, agbk = sig_lowrank(a2b_, (64, 128), sad, c, t0, P(3))
                kb_, kbk = newwk()
                S.op('pool', lambda e: e.tensor_scalar(out=kb_[:], in0=agb[:], scalar1=P(5), scalar2=P(6), op0=ALU.mult, op1=ALU.add), [agbk, 'rwp'], [kbk])
                S.op('pool', lambda e: e.tensor_tensor(out=kb_[:], in0=kb_[:], in1=ks[:], op=ALU.mult), [kbk, ksk], [kbk])
                yield
                S.op('pool', lambda e: e.tensor_tensor(out=kb_[:], in0=kb_[:], in1=kp_[:], op=ALU.add), [kbk, kpk], [kbk])
                S.op('dve', lambda e: e.scalar_tensor_tensor(out=kb_[:], in0=rs[:], scalar=P(7), in1=kb_[:], op0=ALU.mult, op1=ALU.mult), [rsk, kbk, 'rwp'], [kbk])
                p2, pk2 = psfull()
                S.mm(lambda e, p2=p2: e.matmul(p2[:], bones, kb_[:], start=True, stop=True), ['cst', kbk], [pk2])
                S.op('act', lambda e, p2=p2: e.activation(out=bsum[bi][:], in_=p2[:], func=AF.Copy), [pk2], [('bsum', bi)])

        def offchain(dirn, n, bi, ui):
            fwd = (dirn == 0)
            mS = (m_su if fwd else m_sl)
            At = bdt["At"][bi][:, n, 0:128]
            AR = bdt["At"][bi][:, n, :]
            Bt = bdt["Bt"][bi][:, n, :]
            Kt = bdt["Kt"][bi][:, n, :]
            Vt = bdt["Vt"][bi][:, n, :]
            Rs = bdt["At"][bi][:, n, 128:192]
            gcol = gam[bi][:, n * 64 + 63: n * 64 + 64] if fwd else gam[bi][:, n * 64: n * 64 + 1]
            kA, kB, kK, kVt, kR, kG = ('bd', "At", bi), ('bd', "Bt", bi), ('bd', "Kt", bi), ('bd', "Vt", bi), ('Rt', bi), ('gam', bi)
            XM, LM, Z, G = ub["XM"][ui], ub["LM"][ui], ub["Z"][ui], ub["G"][ui]
            X, Tt, Kk, V, PpT, RbT = (ub1[k][ui] for k in ("X", "Tt", "Kk", "V", "PpT", "RbT"))
            Bk = XM[:, 192:320]
            K_ = lambda nm: ('u', nm, ui)
            cpc = [ui]

            def cp(dst, src, reads, writes):
                cpc[0] += 1
                if cpc[0] % 3 != 0:
                    S.op('act', lambda e, dst=dst, src=src: e.activation(out=dst, in_=src, func=AF.Copy), reads, writes)
                else:
                    S.op('dve', lambda e, dst=dst, src=src: e.tensor_copy(out=dst, in_=src), reads, writes)
            for (lh, lk, dst, dk) in ((Bt, kB, XM, 'XM'), (Kt, kK, LM, 'LM')):
                p, pk = psreg()
                S.mm(lambda e, p=p, lh=lh: e.matmul(p[:, 0:192], lh, AR, start=True, stop=True), [lk, kA, kR], [pk])
                S.op('dve', lambda e, p=p, dst=dst: e.tensor_tensor(out=dst[:, 0:192], in0=p[:, 0:192], in1=cmask[dirn][:], op=ALU.mult), [pk, 'cmask'], [K_(dk)])
                yield
            p, pk = psreg()
            S.mm(lambda e, p=p: e.matmul(p[:, 0:128], At, Bt, start=True, stop=True), [kA, kB], [pk])
            mSx = (m_sl if fwd else m_su)
            S.op('dve', lambda e, p=p: e.tensor_tensor(out=X[:], in0=p[:, 0:128], in1=mSx, op=ALU.mult), [pk, 'cst'], [K_('X')])
            S.op('dve', lambda e: e.tensor_tensor(out=Tt[:], in0=XM[:, 0:128], in1=identb, op=ALU.add), [K_('XM'), 'cstb'], [K_('Tt')])
            yield
            for (src, sk, dk) in ((At, kA, 'A'), (Bt, kB, 'Bk'), (Kt, kK, 'Kk'), (Vt, kVt, 'V')):
                p, pk = psreg()
                S.mm(lambda e, p=p, src=src: e.matmul(p[:, 0:128], src, identb, start=True, stop=True), [sk, 'cstb'], [pk])
                if dk == 'A':
                    cp(Z[:, 0:128], p[:, 0:128], [pk], [K_('Z0')])
                elif dk == 'Bk':
                    cp(Bk, p[:, 0:128], [pk], [K_('Bk')])
                elif dk == 'Kk':
                    cp(Kk[:], p[:, 0:128], [pk], [K_('Kk')])
                else:
                    cp(V[:], p[:, 0:128], [pk], [K_('V')])
                yield
            Pc, Pck = X[:], K_('X')
            Ptc, Ptck = XM[:, 0:128], K_('XM')
            for lvl in range(5):
                pn = pw[ui][(2 * lvl) % 4]
                pnk = ('pw', ui, (2 * lvl) % 4)
                p1, pk1 = psreg()
                S.mm(lambda e, p1=p1, Pc=Pc, Ptc=Ptc: e.matmul(p1[:, 0:128], Ptc, Pc, start=True, stop=True), [Pck, Ptck], [pk1])
                if lvl < 4:
                    ptn = pw[ui][(2 * lvl + 1) % 4]
                    ptnk = ('pw', ui, (2 * lvl + 1) % 4)
                    p2, pk2 = psreg()
                    S.mm(lambda e, p2=p2, Pc=Pc, Ptc=Ptc: e.matmul(p2[:, 0:128], Pc, Ptc, start=True, stop=True), [Pck, Ptck], [pk2])
                    if lvl % 2 == 0:
                        cp(ptn[:], p2[:, 0:128], [pk2], [ptnk])
                    else:
                        cp(ptn[:], p2[:, 0:128], [pk2], [ptnk])
                cp(pn[:], p1[:, 0:128], [pk1], [pnk])
                Pc, Pck = pn[:], pnk
                if lvl < 4:
                    Ptc, Ptck = ptn[:], ptnk
                yield
                p3, pk3 = psreg()
                S.mm(lambda e, p3=p3, Pc=Pc: e.matmul(p3[:, 0:128], Pc, Tt[:], start=True, stop=True), [Pck, K_('Tt')], [pk3])
                S.op('dve', lambda e, p3=p3: e.tensor_tensor(out=Tt[:], in0=Tt[:], in1=p3[:, 0:128], op=ALU.add), [pk3, K_('Tt')], [K_('Tt')])
                if lvl == 4:
                    yield
            p, pk = psreg()
            S.mm(lambda e, p=p: e.matmul(p[:, 0:128], LM[:, 0:128], V[:], start=True, stop=True), [K_('LM'), K_('V')], [pk])
            cp(Z[:, 128:256], p[:, 0:128], [pk], [K_('Z1')])
            yield
            p, pk = psreg()
            S.mm(lambda e, p=p: e.matmul(p[:, 0:256], Tt[:], Z[:], start=True, stop=True), [K_('Tt'), K_('Z0'), K_('Z1')], [pk])
            cp(G[:], p[:, 0:256], [pk], [K_('G')])
            yield
            p, pk = psreg()
            S.mm(lambda e, p=p: e.matmul(p[:, 0:192], G[:, 0:128], XM[:, 128:320], start=True, stop=True), [K_('G'), K_('Bk'), K_('XM')], [pk])
            S.op('dve', lambda e, p=p: e.tensor_tensor(out=PpT[:], in0=p[:, 64:192], in1=ident, op=ALU.add), [pk, 'cst'], [K_('PpT')])
            S.op('dve', lambda e, p=p: e.tensor_tensor(out=RbT[:, 0:64], in0=p[:, 0:64], in1=Rs, op=ALU.add), [pk, kR], [K_('RbT')])
            p, pk = psreg()
            S.mm(lambda e, p=p: e.matmul(p[:, 0:128], Bk, G[:, 128:256], start=True, stop=False), [K_('G'), K_('Bk')], [pk])
            S.mm(lambda e, p=p: e.matmul(p[:, 0:128], Kk[:], V[:], start=False, stop=True), [K_('Kk'), K_('V')], [pk])
            S.op('dve', lambda e, p=p: e.tensor_scalar(out=qg[ui][:], in0=p[:, 0:128], scalar1=gcol, scalar2=None, op0=ALU.mult), [pk, kG], [K_('qg')])

        def chain(dirn, n, bi, ui, Hcur, Hnew, ysl, yslk):
            fwd = (dirn == 0)
            gcol = gam[bi][:, n * 64 + 63: n * 64 + 64] if fwd else gam[bi][:, n * 64: n * 64 + 1]
            kG = ('gam', bi)
            XM, LM, G = ub["XM"][ui], ub["LM"][ui], ub["G"][ui]
            V, PpT, RbT = (ub1[k][ui] for k in ("V", "PpT", "RbT"))
            K_ = lambda nm: ('u', nm, ui)
            p, pk = psreg()
            S.mm(lambda e, p=p: e.matmul(p[:, 0:64], G[:, 128:256], XM[:, 128:192], start=True, stop=False), [K_('G'), K_('XM')], [pk])
            S.mm(lambda e, p=p: e.matmul(p[:, 0:64], V[:], LM[:, 128:192], start=False, stop=False), [K_('V'), K_('LM')], [pk])
            S.mm(lambda e, p=p: e.matmul(p[:, 0:64], Hcur[0][:], RbT[:, 0:64], start=False, stop=True), [Hcur[1], K_('RbT')], [pk])
            S.op('act', lambda e, p=p: e.activation(out=ysl, in_=p[:, 0:64], func=AF.Copy), [pk], [yslk])
            p2, pk2 = psreg()
            S.mm(lambda e, p2=p2: e.matmul(p2[:, 0:128], PpT[:], Hcur[0][:], start=True, stop=True), [K_('PpT'), Hcur[1]], [pk2])
            S.op('dve', lambda e, p2=p2: e.scalar_tensor_tensor(out=Hnew[0][:], in0=p2[:, 0:128], scalar=gcol, in1=qg[ui][:], op0=ALU.mult, op1=ALU.add),
                 [pk2, K_('qg'), kG], [Hnew[1]])

        def epilogue(c, seg, t, bi, yt, ytk):
            t0 = t * TT
            P = lambda j: rwp[:, c * 10 + j: c * 10 + j + 1]
            bon, bonk = newwk()
            S.op('pool', lambda e: e.tensor_tensor(out=bon[:], in0=vTs[bi][:], in1=bsum[bi][:], op=ALU.mult), [('vTs', bi), ('bsum', bi)], [bonk])
            dma('sp', ybl[:], ybT_d[c * 128:(c + 1) * 128, seg * SEG + t0: seg * SEG + t0 + TT], [('ybT', c)], ['ybl'])
            ysum, ysk = newwk()
            S.op('dve', lambda e: e.tensor_tensor(out=ysum[:], in0=yt[:], in1=ybl[:], op=ALU.add), [ytk, 'ybl'], [ysk])
            yield
            p, pk = psfull()
            S.mm(lambda e, p=p: e.matmul(p[:], bones, ysum[:], start=True, stop=True), ['cst', ysk], [pk])
            yield
            yc, yck = newwk()
            S.op('dve', lambda e, p=p: e.scalar_tensor_tensor(out=yc[:], in0=p[:], scalar=-1.0 / 64, in1=ysum[:], op0=ALU.mult, op1=ALU.add), [pk, ysk], [yck])
            yield
            sq_, sqk = newwk()
            S.op('act', lambda e: e.activation(out=sq_[:], in_=yc[:], func=AF.Square), [yck], [sqk])
            yield
            p2, pk2 = psfull()
            S.mm(lambda e, p2=p2: e.matmul(p2[:], bones, sq_[:], start=True, stop=True), ['cst', sqk], [pk2])
            yield
            sd, sdk = newwk()
            S.op('act', lambda e, p2=p2: e.activation(out=sd[:], in_=p2[:], func=AF.Sqrt, scale=1.0 / 64, bias=epsl[:]), [pk2, 'epsl'], [sdk])
            yield
            S.op('dve', lambda e: e.reciprocal(out=sd[:], in_=sd[:]), [sdk], [sdk])
            yield
            S.op('dve', lambda e: e.tensor_tensor(out=yc[:], in0=yc[:], in1=sd[:], op=ALU.mult), [yck, sdk], [yck])
            yield
            S.op('pool', lambda e: e.tensor_scalar(out=yc[:], in0=yc[:], scalar1=P(8), scalar2=P(9), op0=ALU.mult, op1=ALU.add), [yck, 'rwp'], [yck])
            yield
            S.op('pool', lambda e: e.tensor_tensor(out=yc[:], in0=yc[:], in1=bon[:], op=ALU.add), [yck, bonk], [yck])
            p3, pk3 = psfull()
            S.mm(lambda e, p3=p3: e.matmul(p3[:], g2b_[:, c * 128:(c + 1) * 128], sgd[:, t0:t0 + TT], start=True, stop=True), ['lrw', 'lr'], [pk3])
            yield
            k = rc['os'] % 2
            rc['os'] += 1
            S.op('dve', lambda e, p3=p3, k=k: e.tensor_tensor(out=ostr[k][:], in0=yc[:], in1=p3[:], op=ALU.mult), [yck, pk3], [('ostr', k)])
            dma('sp', mixT_d[1024 + c * 128: 1024 + (c + 1) * 128, seg * SEG + t0: seg * SEG + t0 + TT], ostr[k][:], [('ostr', k)], [('mixT', seg)])

        tiles = []
        for (dirn, seg, si) in ((1, 1, 0), (1, 0, 1), (0, 0, 0), (0, 1, 1)):
            for c in range(RW_PAIRS):
                trange = range(TPS - 1, -1, -1) if dirn == 1 else range(TPS)
                for ti, t in enumerate(trange):
                    tiles.append(dict(c=c, dirn=dirn, seg=seg, t=t, first=(si == 0 and ti == 0), link=(si == 1 and ti == 0)))
        cur_seg = None
        hic = [0] * 8
        early = None
        pend_epi = None
        for k, tl in enumerate(tiles):
            bi = k % 2
            c, dirn, seg, t = tl['c'], tl['dirn'], tl['seg'], tl['t']
            if cur_seg != seg:
                lowrank_prep(seg)
                cur_seg = seg
            if early is None:
                early = prep_tile(dirn, c, seg, t, bi)
            for _ in early:
                pass
            early = None
            hi = hic[c]
            if tl['first']:
                hi = 0
                S.op('pool', lambda e, c=c: e.memset(Hb[c][0][:], 0.0), [], [('Hb', c, 0)])
            if tl['link']:
                S.op('dve', lambda e, hi=hi, c=c: e.tensor_scalar(out=Hb[c][hi][:], in0=Hb[c][hi][:], scalar1=link[:, 0:1], scalar2=None, op0=ALU.mult), [('Hb', c, hi), 'link'], [('Hb', c, hi)])
            chunks = list(range(7, -1, -1)) if dirn == 1 else list(range(8))
            nprep = None
            if k + 1 < len(tiles) and tiles[k + 1]['seg'] == seg:
                n2 = tiles[k + 1]
                nprep = prep_tile(n2['dirn'], n2['c'], n2['seg'], n2['t'], (k + 1) % 2)
                early = nprep

            def adv():
                nonlocal nprep, pend_epi
                if pend_epi is not None:
                    try:
                        next(pend_epi)
                    except StopIteration:
                        pend_epi = None
                if nprep is not None:
                    try:
                        next(nprep)
                    except StopIteration:
                        nprep = None
            yi = k % 2
            gens = [offchain(dirn, n, bi, ui) for ui, n in enumerate(chunks)]
            live = []
            started = 0
            finished = [False] * 8
            nchain = 0
            while nchain < 8:
                if started < 8:
                    live.append((started, gens[started]))
                    started += 1
                nxt = []
                for (ui, g) in live:
                    try:
                        next(g)
                        nxt.append((ui, g))
                    except StopIteration:
                        finished[ui] = True
                live = nxt
                if finished[nchain]:
                    n = chunks[nchain]
                    chain(dirn, n, bi, nchain, (Hb[c][hi], ('Hb', c, hi)), (Hb[c][1 - hi], ('Hb', c, 1 - hi)), ysb[yi][:, n * 64:(n + 1) * 64], ('ysb', yi))
                    hi = 1 - hi
                    nchain += 1
                adv()
            hic[c] = hi
            if pend_epi is not None:
                for _ in pend_epi:
                    pass
                pend_epi = None
            if dirn == 1:
                dma('sp', ybT_d[c * 128:(c + 1) * 128, seg * SEG + t * TT: seg * SEG + (t + 1) * TT], ysb[yi][:], [('ysb', yi)], [('ybT', c)])
            else:
                pend_epi = epilogue(c, seg, t, bi, ysb[yi], ('ysb', yi))
                if not (k + 1 < len(tiles) and tiles[k + 1]['seg'] == seg):
                    for _ in pend_epi:
                        pass
                    pend_epi = None
        S.barrier()
        pr.close()

    if 'rwkv' in phases:
        phase_rwkv()
    def phase_C():
        pc = ExitStack()
        env = ffn_env(pc, 'C')
        for it in range(NT):
            env['wout'](it)
            env['rmsnorm_to_nT'](2)
            env['ffn'](1)
            env['store_y_tile'](it, 3)
        S.barrier()
        pc.close()

    if 'C' in phases:
        phase_C()
    S.barrier()
    S.emit()
    global LAST_S
    LAST_S = S
    return nc

LAST_S = None

def _const_tables():
    tri = np.ones((64, 64), np.float32)
    su1, sl1, iu1, il1 = np.triu(tri, 1), np.tril(tri, -1), np.triu(tri, 0), np.tril(tri, 0)

    def bdm(m):
        o = np.zeros((128, 128), np.float32)
        o[:64, :64] = m
        o[64:, 64:] = m
        return o
    cst = np.zeros((128, 1024), np.float32)
    cst[:, 0:128] = np.eye(128, dtype=np.float32)
    cst[:, 128:256] = bdm(tri)
    cst[:, 256:384] = bdm(su1)
    cst[:, 384:512] = bdm(sl1)
    cst[:, 512:640] = bdm(iu1)
    cst[:, 640:768] = bdm(il1)
    rmask = np.ones((128, 512), np.float32)
    rmask[:, ::64] = 0.0
    slopes = np.exp2(-8.0 * np.arange(1, 17, dtype=np.float64) / 16)
    kp = np.arange(128)[:, None]
    qp = np.arange(128)[None, :]
    etab = np.zeros((8, 128, 1536), np.float32)
    for h in range(16):
        for di, d in enumerate((1, 4, 16)):
            for j in range(2):
                rel = kp + 128 * j - 64 - qp
                e = np.where(np.abs(rel) <= 64, np.exp(-slopes[h] * d * np.abs(rel)), 0.0)
                o = ((h % 2) * 3 + di) * 256 + j * 128
                etab[h // 2, :, o:o + 128] = e
    return cst, rmask, etab


def _prep_weights(inp):
    f = np.float32
    A = lambda a: np.ascontiguousarray(np.asarray(a, dtype=f))
    out = {}
    for fi, pre in enumerate(("ffn1", "ffn2")):
        g = np.asarray(inp[pre + "_gate"][0]).reshape(DC, 128, FC, 128).transpose(2, 1, 0, 3).reshape(FC, 128, 2048)
        u = np.asarray(inp[pre + "_up"][0]).reshape(DC, 128, FC, 128).transpose(2, 1, 0, 3).reshape(FC, 128, 2048)
        out["wgu%d" % (fi + 1)] = A(np.concatenate([g, u], axis=2).reshape(FF, 4096))
        dn = np.asarray(inp[pre + "_down"][0]).reshape(FC, 128, DC, 128).transpose(2, 1, 0, 3).reshape(D, FF)
        out["wd%d" % (fi + 1)] = A(dn)
    w_in = np.asarray(inp["w_in"][0])
    fm_cols = [c * 128 for c in range(16)] + [3072 + c * 128 for c in range(ZC)]
    fm = [w_in[:, c0:c0 + 128].reshape(DC, 128, 128).transpose(1, 0, 2).reshape(128, 2048) for c0 in fm_cols]
    out["winfm"] = A(np.concatenate(fm, axis=0))
    tm_cols = [2048 + g * 256 for g in range(4)]
    tm = [w_in[:, c0:c0 + 256].reshape(DC, 128, 256).transpose(1, 0, 2).reshape(128, 4096) for c0 in tm_cols]
    out["wintm"] = A(np.concatenate(tm, axis=0))
    out["wo"] = A(np.asarray(inp["w_out"][0]).reshape(DC, 128, DC, 128).transpose(2, 1, 0, 3).reshape(D, D))
    gains = np.zeros((128, 4 * DC), f)
    for gi, g in enumerate((inp["ffn1_norm"][0], inp["mix_norm"][0], inp["ffn2_norm"][0], inp["final_norm"])):
        gains[:, gi * DC:(gi + 1) * DC] = np.asarray(g).reshape(DC, 128).T
    out["gains"] = gains
    mu = np.zeros((128, 2 * ZC), f)
    mu[:, 0:ZC] = np.asarray(inp["mu_prev"][0]).reshape(ZC, 128).T
    mu[:, ZC:] = np.asarray(inp["mu_next"][0]).reshape(ZC, 128).T
    out["mu"] = mu
    rwp = np.zeros((128, 80), f)
    vecs = (inp["w0_f"][0], inp["w0_b"][0], inp["a0_f"][0], inp["a0_b"][0], inp["k_k"][0], inp["k_a"][0], None,
            np.asarray(inp["r_k"][0]).reshape(1024), inp["ln_x_w"][0], inp["ln_x_b"][0])
    for j, v in enumerate(vecs):
        if v is None:
            continue
        rwp[:, j::10] = np.asarray(v).reshape(8, 128).T
    out["rwp"] = rwp
    out["w2"] = A(np.concatenate([inp["w2_f"][0], inp["w2_b"][0]], axis=0))
    out["a2"] = A(np.concatenate([inp["a2_f"][0], inp["a2_b"][0]], axis=0))
    out["g2"] = A(inp["g2"][0])
    cst, rmask, etab = _const_tables()
    out["cst"], out["rmask"], out["etab"] = cst, rmask, etab
    return out


def _core_plan(SEG, x_prompt, x_sample):
    plan = []
    xp, xs = np.asarray(x_prompt), np.asarray(x_sample)
    nprompt_cores = xp.shape[0] * (xp.shape[1] // (NSEG * SEG))
    for b in range(xp.shape[0]):
        for part in range(xp.shape[1] // (NSEG * SEG)):
            assert xp.shape[1] == NSEG * SEG
            plan.append(([('p', b, 0), ('p', b, SEG)], 1.0))
    nb = xs.shape[0]
    assert xs.shape[1] == SEG
    rest = 8 - len(plan)
    two = nb - rest
    i = 0
    for c in range(rest):
        if c < two:
            plan.append(([('s', i, 0), ('s', i + 1, 0)], 0.0))
            i += 2
        elif i < nb:
            plan.append(([('s', i, 0), None], 0.0))
            i += 1
        else:
            plan.append(([None, None], 0.0))
    assert i == nb
    return plan


_NC_CACHE = {}


def run(inputs, SEG, debug=False, phases=('A', 'fix', 'att', 'rwkv', 'C')):
    wts = _prep_weights(inputs)
    xp, xs = np.asarray(inputs["x_prompt"], np.float32), np.asarray(inputs["x_sample"], np.float32)
    plan = _core_plan(SEG, xp, xs)
    in_maps = []
    for segs, lk in plan:
        x = np.zeros((NSEG * SEG, D), np.float32)
        for si, sdesc in enumerate(segs):
            if sdesc is None:
                continue
            kind, b, st = sdesc
            src = xp if kind == 'p' else xs
            x[si * SEG:(si + 1) * SEG] = src[b, st:st + SEG]
        m = dict(wts)
        m["x"] = x
        m["link"] = np.full((128, 1), lk, np.float32)
        in_maps.append(m)
    key = (SEG, debug, tuple(phases))
    if key not in _NC_CACHE:
        _NC_CACHE[key] = build(SEG, debug=debug, phases=phases)
    nc = _NC_CACHE[key]
    res = run_bass_kernel_spmd(nc, in_maps, core_ids=list(range(8)))
    yp = np.zeros(xp.shape, np.float32)
    ys = np.zeros(xs.shape, np.float32)
    for ci, (segs, lk) in enumerate(plan):
        y = res.results[ci]["y"]
        for si, sdesc in enumerate(segs):
            if sdesc is None:
                continue
            kind, b, st = sdesc
            (yp if kind == 'p' else ys)[b, st:st + SEG] = y[si * SEG:(si + 1) * SEG]
    return (yp, ys), res, plan


def kernel(**inputs):
    (yp, ys), _, _ = run(inputs, 4096)
    return (yp, ys)
```

```python
from contextlib import ExitStack
import numpy as np
import concourse.bass as bass
import concourse.mybir as mybir
from concourse.bass_utils import run_bass_kernel_spmd

F32 = mybir.dt.float32
BF16 = mybir.dt.bfloat16
ALU = mybir.AluOpType
AF = mybir.ActivationFunctionType

D = 2048
DC = 16
FF = 5632
FC = 44
TT = 512
NSEG = 2
APAD = 1024
NFM = 43
ZC = 27
C0 = 0.6065306597126334
NORM_EPS = 1e-6
LN_EPS = 64e-5

ENGS = ("pe", "dve", "act", "pool", "sp")
NDMASEM = 64
NHWSEM = 40


class Sched:
    def __init__(self, nc, es):
        self.nc = nc
        self.csem = {e: es.enter_context(nc.semaphore("c_" + e)) for e in ENGS}
        self.dsem = [es.enter_context(nc.semaphore("d%d" % i)) for i in range(NDMASEM)]
        self.dval = [0] * NDMASEM
        self.dnext = 0
        self.dnext_sw = 0
        self.count = {e: 0 for e in ENGS}
        self.prog = {e: [] for e in ENGS}
        self.waited = {e: {} for e in ENGS}
        self.lastw = {}
        self.reads = {}

    def _need(self, eng, tok, deps):
        if tok is None:
            return
        if tok[0] == 'e':
            _, e2, idx = tok
            if e2 == eng:
                if eng == 'pe':
                    return
            key = e2
        else:
            _, i, idx = tok
            key = ('d', i)
        if self.waited[eng].get(key, 0) >= idx:
            return
        if deps.get(key, 0) < idx:
            deps[key] = idx

    def _emit_waits(self, eng, deps):
        for key, idx in deps.items():
            sem = self.csem[key] if isinstance(key, str) else self.dsem[key[1]]
            self.prog[eng].append(('w', sem, idx))
            self.waited[eng][key] = idx

    def _deps(self, eng, reads, writes):
        deps = {}
        for k in reads:
            self._need(eng, self.lastw.get(k), deps)
        for k in writes:
            self._need(eng, self.lastw.get(k), deps)
            for t in self.reads.get(k, ()):
                self._need(eng, t, deps)
        return deps

    def _record(self, tok, reads, writes):
        for k in reads:
            self.reads.setdefault(k, []).append(tok)
        for k in writes:
            self.lastw[k] = tok
            self.reads[k] = []

    def op(self, eng, fn, reads=(), writes=()):
        deps = self._deps(eng, reads, writes)
        self._emit_waits(eng, deps)
        self.count[eng] += 1
        self.prog[eng].append(('o', fn, self.csem[eng], 1))
        tok = ('e', eng, self.count[eng])
        self._record(tok, reads, writes)
        return tok

    def mm(self, fn, reads=(), writes=(), last=True):
        return self.op('pe', fn, reads, writes)

    def dma(self, eng, fn, reads=(), writes=()):
        deps = self._deps(eng, reads, writes)
        if eng == 'pool':
            i = NHWSEM + self.dnext_sw
            self.dnext_sw = (self.dnext_sw + 1) % (NDMASEM - NHWSEM)
        else:
            i = self.dnext
            self.dnext = (self.dnext + 1) % NHWSEM
        if self.dval[i] > 0:
            self._need(eng, ('d', i, self.dval[i]), deps)
        self._emit_waits(eng, deps)
        self.dval[i] += 16
        self.prog[eng].append(('o', fn, self.dsem[i], 16))
        tok = ('d', i, self.dval[i])
        self._record(tok, reads, writes)
        return tok

    def barrier(self):
        for eng in ENGS:
            deps = {}
            for e2 in ENGS:
                if e2 != eng and self.count[e2] > 0:
                    self._need(eng, ('e', e2, self.count[e2]), deps)
            for i in range(NDMASEM):
                if self.dval[i] > 0:
                    self._need(eng, ('d', i, self.dval[i]), deps)
            self._emit_waits(eng, deps)

    def emit(self):
        prog = self.prog

        def run(engobj, items):
            for it in items:
                if it[0] == 'w':
                    engobj.wait_ge(it[1], it[2])
                else:
                    ins = it[1](engobj)
                    if it[2] is not None:
                        ins.then_inc(it[2], it[3])

        with self.nc.Block() as block:
            @block.sync
            def _(e):
                run(e, prog['sp'])

            @block.tensor
            def _(e):
                run(e, prog['pe'])

            @block.vector
            def _(e):
                run(e, prog['dve'])

            @block.scalar
            def _(e):
                run(e, prog['act'])

            @block.gpsimd
            def _(e):
                run(e, prog['pool'])


def build(SEG, debug=False, phases=('A', 'fix', 'att', 'rwkv', 'C'), RW_PAIRS=8, ATT_PAIRS=8):
    NTOK = NSEG * SEG
    NT = NTOK // TT
    TPS = SEG // TT
    SEGA = SEG + 2 * APAD
    SEGZ = SEG + 2
    nc = bass.Bass("TRN2", target_bir_lowering=False)
    es = ExitStack()
    S = Sched(nc, es)

    def din(name, shape, dt=F32):
        return nc.dram_tensor(name, list(shape), dt, kind="ExternalInput").ap()

    def dscr(name, shape, dt):
        if debug:
            return nc.dram_tensor(name, list(shape), dt, kind="ExternalOutput").ap()
        return nc.dram_tensor(name, list(shape), dt).ap()

    x_d = din("x", [NTOK, D])
    link_d = din("link", [128, 1])
    wgu_h = [din("wgu1", [FF, 4096]), din("wgu2", [FF, 4096])]
    wd_h = [din("wd1", [D, FF]), din("wd2", [D, FF])]
    winfm_h = din("winfm", [NFM * 128, 2048])
    wintm_h = din("wintm", [4 * 128, 4096])
    wo_h = din("wo", [D, D])
    gains_d = din("gains", [128, 4 * DC])
    mu_d = din("mu", [128, 2 * ZC])
    rwp_d = din("rwp", [128, 8 * 10])
    w2_d = din("w2", [128, 1024])
    a2_d = din("a2", [128, 1024])
    g2_d = din("g2", [128, 1024])
    etab_d = din("etab", [8, 128, 2 * 3 * 256])
    cst_d = din("cst", [128, 8 * 128])
    rmask_d = din("rmask", [128, 512])
    y_d = nc.dram_tensor("y", [NTOK, D], F32, kind="ExternalOutput").ap()

    wgu_b = [dscr("wgu1b", [FF, 4096], BF16), dscr("wgu2b", [FF, 4096], BF16)]
    wd_b = [dscr("wd1b", [D, FF], BF16), dscr("wd2b", [D, FF], BF16)]
    winfm_b = dscr("winfmb", [NFM * 128, 2048], BF16)
    wintm_b = dscr("wintmb", [4 * 128, 4096], BF16)
    wo_b = dscr("wob", [D, D], BF16)
    x1T_d = dscr("x1T", [D, NTOK], F32)
    qT_d = dscr("qT", [1024, NTOK], BF16)
    kT_d = dscr("kT", [1024, NSEG * SEGA], BF16)
    vatt_d = dscr("vatt", [NSEG * SEGA + 64, 16 * 65], BF16)
    zT_d = dscr("zT", [ZC * 128, NSEG * SEGZ], F32)
    ybT_d = dscr("ybT", [1024, NTOK], F32)
    mixT_d = dscr("mixT", [D, NTOK], BF16)

    def sb(name, shape, dt, stack=es):
        return stack.enter_context(nc.sbuf_tensor("s_" + name, list(shape), dt))

    PS = [es.enter_context(nc.psum_tensor("ps%d" % i, [128, 512], F32)) for i in range(8)]

    def dma(eng, out, in_, reads, writes, **kw):
        return S.dma(eng, lambda e, o=out, i=in_, k=kw: e.dma_start(out=o, in_=i, **k), reads, writes)

    cst = sb("cst", [128, 8 * 128], F32)
    cstb = sb("cstb", [128, 8 * 128], BF16)
    gains = sb("gains", [128, 4 * DC], F32)
    link = sb("link", [128, 1], F32)
    zeros = sb("zeros", [128, 1040], BF16)
    zerosf = sb("zerosf", [128, 128], F32)
    epsn = sb("epsn", [128, 1], F32)
    epsl = sb("epsl", [128, 1], F32)
    onesb = sb("onesb", [128, 128], BF16)
    dma('sp', cst[:], cst_d[:, :], [], ['cst'])
    dma('sp', gains[:], gains_d[:, :], [], ['gains'])
    dma('sp', link[:], link_d[:, :], [], ['link'])
    S.op('dve', lambda e: e.tensor_copy(out=cstb[:], in_=cst[:]), ['cst'], ['cstb'])
    S.op('pool', lambda e: e.memset(zeros[:], 0.0), [], ['zeros'])
    S.op('pool', lambda e: e.memset(zerosf[:], 0.0), [], ['zerosf'])
    S.op('pool', lambda e: e.memset(epsn[:], NORM_EPS), [], ['epsn'])
    S.op('pool', lambda e: e.memset(epsl[:], LN_EPS), [], ['epsl'])
    S.op('pool', lambda e: e.memset(onesb[:], 1.0), [], ['onesb'])
    ident = cst[:, 0:128]
    identb = cstb[:, 0:128]
    bones = cst[:, 128:256]
    m_su = cst[:, 256:384]
    m_sl = cst[:, 384:512]
    m_iu = cst[:, 512:640]
    m_il = cst[:, 640:768]

    def cast_rows(src, dst, r0, r1, key):
        dma('pool', dst[r0:r1, :], src[r0:r1, :], [], [key], max_dma_last_dim=4096)

    for j in range(0, FC, 2):
        cast_rows(wgu_h[0], wgu_b[0], j * 128, (j + 2) * 128, ('wgu0', j // 2))
    for m in range(DC):
        cast_rows(wd_h[0], wd_b[0], m * 128, (m + 1) * 128, ('wd0', m))
    for c in range(0, NFM, 4):
        cast_rows(winfm_h, winfm_b, c * 128, min(NFM, c + 4) * 128, ('winfm', c // 4))
    for g in range(0, 4, 2):
        cast_rows(wintm_h, wintm_b, g * 128, (g + 2) * 128, ('wintm', g // 2))
    for m in range(0, DC, 4):
        cast_rows(wo_h, wo_b, m * 128, (m + 4) * 128, ('wo', m // 4))
    for j in range(0, FC, 2):
        cast_rows(wgu_h[1], wgu_b[1], j * 128, (j + 2) * 128, ('wgu1', j // 2))
    for m in range(DC):
        cast_rows(wd_h[1], wd_b[1], m * 128, (m + 1) * 128, ('wd1', m))

    for s in range(NSEG):
        for off in (0, APAD + SEG):
            for c in range(8):
                dma('act', kT_d[c * 128:(c + 1) * 128, s * SEGA + off: s * SEGA + off + APAD], zeros[:, 0:APAD],
                    ['zeros'], [('kTpad', s, off)])
            for a in range(APAD // 128):
                r0 = s * SEGA + off + a * 128
                dma('act', vatt_d[r0:r0 + 128, :], zeros[:, 0:1040], ['zeros'], [('vapad', s, off)])
        for off in (0, SEG + 1):
            dma('act', zT_d[:, s * SEGZ + off: s * SEGZ + off + 1].rearrange("(c p) o -> p c o", p=128),
                zerosf[:, 0:ZC].unsqueeze(2), ['zerosf'], [('zpad', s, off)], allow_slow_non_contiguous=True)

    _sb_outer = sb

    def ffn_env(pa, tag):
        def sb(name, shape, dt, stack=es):
            return _sb_outer(name + tag, shape, dt, stack)
        xT = sb("xT", [128, DC, TT], F32, pa)
        nT = sb("nT", [128, DC, TT], BF16, pa)
        actT = sb("actT", [128, FC, TT], BF16, pa)
        wring = [sb("wring%d" % i, [128, 4096], BF16, pa) for i in range(3)]
        dring = [sb("dring%d" % i, [128, 22 * 128], BF16, pa) for i in range(3)]
        xtok = [sb("xtok%d" % i, [128, 1024], F32, pa) for i in range(2)]
        sq = [sb("sq%d" % i, [128, TT], BF16, pa) for i in range(2)]
        rstd = sb("rstd", [128, TT], F32, pa)
        sg = [sb("sg%d" % i, [128, TT], F32, pa) for i in range(2)]
        stg = [sb("stg%d" % i, [128, TT], F32, pa) for i in range(3)]
        stgb = [sb("stgb%d" % i, [128, TT], BF16, pa) for i in range(2)]
        vstg = [sb("vstg%d" % i, [128, 4 * 65], BF16, pa) for i in range(2)]
        cnt = {'w': 0, 'd': 0, 'x': 0, 'sq': 0, 'sg': 0, 'stg': 0, 'stgb': 0, 'tp': 0, 'vs': 0, 'ev': 0}
        xT_keys = [('xT', dc) for dc in range(DC)]

        def evac_copy(out, in_, reads, writes):
            cnt['ev'] += 1
            if cnt['ev'] % 2:
                S.op('act', lambda e, o=out, i=in_: e.activation(out=o, in_=i, func=AF.Copy), reads, writes)
            else:
                S.op('dve', lambda e, o=out, i=in_: e.tensor_copy(out=o, in_=i), reads, writes)

        def load_x_tile(it):
            for s in range(4):
                for hf in range(2):
                    xb_ = cnt['x'] % 2
                    cnt['x'] += 1
                    r0 = it * TT + s * 128
                    dma('sp', xtok[xb_][:], x_d[r0:r0 + 128, hf * 1024:(hf + 1) * 1024], [], [('xtok', xb_)])
                    for g in range(2):
                        bank = 6 + cnt['tp'] % 2
                        cnt['tp'] += 1
                        for q in range(4):
                            S.mm(lambda e, b=bank, q=q, g=g, xb_=xb_: e.transpose(PS[b][:, q * 128:(q + 1) * 128],
                                                                                 xtok[xb_][:, (g * 4 + q) * 128:(g * 4 + q + 1) * 128], ident),
                                 [('xtok', xb_), 'cst'], [('ps', bank)], last=(q == 3))
                        dc0 = hf * 8 + g * 4
                        evac_copy(xT[:, dc0:dc0 + 4, s * 128:(s + 1) * 128], PS[bank][:].rearrange("p (q t) -> p q t", q=4),
                                  [('ps', bank)], [('xT', dc0 + q) for q in range(4)])

        def store_y_tile(it, gi):
            rms_stats()
            for dc in range(DC):
                if dc % 2 == 0:
                    S.op('dve', lambda e, dc=dc: e.scalar_tensor_tensor(out=xT[:, dc, :], in0=xT[:, dc, :], scalar=gains[:, gi * DC + dc: gi * DC + dc + 1],
                                                                        in1=rstd[:], op0=ALU.mult, op1=ALU.mult), [('xT', dc), 'rstd', 'gains'], [('xT', dc)])
                else:
                    S.op('pool', lambda e, dc=dc: e.tensor_scalar(out=xT[:, dc, :], in0=xT[:, dc, :], scalar1=gains[:, gi * DC + dc: gi * DC + dc + 1], scalar2=0.0, op0=ALU.mult, op1=ALU.add),
                         [('xT', dc), 'gains'], [('xT', dc)])
                    S.op('pool', lambda e, dc=dc: e.tensor_tensor(out=xT[:, dc, :], in0=xT[:, dc, :], in1=rstd[:], op=ALU.mult), [('xT', dc), 'rstd'], [('xT', dc)])
            for s in range(4):
                for hf in range(2):
                    xb_ = cnt['x'] % 2
                    cnt['x'] += 1
                    for g in range(2):
                        bank = 6 + cnt['tp'] % 2
                        cnt['tp'] += 1
                        for q in range(4):
                            dc = hf * 8 + g * 4 + q
                            S.mm(lambda e, b=bank, q=q, dc=dc, s=s: e.transpose(PS[b][:, q * 128:(q + 1) * 128], xT[:, dc, s * 128:(s + 1) * 128], ident),
                                 [('xT', dc), 'cst'], [('ps', bank)], last=(q == 3))
                        evac_copy(xtok[xb_][:, g * 512:(g + 1) * 512], PS[bank][:], [('ps', bank)], [('xtok', xb_)])
                    r0 = it * TT + s * 128
                    dma('sp', y_d[r0:r0 + 128, hf * 1024:(hf + 1) * 1024], xtok[xb_][:], [('xtok', xb_)], ['y'])

        def rms_stats():
            for dc in range(DC):
                k = cnt['sq'] % 2
                cnt['sq'] += 1
                S.op('act', lambda e, k=k, dc=dc: e.activation(out=sq[k][:], in_=xT[:, dc, :], func=AF.Square), [('xT', dc)], [('sq', k)])
                S.mm(lambda e, k=k, dc=dc: e.matmul(PS[4][:], onesb[:], sq[k][:], start=(dc == 0), stop=(dc == DC - 1)),
                     [('sq', k), 'onesb'], [('ps', 4)], last=(dc == DC - 1))
            S.op('act', lambda e: e.activation(out=rstd[:], in_=PS[4][:], func=AF.Sqrt, scale=1.0 / D, bias=epsn[:]), [('ps', 4), 'epsn'], ['rstd'])
            S.op('dve', lambda e: e.reciprocal(out=rstd[:], in_=rstd[:]), ['rstd'], ['rstd'])

        def rmsnorm_to_nT(gi):
            rms_stats()
            for dc in range(DC):
                if dc % 2 == 0:
                    S.op('dve', lambda e, dc=dc: e.scalar_tensor_tensor(out=nT[:, dc, :], in0=xT[:, dc, :], scalar=gains[:, gi * DC + dc: gi * DC + dc + 1],
                                                                        in1=rstd[:], op0=ALU.mult, op1=ALU.mult), [('xT', dc), 'rstd', 'gains'], [('nT', dc)])
                else:
                    k = cnt['stg'] % 3
                    cnt['stg'] += 1
                    S.op('pool', lambda e, dc=dc, k=k: e.tensor_scalar(out=stg[k][:], in0=xT[:, dc, :], scalar1=gains[:, gi * DC + dc: gi * DC + dc + 1], scalar2=0.0, op0=ALU.mult, op1=ALU.add),
                         [('xT', dc), 'gains'], [('stg', k)])
                    S.op('pool', lambda e, dc=dc, k=k: e.tensor_tensor(out=nT[:, dc, :], in0=stg[k][:], in1=rstd[:], op=ALU.mult), [('stg', k), 'rstd'], [('nT', dc)])

        def ffn(fi):
            for j in range(FC):
                w = cnt['w'] % 3
                cnt['w'] += 1
                dma('sp', wring[w][:], wgu_b[fi][j * 128:(j + 1) * 128, :], [('wgu%d' % fi, j // 2)], [('wring', w)])
                bg = j % 2
                bu = 2 + j % 2
                for kc in range(DC):
                    S.mm(lambda e, w=w, kc=kc, bg=bg: e.matmul(PS[bg][:], wring[w][:, kc * 128:(kc + 1) * 128], nT[:, kc, :], start=(kc == 0), stop=(kc == DC - 1)),
                         [('wring', w), ('nT', kc)], [('ps', bg)], last=(kc == DC - 1))
                for kc in range(DC):
                    S.mm(lambda e, w=w, kc=kc, bu=bu: e.matmul(PS[bu][:], wring[w][:, 2048 + kc * 128: 2048 + (kc + 1) * 128], nT[:, kc, :], start=(kc == 0), stop=(kc == DC - 1)),
                         [('wring', w), ('nT', kc)], [('ps', bu)], last=(kc == DC - 1))
                k = cnt['sg'] % 2
                cnt['sg'] += 1
                S.op('act', lambda e, k=k, bg=bg: e.activation(out=sg[k][:], in_=PS[bg][:], func=AF.Silu), [('ps', bg)], [('sg', k)])
                S.op('dve', lambda e, k=k, bu=bu, j=j: e.tensor_tensor(out=actT[:, j, :], in0=sg[k][:], in1=PS[bu][:], op=ALU.mult), [('sg', k), ('ps', bu)], [('actT', j)])
            for m in range(DC):
                bd_ = 4 + m % 2
                for hf in range(2):
                    dd = cnt['d'] % 3
                    cnt['d'] += 1
                    dma('sp', dring[dd][:], wd_b[fi][m * 128:(m + 1) * 128, hf * 2816:(hf + 1) * 2816], [('wd%d' % fi, m)], [('dring', dd)])
                    for f2 in range(22):
                        fc = hf * 22 + f2
                        S.mm(lambda e, dd=dd, f2=f2, fc=fc, bd_=bd_: e.matmul(PS[bd_][:], dring[dd][:, f2 * 128:(f2 + 1) * 128], actT[:, fc, :], start=(fc == 0), stop=(fc == FC - 1)),
                             [('dring', dd), ('actT', fc)], [('ps', bd_)], last=(fc == FC - 1))
                S.op('dve', lambda e, m=m, bd_=bd_: e.scalar_tensor_tensor(out=xT[:, m, :], in0=PS[bd_][:], scalar=0.5, in1=xT[:, m, :], op0=ALU.mult, op1=ALU.add),
                     [('ps', bd_), ('xT', m)], [('xT', m)])

        fm_dest = [('q', c) for c in range(8)] + [('k', c) for c in range(8)] + [('z', c) for c in range(ZC)]

        def proj(it):
            seg = it // TPS
            t0 = (it % TPS) * TT
            for ci, (kind, c) in enumerate(fm_dest):
                w = cnt['w'] % 3
                cnt['w'] += 1
                dma('sp', wring[w][:, 0:2048], winfm_b[ci * 128:(ci + 1) * 128, :], [('winfm', ci // 4)], [('wring', w)])
                bank = ci % 2
                for kc in range(DC):
                    S.mm(lambda e, w=w, kc=kc, bank=bank: e.matmul(PS[bank][:], wring[w][:, kc * 128:(kc + 1) * 128], nT[:, kc, :], start=(kc == 0), stop=(kc == DC - 1)),
                         [('wring', w), ('nT', kc)], [('ps', bank)], last=(kc == DC - 1))
                if kind in ('q', 'k'):
                    k = cnt['stgb'] % 2
                    cnt['stgb'] += 1
                    evac_copy(stgb[k][:], PS[bank][:], [('ps', bank)], [('stgb', k)])
                    if kind == 'q':
                        dma('act', qT_d[c * 128:(c + 1) * 128, it * TT:(it + 1) * TT], stgb[k][:], [('stgb', k)], [('qT', seg)])
                    else:
                        col = seg * SEGA + APAD + t0
                        dma('act', kT_d[c * 128:(c + 1) * 128, col:col + TT], stgb[k][:], [('stgb', k)], [('kT', seg)])
                else:
                    k = cnt['stg'] % 3
                    cnt['stg'] += 1
                    evac_copy(stg[k][:], PS[bank][:], [('ps', bank)], [('stg', k)])
                    col = seg * SEGZ + 1 + t0
                    dma('act', zT_d[c * 128:(c + 1) * 128, col:col + TT], stg[k][:], [('stg', k)], [('zT', seg)])
            for g in range(4):
                w = cnt['w'] % 3
                cnt['w'] += 1
                dma('sp', wring[w][:], wintm_b[g * 128:(g + 1) * 128, :], [('wintm', g // 2)], [('wring', w)])
                for s in range(4):
                    bank = 2 + (g * 4 + s) % 2
                    for kc in range(DC):
                        S.mm(lambda e, w=w, kc=kc, bank=bank, s=s: e.matmul(PS[bank][:, 0:256], nT[:, kc, s * 128:(s + 1) * 128], wring[w][:, kc * 256:(kc + 1) * 256],
                                                                             start=(kc == 0), stop=(kc == DC - 1)),
                             [('wring', w), ('nT', kc)], [('ps', bank)], last=(kc == DC - 1))
                    if True:
                        k = cnt['vs'] % 2
                        cnt['vs'] += 1
                        S.op('pool', lambda e, k=k: e.memset(vstg[k][:], 1.0), [], [('vstg', k)])
                        evac_copy(vstg[k][:].rearrange("p (h c) -> p h c", h=4)[:, :, 0:64], PS[bank][:, 0:256].rearrange("p (h c) -> p h c", h=4),
                                  [('ps', bank), ('vstg', k)], [('vstg', k)])
                        row = seg * SEGA + APAD + t0 + s * 128
                        dma('act', vatt_d[row:row + 128, g * 260:(g + 1) * 260], vstg[k][:], [('vstg', k)], [('vatt', seg)])

        def wout(it):
            dma('sp', nT[:], mixT_d[:, it * TT:(it + 1) * TT].rearrange("(c p) t -> p c t", p=128), [('mixT', it // TPS)], [('nT', dc) for dc in range(DC)])
            dma('sp', xT[:], x1T_d[:, it * TT:(it + 1) * TT].rearrange("(c p) t -> p c t", p=128), [('x1T', it)], xT_keys)
            for m in range(DC):
                w = cnt['w'] % 3
                cnt['w'] += 1
                dma('sp', wring[w][:, 0:2048], wo_b[m * 128:(m + 1) * 128, :], [('wo', m // 4)], [('wring', w)])
                bank = m % 2
                for kc in range(DC):
                    S.mm(lambda e, w=w, kc=kc, bank=bank: e.matmul(PS[bank][:], wring[w][:, kc * 128:(kc + 1) * 128], nT[:, kc, :], start=(kc == 0), stop=(kc == DC - 1)),
                         [('wring', w), ('nT', kc)], [('ps', bank)], last=(kc == DC - 1))
                S.op('dve', lambda e, m=m, bank=bank: e.tensor_tensor(out=xT[:, m, :], in0=xT[:, m, :], in1=PS[bank][:], op=ALU.add), [('ps', bank), ('xT', m)], [('xT', m)])

        return dict(load_x_tile=load_x_tile, rmsnorm_to_nT=rmsnorm_to_nT, ffn=ffn, proj=proj, wout=wout, store_y_tile=store_y_tile, xT=xT, xT_keys=xT_keys)

    def phase_A():
        pa = ExitStack()
        env = ffn_env(pa, 'A')
        for it in range(NT):
            env['load_x_tile'](it)
            env['rmsnorm_to_nT'](0)
            env['ffn'](0)
            dma('act', x1T_d[:, it * TT:(it + 1) * TT].rearrange("(c p) t -> p c t", p=128), env['xT'][:], env['xT_keys'], [('x1T', it)])
            env['rmsnorm_to_nT'](1)
            env['proj'](it)
        S.barrier()
        pa.close()


    if 'A' in phases:
        phase_A()
    def phase_fix():
        pf = ExitStack()
        fk = sb("fk", [128, APAD], BF16, pf)
        fv = sb("fv", [128, APAD // 128, 1040], BF16, pf)
        fz = sb("fz", [128, ZC], F32, pf)
        for (ds, doff, ss, soff) in ((0, APAD + SEG, 1, APAD), (1, 0, 0, SEG)):
            for c in range(8):
                dma('sp', fk[:], kT_d[c * 128:(c + 1) * 128, ss * SEGA + soff: ss * SEGA + soff + APAD], [('kT', ss)], ['fk'])
                S.op('dve', lambda e: e.tensor_scalar(out=fk[:], in0=fk[:], scalar1=link[:, 0:1], scalar2=None, op0=ALU.mult), ['fk', 'link'], ['fk'])
                dma('sp', kT_d[c * 128:(c + 1) * 128, ds * SEGA + doff: ds * SEGA + doff + APAD], fk[:], ['fk', ('kTpad', ds, doff)], [('kTpad', ds, doff)])
            dma('sp', fv[:], vatt_d[ss * SEGA + soff: ss * SEGA + soff + APAD, :].rearrange("(a p) c -> p a c", p=128), [('vatt', ss)], ['fv'])
            S.op('dve', lambda e: e.tensor_scalar(out=fv[:], in0=fv[:], scalar1=link[:, 0:1], scalar2=None, op0=ALU.mult), ['fv', 'link'], ['fv'])
            dma('sp', vatt_d[ds * SEGA + doff: ds * SEGA + doff + APAD, :].rearrange("(a p) c -> p a c", p=128), fv[:], ['fv', ('vapad', ds, doff)], [('vapad', ds, doff)])
        for (ds, doff, ss, soff) in ((0, SEG + 1, 1, 1), (1, 0, 0, SEG)):
            dma('sp', fz[:].unsqueeze(2), zT_d[:, ss * SEGZ + soff: ss * SEGZ + soff + 1].rearrange("(c p) o -> p c o", p=128), [('zT', ss)], ['fz'], allow_slow_non_contiguous=True)
            S.op('dve', lambda e: e.tensor_scalar(out=fz[:], in0=fz[:], scalar1=link[:, 0:1], scalar2=None, op0=ALU.mult), ['fz', 'link'], ['fz'])
            dma('sp', zT_d[:, ds * SEGZ + doff: ds * SEGZ + doff + 1].rearrange("(c p) o -> p c o", p=128), fz[:].unsqueeze(2), ['fz', ('zpad', ds, doff)], [('zpad', ds, doff)], allow_slow_non_contiguous=True)
        S.barrier()
        pf.close()

    if 'fix' in phases:
        phase_fix()
    def phase_att():
        pt = ExitStack()
        NKT = {1: SEG // 128 + 1, 4: SEG // 512 + 1, 16: SEG // 2048 + 1}
        qt = sb("qt", [128, SEG], BF16, pt)
        kt = sb("kt", [128, SEGA], BF16, pt)
        vres = {d: sb("vres%d" % d, [128, d * NKT[d], 130], BF16, pt) for d in (1, 4, 16)}
        acc = [sb("acc%d" % h, [65, SEG], F32, pt) for h in range(2)]
        etf = sb("etf", [128, 1536], F32, pt)
        etb = sb("etb", [128, 1536], BF16, pt)
        pex = [sb("pex%d" % i, [128, 256], BF16, pt) for i in range(8)]
        pm = [sb("pm%d" % i, [128, 256], BF16, pt) for i in range(8)]
        rden = sb("rden", [64, 512], F32, pt)
        ostg = [sb("ostg%d" % i, [64, 512], BF16, pt) for i in range(2)]
        sel = sb("sel", [65, 64], F32, pt)
        S.op('pool', lambda e: e.memset(sel[:], 0.0), [], ['sel'])
        S.op('pool', lambda e: e.memset(sel[64:65, :], 1.0), ['sel'], ['sel'])
        ac = {'u': 0, 'o': 0, 'os': 0}

        def emit_pv(pi, d, r, b, h):
            ob = 4 + ac['o'] % 2
            ac['o'] += 1
            for j in range(2):
                S.mm(lambda e, ob=ob, j=j, pi=pi, d=d, r=r, b=b, h=h: e.matmul(PS[ob][0:65, 0:128], vres[d][:, r * NKT[d] + b + j, h * 65:(h + 1) * 65],
                                                                               pm[pi][:, j * 128:(j + 1) * 128], start=(j == 0), stop=(j == 1)),
                     [('vres', d), ('pm', pi)], [('ps', ob)])
            asl = acc[h][:, r + d * 128 * b: r + d * 128 * b + d * 127 + 1: d]
            if d == 1:
                S.op('dve', lambda e, asl=asl, ob=ob: e.tensor_copy(out=asl, in_=PS[ob][0:65, 0:128]), [('ps', ob)], [('acc', h)])
            else:
                S.op('dve', lambda e, asl=asl, ob=ob: e.tensor_tensor(out=asl, in0=asl, in1=PS[ob][0:65, 0:128], op=ALU.add), [('ps', ob), ('acc', h)], [('acc', h)])

        pend = []
        for seg in range(NSEG):
            for hp in range(ATT_PAIRS):
                dma('sp', qt[:], qT_d[hp * 128:(hp + 1) * 128, seg * SEG:(seg + 1) * SEG], [('qT', seg)], ['qt'])
                dma('sp', kt[:], kT_d[hp * 128:(hp + 1) * 128, seg * SEGA:(seg + 1) * SEGA], [('kT', seg)] + [('kTpad', seg, o) for o in (0, APAD + SEG)], ['kt'])
                dma('sp', etf[:], etab_d[hp], [], ['etf'])
                S.op('pool', lambda e: e.tensor_copy(out=etb[:], in_=etf[:]), ['etf'], ['etb'])
                for d in (1, 4, 16):
                    for r in range(d):
                        base = seg * SEGA + APAD + r - 64 * d
                        src = vatt_d[base: base + d * 128 * NKT[d], hp * 130:(hp + 1) * 130].rearrange("(k p dd) c -> dd p k c", p=128, dd=d)[0]
                        dma('act', vres[d][:, r * NKT[d]:(r + 1) * NKT[d], :], src, [('vatt', seg)] + [('vapad', seg, o) for o in (0, APAD + SEG)], [('vres', d)])
                for di, d in enumerate((1, 4, 16)):
                    L = SEG // d
                    for r in range(d):
                        for b in range(L // 128):
                            for h in range(2):
                                u = ac['u']
                                ac['u'] += 1
                                sbank = u % 4
                                qs = qt[h * 64:(h + 1) * 64, r + d * 128 * b: r + d * 128 * b + d * 127 + 1: d]
                                for j in range(2):
                                    k0 = APAD + r + d * (128 * b - 64 + 128 * j)
                                    ks = kt[h * 64:(h + 1) * 64, k0: k0 + d * 127 + 1: d]
                                    S.mm(lambda e, sbank=sbank, j=j, ks=ks, qs=qs: e.matmul(PS[sbank][:, j * 128:(j + 1) * 128], ks, qs, start=True, stop=True),
                                         ['kt', 'qt'], [('ps', sbank)])
                                pi = u % 8
                                S.op('act', lambda e, pi=pi, sbank=sbank: e.activation(out=pex[pi][:], in_=PS[sbank][:, 0:256], func=AF.Exp, scale=0.125),
                                     [('ps', sbank)], [('pex', pi)])
                                eo = (h * 3 + di) * 256
                                S.op('pool' if u % 3 != 2 else 'dve', lambda e, pi=pi, eo=eo: e.tensor_tensor(out=pm[pi][:], in0=pex[pi][:], in1=etb[:, eo:eo + 256], op=ALU.mult),
                                     [('pex', pi), 'etb'], [('pm', pi)])
                                pend.append((pi, d, r, b, h))
                                if len(pend) > 4:
                                    emit_pv(*pend.pop(0))
                while pend:
                    emit_pv(*pend.pop(0))
                for h in range(2):
                    for t in range(SEG // 512):
                        S.mm(lambda e, h=h, t=t: e.matmul(PS[6][0:64, :], sel[:], acc[h][:, t * 512:(t + 1) * 512], start=True, stop=True), ['sel', ('acc', h)], [('ps', 6)])
                        S.op('dve', lambda e: e.reciprocal(out=rden[:], in_=PS[6][0:64, :]), [('ps', 6)], ['rden'])
                        k = ac['os'] % 2
                        ac['os'] += 1
                        S.op('dve', lambda e, k=k, h=h, t=t: e.tensor_tensor(out=ostg[k][:], in0=acc[h][0:64, t * 512:(t + 1) * 512], in1=rden[:], op=ALU.mult),
                             [('acc', h), 'rden'], [('ostg', k)])
                        row = hp * 128 + h * 64
                        dma('sp', mixT_d[row:row + 64, seg * SEG + t * 512: seg * SEG + (t + 1) * 512], ostg[k][:], [('ostg', k)], [('mixT', seg)])
        S.barrier()
        pt.close()


    if 'att' in phases:
        phase_att()
    def phase_rwkv():
        pr = ExitStack()
        twd = sb("twd", [128, SEG], BF16, pr)
        sad = sb("sad", [128, SEG], BF16, pr)
        sgd = sb("sgd", [128, SEG], BF16, pr)
        mu = sb("mu", [128, 2 * ZC], F32, pr)
        muc = sb("muc", [128, ZC], F32, pr)
        rwp = sb("rwp", [128, 80], F32, pr)
        lrf = sb("lrf", [128, 1024], F32, pr)
        w2b_ = sb("w2b", [128, 1024], BF16, pr)
        a2b_ = sb("a2b", [128, 1024], BF16, pr)
        g2b_ = sb("g2b", [128, 1024], BF16, pr)
        rmask = sb("rmask", [128, 512], F32, pr)
        cmask = [sb("cmask%d" % i, [128, 192], F32, pr) for i in range(2)]
        mSxb = [sb("mSxb%d" % i, [128, 128], BF16, pr) for i in range(2)]
        zin = [sb("zin%d" % i, [128, TT + 2], F32, pr) for i in range(3)]
        NW = 30
        wk = [sb("wk%d" % i, [128, TT], F32, pr) for i in range(NW)]
        bdt = {n: [sb("bd_%s%d" % (n, i), [128, 8, 192 if n == "At" else 128], BF16, pr) for i in range(2)] for n in ("At", "Bt", "Kt", "Vt")}
        gam = [sb("gam%d" % i, [128, TT], F32, pr) for i in range(2)]
        vTs = [sb("vTs%d" % i, [128, TT], F32, pr) for i in range(2)]
        bsum = [sb("bsum%d" % i, [128, TT], F32, pr) for i in range(2)]
        NU = 8
        ub = {n: [sb("u_%s%d" % (n, i), [128, 320 if n == "XM" else 256], BF16, pr) for i in range(NU)] for n in ("XM", "LM", "Z", "G")}
        ub1 = {n: [sb("u1_%s%d" % (n, i), [128, 128], BF16, pr) for i in range(NU)] for n in ("X", "Tt", "Kk", "V", "PpT", "RbT")}
        pw = [[sb("pw%d_%d" % (u, i), [128, 128], BF16, pr) for i in range(4)] for u in range(NU)]
        qg = [sb("qg%d" % i, [128, 128], F32, pr) for i in range(NU)]
        Hf = sb("Hf", [128, 128], F32, pr)
        Hb = [[sb("Hb%d_%d" % (c_, i), [128, 128], BF16, pr) for i in range(2)] for c_ in range(8)]
        ysb = [sb("ysb%d" % i, [128, TT], F32, pr) for i in range(2)]
        ybl = sb("ybl", [128, TT], F32, pr)
        ostr = [sb("ostr%d" % i, [128, TT], BF16, pr) for i in range(2)]
        for n in bdt:
            for i in range(2):
                S.op('pool', lambda e, n=n, i=i: e.memset(bdt[n][i][:], 0.0), [], [('bd', n, i)])
        dma('sp', mu[:], mu_d[:, :], [], ['mu'])
        dma('sp', rwp[:], rwp_d[:, :], [], ['rwp'])
        dma('sp', rmask[:], rmask_d[:, :], [], ['rmask'])
        for (b_, d_) in ((w2b_, w2_d), (a2b_, a2_d), (g2b_, g2_d)):
            dma('sp', lrf[:], d_[:, :], [], ['lrf'])
            S.op('dve', lambda e, b_=b_: e.tensor_copy(out=b_[:], in_=lrf[:]), ['lrf'], ['lrw'])
        S.op('dve', lambda e: e.tensor_tensor(out=muc[:], in0=mu[:, 0:ZC], in1=mu[:, ZC:2 * ZC], op=ALU.add), ['mu'], ['muc'])
        S.op('dve', lambda e: e.tensor_scalar(out=muc[:], in0=muc[:], scalar1=-1.0, scalar2=1.0, op0=ALU.mult, op1=ALU.add), ['muc'], ['muc'])
        for c in range(8):
            S.op('dve', lambda e, c=c: e.tensor_scalar(out=rwp[:, c * 10 + 6:c * 10 + 7], in0=rwp[:, c * 10 + 5:c * 10 + 6], scalar1=-1.0, scalar2=1.0, op0=ALU.mult, op1=ALU.add), ['rwp'], ['rwp'])
        for dirn in range(2):
            mI_ = m_iu if dirn == 0 else m_il
            mSx_ = m_sl if dirn == 0 else m_su
            mS_ = m_su if dirn == 0 else m_sl
            S.op('dve', lambda e, dirn=dirn, mS_=mS_: e.tensor_copy(out=cmask[dirn][:, 0:128], in_=mS_), ['cst'], ['cmask'])
            S.op('dve', lambda e, dirn=dirn, mI_=mI_: e.tensor_tensor(out=cmask[dirn][:, 128:192], in0=mI_[:, 0:64], in1=mI_[:, 64:128], op=ALU.add), ['cst', 'cmask'], ['cmask'])
            S.op('dve', lambda e, dirn=dirn, mSx_=mSx_: e.tensor_copy(out=mSxb[dirn][:], in_=mSx_), ['cst'], ['mSxb'])
        rc = {'z': 0, 'wk': 0, 'ps': 0, 'pf': 0, 'os': 0}

        def newwk():
            i = rc['wk'] % NW
            rc['wk'] += 1
            return wk[i], ('wk', i)

        def psreg():
            i = rc['ps'] % 6
            rc['ps'] += 1
            return PS[i][:, 0:256], ('psr', i)

        def psfull():
            i = 6 + rc['pf'] % 2
            rc['pf'] += 1
            return PS[i], ('psf', i)

        def load_shift(zc, seg, t0, dst=None, dkey=None):
            if dst is None:
                o, ok = newwk()
            else:
                o, ok = dst, dkey
            i = rc['z'] % 3
            rc['z'] += 1
            col = seg * SEGZ + t0
            dma('sp', zin[i][:], zT_d[zc * 128:(zc + 1) * 128, col:col + TT + 2], [('zT', seg)] + [('zpad', seg, o_) for o_ in (0, SEG + 1)], [('zin', i)])
            t1, t1k = newwk()
            S.op('pool', lambda e, i=i, o=o: e.tensor_scalar(out=o[:], in0=zin[i][:, 1:TT + 1], scalar1=muc[:, zc:zc + 1], scalar2=0.0, op0=ALU.mult, op1=ALU.add), [('zin', i), 'muc'], [ok])
            S.op('pool', lambda e, i=i, t1=t1: e.tensor_scalar(out=t1[:], in0=zin[i][:, 0:TT], scalar1=mu[:, zc:zc + 1], scalar2=0.0, op0=ALU.mult, op1=ALU.add), [('zin', i), 'mu'], [t1k])
            S.op('pool', lambda e, o=o, t1=t1: e.tensor_tensor(out=o[:], in0=o[:], in1=t1[:], op=ALU.add), [ok, t1k], [ok])
            S.op('pool', lambda e, i=i, t1=t1: e.tensor_scalar(out=t1[:], in0=zin[i][:, 2:TT + 2], scalar1=mu[:, ZC + zc:ZC + zc + 1], scalar2=0.0, op0=ALU.mult, op1=ALU.add), [('zin', i), 'mu', t1k], [t1k])
            S.op('pool', lambda e, o=o, t1=t1: e.tensor_tensor(out=o[:], in0=o[:], in1=t1[:], op=ALU.add), [ok, t1k], [ok])
            return o, ok

        def lowrank_prep(seg):
            for t in range(TPS):
                t0 = t * TT
                for (zc, dst, fn) in ((24, twd, AF.Tanh), (25, sad, AF.Copy), (26, sgd, AF.Sigmoid)):
                    o, ok = load_shift(zc, seg, t0)
                    S.op('act', lambda e, o=o, dst=dst, fn=fn, t0=t0: e.activation(out=dst[:, t0:t0 + TT], in_=o[:], func=fn), [ok], ['lr'])

        def sig_lowrank(wb, rows, src, c, t0, bias_ap):
            p, pk = psfull()
            r0, r1 = rows
            S.mm(lambda e, p=p: e.matmul(p[:], wb[r0:r1, c * 128:(c + 1) * 128], src[r0:r1, t0:t0 + TT], start=True, stop=True), ['lrw', 'lr'], [pk])
            o, ok = newwk()
            S.op('act', lambda e, p=p, o=o: e.activation(out=o[:], in_=p[:], func=AF.Sigmoid, bias=bias_ap, scale=1.0), [pk, 'rwp'], [ok])
            return o, ok

        def prep_tile(dirn, c, seg, t, bi):
            t0 = t * TT
            fwd = (dirn == 0)
            P = lambda j: rwp[:, c * 10 + j: c * 10 + j + 1]
            rs, rsk = load_shift(c, seg, t0)
            yield
            ks, ksk = load_shift(8 + c, seg, t0)
            yield
            load_shift(16 + c, seg, t0, vTs[bi], ('vTs', bi))
            yield
            for h in range(2):
                S.op('pool', lambda e, h=h: e.tensor_copy(out=bdt["Vt"][bi][h * 64:(h + 1) * 64, :, h * 64:(h + 1) * 64],
                                                          in_=vTs[bi][h * 64:(h + 1) * 64, :].rearrange("p (n t) -> p n t", t=64)),
                     [('vTs', bi)], [('bd', "Vt", bi)])
            kkr, kkrk = newwk()
            S.op('pool', lambda e: e.tensor_scalar(out=kkr[:], in0=ks[:], scalar1=P(4), scalar2=0.0, op0=ALU.mult, op1=ALU.add), [ksk, 'rwp'], [kkrk])
            sq_, sqk = newwk()
            S.op('act', lambda e: e.activation(out=sq_[:], in_=kkr[:], func=AF.Square), [kkrk], [sqk])
            p, pk = psfull()
            S.mm(lambda e, p=p: e.matmul(p[:], bones, sq_[:], start=True, stop=True), ['cst', sqk], [pk])
            rn, rnk = newwk()
            S.op('act', lambda e, p=p: e.activation(out=rn[:], in_=p[:], func=AF.Sqrt), [pk], [rnk])
            yield
            S.op('dve', lambda e: e.tensor_scalar(out=rn[:], in0=rn[:], scalar1=1e-12, scalar2=None, op0=ALU.max), [rnk], [rnk])
            S.op('dve', lambda e: e.reciprocal(out=rn[:], in_=rn[:]), [rnk], [rnk])
            kkn, kknk = newwk()
            S.op('dve', lambda e: e.scalar_tensor_tensor(out=kkn[:], in0=kkr[:], scalar=-1.0, in1=rn[:], op0=ALU.mult, op1=ALU.mult), [kkrk, rnk], [kknk])
            yield
            rows = (0, 64) if fwd else (64, 128)
            sgw, sgwk = sig_lowrank(w2b_, rows, twd, c, t0, P(0 if fwd else 1))
            ag, agk = sig_lowrank(a2b_, rows, sad, c, t0, P(2 if fwd else 3))
            yield
            kp_, kpk = newwk()
            S.op('pool', lambda e: e.tensor_scalar(out=kp_[:], in0=ag[:], scalar1=P(5), scalar2=P(6), op0=ALU.mult, op1=ALU.add), [agk, 'rwp'], [kpk])
            S.op('pool', lambda e: e.tensor_tensor(out=kp_[:], in0=kp_[:], in1=ks[:], op=ALU.mult), [kpk, ksk], [kpk])
            bb, bbk = newwk()
            S.op('dve', lambda e: e.scalar_tensor_tensor(out=bb[:], in0=kkn[:], scalar=-1.0, in1=ag[:], op0=ALU.mult, op1=ALU.mult), [kknk, agk], [bbk])
            yield
            cs, csk = newwk()
            S.op('dve', lambda e: e.tensor_tensor_scan(out=cs[:], data0=rmask[:], data1=sgw[:], initial=0.0, op0=ALU.mult, op1=ALU.add), ['rmask', sgwk], [csk])
            if fwd:
                lg, lgk = cs, csk
            else:
                lg, lgk = newwk()
                S.op('pool', lambda e: e.tensor_tensor(out=lg[:], in0=sgw[:], in1=cs[:], op=ALU.subtract), [sgwk, csk], [lgk])
                S.op('pool', lambda e: e.tensor_tensor(out=lg[:].rearrange("p (n t) -> p n t", t=64), in0=lg[:].rearrange("p (n t) -> p n t", t=64),
                                                       in1=cs[:].rearrange("p (n t) -> p n t", t=64)[:, :, 63:64].to_broadcast([128, 8, 64]), op=ALU.add), [lgk, csk], [lgk])
            yield
            gkey = ('gam', bi)
            S.op('act', lambda e: e.activation(out=gam[bi][:], in_=lg[:], func=AF.Exp, scale=-C0), [lgk], [gkey])
            ig, igk = newwk()
            S.op('act', lambda e: e.activation(out=ig[:], in_=lg[:], func=AF.Exp, scale=C0), [lgk], [igk])
            gp, gpk = newwk()
            S.op('dve', lambda e: e.tensor_tensor(out=gp[:], in0=lg[:], in1=sgw[:], op=ALU.subtract), [lgk, sgwk], [gpk])
            S.op('act', lambda e: e.activation(out=gp[:], in_=gp[:], func=AF.Exp, scale=-C0), [gpk], [gpk])
            yield
            for (nm, a_, ak, b_, bk) in (("At", kkn, kknk, gp, gpk), ("Bt", bb, bbk, ig, igk), ("Kt", kp_, kpk, ig, igk)):
                for h in range(2):
                    eng = 'pool'
                    S.op(eng, lambda e, nm=nm, a_=a_, b_=b_, h=h: e.tensor_tensor(out=bdt[nm][bi][h * 64:(h + 1) * 64, :, h * 64:(h + 1) * 64],
                                                                                   in0=a_[h * 64:(h + 1) * 64, :].rearrange("p (n t) -> p n t", t=64),
                                                                                   in1=b_[h * 64:(h + 1) * 64, :].rearrange("p (n t) -> p n t", t=64), op=ALU.mult),
                         [ak, bk], [('bd', nm, bi)])
                yield
            S.op('dve', lambda e: e.tensor_tensor(out=bdt["At"][bi][:, :, 128:192], in0=rs[:].rearrange("p (n t) -> p n t", t=64),
                                                  in1=gam[bi][:].rearrange("p (n t) -> p n t", t=64), op=ALU.mult), [rsk, gkey], [('Rt', bi)])
            if fwd:
                yield
                agb, agbk = sig_lowrank(a2b_, (64, 128), sad, c, t0, P(3))
                kb_, kbk = newwk()
                S.op('pool', lambda e: e.tensor_scalar(out=kb_[:], in0=agb[:], scalar1=P(5), scalar2=P(6), op0=ALU.mult, op1=ALU.add), [agbk, 'rwp'], [kbk])
                S.op('pool', lambda e: e.tensor_tensor(out=kb_[:], in0=kb_[:], in1=ks[:], op=ALU.mult), [kbk, ksk], [kbk])
                yield
                S.op('pool', lambda e: e.tensor_tensor(out=kb_[:], in0=kb_[:], in1=kp_[:], op=ALU.add), [kbk, kpk], [kbk])
                S.op('dve', lambda e: e.scalar_tensor_tensor(out=kb_[:], in0=rs[:], scalar=P(7), in1=kb_[:], op0=ALU.mult, op1=ALU.mult), [rsk, kbk, 'rwp'], [kbk])
                p2, pk2 = psfull()
                S.mm(lambda e, p2=p2: e.matmul(p2[:], bones, kb_[:], start=True, stop=True), ['cst', kbk], [pk2])
                S.op('act', lambda e, p2=p2: e.activation(out=bsum[bi][:], in_=p2[:], func=AF.Copy), [pk2], [('bsum', bi)])

        def offchain(dirn, n, bi, ui):
            fwd = (dirn == 0)
            mS = (m_su if fwd else m_sl)
            At = bdt["At"][bi][:, n, 0:128]
            AR = bdt["At"][bi][:, n, :]
            Bt = bdt["Bt"][bi][:, n, :]
            Kt = bdt["Kt"][bi][:, n, :]
            Vt = bdt["Vt"][bi][:, n, :]
            Rs = bdt["At"][bi][:, n, 128:192]
            gcol = gam[bi][:, n * 64 + 63: n * 64 + 64] if fwd else gam[bi][:, n * 64: n * 64 + 1]
            kA, kB, kK, kVt, kR, kG = ('bd', "At", bi), ('bd', "Bt", bi), ('bd', "Kt", bi), ('bd', "Vt", bi), ('Rt', bi), ('gam', bi)
            XM, LM, Z, G = ub["XM"][ui], ub["LM"][ui], ub["Z"][ui], ub["G"][ui]
            X, Tt, Kk, V, PpT, RbT = (ub1[k][ui] for k in ("X", "Tt", "Kk", "V", "PpT", "RbT"))
            Bk = XM[:, 192:320]
            K_ = lambda nm: ('u', nm, ui)
            cpc = [ui]

            def cp(dst, src, reads, writes):
                cpc[0] += 1
                if cpc[0] % 5 != 0:
                    S.op('act', lambda e, dst=dst, src=src: e.activation(out=dst, in_=src, func=AF.Copy), reads, writes)
                else:
                    S.op('dve', lambda e, dst=dst, src=src: e.tensor_copy(out=dst, in_=src), reads, writes)
            for (lh, lk, dst, dk) in ((Bt, kB, XM, 'XM'), (Kt, kK, LM, 'LM')):
                p, pk = psreg()
                S.mm(lambda e, p=p, lh=lh: e.matmul(p[:, 0:192], lh, AR, start=True, stop=True), [lk, kA, kR], [pk])
                S.op('dve', lambda e, p=p, dst=dst: e.tensor_tensor(out=dst[:, 0:192], in0=p[:, 0:192], in1=cmask[dirn][:], op=ALU.mult), [pk, 'cmask'], [K_(dk)])
                yield
            p, pk = psreg()
            S.mm(lambda e, p=p: e.matmul(p[:, 0:128], At, Bt, start=True, stop=True), [kA, kB], [pk])
            mSx = (m_sl if fwd else m_su)
            S.op('dve', lambda e, p=p: e.tensor_tensor(out=X[:], in0=p[:, 0:128], in1=mSx, op=ALU.mult), [pk, 'cst'], [K_('X')])
            S.op('dve', lambda e: e.tensor_tensor(out=Tt[:], in0=XM[:, 0:128], in1=identb, op=ALU.add), [K_('XM'), 'cstb'], [K_('Tt')])
            yield
            for (src, sk, dk) in ((At, kA, 'A'), (Bt, kB, 'Bk'), (Kt, kK, 'Kk'), (Vt, kVt, 'V')):
                p, pk = psreg()
                S.mm(lambda e, p=p, src=src: e.matmul(p[:, 0:128], src, identb, start=True, stop=True), [sk, 'cstb'], [pk])
                if dk == 'A':
                    cp(Z[:, 0:128], p[:, 0:128], [pk], [K_('Z0')])
                elif dk == 'Bk':
                    cp(Bk, p[:, 0:128], [pk], [K_('Bk')])
                elif dk == 'Kk':
                    cp(Kk[:], p[:, 0:128], [pk], [K_('Kk')])
                else:
                    cp(V[:], p[:, 0:128], [pk], [K_('V')])
                yield
            Pc, Pck = X[:], K_('X')
            Ptc, Ptck = XM[:, 0:128], K_('XM')
            for lvl in range(5):
                pn = pw[ui][(2 * lvl) % 4]
                pnk = ('pw', ui, (2 * lvl) % 4)
                p1, pk1 = psreg()
                S.mm(lambda e, p1=p1, Pc=Pc, Ptc=Ptc: e.matmul(p1[:, 0:128], Ptc, Pc, start=True, stop=True), [Pck, Ptck], [pk1])
                if lvl < 4:
                    ptn = pw[ui][(2 * lvl + 1) % 4]
                    ptnk = ('pw', ui, (2 * lvl + 1) % 4)
                    p2, pk2 = psreg()
                    S.mm(lambda e, p2=p2, Pc=Pc, Ptc=Ptc: e.matmul(p2[:, 0:128], Pc, Ptc, start=True, stop=True), [Pck, Ptck], [pk2])
                    if lvl % 2 == 0:
                        cp(ptn[:], p2[:, 0:128], [pk2], [ptnk])
                    else:
                        cp(ptn[:], p2[:, 0:128], [pk2], [ptnk])
                cp(pn[:], p1[:, 0:128], [pk1], [pnk])
                Pc, Pck = pn[:], pnk
                if lvl < 4:
                    Ptc, Ptck = ptn[:], ptnk
                yield
                p3, pk3 = psreg()
                S.mm(lambda e, p3=p3, Pc=Pc: e.matmul(p3[:, 0:128], Pc, Tt[:], start=True, stop=True), [Pck, K_('Tt')], [pk3])
                S.op('dve', lambda e, p3=p3: e.tensor_tensor(out=Tt[:], in0=Tt[:], in1=p3[:, 0:128], op=ALU.add), [pk3, K_('Tt')], [K_('Tt')])
                if lvl == 4:
                    yield
            p, pk = psreg()
            S.mm(lambda e, p=p: e.matmul(p[:, 0:128], LM[:, 0:128], V[:], start=True, stop=True), [K_('LM'), K_('V')], [pk])
            cp(Z[:, 128:256], p[:, 0:128], [pk], [K_('Z1')])
            yield
            p, pk = psreg()
            S.mm(lambda e, p=p: e.matmul(p[:, 0:256], Tt[:], Z[:], start=True, stop=True), [K_('Tt'), K_('Z0'), K_('Z1')], [pk])
            cp(G[:], p[:, 0:256], [pk], [K_('G')])
            yield
            p, pk = psreg()
            S.mm(lambda e, p=p: e.matmul(p[:, 0:192], G[:, 0:128], XM[:, 128:320], start=True, stop=True), [K_('G'), K_('Bk'), K_('XM')], [pk])
            S.op('dve', lambda e, p=p: e.tensor_tensor(out=PpT[:], in0=p[:, 64:192], in1=ident, op=ALU.add), [pk, 'cst'], [K_('PpT')])
            S.op('dve', lambda e, p=p: e.tensor_tensor(out=RbT[:, 0:64], in0=p[:, 0:64], in1=Rs, op=ALU.add), [pk, kR], [K_('RbT')])
            p, pk = psreg()
            S.mm(lambda e, p=p: e.matmul(p[:, 0:128], Bk, G[:, 128:256], start=True, stop=False), [K_('G'), K_('Bk')], [pk])
            S.mm(lambda e, p=p: e.matmul(p[:, 0:128], Kk[:], V[:], start=False, stop=True), [K_('Kk'), K_('V')], [pk])
            S.op('dve', lambda e, p=p: e.tensor_scalar(out=qg[ui][:], in0=p[:, 0:128], scalar1=gcol, scalar2=None, op0=ALU.mult), [pk, kG], [K_('qg')])

        def chain(dirn, n, bi, ui, Hcur, Hnew, ysl, yslk):
            fwd = (dirn == 0)
            gcol = gam[bi][:, n * 64 + 63: n * 64 + 64] if fwd else gam[bi][:, n * 64: n * 64 + 1]
            kG = ('gam', bi)
            XM, LM, G = ub["XM"][ui], ub["LM"][ui], ub["G"][ui]
            V, PpT, RbT = (ub1[k][ui] for k in ("V", "PpT", "RbT"))
            K_ = lambda nm: ('u', nm, ui)
            p, pk = psreg()
            S.mm(lambda e, p=p: e.matmul(p[:, 0:64], G[:, 128:256], XM[:, 128:192], start=True, stop=False), [K_('G'), K_('XM')], [pk])
            S.mm(lambda e, p=p: e.matmul(p[:, 0:64], V[:], LM[:, 128:192], start=False, stop=False), [K_('V'), K_('LM')], [pk])
            S.mm(lambda e, p=p: e.matmul(p[:, 0:64], Hcur[0][:], RbT[:, 0:64], start=False, stop=True), [Hcur[1], K_('RbT')], [pk])
            S.op('act', lambda e, p=p: e.activation(out=ysl, in_=p[:, 0:64], func=AF.Copy), [pk], [yslk])
            p2, pk2 = psreg()
            S.mm(lambda e, p2=p2: e.matmul(p2[:, 0:128], PpT[:], Hcur[0][:], start=True, stop=True), [K_('PpT'), Hcur[1]], [pk2])
            S.op('dve', lambda e, p2=p2: e.scalar_tensor_tensor(out=Hnew[0][:], in0=p2[:, 0:128], scalar=gcol, in1=qg[ui][:], op0=ALU.mult, op1=ALU.add),
                 [pk2, K_('qg'), kG], [Hnew[1]])

        def epilogue(c, seg, t, bi, yt, ytk):
            t0 = t * TT
            P = lambda j: rwp[:, c * 10 + j: c * 10 + j + 1]
            bon, bonk = newwk()
            S.op('pool', lambda e: e.tensor_tensor(out=bon[:], in0=vTs[bi][:], in1=bsum[bi][:], op=ALU.mult), [('vTs', bi), ('bsum', bi)], [bonk])
            dma('sp', ybl[:], ybT_d[c * 128:(c + 1) * 128, seg * SEG + t0: seg * SEG + t0 + TT], [('ybT', c)], ['ybl'])
            ysum, ysk = newwk()
            S.op('dve', lambda e: e.tensor_tensor(out=ysum[:], in0=yt[:], in1=ybl[:], op=ALU.add), [ytk, 'ybl'], [ysk])
            yield
            p, pk = psfull()
            S.mm(lambda e, p=p: e.matmul(p[:], bones, ysum[:], start=True, stop=True), ['cst', ysk], [pk])
            yield
            yc, yck = newwk()
            S.op('dve', lambda e, p=p: e.scalar_tensor_tensor(out=yc[:], in0=p[:], scalar=-1.0 / 64, in1=ysum[:], op0=ALU.mult, op1=ALU.add), [pk, ysk], [yck])
            yield
            sq_, sqk = newwk()
            S.op('act', lambda e: e.activation(out=sq_[:], in_=yc[:], func=AF.Square), [yck], [sqk])
            yield
            p2, pk2 = psfull()
            S.mm(lambda e, p2=p2: e.matmul(p2[:], bones, sq_[:], start=True, stop=True), ['cst', sqk], [pk2])
            yield
            sd, sdk = newwk()
            S.op('act', lambda e, p2=p2: e.activation(out=sd[:], in_=p2[:], func=AF.Sqrt, scale=1.0 / 64, bias=epsl[:]), [pk2, 'epsl'], [sdk])
            yield
            S.op('dve', lambda e: e.reciprocal(out=sd[:], in_=sd[:]), [sdk], [sdk])
            yield
            S.op('dve', lambda e: e.tensor_tensor(out=yc[:], in0=yc[:], in1=sd[:], op=ALU.mult), [yck, sdk], [yck])
            yield
            S.op('pool', lambda e: e.tensor_scalar(out=yc[:], in0=yc[:], scalar1=P(8), scalar2=P(9), op0=ALU.mult, op1=ALU.add), [yck, 'rwp'], [yck])
            yield
            S.op('pool', lambda e: e.tensor_tensor(out=yc[:], in0=yc[:], in1=bon[:], op=ALU.add), [yck, bonk], [yck])
            p3, pk3 = psfull()
            S.mm(lambda e, p3=p3: e.matmul(p3[:], g2b_[:, c * 128:(c + 1) * 128], sgd[:, t0:t0 + TT], start=True, stop=True), ['lrw', 'lr'], [pk3])
            yield
            k = rc['os'] % 2
            rc['os'] += 1
            S.op('dve', lambda e, p3=p3, k=k: e.tensor_tensor(out=ostr[k][:], in0=yc[:], in1=p3[:], op=ALU.mult), [yck, pk3], [('ostr', k)])
            dma('sp', mixT_d[1024 + c * 128: 1024 + (c + 1) * 128, seg * SEG + t0: seg * SEG + t0 + TT], ostr[k][:], [('ostr', k)], [('mixT', seg)])

        tiles = []
        for (dirn, seg, si) in ((1, 1, 0), (1, 0, 1), (0, 0, 0), (0, 1, 1)):
            for c in range(RW_PAIRS):
                trange = range(TPS - 1, -1, -1) if dirn == 1 else range(TPS)
                for ti, t in enumerate(trange):
                    tiles.append(dict(c=c, dirn=dirn, seg=seg, t=t, first=(si == 0 and ti == 0), link=(si == 1 and ti == 0)))
        cur_seg = None
        hic = [0] * 8
        early = None
        pend_epi = None
        for k, tl in enumerate(tiles):
            bi = k % 2
            c, dirn, seg, t = tl['c'], tl['dirn'], tl['seg'], tl['t']
            if cur_seg != seg:
                lowrank_prep(seg)
                cur_seg = seg
            if early is None:
                early = prep_tile(dirn, c, seg, t, bi)
            for _ in early:
                pass
            early = None
            hi = hic[c]
            if tl['first']:
                hi = 0
                S.op('pool', lambda e, c=c: e.memset(Hb[c][0][:], 0.0), [], [('Hb', c, 0)])
            if tl['link']:
                S.op('dve', lambda e, hi=hi, c=c: e.tensor_scalar(out=Hb[c][hi][:], in0=Hb[c][hi][:], scalar1=link[:, 0:1], scalar2=None, op0=ALU.mult), [('Hb', c, hi), 'link'], [('Hb', c, hi)])
            chunks = list(range(7, -1, -1)) if dirn == 1 else list(range(8))
            nprep = None
            if k + 1 < len(tiles) and tiles[k + 1]['seg'] == seg:
                n2 = tiles[k + 1]
                nprep = prep_tile(n2['dirn'], n2['c'], n2['seg'], n2['t'], (k + 1) % 2)
                early = nprep

            def adv():
                nonlocal nprep, pend_epi
                if pend_epi is not None:
                    try:
                        next(pend_epi)
                    except StopIteration:
                        pend_epi = None
                if nprep is not None:
                    try:
                        next(nprep)
                    except StopIteration:
                        nprep = None
            yi = k % 2
            gens = [offchain(dirn, n, bi, ui) for ui, n in enumerate(chunks)]
            live = []
            started = 0
            finished = [False] * 8
            nchain = 0
            while nchain < 8:
                if started < 8:
                    live.append((started, gens[started]))
                    started += 1
                nxt = []
                for (ui, g) in live:
                    try:
                        next(g)
                        nxt.append((ui, g))
                    except StopIteration:
                        finished[ui] = True
                live = nxt
                if finished[nchain]:
                    n = chunks[nchain]
                    chain(dirn, n, bi, nchain, (Hb[c][hi], ('Hb', c, hi)), (Hb[c][1 - hi], ('Hb', c, 1 - hi)), ysb[yi][:, n * 64:(n + 1) * 64], ('ysb', yi))
                    hi = 1 - hi
                    nchain += 1
                adv()
            hic[c] = hi
            if pend_epi is not None:
                for _ in pend_epi:
                    pass
                pend_epi = None
            if dirn == 1:
                dma('sp', ybT_d[c * 128:(c + 1) * 128, seg * SEG + t * TT: seg * SEG + (t + 1) * TT], ysb[yi][:], [('ysb', yi)], [('ybT', c)])
            else:
                pend_epi = epilogue(c, seg, t, bi, ysb[yi], ('ysb', yi))
                if not (k + 1 < len(tiles) and tiles[k + 1]['seg'] == seg):
                    for _ in pend_epi:
                        pass
                    pend_epi = None
        S.barrier()
        pr.close()

    if 'rwkv' in phases:
        phase_rwkv()
    def phase_C():
        pc = ExitStack()
        env = ffn_env(pc, 'C')
        for it in range(NT):
            env['wout'](it)
            env['rmsnorm_to_nT'](2)
            env['ffn'](1)
            env['store_y_tile'](it, 3)
        S.barrier()
        pc.close()

    if 'C' in phases:
        phase_C()
    S.barrier()
    S.emit()
    global LAST_S
    LAST_S = S
    return nc

LAST_S = None

def _const_tables():
    tri = np.ones((64, 64), np.float32)
    su1, sl1, iu1, il1 = np.triu(tri, 1), np.tril(tri, -1), np.triu(tri, 0), np.tril(tri, 0)

    def bdm(m):
        o = np.zeros((128, 128), np.float32)
        o[:64, :64] = m
        o[64:, 64:] = m
        return o
    cst = np.zeros((128, 1024), np.float32)
    cst[:, 0:128] = np.eye(128, dtype=np.float32)
    cst[:, 128:256] = bdm(tri)
    cst[:, 256:384] = bdm(su1)
    cst[:, 384:512] = bdm(sl1)
    cst[:, 512:640] = bdm(iu1)
    cst[:, 640:768] = bdm(il1)
    rmask = np.ones((128, 512), np.float32)
    rmask[:, ::64] = 0.0
    slopes = np.exp2(-8.0 * np.arange(1, 17, dtype=np.float64) / 16)
    kp = np.arange(128)[:, None]
    qp = np.arange(128)[None, :]
    etab = np.zeros((8, 128, 1536), np.float32)
    for h in range(16):
        for di, d in enumerate((1, 4, 16)):
            for j in range(2):
                rel = kp + 128 * j - 64 - qp
                e = np.where(np.abs(rel) <= 64, np.exp(-slopes[h] * d * np.abs(rel)), 0.0)
                o = ((h % 2) * 3 + di) * 256 + j * 128
                etab[h // 2, :, o:o + 128] = e
    return cst, rmask, etab


def _prep_weights(inp):
    f = np.float32
    A = lambda a: np.ascontiguousarray(np.asarray(a, dtype=f))
    out = {}
    for fi, pre in enumerate(("ffn1", "ffn2")):
        g = np.asarray(inp[pre + "_gate"][0]).reshape(DC, 128, FC, 128).transpose(2, 1, 0, 3).reshape(FC, 128, 2048)
        u = np.asarray(inp[pre + "_up"][0]).reshape(DC, 128, FC, 128).transpose(2, 1, 0, 3).reshape(FC, 128, 2048)
        out["wgu%d" % (fi + 1)] = A(np.concatenate([g, u], axis=2).reshape(FF, 4096))
        dn = np.asarray(inp[pre + "_down"][0]).reshape(FC, 128, DC, 128).transpose(2, 1, 0, 3).reshape(D, FF)
        out["wd%d" % (fi + 1)] = A(dn)
    w_in = np.asarray(inp["w_in"][0])
    fm_cols = [c * 128 for c in range(16)] + [3072 + c * 128 for c in range(ZC)]
    fm = [w_in[:, c0:c0 + 128].reshape(DC, 128, 128).transpose(1, 0, 2).reshape(128, 2048) for c0 in fm_cols]
    out["winfm"] = A(np.concatenate(fm, axis=0))
    tm_cols = [2048 + g * 256 for g in range(4)]
    tm = [w_in[:, c0:c0 + 256].reshape(DC, 128, 256).transpose(1, 0, 2).reshape(128, 4096) for c0 in tm_cols]
    out["wintm"] = A(np.concatenate(tm, axis=0))
    out["wo"] = A(np.asarray(inp["w_out"][0]).reshape(DC, 128, DC, 128).transpose(2, 1, 0, 3).reshape(D, D))
    gains = np.zeros((128, 4 * DC), f)
    for gi, g in enumerate((inp["ffn1_norm"][0], inp["mix_norm"][0], inp["ffn2_norm"][0], inp["final_norm"])):
        gains[:, gi * DC:(gi + 1) * DC] = np.asarray(g).reshape(DC, 128).T
    out["gains"] = gains
    mu = np.zeros((128, 2 * ZC), f)
    mu[:, 0:ZC] = np.asarray(inp["mu_prev"][0]).reshape(ZC, 128).T
    mu[:, ZC:] = np.asarray(inp["mu_next"][0]).reshape(ZC, 128).T
    out["mu"] = mu
    rwp = np.zeros((128, 80), f)
    vecs = (inp["w0_f"][0], inp["w0_b"][0], inp["a0_f"][0], inp["a0_b"][0], inp["k_k"][0], inp["k_a"][0], None,
            np.asarray(inp["r_k"][0]).reshape(1024), inp["ln_x_w"][0], inp["ln_x_b"][0])
    for j, v in enumerate(vecs):
        if v is None:
            continue
        rwp[:, j::10] = np.asarray(v).reshape(8, 128).T
    out["rwp"] = rwp
    out["w2"] = A(np.concatenate([inp["w2_f"][0], inp["w2_b"][0]], axis=0))
    out["a2"] = A(np.concatenate([inp["a2_f"][0], inp["a2_b"][0]], axis=0))
    out["g2"] = A(inp["g2"][0])
    cst, rmask, etab = _const_tables()
    out["cst"], out["rmask"], out["etab"] = cst, rmask, etab
    return out


def _core_plan(SEG, x_prompt, x_sample):
    plan = []
    xp, xs = np.asarray(x_prompt), np.asarray(x_sample)
    nprompt_cores = xp.shape[0] * (xp.shape[1] // (NSEG * SEG))
    for b in range(xp.shape[0]):
        for part in range(xp.shape[1] // (NSEG * SEG)):
            assert xp.shape[1] == NSEG * SEG
            plan.append(([('p', b, 0), ('p', b, SEG)], 1.0))
    nb = xs.shape[0]
    assert xs.shape[1] == SEG
    rest = 8 - len(plan)
    two = nb - rest
    i = 0
    for c in range(rest):
        if c < two:
            plan.append(([('s', i, 0), ('s', i + 1, 0)], 0.0))
            i += 2
        elif i < nb:
            plan.append(([('s', i, 0), None], 0.0))
            i += 1
        else:
            plan.append(([None, None], 0.0))
    assert i == nb
    return plan


_NC_CACHE = {}


def run(inputs, SEG, debug=False, phases=('A', 'fix', 'att', 'rwkv', 'C')):
    wts = _prep_weights(inputs)
    xp, xs = np.asarray(inputs["x_prompt"], np.float32), np.asarray(inputs["x_sample"], np.float32)
    plan = _core_plan(SEG, xp, xs)
    in_maps = []
    for segs, lk in plan:
        x = np.zeros((NSEG * SEG, D), np.float32)
        for si, sdesc in enumerate(segs):
            if sdesc is None:
                continue
            kind, b, st = sdesc
            src = xp if kind == 'p' else xs
            x[si * SEG:(si + 1) * SEG] = src[b, st:st + SEG]
        m = dict(wts)
        m["x"] = x
        m["link"] = np.full((128, 1), lk, np.float32)
        in_maps.append(m)
    key = (SEG, debug, tuple(phases))
    if key not in _NC_CACHE:
        _NC_CACHE[key] = build(SEG, debug=debug, phases=phases)
    nc = _NC_CACHE[key]
    res = run_bass_kernel_spmd(nc, in_maps, core_ids=list(range(8)))
    yp = np.zeros(xp.shape, np.float32)
    ys = np.zeros(xs.shape, np.float32)
    for ci, (segs, lk) in enumerate(plan):
        y = res.results[ci]["y"]
        for si, sdesc in enumerate(segs):
            if sdesc is None:
                continue
            kind, b, st = sdesc
            (yp if kind == 'p' else ys)[b, st:st + SEG] = y[si * SEG:(si + 1) * SEG]
    return (yp, ys), res, plan


def kernel(**inputs):
    (yp, ys), _, _ = run(inputs, 4096)
    return (yp, ys)
```
